# Optimizing a Trainium2 kernel written in Bass

```python
import math
import jax, jax.numpy as jnp
from jax import lax
import numpy as np

D_MODEL = 1024
BATCH = 4
SEQ = 4096
DEPTH = 4

N_MIXERS = 2
N_SSD = (DEPTH + 1) // 2
N_FOX = DEPTH // 2
EPS = 1e-6

SSD_EXPAND = 2
SSD_D_INNER = SSD_EXPAND * D_MODEL
SSD_HEAD_DIM = 64
SSD_HEADS = SSD_D_INNER // SSD_HEAD_DIM
SSD_GROUPS = 4
SSD_HPG = SSD_HEADS // SSD_GROUPS
SSD_STATE = 128
SSD_CONV = 4
SSD_CHUNK = 128
SSD_CONV_DIM = SSD_D_INNER + 2 * SSD_GROUPS * SSD_STATE
SSD_IN_DIM = SSD_D_INNER + SSD_CONV_DIM + SSD_HEADS

FOX_HEAD_DIM = 64
FOX_HEADS = D_MODEL // FOX_HEAD_DIM
FOX_D = FOX_HEADS * FOX_HEAD_DIM
FOX_IN_DIM = 4 * FOX_D + FOX_HEADS
Q_BLOCK = 128

D_FF = 2816
FFN_CONV = 3

kernel_name = 'hybrid_ssd_fox_convglu_trunk'


def rmsnorm(x, g):
    xf = x.astype(jnp.float32)
    y = xf * lax.rsqrt(jnp.mean(xf * xf, axis=-1, keepdims=True) + EPS)
    return (y * g.astype(jnp.float32)).astype(x.dtype)


def causal_dwconv(x, w, b):
    k_w, c = w.shape
    y = lax.conv_general_dilated(x, w[:, None, :].astype(x.dtype), window_strides=(1,),
                                 padding=[(k_w - 1, 0)],
                                 dimension_numbers=('NWC', 'WIO', 'NWC'),
                                 feature_group_count=c)
    return y + b.astype(x.dtype)


def ssd_chunked(xs, dt, a, bm, cm):
    bsz, s_len = xs.shape[0], xs.shape[1]
    nc, l_c = s_len // SSD_CHUNK, SSD_CHUNK
    x = xs.reshape(bsz, nc, l_c, SSD_GROUPS, SSD_HPG, SSD_HEAD_DIM)
    dtc = dt.reshape(bsz, nc, l_c, SSD_GROUPS, SSD_HPG)
    bc = bm.reshape(bsz, nc, l_c, SSD_GROUPS, SSD_STATE)
    cc = cm.reshape(bsz, nc, l_c, SSD_GROUPS, SSD_STATE)
    da = jnp.moveaxis(dtc * a.reshape(SSD_GROUPS, SSD_HPG), 2, -1)
    a_cs = jnp.cumsum(da, axis=-1)
    xdt = x * dtc[..., None]
    causal = jnp.tril(jnp.ones((l_c, l_c), dtype=bool))
    decay = jnp.exp(jnp.where(causal, a_cs[..., :, None] - a_cs[..., None, :], -jnp.inf))
    cb = jnp.einsum('bclgn,bcsgn->bcgls', cc, bc)
    y_diag = jnp.einsum('bcgrls,bcsgrp->bclgrp', cb[:, :, :, None] * decay, xdt)
    decay_states = jnp.exp(a_cs[..., -1:] - a_cs)
    states = jnp.einsum('bclgn,bcgrl,bclgrp->bcgrpn', bc, decay_states, xdt).astype(jnp.float32)
    chunk_decay = jnp.exp(a_cs[..., -1])

    def step(h, inp):
        st, dec = inp
        return h * dec[..., None, None] + st, h

    h0 = jnp.zeros((bsz, SSD_GROUPS, SSD_HPG, SSD_HEAD_DIM, SSD_STATE), jnp.float32)
    _, prev = lax.scan(step, h0, (jnp.moveaxis(states, 1, 0), jnp.moveaxis(chunk_decay, 1, 0)))
    prev = jnp.moveaxis(prev, 0, 1)
    y_off = jnp.einsum('bclgn,bcgrpn,bcgrl->bclgrp', cc, prev, jnp.exp(a_cs))
    return (y_diag + y_off).reshape(bsz, s_len, SSD_HEADS, SSD_HEAD_DIM).astype(xs.dtype)


def mamba2_mixer(h, w_in, conv_w, conv_b, dt_bias, a_log, d_skip, norm_g, w_out):
    bsz, s_len, _ = h.shape
    proj = h @ w_in
    z = proj[..., :SSD_D_INNER]
    xbc = proj[..., SSD_D_INNER:SSD_D_INNER + SSD_CONV_DIM]
    dt_raw = proj[..., SSD_D_INNER + SSD_CONV_DIM:]
    xbc = jax.nn.silu(causal_dwconv(xbc, conv_w, conv_b))
    gn = SSD_GROUPS * SSD_STATE
    xs = xbc[..., :SSD_D_INNER].reshape(bsz, s_len, SSD_HEADS, SSD_HEAD_DIM)
    bm = xbc[..., SSD_D_INNER:SSD_D_INNER + gn].reshape(bsz, s_len, SSD_GROUPS, SSD_STATE)
    cm = xbc[..., SSD_D_INNER + gn:].reshape(bsz, s_len, SSD_GROUPS, SSD_STATE)
    dt = jax.nn.softplus((dt_raw + dt_bias).astype(jnp.float32))
    a = -jnp.exp(a_log.astype(jnp.float32))
    y = ssd_chunked(xs, dt, a, bm, cm) + xs * d_skip[:, None]
    y = y.reshape(bsz, s_len, SSD_D_INNER)
    yz = (y * jax.nn.silu(z)).astype(jnp.float32).reshape(bsz, s_len, SSD_GROUPS, SSD_D_INNER // SSD_GROUPS)
    yz = yz * lax.rsqrt(jnp.mean(yz * yz, axis=-1, keepdims=True) + EPS)
    y = (yz.reshape(bsz, s_len, SSD_D_INNER) * norm_g.astype(jnp.float32)).astype(h.dtype)
    return y @ w_out


def fox_attention(h, w_in, b_f, q_norm_g, k_norm_g, w_out):
    bsz, s_len, _ = h.shape
    proj = h @ w_in
    q = rmsnorm(proj[..., :FOX_D].reshape(bsz, s_len, FOX_HEADS, FOX_HEAD_DIM), q_norm_g)
    k = rmsnorm(proj[..., FOX_D:2 * FOX_D].reshape(bsz, s_len, FOX_HEADS, FOX_HEAD_DIM), k_norm_g)
    v = proj[..., 2 * FOX_D:3 * FOX_D].reshape(bsz, s_len, FOX_HEADS, FOX_HEAD_DIM)
    gate = proj[..., 3 * FOX_D:4 * FOX_D]
    log_f = jax.nn.log_sigmoid((proj[..., 4 * FOX_D:] + b_f).astype(jnp.float32))
    cum = jnp.transpose(jnp.cumsum(log_f, axis=1), (0, 2, 1))
    q, k, v = (jnp.transpose(t, (0, 2, 1, 3)) for t in (q, k, v))
    scale = FOX_HEAD_DIM ** -0.5
    outs = []
    for i in range(s_len // Q_BLOCK):
        qs, qe = i * Q_BLOCK, (i + 1) * Q_BLOCK
        sc = jnp.einsum('bhqd,bhkd->bhqk', q[:, :, qs:qe], k[:, :, :qe]).astype(jnp.float32) * scale
        sc = sc + cum[:, :, qs:qe, None] - cum[:, :, None, :qe]
        mask = (qs + jnp.arange(Q_BLOCK))[:, None] >= jnp.arange(qe)[None, :]
        p = jax.nn.softmax(jnp.where(mask, sc, -jnp.inf), axis=-1)
        outs.append(jnp.einsum('bhqk,bhkd->bhqd', p.astype(v.dtype), v[:, :, :qe]))
    o = jnp.transpose(jnp.concatenate(outs, axis=2), (0, 2, 1, 3)).reshape(bsz, s_len, FOX_D)
    return (o * jax.nn.sigmoid(gate)) @ w_out


def conv_glu_ffn(h, w_up, conv_w, conv_b, w_down):
    u = h @ w_up
    gate = causal_dwconv(u[..., :D_FF], conv_w, conv_b)
    return (jax.nn.silu(gate) * u[..., D_FF:]) @ w_down


def setup_inputs(seed: int = 0) -> dict:
    key = jax.random.key(seed)
    ks = jax.random.split(key, 24)
    f32 = jnp.float32

    def nrm(k, shape, scale):
        return jax.random.normal(k, shape, f32) * scale

    res_scale = (2 * DEPTH) ** -0.5
    dt_init = jnp.exp(jax.random.uniform(ks[4], (N_SSD, SSD_HEADS), f32, math.log(1e-3), math.log(1e-1)))
    return {
        'x': nrm(ks[0], (BATCH, SEQ, D_MODEL), 1.0),
        'mix_norm_g': 1.0 + nrm(ks[1], (DEPTH, D_MODEL), 0.02),
        'ffn_norm_g': 1.0 + nrm(ks[2], (DEPTH, D_MODEL), 0.02),
        'ssd_w_in': nrm(ks[3], (N_SSD, D_MODEL, SSD_IN_DIM), D_MODEL ** -0.5),
        'ssd_conv_w': nrm(ks[5], (N_SSD, SSD_CONV, SSD_CONV_DIM), SSD_CONV ** -0.5),
        'ssd_conv_b': nrm(ks[6], (N_SSD, SSD_CONV_DIM), 0.02),
        'ssd_dt_bias': dt_init + jnp.log(-jnp.expm1(-dt_init)),
        'ssd_a_log': jnp.log(jax.random.uniform(ks[7], (N_SSD, SSD_HEADS), f32, 1.0, 16.0)),
        'ssd_d': 1.0 + nrm(ks[8], (N_SSD, SSD_HEADS), 0.1),
        'ssd_norm_g': 1.0 + nrm(ks[9], (N_SSD, SSD_D_INNER), 0.02),
        'ssd_w_out': nrm(ks[10], (N_SSD, SSD_D_INNER, D_MODEL), SSD_D_INNER ** -0.5 * res_scale),
        'fox_w_in': nrm(ks[11], (N_FOX, D_MODEL, FOX_IN_DIM), D_MODEL ** -0.5),
        'fox_b_f': jax.random.uniform(ks[12], (N_FOX, FOX_HEADS), f32, 2.0, 6.0),
        'fox_q_norm_g': 1.0 + nrm(ks[13], (N_FOX, FOX_HEAD_DIM), 0.02),
        'fox_k_norm_g': 1.0 + nrm(ks[14], (N_FOX, FOX_HEAD_DIM), 0.02),
        'fox_w_out': nrm(ks[15], (N_FOX, FOX_D, D_MODEL), FOX_D ** -0.5 * res_scale),
        'ffn_w_up': nrm(ks[16], (DEPTH, D_MODEL, 2 * D_FF), D_MODEL ** -0.5),
        'ffn_conv_w': nrm(ks[17], (DEPTH, FFN_CONV, D_FF), FFN_CONV ** -0.5),
        'ffn_conv_b': nrm(ks[18], (DEPTH, D_FF), 0.02),
        'ffn_w_down': nrm(ks[19], (DEPTH, D_FF, D_MODEL), D_FF ** -0.5 * res_scale),
        'final_norm_g': 1.0 + nrm(ks[20], (D_MODEL,), 0.02),
    }


def reference(x, mix_norm_g, ffn_norm_g,
              ssd_w_in, ssd_conv_w, ssd_conv_b, ssd_dt_bias, ssd_a_log, ssd_d, ssd_norm_g, ssd_w_out,
              fox_w_in, fox_b_f, fox_q_norm_g, fox_k_norm_g, fox_w_out,
              ffn_w_up, ffn_conv_w, ffn_conv_b, ffn_w_down, final_norm_g):
    h = x
    for i in range(DEPTH):
        hn = rmsnorm(h, mix_norm_g[i])
        j = i // N_MIXERS
        if i % N_MIXERS == 0:
            h = h + mamba2_mixer(hn, ssd_w_in[j], ssd_conv_w[j], ssd_conv_b[j], ssd_dt_bias[j],
                                 ssd_a_log[j], ssd_d[j], ssd_norm_g[j], ssd_w_out[j])
        else:
            h = h + fox_attention(hn, fox_w_in[j], fox_b_f[j], fox_q_norm_g[j], fox_k_norm_g[j], fox_w_out[j])
        h = h + conv_glu_ffn(rmsnorm(h, ffn_norm_g[i]), ffn_w_up[i], ffn_conv_w[i], ffn_conv_b[i], ffn_w_down[i])
    return rmsnorm(h, final_norm_g)
```

```python
import numpy as np
from contextlib import ExitStack
import concourse.bass as bass
import concourse.mybir as mybir
from concourse.bass_utils import run_bass_kernel_spmd

F32 = mybir.dt.float32
BF16 = mybir.dt.bfloat16
ALU = mybir.AluOpType
AF = mybir.ActivationFunctionType

D_MODEL = 1024
DEPTH = 4
EPS = 1e-6
SSD_D_INNER = 2048
SSD_HEADS = 32
SSD_CONV = 4
SSD_CONV_DIM = 3072
SSD_IN_DIM = 5152
FOX_HEADS = 16
FOX_D = 1024
FOX_IN_DIM = 4112
D_FF = 2816
FFN_CONV = 3
TT = 512
NEG = -30000.0
KC = 8
TP = 2
FH = FOX_HEADS // TP
FFC = (D_FF // 128) // TP
SH = SSD_HEADS // TP
SG = 4 // TP
SXC = SH * 64 // 128
SNC = SXC + 2 * SG

COMPUTE = ('pe', 'act', 'dve', 'pool')


class Tk:
    __slots__ = ('name', 'w', 'r')

    def __init__(self, name):
        self.name = name
        self.w = None
        self.r = {}


class DSem:
    def __init__(self, handle, name):
        self.h = handle
        self.name = name
        self.count = 0


class Prog:
    def __init__(self, nc, es):
        self.nc = nc
        self.es = es
        self.ops = {e: [] for e in ('pe', 'act', 'dve', 'pool', 'sp')}
        self.esem = {e: es.enter_context(nc.semaphore("s_" + e)) for e in COMPUTE}
        self.nds = 0
        self.dcache = {}
        self.out_deps = []

    def dsem(self, name):
        if name not in self.dcache:
            self.nds += 1
            self.dcache[name] = DSem(self.es.enter_context(self.nc.semaphore("d_%s_%d" % (name, self.nds))), name)
        return self.dcache[name]

    def _deps(self, eng, reads, writes, strict=False):
        deps = set()
        for t in reads:
            if t.w is not None:
                deps.add(t.w)
        for t in writes:
            if t.w is not None:
                deps.add(t.w)
            for v in t.r.values():
                deps.add(v)
        if strict:
            return deps
        return {d for d in deps if not (d[0] == 'E' and d[1] == eng)}

    def op(self, eng, fn, reads=(), writes=(), strict=False):
        deps = self._deps(eng, reads, writes, strict)
        idx = len(self.ops[eng])
        import sys as _s
        self.ops[eng].append({'fn': fn, 'deps': deps, 'inc': False, 'dma': None, 'note': _s._getframe(1).f_lineno})
        me = ('E', eng, idx)
        for t in reads:
            t.r[eng] = me
        for t in writes:
            t.w = me
            t.r = {}
        return me

    def dma(self, q, out, in_, sem, reads=(), writes=()):
        deps = self._deps(q, reads, writes, strict=True)
        sem.count += 16
        me = ('D', sem, sem.count)
        self.ops[q].append({'fn': (lambda e, o=out, i=in_: e.dma_start(out=o, in_=i)), 'deps': deps,
                            'inc': False, 'dma': sem, 'tok': me})
        for t in reads:
            t.r['D' + sem.name + str(id(sem))] = me
        for t in writes:
            t.w = me
            t.r = {}
        return me

    def cc(self, in_ap, out_ap, groups, slot, reads=(), writes=()):
        deps = self._deps('pool', reads, writes, strict=True)
        sem = self.dsem("cc%d" % slot)
        sem.count += 1
        me = ('D', sem, sem.count)

        def fn(e, i=in_ap, o=out_ap):
            return e.collective_compute("AllReduce", ALU.add, replica_groups=groups, ins=[i], outs=[o])
        self.ops['pool'].append({'fn': fn, 'deps': deps, 'inc': False, 'dma': None, 'cc': sem, 'tok': me})
        for t in reads:
            t.r['C' + str(id(sem))] = me
        for t in writes:
            t.w = me
            t.r = {}
        return me

    def dma_acc(self, out, in_, sem, reads=(), writes=()):
        deps = self._deps('pool', reads, writes, strict=True)
        sem.count += 16
        me = ('D', sem, sem.count)
        self.ops['pool'].append({'fn': (lambda e, o=out, i=in_: e.dma_start(out=o, in_=i, accum_op=ALU.add)), 'deps': deps,
                                 'inc': False, 'dma': sem, 'tok': me})
        for t in reads:
            t.r['D' + sem.name + str(id(sem))] = me
        for t in writes:
            t.w = me
            t.r = {}
        return me

    def barrier(self):
        toks = set()
        for e in COMPUTE:
            if self.ops[e]:
                for i in range(len(self.ops[e]) - 1, -1, -1):
                    if self.ops[e][i]['fn'] is not None and self.ops[e][i].get('tok') is None:
                        toks.add(('E', e, i))
                        break
        for sem in self.dcache.values():
            if sem.count > 0:
                toks.add(('D', sem, sem.count))
        for q in self.ops:
            deps = {d for d in toks if not (d[0] == 'E' and d[1] == q)}
            self.ops[q].append({'fn': None, 'deps': deps, 'inc': False, 'dma': None})

    def finalize_wait(self, q, toks):
        self.ops[q].append({'fn': None, 'deps': set(toks), 'inc': False, 'dma': None})

    def check_deadlock(self):
        pos = {e: 0 for e in self.ops}
        done = set()
        dtok = {}
        for e, lst in self.ops.items():
            cnts = {}
            for i, o in enumerate(lst):
                sem = o.get('cc') or o.get('dma')
                if sem is not None:
                    o.setdefault('_tok', None)
        progress = True
        while progress:
            progress = False
            for e, lst in self.ops.items():
                while pos[e] < len(lst):
                    o = lst[pos[e]]
                    ok = True
                    for d in o['deps']:
                        key = (d[0], d[1], d[2]) if d[0] == 'E' else ('D', id(d[1]), d[2])
                        if key not in done:
                            ok = False
                            break
                    if not ok:
                        break
                    done.add(('E', e, pos[e]))
                    if o.get('tok') is not None:
                        t = o['tok']
                        done.add(('D', id(t[1]), t[2]))
                    pos[e] += 1
                    progress = True
        stuck = {e: pos[e] for e in self.ops if pos[e] < len(self.ops[e])}
        if stuck:
            msg = []
            for e, p in stuck.items():
                o = self.ops[e][p]
                missing = []
                for d in o['deps']:
                    key = (d[0], d[1], d[2]) if d[0] == 'E' else ('D', id(d[1]), d[2])
                    if key not in done:
                        missing.append((d[0], d[1] if d[0] == 'E' else d[1].name, d[2]))
                msg.append("%s@%d/%d waits %s [%s]" % (e, p, len(self.ops[e]), missing, o.get('note')))
            raise RuntimeError("DEADLOCK: " + " | ".join(msg))

    def materialize(self, block):
        for e, lst in self.ops.items():
            for o in lst:
                for d in o['deps']:
                    if d[0] == 'E':
                        self.ops[d[1]][d[2]]['inc'] = True
        vals = {}
        for e in COMPUTE:
            c = 0
            for i, o in enumerate(self.ops[e]):
                if o['inc']:
                    c += 1
                    vals[(e, i)] = c
        self.vals = vals

        def run(eng_name, e):
            waited = {}
            for o in self.ops[eng_name]:
                for d in sorted(o['deps'], key=lambda d: (d[0], str(d[1]) if d[0] == 'E' else d[1].name, d[2])):
                    if d[0] == 'E':
                        key = ('E', d[1])
                        v = vals[(d[1], d[2])]
                        sh = self.esem[d[1]]
                    else:
                        key = ('D', id(d[1]))
                        v = d[2]
                        sh = d[1].h
                    if waited.get(key, 0) >= v:
                        continue
                    waited[key] = v
                    e.wait_ge(sh, v)
                if o['fn'] is None:
                    continue
                ins = o['fn'](e)
                if o.get('cc') is not None:
                    ins.then_inc(o['cc'].h)
                elif o['dma'] is not None:
                    ins.then_inc(o['dma'].h, 16)
                elif o['inc']:
                    ins.then_inc(self.esem[eng_name], 1)

        block.tensor(lambda e: run('pe', e))
        block.scalar(lambda e: run('act', e))
        block.vector(lambda e: run('dve', e))
        block.gpsimd(lambda e: run('pool', e))
        block.sync(lambda e: run('sp', e))


class Arena:
    def __init__(self, ap_f32, nwords):
        self.ap = ap_f32
        self.n = nwords
        self.top = 0
        self.peak = 0

    def mark(self):
        return self.top

    def release(self, m):
        self.top = m

    def alloc(self, shape, dtype, parts=128):
        n = int(np.prod(shape))
        words = n if dtype == F32 else (n + 1) // 2
        a = self.top
        self.top += words
        self.peak = max(self.peak, self.top)
        assert self.top <= self.n, "arena overflow %d > %d" % (self.top, self.n)
        v = self.ap[0:parts, a:a + words]
        if dtype != F32:
            v = v.bitcast(dtype)[:, 0:n]
        if len(shape) == 2:
            v = v.rearrange("p (a b) -> p a b", a=shape[0])
        elif len(shape) == 3:
            v = v.rearrange("p (a b c) -> p a b c", a=shape[0], b=shape[1])
        return v


class CL:
    def __init__(self):
        self.off = {}
        self.n = 0

    def add(self, name, width):
        self.off[name] = (self.n, width)
        self.n += width


def const_layout():
    c = CL()
    c.add('ident', 128)
    c.add('U', 128)
    c.add('maskT', 128)
    c.add('bd64', 128)
    c.add('Ubar', 128)
    for i in range(DEPTH):
        c.add('mixg%d' % i, KC)
        c.add('ffng%d' % i, KC)
        c.add('fcw%d' % i, FFC * 3)
        c.add('fcb%d' % i, FFC)
    c.add('fing', KC)
    for j in range(2):
        c.add('scw%d' % j, SNC * 4)
        c.add('scb%d' % j, SNC)
        c.add('dtb%d' % j, 1)
        c.add('alog%d' % j, SH)
        c.add('dsk%d' % j, SH)
        c.add('fbf%d' % j, 1)
        c.add('fqg%d' % j, 1)
        c.add('fkg%d' % j, 1)
    return c


def build_consts(inp, r):
    c = const_layout()
    A = np.zeros((128, c.n), np.float32)

    def put(name, arr):
        o, w = c.off[name]
        arr = np.asarray(arr, np.float32)
        A[:arr.shape[0], o:o + w] = arr.reshape(arr.shape[0], w)

    put('ident', np.eye(128))
    put('U', np.triu(np.ones((128, 128))))
    put('maskT', np.where(np.arange(128)[None, :] >= np.arange(128)[:, None], 0.0, NEG))
    put('bd64', np.kron(np.eye(2), np.ones((64, 64))))
    put('Ubar', np.tril(np.ones((128, 128)), -1))
    fsl = slice(r * FFC * 128, (r + 1) * FFC * 128)
    for i in range(DEPTH):
        put('mixg%d' % i, inp['mix_norm_g'][i].reshape(KC, 128).T)
        put('ffng%d' % i, inp['ffn_norm_g'][i].reshape(KC, 128).T)
        put('fcw%d' % i, inp['ffn_conv_w'][i][:, fsl].reshape(3, FFC, 128).transpose(2, 1, 0).reshape(128, FFC * 3))
        put('fcb%d' % i, inp['ffn_conv_b'][i][fsl].reshape(FFC, 128).T)
    put('fing', inp['final_norm_g'].reshape(KC, 128).T)
    xs_ = np.r_[r * SH * 64:(r + 1) * SH * 64, 2048 + r * SG * 128:2048 + (r + 1) * SG * 128, 2560 + r * SG * 128:2560 + (r + 1) * SG * 128]
    hsl = slice(r * SH, (r + 1) * SH)
    for j in range(2):
        put('scw%d' % j, inp['ssd_conv_w'][j][:, xs_].reshape(4, SNC, 128).transpose(2, 1, 0).reshape(128, SNC * 4))
        put('scb%d' % j, inp['ssd_conv_b'][j][xs_].reshape(SNC, 128).T)
        put('dtb%d' % j, inp['ssd_dt_bias'][j][hsl].reshape(SH, 1))
        put('alog%d' % j, np.broadcast_to(inp['ssd_a_log'][j][hsl][None, :], (128, SH)))
        put('dsk%d' % j, np.broadcast_to(inp['ssd_d'][j][hsl][None, :], (128, SH)))
        put('fbf%d' % j, inp['fox_b_f'][j][r * FH:(r + 1) * FH].reshape(FH, 1))
        put('fqg%d' % j, np.tile(inp['fox_q_norm_g'][j], 2).reshape(128, 1))
        put('fkg%d' % j, np.tile(inp['fox_k_norm_g'][j], 2).reshape(128, 1))
    return c, A


def slice_weights(inp, r):
    w = {}
    f0, f1 = r * FFC * 128, (r + 1) * FFC * 128
    w['ffn_w_up'] = np.ascontiguousarray(np.concatenate([inp['ffn_w_up'][:, :, f0:f1], inp['ffn_w_up'][:, :, D_FF + f0:D_FF + f1]], axis=2))
    w['ffn_w_down'] = np.ascontiguousarray(inp['ffn_w_down'][:, f0:f1, :])
    q0, q1 = r * FH * 64, (r + 1) * FH * 64
    fi = inp['fox_w_in']
    w['fox_w_in'] = np.ascontiguousarray(np.concatenate([fi[:, :, q0:q1], fi[:, :, 1024 + q0:1024 + q1], fi[:, :, 2048 + q0:2048 + q1],
                                                          fi[:, :, 3072 + q0:3072 + q1], fi[:, :, 4096 + r * FH:4096 + (r + 1) * FH]], axis=2))
    w['fox_w_out'] = np.ascontiguousarray(inp['fox_w_out'][:, q0:q1, :])
    x0, x1 = r * SH * 64, (r + 1) * SH * 64
    b0, b1 = r * SG * 128, (r + 1) * SG * 128
    si = inp['ssd_w_in']
    w['ssd_w_in'] = np.ascontiguousarray(np.concatenate([si[:, :, x0:x1], si[:, :, 2048 + x0:2048 + x1], si[:, :, 4096 + b0:4096 + b1],
                                                          si[:, :, 4608 + b0:4608 + b1], si[:, :, 5120 + r * SH:5120 + (r + 1) * SH]], axis=2))
    w['ssd_w_out'] = np.ascontiguousarray(inp['ssd_w_out'][:, x0:x1, :])
    w['sng'] = np.ascontiguousarray(np.broadcast_to(inp['ssd_norm_g'][:, None, x0:x1], (2, 128, SH * 64)), dtype=np.float32)
    return w


WNAMES = ['ssd_w_in', 'ssd_w_out', 'fox_w_in', 'fox_w_out', 'ffn_w_up', 'ffn_w_down']
SSD_LIN = 2 * SH * 64 + 2 * SG * 128 + SH
FOX_LIN = 4 * FH * 64 + FH
WSHAPES = {'ssd_w_in': (2, 1024, SSD_LIN), 'ssd_w_out': (2, SH * 64, 1024), 'fox_w_in': (2, 1024, FOX_LIN),
           'fox_w_out': (2, FH * 64, 1024), 'ffn_w_up': (4, 1024, 2 * FFC * 128), 'ffn_w_down': (4, FFC * 128, 1024)}


def build_program(seq, layers, groups, debug=None):
    NCH = seq // TT
    nc = bass.Bass("TRN2", target_bir_lowering=False)
    cl = const_layout()
    x_d = nc.dram_tensor("x", [seq, D_MODEL], F32, kind="ExternalInput").ap()
    c_d = nc.dram_tensor("consts", [128, cl.n], F32, kind="ExternalInput").ap()
    w_d = {n: nc.dram_tensor(n, list(WSHAPES[n]), F32, kind="ExternalInput").ap() for n in WNAMES}
    sng_d = nc.dram_tensor("sng", [2, 128, SH * 64], F32, kind="ExternalInput").ap()
    out_d = nc.dram_tensor("out", [seq, D_MODEL], F32, kind="ExternalOutput").ap()
    wb_d = {n: nc.dram_tensor(n + "_b", list(WSHAPES[n]), BF16).ap() for n in WNAMES}
    hT_d = nc.dram_tensor("hT_d", [D_MODEL, seq], F32).ap()
    part_t = [[nc.dram_tensor("part%d_%d" % (p, c), [D_MODEL, TT], F32) for c in range(NCH)] for p in range(2)]
    red_t = [[nc.dram_tensor("red%d_%d" % (p, c), [D_MODEL, TT], F32) for c in range(NCH)] for p in range(2)]
    qa_d = nc.dram_tensor("qa_d", [FH, 70, seq], BF16).ap()
    ka_d = nc.dram_tensor("ka_d", [FH, 70, seq], BF16).ap()
    v_d = nc.dram_tensor("v_d", [seq, FH * 65], BF16).ap()
    sg_d = nc.dram_tensor("sg_d", [FH * 64, seq], BF16).ap()
    ot_d = nc.dram_tensor("ot_d", [FH * 64, seq], BF16).ap()
    dbg_d = None
    if debug:
        dbg_d = nc.dram_tensor("dbg", list(debug), F32, kind="ExternalOutput").ap()

    es = ExitStack()
    with es:
        P = Prog(nc, es)
        NW = 48900
        arena_t = es.enter_context(nc.sbuf_tensor("arena", [128, NW], F32))
        AR = Arena(arena_t[:, :], NW)
        banks = [es.enter_context(nc.psum_tensor("bank%d" % i, [128, 512], F32)) for i in range(8)]
        bank_tk = [Tk("bank%d" % i) for i in range(8)]
        ACC = [0, 1, 2, 3]
        ROT = [4, 5, 6]
        MISC = 7
        rot_i = [0]

        def next_rot():
            b = ROT[rot_i[0] % 3]
            rot_i[0] += 1
            return b

        cst = AR.alloc([cl.n], F32)
        cst_tk = Tk("cst")
        s_c = P.dsem("cst")
        P.dma('sp', cst, c_d[:, :], s_c, writes=[cst_tk])

        def C(name, parts=128):
            o, w = cl.off[name]
            return cst[0:parts, o:o + w]

        ident_f = C('ident')
        ident_b = AR.alloc([128], BF16)
        ones_b = AR.alloc([128], BF16)
        nones_b = AR.alloc([128], BF16)
        ones_f = AR.alloc([128], F32)
        U_b = AR.alloc([128], BF16)
        mask_b = AR.alloc([4, 128], BF16)
        k_tk = Tk("konst")
        P.op('dve', lambda e: e.tensor_copy(out=ident_b, in_=ident_f), reads=[cst_tk], writes=[k_tk])
        P.op('dve', lambda e: e.memset(ones_b, 1.0), writes=[k_tk])
        P.op('dve', lambda e: e.memset(nones_b, -1.0), writes=[k_tk])
        P.op('dve', lambda e: e.memset(ones_f, 1.0), writes=[k_tk])
        P.op('dve', lambda e: e.tensor_copy(out=U_b, in_=C('U')), reads=[cst_tk], writes=[k_tk])
        for r in range(4):
            P.op('dve', lambda e, r=r: e.tensor_copy(out=mask_b[:, r, :], in_=C('maskT')), reads=[cst_tk], writes=[k_tk])

        wtk = {}
        used = set()
        for L in layers:
            kind, j = L[:3], int(L[3:])
            if kind == 'ssd':
                used |= {('ssd_w_in', j), ('ssd_w_out', j)}
            elif kind == 'fox':
                used |= {('fox_w_in', j), ('fox_w_out', j)}
            elif kind == 'ffn':
                used |= {('ffn_w_up', j), ('ffn_w_down', j)}
        order = []
        for L in layers:
            kind, j = L[:3], int(L[3:])
            names = {'ssd': ['ssd_w_in', 'ssd_w_out'], 'fox': ['fox_w_in', 'fox_w_out'], 'ffn': ['ffn_w_up', 'ffn_w_down']}[kind]
            for n in names:
                if (n, j) not in order:
                    order.append((n, j))
        for (n, j) in order:
            t = Tk("w_%s_%d" % (n, j))
            wtk[(n, j)] = t
            s = P.dsem("wc_%s_%d" % (n, j))
            K = WSHAPES[n][1]
            half = K // 2
            P.dma('pool', wb_d[n][j, 0:half, :], w_d[n][j, 0:half, :], s, writes=[t])
            P.dma('pool', wb_d[n][j, half:K, :], w_d[n][j, half:K, :], s, writes=[t])

        hc = [AR.alloc([KC, TT], F32) for _ in range(2)]
        hc_tk = [Tk("hc0"), Tk("hc1")]
        hc_sem = [P.dsem("hc0"), P.dsem("hc1")]
        hnb = [AR.alloc([KC, TT], BF16) for _ in range(2)]
        hnb_tk = [Tk("hn0"), Tk("hn1")]
        sq = AR.alloc([KC, TT], BF16)
        sq_tk = Tk("sq")
        rstd = AR.alloc([TT], F32)
        rstd_tk = Tk("rstd")
        pst = AR.alloc([KC, TT], F32)
        pst_tk = Tk("pst")
        pst_sem = P.dsem("pst")
        NWS = 3
        wslab = [AR.alloc([KC, 512], BF16) for _ in range(NWS)]
        ws_tk = [Tk("ws%d" % i) for i in range(NWS)]
        ws_sem = [P.dsem("ws%d" % i) for i in range(NWS)]
        ws_i = [0]
        NW2 = 2
        w2slab = [AR.alloc([4, 512], BF16) for _ in range(NW2)]
        w2_tk = [Tk("w2%d" % i) for i in range(NW2)]
        w2_sem = [P.dsem("w2%d" % i) for i in range(NW2)]
        w2_i = [0]
        hT_tk = [Tk("hTd%d" % c) for c in range(NCH)]
        part_tk = [[Tk("part%d_%d" % (p, c)) for c in range(NCH)] for p in range(2)]
        red_tk = [[Tk("red%d_%d" % (p, c)) for c in range(NCH)] for p in range(2)]
        st_sem = P.dsem("store")
        evac_i = [0]
        pend = [None]
        sub_i = [0]

        def evac_copy(out, in_, reads, writes):
            evac_i[0] += 1
            if evac_i[0] % 2:
                P.op('act', lambda e: e.copy(out=out, in_=in_), reads=reads, writes=writes)
            else:
                P.op('dve', lambda e: e.tensor_copy(out=out, in_=in_), reads=reads, writes=writes)

        def load_ws(wname, j, cols):
            i = ws_i[0] % NWS
            ws_i[0] += 1
            for (c0, ncol, dst) in cols:
                src = wb_d[wname][j, :, c0:c0 + ncol].rearrange("(k p) n -> p k n", p=128)
                P.dma('sp', wslab[i][:, :, dst:dst + ncol], src, ws_sem[i], reads=[wtk[(wname, j)]], writes=[ws_tk[i]])
            return wslab[i], ws_tk[i]

        def load_hc(c, buf, pp):
            src = hT_d[:, c * TT:(c + 1) * TT].rearrange("(k p) t -> p k t", p=128)
            P.dma('sp', hc[buf], src, hc_sem[buf], reads=[hT_tk[c]], writes=[hc_tk[buf]])
            if pp is not None:
                P.dma_acc(hc[buf], red_t[pp][c].ap().rearrange("(k p) t -> p k t", p=128), hc_sem[buf], reads=[red_tk[pp][c]], writes=[hc_tk[buf]])
                P.dma('pool', src, hc[buf], st_sem, reads=[hc_tk[buf]], writes=[hT_tk[c]])

        def store_hc(c, buf):
            dst = hT_d[:, c * TT:(c + 1) * TT].rearrange("(k p) t -> p k t", p=128)
            P.dma('pool', dst, hc[buf], st_sem, reads=[hc_tk[buf]], writes=[hT_tk[c]])

        def rmsnorm_chunk(buf, gname, hb):
            h = hc[buf]
            hn, hn_tk = hnb[hb], hnb_tk[hb]
            P.op('act', lambda e: e.activation(out=sq, in_=h, func=AF.Square), reads=[hc_tk[buf]], writes=[sq_tk])
            bk = banks[MISC]
            for kc in range(KC):
                P.op('pe', lambda e, kc=kc: e.matmul(bk[:, :], lhsT=ones_b, rhs=sq[:, kc, :], start=(kc == 0), stop=(kc == KC - 1)),
                     reads=[sq_tk, k_tk], writes=[bank_tk[MISC]])
            P.op('act', lambda e: e.activation(out=rstd, in_=bk[:, :], func=AF.Ln, scale=1.0 / D_MODEL, bias=epsc[:, 0:1]),
                 reads=[bank_tk[MISC], cst_tk, k_tk], writes=[rstd_tk])
            P.op('act', lambda e: e.activation(out=rstd, in_=rstd, func=AF.Exp, scale=-0.5), reads=[rstd_tk], writes=[rstd_tk], strict=True)
            g = C(gname)
            for kc in range(KC):
                P.op('dve', lambda e, kc=kc: e.scalar_tensor_tensor(out=hn[:, kc, :], in0=h[:, kc, :], scalar=g[:, kc:kc + 1], in1=rstd,
                                                                  op0=ALU.mult, op1=ALU.mult),
                     reads=[hc_tk[buf], rstd_tk, cst_tk], writes=[hn_tk])

        def proj_T(slab, slab_tk, col0, M, rhs_fn, bank, n=TT, extra_reads=()):
            for kc in range(KC):
                r_ = rhs_fn(kc)
                P.op('pe', lambda e, kc=kc, r_=r_: e.matmul(banks[bank][0:M, 0:n], lhsT=slab[:, kc, col0:col0 + M], rhs=r_,
                                                            start=(kc == 0), stop=(kc == KC - 1)),
                     reads=[slab_tk] + list(extra_reads), writes=[bank_tk[bank]])

        pending_finish = [None]

        def flush_finish():
            if pending_finish[0] is not None:
                f = pending_finish[0]
                pending_finish[0] = None
                f()

        def dense_out(wname, j, nfc, rhs_fn, rhs_tks, c, par, alt=False):
            flush_finish()
            for half in range(2):
                ngrp = (nfc + 3) // 4
                for gi in range(ngrp):
                    f0 = gi * 4
                    nf = min(4, nfc - f0)
                    i = w2_i[0] % NW2
                    w2_i[0] += 1
                    src = wb_d[wname][j, f0 * 128:(f0 + nf) * 128, half * 512:(half + 1) * 512].rearrange("(f p) n -> p f n", p=128)
                    P.dma('sp', w2slab[i][:, 0:nf, :], src, w2_sem[i], reads=[wtk[(wname, j)]], writes=[w2_tk[i]])
                    for f in range(nf):
                        fc = f0 + f
                        for dmi in range(4):
                            bk_ = (4 + dmi) if (alt and half == 1) else ACC[dmi]
                            r_ = rhs_fn(fc)
                            P.op('pe', lambda e, i=i, f=f, fc=fc, dmi=dmi, bk_=bk_, r_=r_: e.matmul(
                                banks[bk_][:, :], lhsT=w2slab[i][:, f, dmi * 128:(dmi + 1) * 128], rhs=r_,
                                start=(fc == 0), stop=(fc == nfc - 1)),
                                 reads=[w2_tk[i]] + list(rhs_tks), writes=[bank_tk[bk_]])
                for dmi in range(4):
                    kc = half * 4 + dmi
                    bk_ = (4 + dmi) if (alt and half == 1) else ACC[dmi]
                    evac_copy(pst[:, kc, :], banks[bk_][:, :], [bank_tk[bk_]], [pst_tk])
            def finish():
                P.dma('sp', part_t[par][c].ap().rearrange("(k p) t -> p k t", p=128), pst, pst_sem, reads=[pst_tk], writes=[part_tk[par][c]])
                P.cc(part_t[par][c].ap().opt(), red_t[par][c].ap().opt(), groups, c % 8, reads=[part_tk[par][c]], writes=[red_tk[par][c]])
            pending_finish[0] = finish

        def conv_silu(bank, M, K, wcol_fn, bcol, pc, pc_tk, acc, acc_tk, tails, tails_tk, out, out_reads, out_writes):
            H = K - 1
            P.op('act', lambda e: e.copy(out=pc[0:M, H:H + TT], in_=banks[bank][0:M, :]), reads=[bank_tk[bank]], writes=[pc_tk])
            P.op('pool', lambda e: e.tensor_copy(out=pc[0:M, 0:H], in_=tails[0:M, :]), reads=[tails_tk], writes=[pc_tk])
            P.op('dve', lambda e: e.tensor_scalar(out=acc[0:M, :], in0=pc[0:M, 0:TT], scalar1=wcol_fn(0), scalar2=bcol, op0=ALU.mult, op1=ALU.add),
                 reads=[pc_tk, cst_tk], writes=[acc_tk])
            for k in range(1, K):
                P.op('dve', lambda e, k=k: e.scalar_tensor_tensor(out=acc[0:M, :], in0=pc[0:M, k:k + TT], scalar=wcol_fn(k), in1=acc[0:M, :],
                                                                  op0=ALU.mult, op1=ALU.add),
                     reads=[pc_tk, cst_tk, acc_tk], writes=[acc_tk])
            P.op('pool', lambda e: e.tensor_copy(out=tails[0:M, :], in_=pc[0:M, TT:TT + H]), reads=[pc_tk], writes=[tails_tk])
            P.op('act', lambda e: e.activation(out=out, in_=acc[0:M, :], func=AF.Silu), reads=[acc_tk] + list(out_reads), writes=list(out_writes))

        epsc = AR.alloc([1], F32)
        P.op('dve', lambda e: e.memset(epsc, EPS), writes=[k_tk])
        eps64 = AR.alloc([1], F32)
        P.op('dve', lambda e: e.memset(eps64, 64 * EPS), writes=[k_tk])
        onec = AR.alloc([1], F32)
        P.op('dve', lambda e: e.memset(onec, 1.0), writes=[k_tk])
        bd64_b = AR.alloc([128], BF16)
        P.op('dve', lambda e: e.tensor_copy(out=bd64_b, in_=C('bd64')), reads=[cst_tk], writes=[k_tk])

        def phase_load():
            m = AR.mark()
            xs = AR.alloc([4, D_MODEL], F32)
            xs_tk = Tk("xs")
            xs_sem = P.dsem("xs")
            for c in range(NCH):
                buf = c % 2
                P.dma('sp', xs, x_d[c * TT:(c + 1) * TT, :].rearrange("(j p) d -> p j d", p=128), xs_sem, writes=[xs_tk])
                for kc in range(KC):
                    b = next_rot()
                    for j in range(4):
                        P.op('pe', lambda e, b=b, j=j, kc=kc: e.transpose(banks[b][:, j * 128:(j + 1) * 128], xs[:, j, kc * 128:(kc + 1) * 128], ident_f),
                             reads=[xs_tk, cst_tk], writes=[bank_tk[b]])
                    evac_copy(hc[buf][:, kc, :], banks[b][:, :], [bank_tk[b]], [hc_tk[buf]])
                store_hc(c, buf)
            AR.release(m)

        def phase_final():
            P.barrier()
            m = AR.mark()
            pp = pend[0]
            ho = AR.alloc([KC, TT], F32)
            ho_tk = Tk("ho")
            ys = AR.alloc([4, D_MODEL], F32)
            ys_tk = Tk("ys")
            o_sem = P.dsem("out")
            g = C('fing')
            toks = []
            load_hc(0, 0, pp)
            for c in range(NCH):
                buf = c % 2
                if c + 1 < NCH:
                    load_hc(c + 1, 1 - buf, pp)
                h = hc[buf]
                P.op('act', lambda e, h=h: e.activation(out=sq, in_=h, func=AF.Square), reads=[hc_tk[buf]], writes=[sq_tk])
                bk = banks[MISC]
                for kc in range(KC):
                    P.op('pe', lambda e, kc=kc: e.matmul(bk[:, :], lhsT=ones_b, rhs=sq[:, kc, :], start=(kc == 0), stop=(kc == KC - 1)),
                         reads=[sq_tk, k_tk], writes=[bank_tk[MISC]])
                P.op('act', lambda e: e.activation(out=rstd, in_=bk[:, :], func=AF.Ln, scale=1.0 / D_MODEL, bias=epsc[:, 0:1]),
                     reads=[bank_tk[MISC], k_tk], writes=[rstd_tk])
                P.op('act', lambda e: e.activation(out=rstd, in_=rstd, func=AF.Exp, scale=-0.5), reads=[rstd_tk], writes=[rstd_tk], strict=True)
                for kc in range(KC):
                    P.op('dve', lambda e, kc=kc, h=h: e.scalar_tensor_tensor(out=ho[:, kc, :], in0=h[:, kc, :], scalar=g[:, kc:kc + 1], in1=rstd,
                                                                             op0=ALU.mult, op1=ALU.mult),
                         reads=[hc_tk[buf], rstd_tk, cst_tk], writes=[ho_tk])
                for j in range(4):
                    for half in range(2):
                        b = next_rot()
                        for q in range(4):
                            kc = half * 4 + q
                            P.op('pe', lambda e, b=b, q=q, kc=kc, j=j: e.transpose(banks[b][:, q * 128:(q + 1) * 128], ho[:, kc, j * 128:(j + 1) * 128], ident_f),
                                 reads=[ho_tk, cst_tk], writes=[bank_tk[b]])
                        evac_copy(ys[:, j, half * 512:(half + 1) * 512], banks[b][:, :], [bank_tk[b]], [ys_tk])
                t = P.dma('pool', out_d[c * TT:(c + 1) * TT, :].rearrange("(j p) d -> p j d", p=128), ys, o_sem, reads=[ys_tk])
                toks.append(t)
            P.finalize_wait('pool', [toks[-1]])
            AR.release(m)

        def phase_ffn(i):
            P.barrier()
            m = AR.mark()
            pp = pend[0]
            par = sub_i[0] % 2
            sub_i[0] += 1
            aT = AR.alloc([FFC, TT], BF16)
            aT_tk = [Tk("aT%d" % f) for f in range(FFC)]
            pcs = [AR.alloc([TT + 4], F32) for _ in range(2)]
            pc_tks = [Tk("pc0"), Tk("pc1")]
            accs = [AR.alloc([TT], F32) for _ in range(2)]
            acc_tks = [Tk("acc0"), Tk("acc1")]
            gs = [AR.alloc([TT], BF16) for _ in range(2)]
            gs_tks = [Tk("gs0"), Tk("gs1")]
            tails = AR.alloc([FFC, 2], F32)
            tails_tk = [Tk("tl%d" % f) for f in range(FFC)]
            P.op('pool', lambda e: e.memset(tails, 0.0), writes=tails_tk)
            fcw = C('fcw%d' % i)
            fcb = C('fcb%d' % i)
            GW = FFC * 128
            load_hc(0, 0, pp)
            rmsnorm_chunk(0, 'ffng%d' % i, 0)
            for c in range(NCH):
                buf = c % 2
                hn, hn_tk = hnb[buf], hnb_tk[buf]
                if c + 1 < NCH:
                    load_hc(c + 1, 1 - buf, pp)
                j0 = 0
                while j0 < FFC:
                    nj = min(2, FFC - j0)
                    slab, stk = load_ws('ffn_w_up', i, [(j0 * 128, nj * 128, 0), (GW + j0 * 128, nj * 128, 256)])
                    for u in range(nj):
                        j = j0 + u
                        pi = j % 2
                        b = next_rot()
                        proj_T(slab, stk, u * 128, 128, lambda kc: hn[:, kc, :], b, extra_reads=[hn_tk])
                        conv_silu(b, 128, 3, lambda k, j=j: fcw[:, j * 3 + k:j * 3 + k + 1], fcb[:, j:j + 1], pcs[pi], pc_tks[pi], accs[pi], acc_tks[pi],
                                  tails[:, j, :], tails_tk[j], gs[pi], [], [gs_tks[pi]])
                    for u in range(nj):
                        j = j0 + u
                        pi = j % 2
                        b = next_rot()
                        proj_T(slab, stk, 256 + u * 128, 128, lambda kc: hn[:, kc, :], b, extra_reads=[hn_tk])
                        P.op('dve', lambda e, j=j, pi=pi, b=b: e.tensor_tensor(out=aT[:, j, :], in0=gs[pi], in1=banks[b][:, :], op=ALU.mult),
                             reads=[gs_tks[pi], bank_tk[b]], writes=[aT_tk[j]])
                    j0 += nj
                    if j0 >= 4:
                        flush_finish()
                if c + 1 < NCH:
                    rmsnorm_chunk(1 - buf, 'ffng%d' % i, 1 - buf)
                dense_out('ffn_w_down', i, FFC, lambda fc: aT[:, fc, :], aT_tk, c, par)
            flush_finish()
            pend[0] = par
            AR.release(m)

        def phase_fox(jl, li):
            P.barrier()
            m = AR.mark()
            pp = pend[0]
            par = sub_i[0] % 2
            sub_i[0] += 1
            fw = 'fox_w_in'
            HW = FH * 64
            NPR = FH // 2
            qk_st = [AR.alloc([NPR, TT], BF16) for _ in range(2)]
            qk_tk = [Tk("qkst0"), Tk("qkst1")]
            qk_sem = [P.dsem("qkst0"), P.dsem("qkst1")]
            sqh = [AR.alloc([TT], BF16) for _ in range(2)]
            sqh_tk = [Tk("sqh0"), Tk("sqh1")]
            rh = [AR.alloc([TT], F32) for _ in range(2)]
            rh_tk = [Tk("rh0"), Tk("rh1")]
            vst = AR.alloc([4, FH, 65], BF16)
            vst_tk = Tk("vst")
            v_sem = P.dsem("vst")
            sgst = AR.alloc([HW // 128, TT], BF16)
            sgst_tk = Tk("sgst")
            sg_sem = P.dsem("sgst")
            wf = AR.alloc([KC, FH], BF16)
            wf_tk = Tk("wf")
            wf_sem = P.dsem("wf")
            ef = AR.alloc([TT], F32)
            sA = AR.alloc([TT], F32)
            sB = AR.alloc([TT], F32)
            f_tk = Tk("fchain")
            carry = AR.alloc([1], F32)
            r1 = AR.alloc([TT], F32)
            c3q = AR.alloc([3, TT], BF16)
            c3k = AR.alloc([3, TT], BF16)
            c3_tk = Tk("c3")
            c3_sem = P.dsem("c3")
            ones3 = AR.alloc([3, TT], BF16)
            o3_tk = Tk("ones3")
            o3_sem = P.dsem("o3")
            qa_tk = Tk("qa_d")
            ka_tk = Tk("ka_d")
            qa1_tk, qa2_tk, ka1_tk, ka2_tk = Tk("qa1"), Tk("qa2"), Tk("ka1"), Tk("ka2")
            vd_tk = Tk("v_d")
            sgd_tk = Tk("sg_d")
            otd_tk = Tk("ot_d")
            nones3 = AR.alloc([3, TT], BF16)
            onesF = AR.alloc([TT], F32)
            P.op('pool', lambda e: e.memset(ones3, 1.0), writes=[o3_tk])
            P.op('pool', lambda e: e.memset(nones3, -1.0), writes=[o3_tk])
            P.op('pool', lambda e: e.memset(onesF, 1.0), writes=[o3_tk])
            P.op('pool', lambda e: e.memset(carry, 0.0), writes=[f_tk])
            P.op('pool', lambda e: e.memset(vst, 1.0), writes=[vst_tk])
            for c in range(NCH):
                P.dma('sp', qa_d[:, 67:70, c * TT:(c + 1) * TT], ones3[0:FH], o3_sem, reads=[o3_tk], writes=[qa1_tk])
                P.dma('sp', ka_d[:, 64:67, c * TT:(c + 1) * TT], nones3[0:FH], o3_sem, reads=[o3_tk], writes=[ka1_tk])
            P.dma('sp', wf, wb_d[fw][jl, :, 4 * HW:4 * HW + FH].rearrange("(k p) n -> p k n", p=128), wf_sem, reads=[wtk[(fw, jl)]], writes=[wf_tk])
            qg = C('fqg%d' % jl)
            kg = C('fkg%d' % jl)
            nbf = C('fbf%d' % jl, FH)
            load_hc(0, 0, pp)
            rmsnorm_chunk(0, 'mixg%d' % li, 0)
            for c in range(NCH):
                buf = c % 2
                hn, hn_tk = hnb[buf], hnb_tk[buf]
                cols = slice(c * TT, (c + 1) * TT)
                if c + 1 < NCH:
                    load_hc(c + 1, 1 - buf, pp)
                bm = MISC
                proj_T(wf, wf_tk, 0, FH, lambda kc: hn[:, kc, :], bm, extra_reads=[hn_tk])
                P.op('dve', lambda e: e.tensor_scalar(out=ef[0:FH], in0=banks[bm][0:FH, :], scalar1=nbf[:, 0:1], scalar2=-1.0, op0=ALU.add, op1=ALU.mult),
                     reads=[bank_tk[bm], cst_tk, f_tk], writes=[f_tk])
                P.op('act', lambda e: e.activation(out=ef[0:FH], in_=ef[0:FH], func=AF.Exp), reads=[f_tk], writes=[f_tk])
                P.op('act', lambda e: e.activation(out=sA[0:FH], in_=ef[0:FH], func=AF.Ln, bias=onec[0:FH, 0:1]), reads=[f_tk, k_tk], writes=[f_tk], strict=True)
                P.op('dve', lambda e: e.tensor_tensor_scan(out=sB[0:FH], data0=onesF[0:FH], data1=sA[0:FH], initial=carry[0:FH, 0:1], op0=ALU.mult, op1=ALU.add),
                     reads=[f_tk, o3_tk], writes=[f_tk], strict=True)
                P.op('dve', lambda e: e.tensor_copy(out=carry[0:FH], in_=sB[0:FH, TT - 1:TT]), reads=[f_tk], writes=[f_tk], strict=True)
                P.op('act', lambda e: e.copy(out=c3k[0:FH, 0, :], in_=sB[0:FH]), reads=[f_tk, c3_tk], writes=[c3_tk])
                P.op('dve', lambda e: e.tensor_tensor(out=r1[0:FH], in0=sB[0:FH], in1=c3k[0:FH, 0, :], op=ALU.subtract), reads=[f_tk, c3_tk], writes=[f_tk])
                P.op('act', lambda e: e.copy(out=c3k[0:FH, 1, :], in_=r1[0:FH]), reads=[f_tk, c3_tk], writes=[c3_tk])
                P.op('dve', lambda e: e.tensor_tensor(out=sA[0:FH], in0=r1[0:FH], in1=c3k[0:FH, 1, :], op=ALU.subtract), reads=[f_tk, c3_tk], writes=[f_tk])
                P.op('act', lambda e: e.copy(out=c3k[0:FH, 2, :], in_=sA[0:FH]), reads=[f_tk, c3_tk], writes=[c3_tk])
                P.dma('pool', qa_d[:, 64:67, cols], c3k[0:FH], c3_sem, reads=[c3_tk], writes=[qa2_tk])
                P.dma('pool', ka_d[:, 67:70, cols], c3k[0:FH], c3_sem, reads=[c3_tk], writes=[ka2_tk])
                tasks = []
                for which in range(2):
                    for pr in range(NPR):
                        tasks.append((which, pr))
                slabs = {}
                tb = {}

                def t_front(ti):
                    which, pr = tasks[ti]
                    if pr == 0:
                        slabs[which] = load_ws(fw, jl, [(which * HW, HW, 0)])
                    slab, stk = slabs[which]
                    b = next_rot()
                    tb[ti] = b
                    proj_T(slab, stk, pr * 128, 128, lambda kc: hn[:, kc, :], b, extra_reads=[hn_tk])

                def t_back(ti):
                    which, pr = tasks[ti]
                    b = tb[ti]
                    x_ = ti % 2
                    P.op('act', lambda e: e.activation(out=sqh[x_], in_=banks[b][:, :], func=AF.Square), reads=[bank_tk[b]], writes=[sqh_tk[x_]])
                    P.op('pe', lambda e: e.matmul(banks[MISC][:, :], lhsT=bd64_b, rhs=sqh[x_], start=True, stop=True),
                         reads=[sqh_tk[x_], k_tk], writes=[bank_tk[MISC]])
                    if which == 0:
                        P.op('act', lambda e: e.activation(out=rh[x_], in_=banks[MISC][:, :], func=AF.Ln, scale=1.0, bias=eps64[:, 0:1]),
                             reads=[bank_tk[MISC], k_tk], writes=[rh_tk[x_]])
                    else:
                        P.op('act', lambda e: e.activation(out=rh[x_], in_=banks[MISC][:, :], func=AF.Ln, scale=1.0 / 64, bias=epsc[:, 0:1]),
                             reads=[bank_tk[MISC], k_tk], writes=[rh_tk[x_]])
                    P.op('act', lambda e: e.activation(out=rh[x_], in_=rh[x_], func=AF.Exp, scale=-0.5), reads=[rh_tk[x_]], writes=[rh_tk[x_]], strict=True)
                    gcol = qg if which == 0 else kg
                    P.op('dve', lambda e: e.scalar_tensor_tensor(out=qk_st[which][:, pr, :], in0=banks[b][:, :], scalar=gcol[:, 0:1], in1=rh[x_],
                                                                 op0=ALU.mult, op1=ALU.mult),
                         reads=[bank_tk[b], rh_tk[x_], cst_tk], writes=[qk_tk[which]])
                    if pr == NPR - 1:
                        dv = (qa_d if which == 0 else ka_d)[:, 0:64, cols].rearrange("(hp two) d t -> two d hp t", two=2)
                        for two in range(2):
                            P.dma('pool', dv[two], qk_st[which][two * 64:(two + 1) * 64], qk_sem[which], reads=[qk_tk[which]],
                                  writes=[qa_tk if which == 0 else ka_tk])

                for ti in range(len(tasks) + 1):
                    if ti < len(tasks):
                        t_front(ti)
                    if ti >= 1:
                        t_back(ti - 1)
                slab, stk = load_ws(fw, jl, [(2 * HW, HW, 0)])
                for jb in range(4):
                    b = next_rot()
                    for kc in range(KC):
                        P.op('pe', lambda e, kc=kc, jb=jb, b=b, slab=slab, hn=hn: e.matmul(banks[b][:, 0:HW], lhsT=hn[:, kc, jb * 128:(jb + 1) * 128], rhs=slab[:, kc, 0:HW],
                                                                                  start=(kc == 0), stop=(kc == KC - 1)),
                             reads=[stk, hn_tk], writes=[bank_tk[b]])
                    evac_copy(vst[:, jb, :, 0:64], banks[b][:, 0:HW].rearrange("p (h d) -> p h d", h=FH), [bank_tk[b]], [vst_tk])
                P.dma('pool', v_d[cols, :].rearrange("(j p) e -> p j e", p=128), vst.rearrange("p j h e -> p j (h e)"), v_sem, reads=[vst_tk], writes=[vd_tk])
                slab, stk = load_ws(fw, jl, [(3 * HW, HW, 0)])
                if c + 1 < NCH:
                    rmsnorm_chunk(1 - buf, 'mixg%d' % li, 1 - buf)
                for u in range(HW // 128):
                    b = next_rot()
                    proj_T(slab, stk, u * 128, 128, lambda kc: hn[:, kc, :], b, extra_reads=[hn_tk])
                    P.op('act', lambda e, b=b, u=u: e.activation(out=sgst[:, u, :], in_=banks[b][:, :], func=AF.Sigmoid),
                         reads=[bank_tk[b]], writes=[sgst_tk])
                P.dma('pool', sg_d[:, cols].rearrange("(k p) t -> p k t", p=128), sgst, sg_sem, reads=[sgst_tk], writes=[sgd_tk])
            AR.release(m)

            P.barrier()
            m = AR.mark()
            NB = seq // 128
            NQH = max(1, seq // 2048)
            QW = seq // NQH
            NBK = QW // 512
            Ka = [AR.alloc([seq], BF16) for _ in range(2)]
            Qa = [AR.alloc([seq], BF16) for _ in range(2)]
            Vh = [AR.alloc([NB, 65], BF16) for _ in range(2)]
            kqv_tk = [Tk("kqv0"), Tk("kqv1")]
            kqv_sem = [P.dsem("kqv0"), P.dsem("kqv1")]
            PT = [AR.alloc([512], BF16) for _ in range(3)]
            pt_tk = [Tk("pt%d" % i) for i in range(3)]
            pt_i = 0
            rrow = AR.alloc([512], F32)
            rrow_tk = Tk("rrow")
            bcs = AR.alloc([512], F32)
            bcs_tk = Tk("bcs")
            Ost = [AR.alloc([QW], BF16) for _ in range(2)]
            ost_tk = [Tk("ost0"), Tk("ost1")]
            ost_sem = [P.dsem("ost0"), P.dsem("ost1")]
            oi = 0
            NHL = FH

            def load_head(h, s):
                P.dma('sp', Ka[s][0:70], ka_d[h, :, :], kqv_sem[s], reads=[ka_tk, ka1_tk, ka2_tk], writes=[kqv_tk[s]])
                P.dma('sp', Qa[s][0:70], qa_d[h, :, :], kqv_sem[s], reads=[qa_tk, qa1_tk, qa2_tk], writes=[kqv_tk[s]])
                P.dma('sp', Vh[s], v_d[:, h * 65:(h + 1) * 65].rearrange("(kb p) e -> p kb e", p=128), kqv_sem[s], reads=[vd_tk], writes=[kqv_tk[s]])

            items = []
            for h in range(NHL):
                items.append(('load', h))
                for qh in range(NQH):
                    qb0 = qh * (QW // 128)
                    nqb = QW // 128
                    last_kb = qb0 + nqb - 1
                    for kb in range(last_kb + 1):
                        for a in range(NBK):
                            blk0 = qb0 + 4 * a
                            lo = max(kb, blk0)
                            hi = blk0 + 4
                            if lo >= hi:
                                continue
                            items.append(('step', h, kb, a, blk0, lo, (hi - lo) * 128))
                    items.append(('norm', h, qh))
            LOOK = 2
            st_bank = {}
            load_head(0, 0)

            def front(it):
                if it[0] != 'step':
                    return
                _, h, kb, a, blk0, lo, ncol = it
                s_ = h % 2
                diag = (lo == kb)
                b = next_rot()
                st_bank[id(it)] = b
                P.op('pe', lambda e: e.matmul(banks[b][:, 0:ncol], lhsT=Ka[s_][0:70, kb * 128:(kb + 1) * 128], rhs=Qa[s_][0:70, lo * 128:lo * 128 + ncol],
                                              start=True, stop=(not diag)), reads=[kqv_tk[s_]], writes=[bank_tk[b]])
                if diag:
                    P.op('pe', lambda e: e.matmul(banks[b][:, 0:128], lhsT=ident_b, rhs=mask_b[:, 0, :], start=False, stop=True),
                         reads=[k_tk], writes=[bank_tk[b]])

            def back(it):
                nonlocal pt_i, oi
                if it[0] == 'load':
                    if it[1] + 1 < NHL:
                        load_head(it[1] + 1, (it[1] + 1) % 2)
                    return
                if it[0] == 'step':
                    _, h, kb, a, blk0, lo, ncol = it
                    s_ = h % 2
                    b = st_bank.pop(id(it))
                    pi = pt_i % 3
                    pt_i += 1
                    P.op('act', lambda e: e.activation(out=PT[pi][:, 0:ncol], in_=banks[b][:, 0:ncol], func=AF.Exp),
                         reads=[bank_tk[b]], writes=[pt_tk[pi]])
                    c0 = (lo - blk0) * 128
                    lastk = (kb == blk0 + 3)
                    P.op('pe', lambda e: e.matmul(banks[ACC[a]][0:65, c0:c0 + ncol], lhsT=Vh[s_][:, kb, :], rhs=PT[pi][:, 0:ncol],
                                                  start=(kb == 0), stop=lastk, skip_group_check=True),
                         reads=[kqv_tk[s_], pt_tk[pi]], writes=[bank_tk[ACC[a]]])
                elif it[0] == 'norm':
                    _, h, qh = it
                    os_ = oi % 2
                    oi += 1
                    for a in range(NBK):
                        P.op('dve', lambda e, a=a: e.reciprocal(out=rrow[64:65, :], in_=banks[ACC[a]][64:65, :]), reads=[bank_tk[ACC[a]]], writes=[rrow_tk])
                        P.op('pe', lambda e: e.matmul(banks[MISC][0:64, :], lhsT=ones_f[64:65, 0:64], rhs=rrow[64:65, :], start=True, stop=True),
                             reads=[rrow_tk, k_tk], writes=[bank_tk[MISC]])
                        P.op('act', lambda e: e.copy(out=bcs[0:64], in_=banks[MISC][0:64, :]), reads=[bank_tk[MISC]], writes=[bcs_tk])
                        P.op('dve', lambda e, a=a, os_=os_: e.tensor_tensor(out=Ost[os_][0:64, a * 512:(a + 1) * 512], in0=banks[ACC[a]][0:64, :], in1=bcs[0:64], op=ALU.mult),
                             reads=[bank_tk[ACC[a]], bcs_tk], writes=[ost_tk[os_]])
                    P.dma('pool', ot_d[h * 64:(h + 1) * 64, qh * QW:(qh + 1) * QW], Ost[os_][0:64], ost_sem[os_], reads=[ost_tk[os_]], writes=[otd_tk])

            for i in range(len(items) + LOOK):
                if i < len(items):
                    front(items[i])
                if i - LOOK >= 0:
                    back(items[i - LOOK])
            AR.release(m)

            P.barrier()
            m = AR.mark()
            NFC = HW // 128
            oc = [AR.alloc([NFC, TT], BF16) for _ in range(2)]
            gc = [AR.alloc([NFC, TT], BF16) for _ in range(2)]
            og_tk = [Tk("og0"), Tk("og1")]
            og_sem = [P.dsem("og0"), P.dsem("og1")]
            yT = AR.alloc([NFC, TT], BF16)
            yT_tk = Tk("yT")

            def load_og(c, s):
                cols = slice(c * TT, (c + 1) * TT)
                P.dma('sp', oc[s], ot_d[:, cols].rearrange("(k p) t -> p k t", p=128), og_sem[s], reads=[otd_tk], writes=[og_tk[s]])
                P.dma('sp', gc[s], sg_d[:, cols].rearrange("(k p) t -> p k t", p=128), og_sem[s], reads=[sgd_tk], writes=[og_tk[s]])

            load_og(0, 0)
            for c in range(NCH):
                buf = c % 2
                if c + 1 < NCH:
                    load_og(c + 1, 1 - buf)
                for kc in range(NFC):
                    eng = 'pool' if kc % 2 else 'dve'
                    P.op(eng, lambda e, kc=kc, buf=buf: e.tensor_tensor(out=yT[:, kc, :], in0=oc[buf][:, kc, :], in1=gc[buf][:, kc, :], op=ALU.mult),
                         reads=[og_tk[buf]], writes=[yT_tk])
                dense_out('fox_w_out', jl, NFC, lambda fc: yT[:, fc, :], [yT_tk], c, par, alt=True)
            flush_finish()
            pend[0] = par
            AR.release(m)

        def phase_ssd(jl, li):
            P.barrier()
            m = AR.mark()
            pp = pend[0]
            par = sub_i[0] % 2
            sub_i[0] += 1
            sw = 'ssd_w_in'
            XW = SH * 64
            NHG = SH // 4
            zs = AR.alloc([4, XW], BF16)
            zs_tk = [Tk("zs%d" % j) for j in range(4)]
            xbc = AR.alloc([SNC, TT], BF16)
            xbc_tk = [Tk("xbc%d" % f) for f in range(SNC)]
            BOF, COF = SXC, SXC + SG
            pcs = [AR.alloc([TT + 4], F32) for _ in range(2)]
            pc_tks = [Tk("pc0"), Tk("pc1")]
            accs = [AR.alloc([TT], F32) for _ in range(2)]
            acc_tks = [Tk("acc0"), Tk("acc1")]
            tails = AR.alloc([SNC, 3], F32)
            tails_tk = [Tk("tl%d" % f) for f in range(SNC)]
            wdt = AR.alloc([KC, SH], BF16)
            wdt_tk = Tk("wdt")
            wdt_sem = P.dsem("wdt")
            dtT = AR.alloc([TT], F32)
            dtT_tk = Tk("dtT")
            arow = AR.alloc([SH], F32)
            arow_tk = Tk("arow")
            dtk = AR.alloc([4, SH], F32)
            dak = AR.alloc([4, SH], F32)
            dtk_tk = Tk("dtk")
            NSET = 2
            hst = AR.alloc([XW], F32)
            hst_tk = Tk("hst")
            hpbs = [AR.alloc([XW], BF16) for _ in range(3)]
            hpb_tks = [Tk("hpb0"), Tk("hpb1"), Tk("hpb2")]
            junk = AR.alloc([512], BF16)
            SETS = []
            for si in range(NSET):
                d = {}
                d['sm'] = AR.alloc([4, SH], F32); d['sm_tk'] = Tk("sm%d" % si)
                d['dcy'] = AR.alloc([SH, 128], BF16); d['dcy_tk'] = Tk("dcy%d" % si)
                d['cb'] = AR.alloc([SG, 128], BF16); d['cb_tk'] = Tk("cb%d" % si)
                d['xdt'] = AR.alloc([XW], BF16); d['xdt_tk'] = Tk("xdt%d" % si)
                d['xsk'] = AR.alloc([XW], BF16); d['xsk_tk'] = Tk("xsk%d" % si)
                d['Btk'] = AR.alloc([SG, 128], BF16); d['Btk_tk'] = Tk("Btk%d" % si)
                d['yy'] = AR.alloc([XW], F32); d['yy_tk'] = Tk("yy%d" % si)
                d['Dm'] = d['yy'].bitcast(BF16)[:, 0:2 * XW].rearrange("p (a b) -> p a b", a=SH); d['Dm_tk'] = d['yy_tk']
                d['ssq'] = AR.alloc([SG], F32); d['ssq_tk'] = Tk("ssq%d" % si)
                d['yn'] = AR.alloc([XW], BF16); d['yn_tk'] = Tk("yn%d" % si)
                d['xdw'] = d['yn']; d['xdw_tk'] = d['yn_tk']
                SETS.append(d)
            sng = AR.alloc([XW], F32)
            sng_tk = Tk("sng")
            sng_sem = P.dsem("sng")
            P.dma('sp', sng, sng_d[jl, :, :], sng_sem, writes=[sng_tk])
            dskr = C('dsk%d' % jl)
            scw = C('scw%d' % jl)
            scb = C('scb%d' % jl)
            dtb = C('dtb%d' % jl, SH)
            P.op('pool', lambda e: e.memset(tails, 0.0), writes=tails_tk)
            P.op('pool', lambda e: e.memset(hst, 0.0), writes=[hst_tk])
            P.op('pool', lambda e: e.memset(hpbs[0], 0.0), writes=[hpb_tks[0]])
            P.op('act', lambda e: e.activation(out=arow, in_=C('alog%d' % jl), func=AF.Exp), reads=[cst_tk], writes=[arow_tk])
            P.op('dve', lambda e: e.tensor_scalar(out=arow, in0=arow, scalar1=-1.0, scalar2=None, op0=ALU.mult), reads=[arow_tk], writes=[arow_tk])
            DTC = 2 * XW + 2 * SG * 128
            P.dma('sp', wdt, wb_d[sw][jl, :, DTC:DTC + SH].rearrange("(k p) n -> p k n", p=128), wdt_sem, reads=[wtk[(sw, jl)]], writes=[wdt_tk])
            load_hc(0, 0, pp)
            rmsnorm_chunk(0, 'mixg%d' % li, 0)
            for c in range(NCH):
                buf = c % 2
                hn, hn_tk = hnb[buf], hnb_tk[buf]
                if c + 1 < NCH:
                    load_hc(c + 1, 1 - buf, pp)
                proj_T(wdt, wdt_tk, 0, SH, lambda kc: hn[:, kc, :], MISC, extra_reads=[hn_tk])
                P.op('act', lambda e: e.activation(out=dtT[0:SH], in_=banks[MISC][0:SH, :], func=AF.Exp, bias=dtb[:, 0:1]),
                     reads=[bank_tk[MISC], cst_tk], writes=[dtT_tk])
                P.op('act', lambda e: e.activation(out=dtT[0:SH], in_=dtT[0:SH], func=AF.Ln, bias=onec[0:SH, 0:1]), reads=[dtT_tk, k_tk], writes=[dtT_tk], strict=True)
                for jb in range(4):
                    P.op('pe', lambda e, jb=jb: e.transpose(banks[MISC][:, jb * SH:(jb + 1) * SH], dtT[0:SH, jb * 128:(jb + 1) * 128], ident_f[0:SH, 0:SH]),
                         reads=[dtT_tk, cst_tk], writes=[bank_tk[MISC]])
                P.op('dve', lambda e: e.tensor_copy(out=dtk, in_=banks[MISC][:, 0:4 * SH].rearrange("p (j h) -> p j h", j=4)), reads=[bank_tk[MISC]], writes=[dtk_tk])
                for jb in range(4):
                    P.op('dve', lambda e, jb=jb: e.tensor_tensor(out=dak[:, jb, :], in0=dtk[:, jb, :], in1=arow, op=ALU.mult), reads=[dtk_tk, arow_tk], writes=[dtk_tk], strict=True)
                for sl in range(XW // 512):
                    slab, stk = load_ws(sw, jl, [(sl * 512, 512, 0)])
                    for jb in range(4):
                        b = next_rot()
                        for kc in range(KC):
                            P.op('pe', lambda e, kc=kc, jb=jb, b=b, slab=slab, hn=hn: e.matmul(banks[b][:, :], lhsT=hn[:, kc, jb * 128:(jb + 1) * 128], rhs=slab[:, kc, :],
                                                                                      start=(kc == 0), stop=(kc == KC - 1)),
                                 reads=[stk, hn_tk], writes=[bank_tk[b]])
                        P.op('act', lambda e, b=b, jb=jb, sl=sl: e.activation(out=zs[:, jb, sl * 512:(sl + 1) * 512], in_=banks[b][:, :], func=AF.Silu),
                             reads=[bank_tk[b]], writes=[zs_tk[jb]])
                flush_finish()
                for sl in range(SNC // 4):
                    slab, stk = load_ws(sw, jl, [(XW + sl * 512, 512, 0)])
                    for u in range(4):
                        f = sl * 4 + u
                        pi = f % 2
                        b = next_rot()
                        proj_T(slab, stk, u * 128, 128, lambda kc: hn[:, kc, :], b, extra_reads=[hn_tk])
                        conv_silu(b, 128, 4, lambda k, f=f: scw[:, f * 4 + k:f * 4 + k + 1], scb[:, f:f + 1], pcs[pi], pc_tks[pi], accs[pi], acc_tks[pi],
                                  tails[:, f, :], tails_tk[f], xbc[:, f, :], [], [xbc_tk[f]])
                def prep_stages(jb):
                    S = SETS[jb % NSET]
                    sm, sm_tk, Dm, Dm_tk, dcy, dcy_tk, cb, cb_tk = S['sm'], S['sm_tk'], S['Dm'], S['Dm_tk'], S['dcy'], S['dcy_tk'], S['cb'], S['cb_tk']
                    xdt, xdt_tk, xsk, xsk_tk, xdw, xdw_tk, Btk, Btk_tk = S['xdt'], S['xdt_tk'], S['xsk'], S['xsk_tk'], S['xdw'], S['xdw_tk'], S['Btk'], S['Btk_tk']
                    Mt, Mt_tk = dcy, dcy_tk
                    bc = slice(jb * 128, (jb + 1) * 128)
                    mo = (jb % NSET) * 4 * SH
                    st = []

                    def s_acs():
                        P.op('pe', lambda e: e.matmul(banks[MISC][:, mo:mo + SH], lhsT=C('U'), rhs=dak[:, jb, :], start=True, stop=True),
                             reads=[dtk_tk, cst_tk], writes=[bank_tk[MISC]])
                        P.op('pe', lambda e: e.matmul(banks[MISC][:, mo + SH:mo + 2 * SH], lhsT=C('Ubar'), rhs=dak[:, jb, :], start=True, stop=True),
                             reads=[dtk_tk, cst_tk], writes=[bank_tk[MISC]])
                        P.op('pe', lambda e: e.matmul(banks[MISC][:, mo + 2 * SH:mo + 3 * SH], lhsT=ones_f, rhs=dak[:, jb, :], start=True, stop=True),
                             reads=[dtk_tk, k_tk], writes=[bank_tk[MISC]])
                        P.op('act', lambda e: e.activation(out=sm[:, 0:3, :].rearrange("p a b -> p (a b)"), in_=banks[MISC][:, mo:mo + 3 * SH], func=AF.Exp),
                             reads=[bank_tk[MISC]], writes=[sm_tk])
                    st.append(s_acs)

                    def s_dm():
                        P.op('dve', lambda e: e.tensor_tensor(out=Dm, in0=U_b.unsqueeze(1).to_broadcast([128, SH, 128]),
                                                              in1=dak[:, jb, :].unsqueeze(2).to_broadcast([128, SH, 128]), op=ALU.mult),
                             reads=[dtk_tk, k_tk], writes=[Dm_tk])
                    st.append(s_dm)

                    def s_xT():
                        b = next_rot()
                        bkb = banks[b][:, :].bitcast(BF16)
                        for q in range(SXC):
                            P.op('pe', lambda e, q=q: e.transpose(bkb[:, q * 128:(q + 1) * 128], xbc[:, q, bc], ident_b),
                                 reads=[xbc_tk[q], k_tk], writes=[bank_tk[b]])
                        P.op('dve', lambda e: e.tensor_tensor(
                            out=xdt.rearrange("p (h d) -> p h d", h=SH), in0=bkb[:, 0:XW].rearrange("p (h d) -> p h d", h=SH),
                            in1=dtk[:, jb, :].unsqueeze(2).to_broadcast([128, SH, 64]), op=ALU.mult),
                             reads=[bank_tk[b], dtk_tk], writes=[xdt_tk])
                        P.op('dve', lambda e: e.tensor_tensor(
                            out=xsk.rearrange("p (h d) -> p h d", h=SH), in0=bkb[:, 0:XW].rearrange("p (h d) -> p h d", h=SH),
                            in1=dskr.unsqueeze(2).to_broadcast([128, SH, 64]), op=ALU.mult),
                             reads=[bank_tk[b], cst_tk], writes=[xsk_tk])
                    st.append(s_xT)

                    def mk_E(hg):
                        def s_E():
                            b = next_rot()
                            P.op('pe', lambda e: e.matmul(banks[b][:, :], lhsT=ones_b, rhs=Dm[:, hg * 4:(hg + 1) * 4, :].rearrange("p a b -> p (a b)"),
                                                          start=True, stop=False), reads=[Dm_tk, k_tk], writes=[bank_tk[b]])
                            for hh in range(4):
                                P.op('pe', lambda e, hh=hh: e.matmul(banks[b][:, hh * 128:(hh + 1) * 128], lhsT=Dm[:, hg * 4 + hh, :], rhs=nones_b,
                                                                     start=False, stop=False, skip_group_check=True), reads=[Dm_tk, k_tk], writes=[bank_tk[b]])
                            P.op('pe', lambda e: e.matmul(banks[b][:, :], lhsT=ident_b, rhs=mask_b.rearrange("p a b -> p (a b)"), start=False, stop=True, skip_group_check=True),
                                 reads=[k_tk], writes=[bank_tk[b]])
                            P.op('act', lambda e: e.activation(out=dcy[:, hg * 4:(hg + 1) * 4, :].rearrange("p a b -> p (a b)"), in_=banks[b][:, :], func=AF.Exp),
                                 reads=[bank_tk[b]], writes=[dcy_tk])
                        return s_E
                    for hg in range(NHG):
                        st.append(mk_E(hg))

                    def s_cb():
                        b = next_rot()
                        for g in range(SG):
                            P.op('pe', lambda e, g=g: e.matmul(banks[b][:, g * 128:(g + 1) * 128], lhsT=xbc[:, BOF + g, bc], rhs=xbc[:, COF + g, bc], start=True, stop=True),
                                 reads=[xbc_tk[BOF + g], xbc_tk[COF + g]], writes=[bank_tk[b]])
                        evac_copy(cb, banks[b][:, 0:SG * 128].rearrange("p (g l) -> p g l", g=SG), [bank_tk[b]], [cb_tk])
                        b2 = next_rot()
                        bkb2 = banks[b2][:, :].bitcast(BF16)
                        for g in range(SG):
                            P.op('pe', lambda e, g=g: e.transpose(bkb2[:, g * 128:(g + 1) * 128], xbc[:, BOF + g, bc], ident_b),
                                 reads=[xbc_tk[BOF + g], k_tk], writes=[bank_tk[b2]])
                        evac_copy(Btk, bkb2[:, 0:SG * 128].rearrange("p (g n) -> p g n", g=SG), [bank_tk[b2]], [Btk_tk])
                    st.append(s_cb)

                    def s_mt():
                        P.op('dve', lambda e: e.tensor_tensor(out=Mt.rearrange("p (g r) l -> p g r l", g=SG), in0=dcy.rearrange("p (g r) l -> p g r l", g=SG),
                                                              in1=cb.unsqueeze(2).to_broadcast([128, SG, 8, 128]), op=ALU.mult),
                             reads=[dcy_tk, cb_tk], writes=[Mt_tk])
                    st.append(s_mt)
                    return st

                def back_stages(jb):
                    S = SETS[jb % NSET]
                    sm, sm_tk, dcy, dcy_tk = S['sm'], S['sm_tk'], S['dcy'], S['dcy_tk']
                    xdt, xdt_tk, xsk, xsk_tk, xdw, xdw_tk, Btk, Btk_tk = S['xdt'], S['xdt_tk'], S['xsk'], S['xsk_tk'], S['xdw'], S['xdw_tk'], S['Btk'], S['Btk_tk']
                    yy, yy_tk, ssq, ssq_tk, yn, yn_tk = S['yy'], S['yy_tk'], S['ssq'], S['ssq_tk'], S['yn'], S['yn_tk']
                    Mt, Mt_tk = dcy, dcy_tk
                    gblk = c * 4 + jb
                    hin, hin_tk = hpbs[gblk % 3], hpb_tks[gblk % 3]
                    hout, hout_tk = hpbs[(gblk + 1) % 3], hpb_tks[(gblk + 1) % 3]
                    A0 = (jb % NSET) * SG
                    bc = slice(jb * 128, (jb + 1) * 128)
                    st = []

                    def s_yoff():
                        for g in range(SG):
                            P.op('pe', lambda e, g=g: e.matmul(banks[ACC[A0 + g]][:, :], lhsT=xbc[:, COF + g, bc], rhs=hin[:, g * 512:(g + 1) * 512], start=True, stop=True),
                                 reads=[xbc_tk[COF + g], hin_tk], writes=[bank_tk[ACC[A0 + g]]])
                            P.op('dve', lambda e, g=g: e.tensor_tensor(out=yy[:, g * 512:(g + 1) * 512].rearrange("p (h d) -> p h d", h=8),
                                                                       in0=banks[ACC[A0 + g]][:, :].rearrange("p (h d) -> p h d", h=8),
                                                                       in1=sm[:, 0, g * 8:(g + 1) * 8].unsqueeze(2).to_broadcast([128, 8, 64]), op=ALU.mult),
                                 reads=[bank_tk[ACC[A0 + g]], sm_tk], writes=[yy_tk])
                            P.op('pool', lambda e, g=g: e.tensor_tensor(out=yy[:, g * 512:(g + 1) * 512], in0=yy[:, g * 512:(g + 1) * 512], in1=xsk[:, g * 512:(g + 1) * 512], op=ALU.add),
                                 reads=[yy_tk, xsk_tk], writes=[yy_tk])
                        P.op('pool', lambda e: e.memset(ssq, 0.0), writes=[ssq_tk])
                    st.append(s_yoff)

                    def s_ydiag():
                        for h in range(SH):
                            g = h // 8
                            P.op('pe', lambda e, h=h, g=g: e.matmul(banks[ACC[A0 + g]][:, (h % 8) * 64:(h % 8 + 1) * 64], lhsT=Mt[:, h, :], rhs=xdt[:, h * 64:(h + 1) * 64],
                                                                    start=True, stop=True, skip_group_check=True),
                                 reads=[Mt_tk, xdt_tk], writes=[bank_tk[ACC[A0 + g]]])
                    st.append(s_ydiag)

                    def s_state():
                        P.op('pool', lambda e: e.tensor_tensor(out=xdw.rearrange("p (h d) -> p h d", h=SH), in0=xdt.rearrange("p (h d) -> p h d", h=SH),
                                                               in1=sm[:, 1, :].unsqueeze(2).to_broadcast([128, SH, 64]), op=ALU.mult),
                             reads=[xdt_tk, sm_tk], writes=[xdw_tk])
                        for g in range(SG):
                            b = next_rot()
                            gs_ = slice(g * 512, (g + 1) * 512)
                            P.op('pe', lambda e, b=b, g=g, gs_=gs_: e.matmul(banks[b][:, :], lhsT=Btk[:, g, :], rhs=xdw[:, gs_], start=True, stop=True),
                                 reads=[Btk_tk, xdw_tk], writes=[bank_tk[b]])
                            P.op('pool', lambda e, g=g, gs_=gs_: e.tensor_tensor(out=hst[:, gs_].rearrange("p (h d) -> p h d", h=8), in0=hst[:, gs_].rearrange("p (h d) -> p h d", h=8),
                                                                                in1=sm[:, 2, g * 8:(g + 1) * 8].unsqueeze(2).to_broadcast([128, 8, 64]), op=ALU.mult),
                                 reads=[hst_tk, sm_tk], writes=[hst_tk])
                            P.op('dve', lambda e, b=b, gs_=gs_: e.tensor_tensor(out=hst[:, gs_], in0=hst[:, gs_], in1=banks[b][:, :], op=ALU.add),
                                 reads=[bank_tk[b], hst_tk], writes=[hst_tk])
                            P.op('act', lambda e, gs_=gs_: e.copy(out=hout[:, gs_], in_=hst[:, gs_]), reads=[hst_tk], writes=[hout_tk])
                    st.insert(0, s_state)

                    def s_comb():
                        for g in range(SG):
                            gs_ = slice(g * 512, (g + 1) * 512)
                            P.op('dve', lambda e, g=g, gs_=gs_: e.tensor_tensor(out=yy[:, gs_], in0=yy[:, gs_], in1=banks[ACC[A0 + g]][:, :], op=ALU.add),
                                 reads=[bank_tk[ACC[A0 + g]], yy_tk], writes=[yy_tk])
                            P.op('dve', lambda e, gs_=gs_: e.tensor_tensor(out=yy[:, gs_], in0=yy[:, gs_], in1=zs[:, jb, gs_], op=ALU.mult),
                                 reads=[yy_tk, zs_tk[jb]], writes=[yy_tk])
                            P.op('act', lambda e, g=g, gs_=gs_: e.activation(out=junk, in_=yy[:, gs_], func=AF.Square, accum_out=ssq[:, g:g + 1]),
                                 reads=[yy_tk, ssq_tk], writes=[ssq_tk])
                    st.append(s_comb)

                    def s_norm():
                        P.op('act', lambda e: e.activation(out=ssq, in_=ssq, func=AF.Ln, scale=1.0 / 512, bias=epsc[:, 0:1]), reads=[ssq_tk, k_tk], writes=[ssq_tk])
                        P.op('act', lambda e: e.activation(out=ssq, in_=ssq, func=AF.Exp, scale=-0.5), reads=[ssq_tk], writes=[ssq_tk], strict=True)
                        for g in range(SG):
                            gs_ = slice(g * 512, (g + 1) * 512)
                            P.op('dve', lambda e, g=g, gs_=gs_: e.scalar_tensor_tensor(out=yn[:, gs_], in0=yy[:, gs_], scalar=ssq[:, g:g + 1], in1=sng[:, gs_],
                                                                                       op0=ALU.mult, op1=ALU.mult),
                                 reads=[yy_tk, ssq_tk, sng_tk], writes=[yn_tk], strict=True)
                    st.append(s_norm)

                    def s_ynT():
                        b = next_rot()
                        bkb3 = banks[b][:, :].bitcast(BF16)
                        for q in range(SXC):
                            P.op('pe', lambda e, q=q: e.transpose(bkb3[:, q * 128:(q + 1) * 128], yn[:, q * 128:(q + 1) * 128], ident_b),
                                 reads=[yn_tk, k_tk], writes=[bank_tk[b]])
                        evac_copy(xbc[:, 0:SXC, bc], bkb3[:, 0:XW].rearrange("p (f t) -> p f t", f=SXC), [bank_tk[b]], xbc_tk[0:SXC])
                    st.append(s_ynT)
                    return st

                def interleave(lists):
                    n = max(len(l) for l in lists)
                    for k in range(n):
                        for l in lists:
                            if k < len(l):
                                l[k]()

                for pr in range(2):
                    jbs = [2 * pr, 2 * pr + 1]
                    interleave([prep_stages(jb) for jb in jbs])
                    if pr == 1 and c + 1 < NCH:
                        rmsnorm_chunk(1 - buf, 'mixg%d' % li, 1 - buf)
                    interleave([back_stages(jb) for jb in jbs])
                dense_out('ssd_w_out', jl, SXC, lambda fc: xbc[:, fc, :], xbc_tk[0:SXC], c, par)
            flush_finish()
            pend[0] = par
            AR.release(m)

        phase_load()
        for L in layers:
            kind, j = L[:3], int(L[3:])
            if kind == 'ffn':
                phase_ffn(j)
            elif kind == 'fox':
                phase_fox(j, 2 * j + 1)
            elif kind == 'ssd':
                phase_ssd(j, 2 * j)
        phase_final()
        P.check_deadlock()
        block = es.enter_context(nc.Block())
        P.materialize(block)
        build_program.stats = {e: len(v) for e, v in P.ops.items()}
        build_program.stats['arena_peak'] = AR.peak
        build_program.stats['nsem'] = P.nds
    return nc


FULL_LAYERS = ['ssd0', 'ffn0', 'fox0', 'ffn1', 'ssd1', 'ffn2', 'fox1', 'ffn3']
_cache = {}


def kernel(**inputs):
    inp = {k: np.asarray(v) for k, v in inputs.items()}
    x = inp['x']
    B, S, D = x.shape
    groups = [[2 * b, 2 * b + 1] for b in range(4)]
    if 'nc' not in _cache:
        _cache['nc'] = build_program(S, FULL_LAYERS, groups)
    nc = _cache['nc']
    per_r = []
    for r in range(TP):
        cl, consts = build_consts(inp, r)
        w = slice_weights(inp, r)
        w['consts'] = consts
        per_r.append(w)
    in_maps = []
    for core in range(8):
        b, r = core // 2, core % 2
        m = dict(per_r[r])
        m['x'] = np.ascontiguousarray(x[b])
        in_maps.append(m)
    res = run_bass_kernel_spmd(nc, in_maps, core_ids=list(range(8)))
    out = np.stack([np.asarray(res.results[2 * b]['out']) for b in range(B)], axis=0).astype(np.float32)
    return out
```

```python
import numpy as np
from contextlib import ExitStack
import concourse.bass as bass
import concourse.mybir as mybir
from concourse.bass_utils import run_bass_kernel_spmd

F32 = mybir.dt.float32
BF16 = mybir.dt.bfloat16
ALU = mybir.AluOpType
AF = mybir.ActivationFunctionType

D_MODEL = 1024
DEPTH = 4
EPS = 1e-6
SSD_D_INNER = 2048
SSD_HEADS = 32
SSD_CONV = 4
SSD_CONV_DIM = 3072
SSD_IN_DIM = 5152
FOX_HEADS = 16
FOX_D = 1024
FOX_IN_DIM = 4112
D_FF = 2816
FFN_CONV = 3
TT = 512
NEG = -30000.0
KC = 8
TP = 2
FH = FOX_HEADS // TP
FFC = (D_FF // 128) // TP
SH = SSD_HEADS // TP
SG = 4 // TP
SXC = SH * 64 // 128
SNC = SXC + 2 * SG

COMPUTE = ('pe', 'act', 'dve', 'pool')


class Tk:
    __slots__ = ('name', 'w', 'r')

    def __init__(self, name):
        self.name = name
        self.w = None
        self.r = {}


class DSem:
    def __init__(self, handle, name):
        self.h = handle
        self.name = name
        self.count = 0


class Prog:
    def __init__(self, nc, es):
        self.nc = nc
        self.es = es
        self.ops = {e: [] for e in ('pe', 'act', 'dve', 'pool', 'sp')}
        self.esem = {e: es.enter_context(nc.semaphore("s_" + e)) for e in COMPUTE}
        self.nds = 0
        self.dcache = {}
        self.out_deps = []

    def dsem(self, name):
        if name not in self.dcache:
            self.nds += 1
            self.dcache[name] = DSem(self.es.enter_context(self.nc.semaphore("d_%s_%d" % (name, self.nds))), name)
        return self.dcache[name]

    def _deps(self, eng, reads, writes, strict=False):
        deps = set()
        for t in reads:
            if t.w is not None:
                deps.add(t.w)
        for t in writes:
            if t.w is not None:
                deps.add(t.w)
            for v in t.r.values():
                deps.add(v)
        if strict:
            return deps
        return {d for d in deps if not (d[0] == 'E' and d[1] == eng)}

    def op(self, eng, fn, reads=(), writes=(), strict=False):
        deps = self._deps(eng, reads, writes, strict)
        idx = len(self.ops[eng])
        import sys as _s
        self.ops[eng].append({'fn': fn, 'deps': deps, 'inc': False, 'dma': None, 'note': _s._getframe(1).f_lineno})
        me = ('E', eng, idx)
        for t in reads:
            t.r[eng] = me
        for t in writes:
            t.w = me
            t.r = {}
        return me

    def dma(self, q, out, in_, sem, reads=(), writes=()):
        deps = self._deps(q, reads, writes, strict=True)
        sem.count += 16
        me = ('D', sem, sem.count)
        self.ops[q].append({'fn': (lambda e, o=out, i=in_: e.dma_start(out=o, in_=i)), 'deps': deps,
                            'inc': False, 'dma': sem, 'tok': me})
        for t in reads:
            t.r['D' + sem.name + str(id(sem))] = me
        for t in writes:
            t.w = me
            t.r = {}
        return me

    def cc(self, in_ap, out_ap, groups, slot, reads=(), writes=()):
        deps = self._deps('pool', reads, writes, strict=True)
        sem = self.dsem("cc%d" % slot)
        sem.count += 1
        me = ('D', sem, sem.count)

        def fn(e, i=in_ap, o=out_ap):
            return e.collective_compute("AllReduce", ALU.add, replica_groups=groups, ins=[i], outs=[o])
        self.ops['pool'].append({'fn': fn, 'deps': deps, 'inc': False, 'dma': None, 'cc': sem, 'tok': me})
        for t in reads:
            t.r['C' + str(id(sem))] = me
        for t in writes:
            t.w = me
            t.r = {}
        return me

    def dma_acc(self, out, in_, sem, reads=(), writes=()):
        deps = self._deps('pool', reads, writes, strict=True)
        sem.count += 16
        me = ('D', sem, sem.count)
        self.ops['pool'].append({'fn': (lambda e, o=out, i=in_: e.dma_start(out=o, in_=i, accum_op=ALU.add)), 'deps': deps,
                                 'inc': False, 'dma': sem, 'tok': me})
        for t in reads:
            t.r['D' + sem.name + str(id(sem))] = me
        for t in writes:
            t.w = me
            t.r = {}
        return me

    def barrier(self):
        toks = set()
        for e in COMPUTE:
            if self.ops[e]:
                for i in range(len(self.ops[e]) - 1, -1, -1):
                    if self.ops[e][i]['fn'] is not None and self.ops[e][i].get('tok') is None:
                        toks.add(('E', e, i))
                        break
        for sem in self.dcache.values():
            if sem.count > 0:
                toks.add(('D', sem, sem.count))
        for q in self.ops:
            deps = {d for d in toks if not (d[0] == 'E' and d[1] == q)}
            self.ops[q].append({'fn': None, 'deps': deps, 'inc': False, 'dma': None})

    def finalize_wait(self, q, toks):
        self.ops[q].append({'fn': None, 'deps': set(toks), 'inc': False, 'dma': None})

    def check_deadlock(self):
        pos = {e: 0 for e in self.ops}
        done = set()
        dtok = {}
        for e, lst in self.ops.items():
            cnts = {}
            for i, o in enumerate(lst):
                sem = o.get('cc') or o.get('dma')
                if sem is not None:
                    o.setdefault('_tok', None)
        progress = True
        while progress:
            progress = False
            for e, lst in self.ops.items():
                while pos[e] < len(lst):
                    o = lst[pos[e]]
                    ok = True
                    for d in o['deps']:
                        key = (d[0], d[1], d[2]) if d[0] == 'E' else ('D', id(d[1]), d[2])
                        if key not in done:
                            ok = False
                            break
                    if not ok:
                        break
                    done.add(('E', e, pos[e]))
                    if o.get('tok') is not None:
                        t = o['tok']
                        done.add(('D', id(t[1]), t[2]))
                    pos[e] += 1
                    progress = True
        stuck = {e: pos[e] for e in self.ops if pos[e] < len(self.ops[e])}
        if stuck:
            msg = []
            for e, p in stuck.items():
                o = self.ops[e][p]
                missing = []
                for d in o['deps']:
                    key = (d[0], d[1], d[2]) if d[0] == 'E' else ('D', id(d[1]), d[2])
                    if key not in done:
                        missing.append((d[0], d[1] if d[0] == 'E' else d[1].name, d[2]))
                msg.append("%s@%d/%d waits %s [%s]" % (e, p, len(self.ops[e]), missing, o.get('note')))
            raise RuntimeError("DEADLOCK: " + " | ".join(msg))

    def materialize(self, block):
        for e, lst in self.ops.items():
            for o in lst:
                for d in o['deps']:
                    if d[0] == 'E':
                        self.ops[d[1]][d[2]]['inc'] = True
        vals = {}
        for e in COMPUTE:
            c = 0
            for i, o in enumerate(self.ops[e]):
                if o['inc']:
                    c += 1
                    vals[(e, i)] = c
        self.vals = vals

        def run(eng_name, e):
            waited = {}
            for o in self.ops[eng_name]:
                for d in sorted(o['deps'], key=lambda d: (d[0], str(d[1]) if d[0] == 'E' else d[1].name, d[2])):
                    if d[0] == 'E':
                        key = ('E', d[1])
                        v = vals[(d[1], d[2])]
                        sh = self.esem[d[1]]
                    else:
                        key = ('D', id(d[1]))
                        v = d[2]
                        sh = d[1].h
                    if waited.get(key, 0) >= v:
                        continue
                    waited[key] = v
                    e.wait_ge(sh, v)
                if o['fn'] is None:
                    continue
                ins = o['fn'](e)
                if o.get('cc') is not None:
                    ins.then_inc(o['cc'].h)
                elif o['dma'] is not None:
                    ins.then_inc(o['dma'].h, 16)
                elif o['inc']:
                    ins.then_inc(self.esem[eng_name], 1)

        block.tensor(lambda e: run('pe', e))
        block.scalar(lambda e: run('act', e))
        block.vector(lambda e: run('dve', e))
        block.gpsimd(lambda e: run('pool', e))
        block.sync(lambda e: run('sp', e))


class Arena:
    def __init__(self, ap_f32, nwords):
        self.ap = ap_f32
        self.n = nwords
        self.top = 0
        self.peak = 0

    def mark(self):
        return self.top

    def release(self, m):
        self.top = m

    def alloc(self, shape, dtype, parts=128):
        n = int(np.prod(shape))
        words = n if dtype == F32 else (n + 1) // 2
        a = self.top
        self.top += words
        self.peak = max(self.peak, self.top)
        assert self.top <= self.n, "arena overflow %d > %d" % (self.top, self.n)
        v = self.ap[0:parts, a:a + words]
        if dtype != F32:
            v = v.bitcast(dtype)[:, 0:n]
        if len(shape) == 2:
            v = v.rearrange("p (a b) -> p a b", a=shape[0])
        elif len(shape) == 3:
            v = v.rearrange("p (a b c) -> p a b c", a=shape[0], b=shape[1])
        return v


class CL:
    def __init__(self):
        self.off = {}
        self.n = 0

    def add(self, name, width):
        self.off[name] = (self.n, width)
        self.n += width


def const_layout():
    c = CL()
    c.add('ident', 128)
    c.add('U', 128)
    c.add('maskT', 128)
    c.add('bd64', 128)
    c.add('Ubar', 128)
    for i in range(DEPTH):
        c.add('mixg%d' % i, KC)
        c.add('ffng%d' % i, KC)
        c.add('fcw%d' % i, FFC * 3)
        c.add('fcb%d' % i, FFC)
    c.add('fing', KC)
    for j in range(2):
        c.add('scw%d' % j, SNC * 4)
        c.add('scb%d' % j, SNC)
        c.add('dtb%d' % j, 1)
        c.add('alog%d' % j, SH)
        c.add('dsk%d' % j, SH)
        c.add('fbf%d' % j, 1)
        c.add('fqg%d' % j, 1)
        c.add('fkg%d' % j, 1)
    return c


def build_consts(inp, r):
    c = const_layout()
    A = np.zeros((128, c.n), np.float32)

    def put(name, arr):
        o, w = c.off[name]
        arr = np.asarray(arr, np.float32)
        A[:arr.shape[0], o:o + w] = arr.reshape(arr.shape[0], w)

    put('ident', np.eye(128))
    put('U', np.triu(np.ones((128, 128))))
    put('maskT', np.where(np.arange(128)[None, :] >= np.arange(128)[:, None], 0.0, NEG))
    put('bd64', np.kron(np.eye(2), np.ones((64, 64))))
    put('Ubar', np.tril(np.ones((128, 128)), -1))
    fsl = slice(r * FFC * 128, (r + 1) * FFC * 128)
    for i in range(DEPTH):
        put('mixg%d' % i, inp['mix_norm_g'][i].reshape(KC, 128).T)
        put('ffng%d' % i, inp['ffn_norm_g'][i].reshape(KC, 128).T)
        put('fcw%d' % i, inp['ffn_conv_w'][i][:, fsl].reshape(3, FFC, 128).transpose(2, 1, 0).reshape(128, FFC * 3))
        put('fcb%d' % i, inp['ffn_conv_b'][i][fsl].reshape(FFC, 128).T)
    put('fing', inp['final_norm_g'].reshape(KC, 128).T)
    xs_ = np.r_[r * SH * 64:(r + 1) * SH * 64, 2048 + r * SG * 128:2048 + (r + 1) * SG * 128, 2560 + r * SG * 128:2560 + (r + 1) * SG * 128]
    hsl = slice(r * SH, (r + 1) * SH)
    for j in range(2):
        put('scw%d' % j, inp['ssd_conv_w'][j][:, xs_].reshape(4, SNC, 128).transpose(2, 1, 0).reshape(128, SNC * 4))
        put('scb%d' % j, inp['ssd_conv_b'][j][xs_].reshape(SNC, 128).T)
        put('dtb%d' % j, inp['ssd_dt_bias'][j][hsl].reshape(SH, 1))
        put('alog%d' % j, np.broadcast_to(inp['ssd_a_log'][j][hsl][None, :], (128, SH)))
        put('dsk%d' % j, np.broadcast_to(inp['ssd_d'][j][hsl][None, :], (128, SH)))
        put('fbf%d' % j, inp['fox_b_f'][j][r * FH:(r + 1) * FH].reshape(FH, 1))
        put('fqg%d' % j, np.tile(inp['fox_q_norm_g'][j], 2).reshape(128, 1))
        put('fkg%d' % j, np.tile(inp['fox_k_norm_g'][j], 2).reshape(128, 1))
    return c, A


def slice_weights(inp, r):
    w = {}
    f0, f1 = r * FFC * 128, (r + 1) * FFC * 128
    w['ffn_w_up'] = np.ascontiguousarray(np.concatenate([inp['ffn_w_up'][:, :, f0:f1], inp['ffn_w_up'][:, :, D_FF + f0:D_FF + f1]], axis=2))
    w['ffn_w_down'] = np.ascontiguousarray(inp['ffn_w_down'][:, f0:f1, :])
    q0, q1 = r * FH * 64, (r + 1) * FH * 64
    fi = inp['fox_w_in']
    w['fox_w_in'] = np.ascontiguousarray(np.concatenate([fi[:, :, q0:q1], fi[:, :, 1024 + q0:1024 + q1], fi[:, :, 2048 + q0:2048 + q1],
                                                          fi[:, :, 3072 + q0:3072 + q1], fi[:, :, 4096 + r * FH:4096 + (r + 1) * FH]], axis=2))
    w['fox_w_out'] = np.ascontiguousarray(inp['fox_w_out'][:, q0:q1, :])
    x0, x1 = r * SH * 64, (r + 1) * SH * 64
    b0, b1 = r * SG * 128, (r + 1) * SG * 128
    si = inp['ssd_w_in']
    w['ssd_w_in'] = np.ascontiguousarray(np.concatenate([si[:, :, x0:x1], si[:, :, 2048 + x0:2048 + x1], si[:, :, 4096 + b0:4096 + b1],
                                                          si[:, :, 4608 + b0:4608 + b1], si[:, :, 5120 + r * SH:5120 + (r + 1) * SH]], axis=2))
    w['ssd_w_out'] = np.ascontiguousarray(inp['ssd_w_out'][:, x0:x1, :])
    w['sng'] = np.ascontiguousarray(np.broadcast_to(inp['ssd_norm_g'][:, None, x0:x1], (2, 128, SH * 64)), dtype=np.float32)
    return w


WNAMES = ['ssd_w_in', 'ssd_w_out', 'fox_w_in', 'fox_w_out', 'ffn_w_up', 'ffn_w_down']
SSD_LIN = 2 * SH * 64 + 2 * SG * 128 + SH
FOX_LIN = 4 * FH * 64 + FH
WSHAPES = {'ssd_w_in': (2, 1024, SSD_LIN), 'ssd_w_out': (2, SH * 64, 1024), 'fox_w_in': (2, 1024, FOX_LIN),
           'fox_w_out': (2, FH * 64, 1024), 'ffn_w_up': (4, 1024, 2 * FFC * 128), 'ffn_w_down': (4, FFC * 128, 1024)}


def build_program(seq, layers, groups, debug=None):
    NCH = seq // TT
    nc = bass.Bass("TRN2", target_bir_lowering=False)
    cl = const_layout()
    x_d = nc.dram_tensor("x", [seq, D_MODEL], F32, kind="ExternalInput").ap()
    c_d = nc.dram_tensor("consts", [128, cl.n], F32, kind="ExternalInput").ap()
    w_d = {n: nc.dram_tensor(n, list(WSHAPES[n]), F32, kind="ExternalInput").ap() for n in WNAMES}
    sng_d = nc.dram_tensor("sng", [2, 128, SH * 64], F32, kind="ExternalInput").ap()
    out_d = nc.dram_tensor("out", [seq, D_MODEL], F32, kind="ExternalOutput").ap()
    wb_d = {n: nc.dram_tensor(n + "_b", list(WSHAPES[n]), BF16).ap() for n in WNAMES}
    hT_d = nc.dram_tensor("hT_d", [D_MODEL, seq], F32).ap()
    part_t = [[nc.dram_tensor("part%d_%d" % (p, c), [D_MODEL, TT], F32) for c in range(NCH)] for p in range(2)]
    red_t = [[nc.dram_tensor("red%d_%d" % (p, c), [D_MODEL, TT], F32) for c in range(NCH)] for p in range(2)]
    qa_d = nc.dram_tensor("qa_d", [FH, 70, seq], BF16).ap()
    ka_d = nc.dram_tensor("ka_d", [FH, 70, seq], BF16).ap()
    v_d = nc.dram_tensor("v_d", [seq, FH * 65], BF16).ap()
    sg_d = nc.dram_tensor("sg_d", [FH * 64, seq], BF16).ap()
    ot_d = nc.dram_tensor("ot_d", [FH * 64, seq], BF16).ap()
    dbg_d = None
    if debug:
        dbg_d = nc.dram_tensor("dbg", list(debug), F32, kind="ExternalOutput").ap()

    es = ExitStack()
    with es:
        P = Prog(nc, es)
        NW = 48900
        arena_t = es.enter_context(nc.sbuf_tensor("arena", [128, NW], F32))
        AR = Arena(arena_t[:, :], NW)
        banks = [es.enter_context(nc.psum_tensor("bank%d" % i, [128, 512], F32)) for i in range(8)]
        bank_tk = [Tk("bank%d" % i) for i in range(8)]
        ACC = [0, 1, 2, 3]
        ROT = [4, 5, 6]
        MISC = 7
        rot_i = [0]

        def next_rot():
            b = ROT[rot_i[0] % 3]
            rot_i[0] += 1
            return b

        cst = AR.alloc([cl.n], F32)
        cst_tk = Tk("cst")
        s_c = P.dsem("cst")
        P.dma('sp', cst, c_d[:, :], s_c, writes=[cst_tk])

        def C(name, parts=128):
            o, w = cl.off[name]
            return cst[0:parts, o:o + w]

        ident_f = C('ident')
        ident_b = AR.alloc([128], BF16)
        ones_b = AR.alloc([128], BF16)
        nones_b = AR.alloc([128], BF16)
        ones_f = AR.alloc([128], F32)
        U_b = AR.alloc([128], BF16)
        mask_b = AR.alloc([4, 128], BF16)
        k_tk = Tk("konst")
        P.op('dve', lambda e: e.tensor_copy(out=ident_b, in_=ident_f), reads=[cst_tk], writes=[k_tk])
        P.op('dve', lambda e: e.memset(ones_b, 1.0), writes=[k_tk])
        P.op('dve', lambda e: e.memset(nones_b, -1.0), writes=[k_tk])
        P.op('dve', lambda e: e.memset(ones_f, 1.0), writes=[k_tk])
        P.op('dve', lambda e: e.tensor_copy(out=U_b, in_=C('U')), reads=[cst_tk], writes=[k_tk])
        for r in range(4):
            P.op('dve', lambda e, r=r: e.tensor_copy(out=mask_b[:, r, :], in_=C('maskT')), reads=[cst_tk], writes=[k_tk])

        wtk = {}
        used = set()
        for L in layers:
            kind, j = L[:3], int(L[3:])
            if kind == 'ssd':
                used |= {('ssd_w_in', j), ('ssd_w_out', j)}
            elif kind == 'fox':
                used |= {('fox_w_in', j), ('fox_w_out', j)}
            elif kind == 'ffn':
                used |= {('ffn_w_up', j), ('ffn_w_down', j)}
        order = []
        for L in layers:
            kind, j = L[:3], int(L[3:])
            names = {'ssd': ['ssd_w_in', 'ssd_w_out'], 'fox': ['fox_w_in', 'fox_w_out'], 'ffn': ['ffn_w_up', 'ffn_w_down']}[kind]
            for n in names:
                if (n, j) not in order:
                    order.append((n, j))
        for (n, j) in order:
            t = Tk("w_%s_%d" % (n, j))
            wtk[(n, j)] = t
            s = P.dsem("wc_%s_%d" % (n, j))
            K = WSHAPES[n][1]
            half = K // 2
            P.dma('pool', wb_d[n][j, 0:half, :], w_d[n][j, 0:half, :], s, writes=[t])
            P.dma('pool', wb_d[n][j, half:K, :], w_d[n][j, half:K, :], s, writes=[t])

        hc = [AR.alloc([KC, TT], F32) for _ in range(2)]
        hc_tk = [Tk("hc0"), Tk("hc1")]
        hc_sem = [P.dsem("hc0"), P.dsem("hc1")]
        hnb = [AR.alloc([KC, TT], BF16) for _ in range(2)]
        hnb_tk = [Tk("hn0"), Tk("hn1")]
        sq = AR.alloc([KC, TT], BF16)
        sq_tk = Tk("sq")
        rstd = AR.alloc([TT], F32)
        rstd_tk = Tk("rstd")
        pst = AR.alloc([KC, TT], F32)
        pst_tk = Tk("pst")
        pst_sem = P.dsem("pst")
        NWS = 3
        wslab = [AR.alloc([KC, 512], BF16) for _ in range(NWS)]
        ws_tk = [Tk("ws%d" % i) for i in range(NWS)]
        ws_sem = [P.dsem("ws%d" % i) for i in range(NWS)]
        ws_i = [0]
        NW2 = 2
        w2slab = [AR.alloc([4, 512], BF16) for _ in range(NW2)]
        w2_tk = [Tk("w2%d" % i) for i in range(NW2)]
        w2_sem = [P.dsem("w2%d" % i) for i in range(NW2)]
        w2_i = [0]
        hT_tk = [Tk("hTd%d" % c) for c in range(NCH)]
        part_tk = [[Tk("part%d_%d" % (p, c)) for c in range(NCH)] for p in range(2)]
        red_tk = [[Tk("red%d_%d" % (p, c)) for c in range(NCH)] for p in range(2)]
        st_sem = P.dsem("store")
        evac_i = [0]
        pend = [None]
        sub_i = [0]

        def evac_copy(out, in_, reads, writes):
            evac_i[0] += 1
            if evac_i[0] % 2:
                P.op('act', lambda e: e.copy(out=out, in_=in_), reads=reads, writes=writes)
            else:
                P.op('dve', lambda e: e.tensor_copy(out=out, in_=in_), reads=reads, writes=writes)

        def load_ws(wname, j, cols):
            i = ws_i[0] % NWS
            ws_i[0] += 1
            for (c0, ncol, dst) in cols:
                src = wb_d[wname][j, :, c0:c0 + ncol].rearrange("(k p) n -> p k n", p=128)
                P.dma('sp', wslab[i][:, :, dst:dst + ncol], src, ws_sem[i], reads=[wtk[(wname, j)]], writes=[ws_tk[i]])
            return wslab[i], ws_tk[i]

        def load_hc(c, buf, pp):
            src = hT_d[:, c * TT:(c + 1) * TT].rearrange("(k p) t -> p k t", p=128)
            P.dma('sp', hc[buf], src, hc_sem[buf], reads=[hT_tk[c]], writes=[hc_tk[buf]])
            if pp is not None:
                P.dma_acc(hc[buf], red_t[pp][c].ap().rearrange("(k p) t -> p k t", p=128), hc_sem[buf], reads=[red_tk[pp][c]], writes=[hc_tk[buf]])
                P.dma('pool', src, hc[buf], st_sem, reads=[hc_tk[buf]], writes=[hT_tk[c]])

        def store_hc(c, buf):
            dst = hT_d[:, c * TT:(c + 1) * TT].rearrange("(k p) t -> p k t", p=128)
            P.dma('pool', dst, hc[buf], st_sem, reads=[hc_tk[buf]], writes=[hT_tk[c]])

        def rmsnorm_chunk(buf, gname, hb):
            h = hc[buf]
            hn, hn_tk = hnb[hb], hnb_tk[hb]
            P.op('act', lambda e: e.activation(out=sq, in_=h, func=AF.Square), reads=[hc_tk[buf]], writes=[sq_tk])
            bk = banks[MISC]
            for kc in range(KC):
                P.op('pe', lambda e, kc=kc: e.matmul(bk[:, :], lhsT=ones_b, rhs=sq[:, kc, :], start=(kc == 0), stop=(kc == KC - 1)),
                     reads=[sq_tk, k_tk], writes=[bank_tk[MISC]])
            P.op('act', lambda e: e.activation(out=rstd, in_=bk[:, :], func=AF.Ln, scale=1.0 / D_MODEL, bias=epsc[:, 0:1]),
                 reads=[bank_tk[MISC], cst_tk, k_tk], writes=[rstd_tk])
            P.op('act', lambda e: e.activation(out=rstd, in_=rstd, func=AF.Exp, scale=-0.5), reads=[rstd_tk], writes=[rstd_tk], strict=True)
            g = C(gname)
            for kc in range(KC):
                P.op('dve', lambda e, kc=kc: e.scalar_tensor_tensor(out=hn[:, kc, :], in0=h[:, kc, :], scalar=g[:, kc:kc + 1], in1=rstd,
                                                                  op0=ALU.mult, op1=ALU.mult),
                     reads=[hc_tk[buf], rstd_tk, cst_tk], writes=[hn_tk])

        def proj_T(slab, slab_tk, col0, M, rhs_fn, bank, n=TT, extra_reads=()):
            for kc in range(KC):
                r_ = rhs_fn(kc)
                P.op('pe', lambda e, kc=kc, r_=r_: e.matmul(banks[bank][0:M, 0:n], lhsT=slab[:, kc, col0:col0 + M], rhs=r_,
                                                            start=(kc == 0), stop=(kc == KC - 1)),
                     reads=[slab_tk] + list(extra_reads), writes=[bank_tk[bank]])

        pending_finish = [None]

        def flush_finish():
            if pending_finish[0] is not None:
                f = pending_finish[0]
                pending_finish[0] = None
                f()

        def dense_out(wname, j, nfc, rhs_fn, rhs_tks, c, par, alt=False):
            flush_finish()
            for half in range(2):
                ngrp = (nfc + 3) // 4
                for gi in range(ngrp):
                    f0 = gi * 4
                    nf = min(4, nfc - f0)
                    i = w2_i[0] % NW2
                    w2_i[0] += 1
                    src = wb_d[wname][j, f0 * 128:(f0 + nf) * 128, half * 512:(half + 1) * 512].rearrange("(f p) n -> p f n", p=128)
                    P.dma('sp', w2slab[i][:, 0:nf, :], src, w2_sem[i], reads=[wtk[(wname, j)]], writes=[w2_tk[i]])
                    for f in range(nf):
                        fc = f0 + f
                        for dmi in range(4):
                            bk_ = (4 + dmi) if (alt and half == 1) else ACC[dmi]
                            r_ = rhs_fn(fc)
                            P.op('pe', lambda e, i=i, f=f, fc=fc, dmi=dmi, bk_=bk_, r_=r_: e.matmul(
                                banks[bk_][:, :], lhsT=w2slab[i][:, f, dmi * 128:(dmi + 1) * 128], rhs=r_,
                                start=(fc == 0), stop=(fc == nfc - 1)),
                                 reads=[w2_tk[i]] + list(rhs_tks), writes=[bank_tk[bk_]])
                for dmi in range(4):
                    kc = half * 4 + dmi
                    bk_ = (4 + dmi) if (alt and half == 1) else ACC[dmi]
                    evac_copy(pst[:, kc, :], banks[bk_][:, :], [bank_tk[bk_]], [pst_tk])
            def finish():
                P.dma('sp', part_t[par][c].ap().rearrange("(k p) t -> p k t", p=128), pst, pst_sem, reads=[pst_tk], writes=[part_tk[par][c]])
                P.cc(part_t[par][c].ap().opt(), red_t[par][c].ap().opt(), groups, c % 8, reads=[part_tk[par][c]], writes=[red_tk[par][c]])
            pending_finish[0] = finish

        def conv_silu(bank, M, K, wcol_fn, bcol, pc, pc_tk, acc, acc_tk, tails, tails_tk, out, out_reads, out_writes):
            H = K - 1
            P.op('act', lambda e: e.copy(out=pc[0:M, H:H + TT], in_=banks[bank][0:M, :]), reads=[bank_tk[bank]], writes=[pc_tk])
            P.op('pool', lambda e: e.tensor_copy(out=pc[0:M, 0:H], in_=tails[0:M, :]), reads=[tails_tk], writes=[pc_tk])
            P.op('dve', lambda e: e.tensor_scalar(out=acc[0:M, :], in0=pc[0:M, 0:TT], scalar1=wcol_fn(0), scalar2=bcol, op0=ALU.mult, op1=ALU.add),
                 reads=[pc_tk, cst_tk], writes=[acc_tk])
            for k in range(1, K):
                P.op('dve', lambda e, k=k: e.scalar_tensor_tensor(out=acc[0:M, :], in0=pc[0:M, k:k + TT], scalar=wcol_fn(k), in1=acc[0:M, :],
                                                                  op0=ALU.mult, op1=ALU.add),
                     reads=[pc_tk, cst_tk, acc_tk], writes=[acc_tk])
            P.op('pool', lambda e: e.tensor_copy(out=tails[0:M, :], in_=pc[0:M, TT:TT + H]), reads=[pc_tk], writes=[tails_tk])
            P.op('act', lambda e: e.activation(out=out, in_=acc[0:M, :], func=AF.Silu), reads=[acc_tk] + list(out_reads), writes=list(out_writes))

        epsc = AR.alloc([1], F32)
        P.op('dve', lambda e: e.memset(epsc, EPS), writes=[k_tk])
        eps64 = AR.alloc([1], F32)
        P.op('dve', lambda e: e.memset(eps64, 64 * EPS), writes=[k_tk])
        onec = AR.alloc([1], F32)
        P.op('dve', lambda e: e.memset(onec, 1.0), writes=[k_tk])
        bd64_b = AR.alloc([128], BF16)
        P.op('dve', lambda e: e.tensor_copy(out=bd64_b, in_=C('bd64')), reads=[cst_tk], writes=[k_tk])

        def phase_load():
            m = AR.mark()
            xsb = [AR.alloc([4, D_MODEL], F32) for _ in range(2)]
            xsb_tk = [Tk("xs0"), Tk("xs1")]
            xsb_sem = [P.dsem("xs0"), P.dsem("xs1")]

            def ld_x(c):
                P.dma('sp', xsb[c % 2], x_d[c * TT:(c + 1) * TT, :].rearrange("(j p) d -> p j d", p=128), xsb_sem[c % 2], writes=[xsb_tk[c % 2]])
            ld_x(0)
            for c in range(NCH):
                buf = c % 2
                xs, xs_tk = xsb[c % 2], xsb_tk[c % 2]
                if c + 1 < NCH:
                    ld_x(c + 1)
                for kc in range(KC):
                    b = next_rot()
                    for j in range(4):
                        P.op('pe', lambda e, b=b, j=j, kc=kc, xs=xs: e.transpose(banks[b][:, j * 128:(j + 1) * 128], xs[:, j, kc * 128:(kc + 1) * 128], ident_f),
                             reads=[xs_tk, cst_tk], writes=[bank_tk[b]])
                    evac_copy(hc[buf][:, kc, :], banks[b][:, :], [bank_tk[b]], [hc_tk[buf]])
                store_hc(c, buf)
            AR.release(m)

        def phase_final():
            P.barrier()
            m = AR.mark()
            pp = pend[0]
            hob = [AR.alloc([KC, TT], F32) for _ in range(2)]
            hob_tk = [Tk("ho0"), Tk("ho1")]
            ysb = [AR.alloc([4, D_MODEL], F32) for _ in range(2)]
            ysb_tk = [Tk("ys0"), Tk("ys1")]
            o_sem = P.dsem("out")
            g = C('fing')
            toks = []
            load_hc(0, 0, pp)
            for c in range(NCH):
                buf = c % 2
                if c + 1 < NCH:
                    load_hc(c + 1, 1 - buf, pp)
                h = hc[buf]
                ho, ho_tk, ys, ys_tk = hob[buf], hob_tk[buf], ysb[buf], ysb_tk[buf]
                P.op('act', lambda e, h=h: e.activation(out=sq, in_=h, func=AF.Square), reads=[hc_tk[buf]], writes=[sq_tk])
                bk = banks[MISC]
                for kc in range(KC):
                    P.op('pe', lambda e, kc=kc: e.matmul(bk[:, :], lhsT=ones_b, rhs=sq[:, kc, :], start=(kc == 0), stop=(kc == KC - 1)),
                         reads=[sq_tk, k_tk], writes=[bank_tk[MISC]])
                P.op('act', lambda e: e.activation(out=rstd, in_=bk[:, :], func=AF.Ln, scale=1.0 / D_MODEL, bias=epsc[:, 0:1]),
                     reads=[bank_tk[MISC], k_tk], writes=[rstd_tk])
                P.op('act', lambda e: e.activation(out=rstd, in_=rstd, func=AF.Exp, scale=-0.5), reads=[rstd_tk], writes=[rstd_tk], strict=True)
                for kc in range(KC):
                    P.op('dve', lambda e, kc=kc, h=h, ho=ho: e.scalar_tensor_tensor(out=ho[:, kc, :], in0=h[:, kc, :], scalar=g[:, kc:kc + 1], in1=rstd,
                                                                             op0=ALU.mult, op1=ALU.mult),
                         reads=[hc_tk[buf], rstd_tk, cst_tk], writes=[ho_tk])
                for j in range(4):
                    for half in range(2):
                        b = next_rot()
                        for q in range(4):
                            kc = half * 4 + q
                            P.op('pe', lambda e, b=b, q=q, kc=kc, j=j, ho=ho: e.transpose(banks[b][:, q * 128:(q + 1) * 128], ho[:, kc, j * 128:(j + 1) * 128], ident_f),
                                 reads=[ho_tk, cst_tk], writes=[bank_tk[b]])
                        evac_copy(ys[:, j, half * 512:(half + 1) * 512], banks[b][:, :], [bank_tk[b]], [ys_tk])
                t = P.dma('pool', out_d[c * TT:(c + 1) * TT, :].rearrange("(j p) d -> p j d", p=128), ys, o_sem, reads=[ys_tk])
                toks.append(t)
            P.finalize_wait('pool', [toks[-1]])
            AR.release(m)

        def phase_ffn(i):
            P.barrier()
            m = AR.mark()
            pp = pend[0]
            par = sub_i[0] % 2
            sub_i[0] += 1
            aT = AR.alloc([FFC, TT], BF16)
            aT_tk = [Tk("aT%d" % f) for f in range(FFC)]
            pcs = [AR.alloc([TT + 4], F32) for _ in range(2)]
            pc_tks = [Tk("pc0"), Tk("pc1")]
            accs = [AR.alloc([TT], F32) for _ in range(2)]
            acc_tks = [Tk("acc0"), Tk("acc1")]
            gs = [AR.alloc([TT], BF16) for _ in range(2)]
            gs_tks = [Tk("gs0"), Tk("gs1")]
            tails = AR.alloc([FFC, 2], F32)
            tails_tk = [Tk("tl%d" % f) for f in range(FFC)]
            P.op('pool', lambda e: e.memset(tails, 0.0), writes=tails_tk)
            fcw = C('fcw%d' % i)
            fcb = C('fcb%d' % i)
            GW = FFC * 128
            load_hc(0, 0, pp)
            rmsnorm_chunk(0, 'ffng%d' % i, 0)
            for c in range(NCH):
                buf = c % 2
                hn, hn_tk = hnb[buf], hnb_tk[buf]
                if c + 1 < NCH:
                    load_hc(c + 1, 1 - buf, pp)
                j0 = 0
                while j0 < FFC:
                    nj = min(2, FFC - j0)
                    slab, stk = load_ws('ffn_w_up', i, [(j0 * 128, nj * 128, 0), (GW + j0 * 128, nj * 128, 256)])
                    for u in range(nj):
                        j = j0 + u
                        pi = j % 2
                        b = next_rot()
                        proj_T(slab, stk, u * 128, 128, lambda kc: hn[:, kc, :], b, extra_reads=[hn_tk])
                        conv_silu(b, 128, 3, lambda k, j=j: fcw[:, j * 3 + k:j * 3 + k + 1], fcb[:, j:j + 1], pcs[pi], pc_tks[pi], accs[pi], acc_tks[pi],
                                  tails[:, j, :], tails_tk[j], gs[pi], [], [gs_tks[pi]])
                    for u in range(nj):
                        j = j0 + u
                        pi = j % 2
                        b = next_rot()
                        proj_T(slab, stk, 256 + u * 128, 128, lambda kc: hn[:, kc, :], b, extra_reads=[hn_tk])
                        P.op('dve', lambda e, j=j, pi=pi, b=b: e.tensor_tensor(out=aT[:, j, :], in0=gs[pi], in1=banks[b][:, :], op=ALU.mult),
                             reads=[gs_tks[pi], bank_tk[b]], writes=[aT_tk[j]])
                    j0 += nj
                    if j0 >= 4:
                        flush_finish()
                if c + 1 < NCH:
                    rmsnorm_chunk(1 - buf, 'ffng%d' % i, 1 - buf)
                dense_out('ffn_w_down', i, FFC, lambda fc: aT[:, fc, :], aT_tk, c, par)
            flush_finish()
            pend[0] = par
            AR.release(m)

        def phase_fox(jl, li):
            P.barrier()
            m = AR.mark()
            pp = pend[0]
            par = sub_i[0] % 2
            sub_i[0] += 1
            fw = 'fox_w_in'
            HW = FH * 64
            NPR = FH // 2
            qk_st = [AR.alloc([NPR, TT], BF16) for _ in range(2)]
            qk_tk = [Tk("qkst0"), Tk("qkst1")]
            qk_sem = [P.dsem("qkst0"), P.dsem("qkst1")]
            sqh = [AR.alloc([TT], BF16) for _ in range(2)]
            sqh_tk = [Tk("sqh0"), Tk("sqh1")]
            rh = [AR.alloc([TT], F32) for _ in range(2)]
            rh_tk = [Tk("rh0"), Tk("rh1")]
            vst = AR.alloc([4, FH, 65], BF16)
            vst_tk = Tk("vst")
            v_sem = P.dsem("vst")
            sgst = AR.alloc([HW // 128, TT], BF16)
            sgst_tk = Tk("sgst")
            sg_sem = P.dsem("sgst")
            wf = AR.alloc([KC, FH], BF16)
            wf_tk = Tk("wf")
            wf_sem = P.dsem("wf")
            ef = AR.alloc([TT], F32)
            sA = AR.alloc([TT], F32)
            sB = AR.alloc([TT], F32)
            f_tk = Tk("fchain")
            carry = AR.alloc([1], F32)
            r1 = AR.alloc([TT], F32)
            c3q = AR.alloc([3, TT], BF16)
            c3k = AR.alloc([3, TT], BF16)
            c3_tk = Tk("c3")
            c3_sem = P.dsem("c3")
            ones3 = AR.alloc([3, TT], BF16)
            o3_tk = Tk("ones3")
            o3_sem = P.dsem("o3")
            qa_tk = Tk("qa_d")
            ka_tk = Tk("ka_d")
            qa1_tk, qa2_tk, ka1_tk, ka2_tk = Tk("qa1"), Tk("qa2"), Tk("ka1"), Tk("ka2")
            vd_tk = Tk("v_d")
            sgd_tk = Tk("sg_d")
            otd_tk = Tk("ot_d")
            nones3 = AR.alloc([3, TT], BF16)
            onesF = AR.alloc([TT], F32)
            P.op('pool', lambda e: e.memset(ones3, 1.0), writes=[o3_tk])
            P.op('pool', lambda e: e.memset(nones3, -1.0), writes=[o3_tk])
            P.op('pool', lambda e: e.memset(onesF, 1.0), writes=[o3_tk])
            P.op('pool', lambda e: e.memset(carry, 0.0), writes=[f_tk])
            P.op('pool', lambda e: e.memset(vst, 1.0), writes=[vst_tk])
            for c in range(NCH):
                P.dma('sp', qa_d[:, 67:70, c * TT:(c + 1) * TT], ones3[0:FH], o3_sem, reads=[o3_tk], writes=[qa1_tk])
                P.dma('sp', ka_d[:, 64:67, c * TT:(c + 1) * TT], nones3[0:FH], o3_sem, reads=[o3_tk], writes=[ka1_tk])
            P.dma('sp', wf, wb_d[fw][jl, :, 4 * HW:4 * HW + FH].rearrange("(k p) n -> p k n", p=128), wf_sem, reads=[wtk[(fw, jl)]], writes=[wf_tk])
            qg = C('fqg%d' % jl)
            kg = C('fkg%d' % jl)
            nbf = C('fbf%d' % jl, FH)
            load_hc(0, 0, pp)
            rmsnorm_chunk(0, 'mixg%d' % li, 0)
            for c in range(NCH):
                buf = c % 2
                hn, hn_tk = hnb[buf], hnb_tk[buf]
                cols = slice(c * TT, (c + 1) * TT)
                if c + 1 < NCH:
                    load_hc(c + 1, 1 - buf, pp)
                bm = MISC
                proj_T(wf, wf_tk, 0, FH, lambda kc: hn[:, kc, :], bm, extra_reads=[hn_tk])
                P.op('dve', lambda e: e.tensor_scalar(out=ef[0:FH], in0=banks[bm][0:FH, :], scalar1=nbf[:, 0:1], scalar2=-1.0, op0=ALU.add, op1=ALU.mult),
                     reads=[bank_tk[bm], cst_tk, f_tk], writes=[f_tk])
                P.op('act', lambda e: e.activation(out=ef[0:FH], in_=ef[0:FH], func=AF.Exp), reads=[f_tk], writes=[f_tk])
                P.op('act', lambda e: e.activation(out=sA[0:FH], in_=ef[0:FH], func=AF.Ln, bias=onec[0:FH, 0:1]), reads=[f_tk, k_tk], writes=[f_tk], strict=True)
                P.op('dve', lambda e: e.tensor_tensor_scan(out=sB[0:FH], data0=onesF[0:FH], data1=sA[0:FH], initial=carry[0:FH, 0:1], op0=ALU.mult, op1=ALU.add),
                     reads=[f_tk, o3_tk], writes=[f_tk], strict=True)
                P.op('dve', lambda e: e.tensor_copy(out=carry[0:FH], in_=sB[0:FH, TT - 1:TT]), reads=[f_tk], writes=[f_tk], strict=True)
                P.op('act', lambda e: e.copy(out=c3k[0:FH, 0, :], in_=sB[0:FH]), reads=[f_tk, c3_tk], writes=[c3_tk])
                P.op('dve', lambda e: e.tensor_tensor(out=r1[0:FH], in0=sB[0:FH], in1=c3k[0:FH, 0, :], op=ALU.subtract), reads=[f_tk, c3_tk], writes=[f_tk])
                P.op('act', lambda e: e.copy(out=c3k[0:FH, 1, :], in_=r1[0:FH]), reads=[f_tk, c3_tk], writes=[c3_tk])
                P.op('dve', lambda e: e.tensor_tensor(out=sA[0:FH], in0=r1[0:FH], in1=c3k[0:FH, 1, :], op=ALU.subtract), reads=[f_tk, c3_tk], writes=[f_tk])
                P.op('act', lambda e: e.copy(out=c3k[0:FH, 2, :], in_=sA[0:FH]), reads=[f_tk, c3_tk], writes=[c3_tk])
                P.dma('pool', qa_d[:, 64:67, cols], c3k[0:FH], c3_sem, reads=[c3_tk], writes=[qa2_tk])
                P.dma('pool', ka_d[:, 67:70, cols], c3k[0:FH], c3_sem, reads=[c3_tk], writes=[ka2_tk])
                tasks = []
                for which in range(2):
                    for pr in range(NPR):
                        tasks.append((which, pr))
                slabs = {}
                tb = {}

                def t_front(ti):
                    which, pr = tasks[ti]
                    if pr == 0:
                        slabs[which] = load_ws(fw, jl, [(which * HW, HW, 0)])
                    slab, stk = slabs[which]
                    b = next_rot()
                    tb[ti] = b
                    proj_T(slab, stk, pr * 128, 128, lambda kc: hn[:, kc, :], b, extra_reads=[hn_tk])

                def t_back(ti):
                    which, pr = tasks[ti]
                    b = tb[ti]
                    x_ = ti % 2
                    P.op('act', lambda e: e.activation(out=sqh[x_], in_=banks[b][:, :], func=AF.Square), reads=[bank_tk[b]], writes=[sqh_tk[x_]])
                    P.op('pe', lambda e: e.matmul(banks[MISC][:, :], lhsT=bd64_b, rhs=sqh[x_], start=True, stop=True),
                         reads=[sqh_tk[x_], k_tk], writes=[bank_tk[MISC]])
                    if which == 0:
                        P.op('act', lambda e: e.activation(out=rh[x_], in_=banks[MISC][:, :], func=AF.Ln, scale=1.0, bias=eps64[:, 0:1]),
                             reads=[bank_tk[MISC], k_tk], writes=[rh_tk[x_]])
                    else:
                        P.op('act', lambda e: e.activation(out=rh[x_], in_=banks[MISC][:, :], func=AF.Ln, scale=1.0 / 64, bias=epsc[:, 0:1]),
                             reads=[bank_tk[MISC], k_tk], writes=[rh_tk[x_]])
                    P.op('act', lambda e: e.activation(out=rh[x_], in_=rh[x_], func=AF.Exp, scale=-0.5), reads=[rh_tk[x_]], writes=[rh_tk[x_]], strict=True)
                    gcol = qg if which == 0 else kg
                    P.op('dve', lambda e: e.scalar_tensor_tensor(out=qk_st[which][:, pr, :], in0=banks[b][:, :], scalar=gcol[:, 0:1], in1=rh[x_],
                                                                 op0=ALU.mult, op1=ALU.mult),
                         reads=[bank_tk[b], rh_tk[x_], cst_tk], writes=[qk_tk[which]])
                    if pr == NPR - 1:
                        dv = (qa_d if which == 0 else ka_d)[:, 0:64, cols].rearrange("(hp two) d t -> two d hp t", two=2)
                        for two in range(2):
                            P.dma('pool', dv[two], qk_st[which][two * 64:(two + 1) * 64], qk_sem[which], reads=[qk_tk[which]],
                                  writes=[qa_tk if which == 0 else ka_tk])

                for ti in range(len(tasks) + 1):
                    if ti < len(tasks):
                        t_front(ti)
                    if ti >= 1:
                        t_back(ti - 1)
                slab, stk = load_ws(fw, jl, [(2 * HW, HW, 0)])
                for jb in range(4):
                    b = next_rot()
                    for kc in range(KC):
                        P.op('pe', lambda e, kc=kc, jb=jb, b=b, slab=slab, hn=hn: e.matmul(banks[b][:, 0:HW], lhsT=hn[:, kc, jb * 128:(jb + 1) * 128], rhs=slab[:, kc, 0:HW],
                                                                                  start=(kc == 0), stop=(kc == KC - 1)),
                             reads=[stk, hn_tk], writes=[bank_tk[b]])
                    evac_copy(vst[:, jb, :, 0:64], banks[b][:, 0:HW].rearrange("p (h d) -> p h d", h=FH), [bank_tk[b]], [vst_tk])
                P.dma('pool', v_d[cols, :].rearrange("(j p) e -> p j e", p=128), vst.rearrange("p j h e -> p j (h e)"), v_sem, reads=[vst_tk], writes=[vd_tk])
                slab, stk = load_ws(fw, jl, [(3 * HW, HW, 0)])
                if c + 1 < NCH:
                    rmsnorm_chunk(1 - buf, 'mixg%d' % li, 1 - buf)
                for u in range(HW // 128):
                    b = next_rot()
                    proj_T(slab, stk, u * 128, 128, lambda kc: hn[:, kc, :], b, extra_reads=[hn_tk])
                    P.op('act', lambda e, b=b, u=u: e.activation(out=sgst[:, u, :], in_=banks[b][:, :], func=AF.Sigmoid),
                         reads=[bank_tk[b]], writes=[sgst_tk])
                P.dma('pool', sg_d[:, cols].rearrange("(k p) t -> p k t", p=128), sgst, sg_sem, reads=[sgst_tk], writes=[sgd_tk])
            AR.release(m)

            P.barrier()
            m = AR.mark()
            NB = seq // 128
            NQH = max(1, seq // 2048)
            QW = seq // NQH
            NBK = QW // 512
            Ka = [AR.alloc([seq], BF16) for _ in range(2)]
            Qa = [AR.alloc([seq], BF16) for _ in range(2)]
            Vh = [AR.alloc([NB, 65], BF16) for _ in range(2)]
            kqv_tk = [Tk("kqv0"), Tk("kqv1")]
            kqv_sem = [P.dsem("kqv0"), P.dsem("kqv1")]
            PT = [AR.alloc([512], BF16) for _ in range(3)]
            pt_tk = [Tk("pt%d" % i) for i in range(3)]
            pt_i = 0
            rrow = AR.alloc([512], F32)
            rrow_tk = Tk("rrow")
            bcs = AR.alloc([512], F32)
            bcs_tk = Tk("bcs")
            Ost = [AR.alloc([QW], BF16) for _ in range(2)]
            ost_tk = [Tk("ost0"), Tk("ost1")]
            ost_sem = [P.dsem("ost0"), P.dsem("ost1")]
            oi = 0
            NHL = FH

            def load_head(h, s):
                P.dma('sp', Ka[s][0:70], ka_d[h, :, :], kqv_sem[s], reads=[ka_tk, ka1_tk, ka2_tk], writes=[kqv_tk[s]])
                P.dma('sp', Qa[s][0:70], qa_d[h, :, :], kqv_sem[s], reads=[qa_tk, qa1_tk, qa2_tk], writes=[kqv_tk[s]])
                P.dma('sp', Vh[s], v_d[:, h * 65:(h + 1) * 65].rearrange("(kb p) e -> p kb e", p=128), kqv_sem[s], reads=[vd_tk], writes=[kqv_tk[s]])

            items = []
            for h in range(NHL):
                items.append(('load', h))
                for qh in range(NQH):
                    qb0 = qh * (QW // 128)
                    nqb = QW // 128
                    last_kb = qb0 + nqb - 1
                    for kb in range(last_kb + 1):
                        for a in range(NBK):
                            blk0 = qb0 + 4 * a
                            lo = max(kb, blk0)
                            hi = blk0 + 4
                            if lo >= hi:
                                continue
                            items.append(('step', h, kb, a, blk0, lo, (hi - lo) * 128))
                    items.append(('norm', h, qh))
            LOOK = 2
            st_bank = {}
            load_head(0, 0)

            def front(it):
                if it[0] != 'step':
                    return
                _, h, kb, a, blk0, lo, ncol = it
                s_ = h % 2
                diag = (lo == kb)
                b = next_rot()
                st_bank[id(it)] = b
                P.op('pe', lambda e: e.matmul(banks[b][:, 0:ncol], lhsT=Ka[s_][0:70, kb * 128:(kb + 1) * 128], rhs=Qa[s_][0:70, lo * 128:lo * 128 + ncol],
                                              start=True, stop=(not diag)), reads=[kqv_tk[s_]], writes=[bank_tk[b]])
                if diag:
                    P.op('pe', lambda e: e.matmul(banks[b][:, 0:128], lhsT=ident_b, rhs=mask_b[:, 0, :], start=False, stop=True),
                         reads=[k_tk], writes=[bank_tk[b]])

            def back(it):
                nonlocal pt_i, oi
                if it[0] == 'load':
                    if it[1] + 1 < NHL:
                        load_head(it[1] + 1, (it[1] + 1) % 2)
                    return
                if it[0] == 'step':
                    _, h, kb, a, blk0, lo, ncol = it
                    s_ = h % 2
                    b = st_bank.pop(id(it))
                    pi = pt_i % 3
                    pt_i += 1
                    P.op('act', lambda e: e.activation(out=PT[pi][:, 0:ncol], in_=banks[b][:, 0:ncol], func=AF.Exp),
                         reads=[bank_tk[b]], writes=[pt_tk[pi]])
                    c0 = (lo - blk0) * 128
                    lastk = (kb == blk0 + 3)
                    P.op('pe', lambda e: e.matmul(banks[ACC[a]][0:65, c0:c0 + ncol], lhsT=Vh[s_][:, kb, :], rhs=PT[pi][:, 0:ncol],
                                                  start=(kb == 0), stop=lastk, skip_group_check=True),
                         reads=[kqv_tk[s_], pt_tk[pi]], writes=[bank_tk[ACC[a]]])
                elif it[0] == 'norm':
                    _, h, qh = it
                    os_ = oi % 2
                    oi += 1
                    for a in range(NBK):
                        P.op('dve', lambda e, a=a: e.reciprocal(out=rrow[64:65, :], in_=banks[ACC[a]][64:65, :]), reads=[bank_tk[ACC[a]]], writes=[rrow_tk])
                        P.op('pe', lambda e: e.matmul(banks[MISC][0:64, :], lhsT=ones_f[64:65, 0:64], rhs=rrow[64:65, :], start=True, stop=True),
                             reads=[rrow_tk, k_tk], writes=[bank_tk[MISC]])
                        P.op('act', lambda e: e.copy(out=bcs[0:64], in_=banks[MISC][0:64, :]), reads=[bank_tk[MISC]], writes=[bcs_tk])
                        P.op('dve', lambda e, a=a, os_=os_: e.tensor_tensor(out=Ost[os_][0:64, a * 512:(a + 1) * 512], in0=banks[ACC[a]][0:64, :], in1=bcs[0:64], op=ALU.mult),
                             reads=[bank_tk[ACC[a]], bcs_tk], writes=[ost_tk[os_]])
                    P.dma('pool', ot_d[h * 64:(h + 1) * 64, qh * QW:(qh + 1) * QW], Ost[os_][0:64], ost_sem[os_], reads=[ost_tk[os_]], writes=[otd_tk])

            for i in range(len(items) + LOOK):
                if i < len(items):
                    front(items[i])
                if i - LOOK >= 0:
                    back(items[i - LOOK])
            AR.release(m)

            P.barrier()
            m = AR.mark()
            NFC = HW // 128
            oc = [AR.alloc([NFC, TT], BF16) for _ in range(2)]
            gc = [AR.alloc([NFC, TT], BF16) for _ in range(2)]
            og_tk = [Tk("og0"), Tk("og1")]
            og_sem = [P.dsem("og0"), P.dsem("og1")]
            yT = AR.alloc([NFC, TT], BF16)
            yT_tk = Tk("yT")

            def load_og(c, s):
                cols = slice(c * TT, (c + 1) * TT)
                P.dma('sp', oc[s], ot_d[:, cols].rearrange("(k p) t -> p k t", p=128), og_sem[s], reads=[otd_tk], writes=[og_tk[s]])
                P.dma('sp', gc[s], sg_d[:, cols].rearrange("(k p) t -> p k t", p=128), og_sem[s], reads=[sgd_tk], writes=[og_tk[s]])

            load_og(0, 0)
            for c in range(NCH):
                buf = c % 2
                if c + 1 < NCH:
                    load_og(c + 1, 1 - buf)
                for kc in range(NFC):
                    eng = 'pool' if kc % 2 else 'dve'
                    P.op(eng, lambda e, kc=kc, buf=buf: e.tensor_tensor(out=yT[:, kc, :], in0=oc[buf][:, kc, :], in1=gc[buf][:, kc, :], op=ALU.mult),
                         reads=[og_tk[buf]], writes=[yT_tk])
                dense_out('fox_w_out', jl, NFC, lambda fc: yT[:, fc, :], [yT_tk], c, par, alt=True)
            flush_finish()
            pend[0] = par
            AR.release(m)

        def phase_ssd(jl, li):
            P.barrier()
            m = AR.mark()
            pp = pend[0]
            par = sub_i[0] % 2
            sub_i[0] += 1
            sw = 'ssd_w_in'
            XW = SH * 64
            NHG = SH // 4
            zs = AR.alloc([4, XW], BF16)
            zs_tk = [Tk("zs%d" % j) for j in range(4)]
            xbc = AR.alloc([SNC, TT], BF16)
            xbc_tk = [Tk("xbc%d" % f) for f in range(SNC)]
            BOF, COF = SXC, SXC + SG
            pcs = [AR.alloc([TT + 4], F32) for _ in range(2)]
            pc_tks = [Tk("pc0"), Tk("pc1")]
            accs = [AR.alloc([TT], F32) for _ in range(2)]
            acc_tks = [Tk("acc0"), Tk("acc1")]
            tails = AR.alloc([SNC, 3], F32)
            tails_tk = [Tk("tl%d" % f) for f in range(SNC)]
            wdt = AR.alloc([KC, SH], BF16)
            wdt_tk = Tk("wdt")
            wdt_sem = P.dsem("wdt")
            dtT = AR.alloc([TT], F32)
            dtT_tk = Tk("dtT")
            arow = AR.alloc([SH], F32)
            arow_tk = Tk("arow")
            dtk = AR.alloc([4, SH], F32)
            dak = AR.alloc([4, SH], F32)
            dtk_tk = Tk("dtk")
            NSET = 2
            hst = AR.alloc([XW], F32)
            hst_tk = Tk("hst")
            hpbs = [AR.alloc([XW], BF16) for _ in range(3)]
            hpb_tks = [Tk("hpb0"), Tk("hpb1"), Tk("hpb2")]
            junk = AR.alloc([512], BF16)
            SETS = []
            for si in range(NSET):
                d = {}
                d['sm'] = AR.alloc([4, SH], F32); d['sm_tk'] = Tk("sm%d" % si)
                d['dcy'] = AR.alloc([SH, 128], BF16); d['dcy_tk'] = Tk("dcy%d" % si)
                d['cb'] = AR.alloc([SG, 128], BF16); d['cb_tk'] = Tk("cb%d" % si)
                d['xdt'] = AR.alloc([XW], BF16); d['xdt_tk'] = Tk("xdt%d" % si)
                d['xsk'] = AR.alloc([XW], BF16); d['xsk_tk'] = Tk("xsk%d" % si)
                d['Btk'] = AR.alloc([SG, 128], BF16); d['Btk_tk'] = Tk("Btk%d" % si)
                d['yy'] = AR.alloc([XW], F32); d['yy_tk'] = Tk("yy%d" % si)
                d['Dm'] = d['yy'].bitcast(BF16)[:, 0:2 * XW].rearrange("p (a b) -> p a b", a=SH); d['Dm_tk'] = d['yy_tk']
                d['ssq'] = AR.alloc([SG], F32); d['ssq_tk'] = Tk("ssq%d" % si)
                d['yn'] = AR.alloc([XW], BF16); d['yn_tk'] = Tk("yn%d" % si)
                d['xdw'] = d['yn']; d['xdw_tk'] = d['yn_tk']
                SETS.append(d)
            sng = AR.alloc([XW], F32)
            sng_tk = Tk("sng")
            sng_sem = P.dsem("sng")
            P.dma('sp', sng, sng_d[jl, :, :], sng_sem, writes=[sng_tk])
            dskr = C('dsk%d' % jl)
            scw = C('scw%d' % jl)
            scb = C('scb%d' % jl)
            dtb = C('dtb%d' % jl, SH)
            P.op('pool', lambda e: e.memset(tails, 0.0), writes=tails_tk)
            P.op('pool', lambda e: e.memset(hst, 0.0), writes=[hst_tk])
            P.op('pool', lambda e: e.memset(hpbs[0], 0.0), writes=[hpb_tks[0]])
            P.op('act', lambda e: e.activation(out=arow, in_=C('alog%d' % jl), func=AF.Exp), reads=[cst_tk], writes=[arow_tk])
            P.op('dve', lambda e: e.tensor_scalar(out=arow, in0=arow, scalar1=-1.0, scalar2=None, op0=ALU.mult), reads=[arow_tk], writes=[arow_tk])
            DTC = 2 * XW + 2 * SG * 128
            P.dma('sp', wdt, wb_d[sw][jl, :, DTC:DTC + SH].rearrange("(k p) n -> p k n", p=128), wdt_sem, reads=[wtk[(sw, jl)]], writes=[wdt_tk])
            load_hc(0, 0, pp)
            rmsnorm_chunk(0, 'mixg%d' % li, 0)
            for c in range(NCH):
                buf = c % 2
                hn, hn_tk = hnb[buf], hnb_tk[buf]
                if c + 1 < NCH:
                    load_hc(c + 1, 1 - buf, pp)
                proj_T(wdt, wdt_tk, 0, SH, lambda kc: hn[:, kc, :], MISC, extra_reads=[hn_tk])
                P.op('act', lambda e: e.activation(out=dtT[0:SH], in_=banks[MISC][0:SH, :], func=AF.Exp, bias=dtb[:, 0:1]),
                     reads=[bank_tk[MISC], cst_tk], writes=[dtT_tk])
                P.op('act', lambda e: e.activation(out=dtT[0:SH], in_=dtT[0:SH], func=AF.Ln, bias=onec[0:SH, 0:1]), reads=[dtT_tk, k_tk], writes=[dtT_tk], strict=True)
                for jb in range(4):
                    P.op('pe', lambda e, jb=jb: e.transpose(banks[MISC][:, jb * SH:(jb + 1) * SH], dtT[0:SH, jb * 128:(jb + 1) * 128], ident_f[0:SH, 0:SH]),
                         reads=[dtT_tk, cst_tk], writes=[bank_tk[MISC]])
                P.op('dve', lambda e: e.tensor_copy(out=dtk, in_=banks[MISC][:, 0:4 * SH].rearrange("p (j h) -> p j h", j=4)), reads=[bank_tk[MISC]], writes=[dtk_tk])
                for jb in range(4):
                    P.op('dve', lambda e, jb=jb: e.tensor_tensor(out=dak[:, jb, :], in0=dtk[:, jb, :], in1=arow, op=ALU.mult), reads=[dtk_tk, arow_tk], writes=[dtk_tk], strict=True)
                for sl in range(XW // 512):
                    slab, stk = load_ws(sw, jl, [(sl * 512, 512, 0)])
                    for jb in range(4):
                        b = next_rot()
                        for kc in range(KC):
                            P.op('pe', lambda e, kc=kc, jb=jb, b=b, slab=slab, hn=hn: e.matmul(banks[b][:, :], lhsT=hn[:, kc, jb * 128:(jb + 1) * 128], rhs=slab[:, kc, :],
                                                                                      start=(kc == 0), stop=(kc == KC - 1)),
                                 reads=[stk, hn_tk], writes=[bank_tk[b]])
                        P.op('act', lambda e, b=b, jb=jb, sl=sl: e.activation(out=zs[:, jb, sl * 512:(sl + 1) * 512], in_=banks[b][:, :], func=AF.Silu),
                             reads=[bank_tk[b]], writes=[zs_tk[jb]])
                flush_finish()
                for sl in range(SNC // 4):
                    slab, stk = load_ws(sw, jl, [(XW + sl * 512, 512, 0)])
                    for u in range(4):
                        f = sl * 4 + u
                        pi = f % 2
                        b = next_rot()
                        proj_T(slab, stk, u * 128, 128, lambda kc: hn[:, kc, :], b, extra_reads=[hn_tk])
                        conv_silu(b, 128, 4, lambda k, f=f: scw[:, f * 4 + k:f * 4 + k + 1], scb[:, f:f + 1], pcs[pi], pc_tks[pi], accs[pi], acc_tks[pi],
                                  tails[:, f, :], tails_tk[f], xbc[:, f, :], [], [xbc_tk[f]])
                def prep_stages(jb):
                    S = SETS[jb % NSET]
                    sm, sm_tk, Dm, Dm_tk, dcy, dcy_tk, cb, cb_tk = S['sm'], S['sm_tk'], S['Dm'], S['Dm_tk'], S['dcy'], S['dcy_tk'], S['cb'], S['cb_tk']
                    xdt, xdt_tk, xsk, xsk_tk, xdw, xdw_tk, Btk, Btk_tk = S['xdt'], S['xdt_tk'], S['xsk'], S['xsk_tk'], S['xdw'], S['xdw_tk'], S['Btk'], S['Btk_tk']
                    Mt, Mt_tk = dcy, dcy_tk
                    bc = slice(jb * 128, (jb + 1) * 128)
                    mo = (jb % NSET) * 4 * SH
                    st = []

                    def s_acs():
                        P.op('pe', lambda e: e.matmul(banks[MISC][:, mo:mo + SH], lhsT=C('U'), rhs=dak[:, jb, :], start=True, stop=True),
                             reads=[dtk_tk, cst_tk], writes=[bank_tk[MISC]])
                        P.op('pe', lambda e: e.matmul(banks[MISC][:, mo + SH:mo + 2 * SH], lhsT=C('Ubar'), rhs=dak[:, jb, :], start=True, stop=True),
                             reads=[dtk_tk, cst_tk], writes=[bank_tk[MISC]])
                        P.op('pe', lambda e: e.matmul(banks[MISC][:, mo + 2 * SH:mo + 3 * SH], lhsT=ones_f, rhs=dak[:, jb, :], start=True, stop=True),
                             reads=[dtk_tk, k_tk], writes=[bank_tk[MISC]])
                        P.op('act', lambda e: e.activation(out=sm[:, 0:3, :].rearrange("p a b -> p (a b)"), in_=banks[MISC][:, mo:mo + 3 * SH], func=AF.Exp),
                             reads=[bank_tk[MISC]], writes=[sm_tk])
                    st.append(s_acs)

                    def s_dm():
                        P.op('dve', lambda e: e.tensor_tensor(out=Dm, in0=U_b.unsqueeze(1).to_broadcast([128, SH, 128]),
                                                              in1=dak[:, jb, :].unsqueeze(2).to_broadcast([128, SH, 128]), op=ALU.mult),
                             reads=[dtk_tk, k_tk], writes=[Dm_tk])
                    st.append(s_dm)

                    def s_xT():
                        b = next_rot()
                        bkb = banks[b][:, :].bitcast(BF16)
                        for q in range(SXC):
                            P.op('pe', lambda e, q=q: e.transpose(bkb[:, q * 128:(q + 1) * 128], xbc[:, q, bc], ident_b),
                                 reads=[xbc_tk[q], k_tk], writes=[bank_tk[b]])
                        P.op('dve', lambda e: e.tensor_tensor(
                            out=xdt.rearrange("p (h d) -> p h d", h=SH), in0=bkb[:, 0:XW].rearrange("p (h d) -> p h d", h=SH),
                            in1=dtk[:, jb, :].unsqueeze(2).to_broadcast([128, SH, 64]), op=ALU.mult),
                             reads=[bank_tk[b], dtk_tk], writes=[xdt_tk])
                        P.op('dve', lambda e: e.tensor_tensor(
                            out=xsk.rearrange("p (h d) -> p h d", h=SH), in0=bkb[:, 0:XW].rearrange("p (h d) -> p h d", h=SH),
                            in1=dskr.unsqueeze(2).to_broadcast([128, SH, 64]), op=ALU.mult),
                             reads=[bank_tk[b], cst_tk], writes=[xsk_tk])
                    st.append(s_xT)

                    def mk_E(hg):
                        def s_E():
                            b = next_rot()
                            P.op('pe', lambda e: e.matmul(banks[b][:, :], lhsT=ones_b, rhs=Dm[:, hg * 4:(hg + 1) * 4, :].rearrange("p a b -> p (a b)"),
                                                          start=True, stop=False), reads=[Dm_tk, k_tk], writes=[bank_tk[b]])
                            for hh in range(4):
                                P.op('pe', lambda e, hh=hh: e.matmul(banks[b][:, hh * 128:(hh + 1) * 128], lhsT=Dm[:, hg * 4 + hh, :], rhs=nones_b,
                                                                     start=False, stop=False, skip_group_check=True), reads=[Dm_tk, k_tk], writes=[bank_tk[b]])
                            P.op('pe', lambda e: e.matmul(banks[b][:, :], lhsT=ident_b, rhs=mask_b.rearrange("p a b -> p (a b)"), start=False, stop=True, skip_group_check=True),
                                 reads=[k_tk], writes=[bank_tk[b]])
                            P.op('act', lambda e: e.activation(out=dcy[:, hg * 4:(hg + 1) * 4, :].rearrange("p a b -> p (a b)"), in_=banks[b][:, :], func=AF.Exp),
                                 reads=[bank_tk[b]], writes=[dcy_tk])
                        return s_E
                    for hg in range(NHG):
                        st.append(mk_E(hg))

                    def s_cb():
                        b = next_rot()
                        for g in range(SG):
                            P.op('pe', lambda e, g=g: e.matmul(banks[b][:, g * 128:(g + 1) * 128], lhsT=xbc[:, BOF + g, bc], rhs=xbc[:, COF + g, bc], start=True, stop=True),
                                 reads=[xbc_tk[BOF + g], xbc_tk[COF + g]], writes=[bank_tk[b]])
                        evac_copy(cb, banks[b][:, 0:SG * 128].rearrange("p (g l) -> p g l", g=SG), [bank_tk[b]], [cb_tk])
                        b2 = next_rot()
                        bkb2 = banks[b2][:, :].bitcast(BF16)
                        for g in range(SG):
                            P.op('pe', lambda e, g=g: e.transpose(bkb2[:, g * 128:(g + 1) * 128], xbc[:, BOF + g, bc], ident_b),
                                 reads=[xbc_tk[BOF + g], k_tk], writes=[bank_tk[b2]])
                        evac_copy(Btk, bkb2[:, 0:SG * 128].rearrange("p (g n) -> p g n", g=SG), [bank_tk[b2]], [Btk_tk])
                    st.append(s_cb)

                    def s_mt():
                        P.op('dve', lambda e: e.tensor_tensor(out=Mt.rearrange("p (g r) l -> p g r l", g=SG), in0=dcy.rearrange("p (g r) l -> p g r l", g=SG),
                                                              in1=cb.unsqueeze(2).to_broadcast([128, SG, 8, 128]), op=ALU.mult),
                             reads=[dcy_tk, cb_tk], writes=[Mt_tk])
                    st.append(s_mt)
                    return st

                def back_stages(jb):
                    S = SETS[jb % NSET]
                    sm, sm_tk, dcy, dcy_tk = S['sm'], S['sm_tk'], S['dcy'], S['dcy_tk']
                    xdt, xdt_tk, xsk, xsk_tk, xdw, xdw_tk, Btk, Btk_tk = S['xdt'], S['xdt_tk'], S['xsk'], S['xsk_tk'], S['xdw'], S['xdw_tk'], S['Btk'], S['Btk_tk']
                    yy, yy_tk, ssq, ssq_tk, yn, yn_tk = S['yy'], S['yy_tk'], S['ssq'], S['ssq_tk'], S['yn'], S['yn_tk']
                    Mt, Mt_tk = dcy, dcy_tk
                    gblk = c * 4 + jb
                    hin, hin_tk = hpbs[gblk % 3], hpb_tks[gblk % 3]
                    hout, hout_tk = hpbs[(gblk + 1) % 3], hpb_tks[(gblk + 1) % 3]
                    A0 = (jb % NSET) * SG
                    bc = slice(jb * 128, (jb + 1) * 128)
                    st = []

                    def s_yoff():
                        for g in range(SG):
                            P.op('pe', lambda e, g=g: e.matmul(banks[ACC[A0 + g]][:, :], lhsT=xbc[:, COF + g, bc], rhs=hin[:, g * 512:(g + 1) * 512], start=True, stop=True),
                                 reads=[xbc_tk[COF + g], hin_tk], writes=[bank_tk[ACC[A0 + g]]])
                            P.op('dve', lambda e, g=g: e.tensor_tensor(out=yy[:, g * 512:(g + 1) * 512].rearrange("p (h d) -> p h d", h=8),
                                                                       in0=banks[ACC[A0 + g]][:, :].rearrange("p (h d) -> p h d", h=8),
                                                                       in1=sm[:, 0, g * 8:(g + 1) * 8].unsqueeze(2).to_broadcast([128, 8, 64]), op=ALU.mult),
                                 reads=[bank_tk[ACC[A0 + g]], sm_tk], writes=[yy_tk])
                            P.op('pool', lambda e, g=g: e.tensor_tensor(out=yy[:, g * 512:(g + 1) * 512], in0=yy[:, g * 512:(g + 1) * 512], in1=xsk[:, g * 512:(g + 1) * 512], op=ALU.add),
                                 reads=[yy_tk, xsk_tk], writes=[yy_tk])
                        P.op('pool', lambda e: e.memset(ssq, 0.0), writes=[ssq_tk])
                    st.append(s_yoff)

                    def s_ydiag():
                        for h in range(SH):
                            g = h // 8
                            P.op('pe', lambda e, h=h, g=g: e.matmul(banks[ACC[A0 + g]][:, (h % 8) * 64:(h % 8 + 1) * 64], lhsT=Mt[:, h, :], rhs=xdt[:, h * 64:(h + 1) * 64],
                                                                    start=True, stop=True, skip_group_check=True),
                                 reads=[Mt_tk, xdt_tk], writes=[bank_tk[ACC[A0 + g]]])
                    st.append(s_ydiag)

                    def s_state():
                        P.op('pool', lambda e: e.tensor_tensor(out=xdw.rearrange("p (h d) -> p h d", h=SH), in0=xdt.rearrange("p (h d) -> p h d", h=SH),
                                                               in1=sm[:, 1, :].unsqueeze(2).to_broadcast([128, SH, 64]), op=ALU.mult),
                             reads=[xdt_tk, sm_tk], writes=[xdw_tk])
                        for g in range(SG):
                            b = next_rot()
                            gs_ = slice(g * 512, (g + 1) * 512)
                            P.op('pe', lambda e, b=b, g=g, gs_=gs_: e.matmul(banks[b][:, :], lhsT=Btk[:, g, :], rhs=xdw[:, gs_], start=True, stop=True),
                                 reads=[Btk_tk, xdw_tk], writes=[bank_tk[b]])
                            P.op('pool', lambda e, g=g, gs_=gs_: e.tensor_tensor(out=hst[:, gs_].rearrange("p (h d) -> p h d", h=8), in0=hst[:, gs_].rearrange("p (h d) -> p h d", h=8),
                                                                                in1=sm[:, 2, g * 8:(g + 1) * 8].unsqueeze(2).to_broadcast([128, 8, 64]), op=ALU.mult),
                                 reads=[hst_tk, sm_tk], writes=[hst_tk])
                            P.op('dve', lambda e, b=b, gs_=gs_: e.tensor_tensor(out=hst[:, gs_], in0=hst[:, gs_], in1=banks[b][:, :], op=ALU.add),
                                 reads=[bank_tk[b], hst_tk], writes=[hst_tk])
                            P.op('act', lambda e, gs_=gs_: e.copy(out=hout[:, gs_], in_=hst[:, gs_]), reads=[hst_tk], writes=[hout_tk])
                    st.insert(0, s_state)

                    def s_comb():
                        for g in range(SG):
                            gs_ = slice(g * 512, (g + 1) * 512)
                            P.op('dve', lambda e, g=g, gs_=gs_: e.tensor_tensor(out=yy[:, gs_], in0=yy[:, gs_], in1=banks[ACC[A0 + g]][:, :], op=ALU.add),
                                 reads=[bank_tk[ACC[A0 + g]], yy_tk], writes=[yy_tk])
                            P.op('dve', lambda e, gs_=gs_: e.tensor_tensor(out=yy[:, gs_], in0=yy[:, gs_], in1=zs[:, jb, gs_], op=ALU.mult),
                                 reads=[yy_tk, zs_tk[jb]], writes=[yy_tk])
                            P.op('act', lambda e, g=g, gs_=gs_: e.activation(out=junk, in_=yy[:, gs_], func=AF.Square, accum_out=ssq[:, g:g + 1]),
                                 reads=[yy_tk, ssq_tk], writes=[ssq_tk])
                    st.append(s_comb)

                    def s_norm():
                        P.op('act', lambda e: e.activation(out=ssq, in_=ssq, func=AF.Ln, scale=1.0 / 512, bias=epsc[:, 0:1]), reads=[ssq_tk, k_tk], writes=[ssq_tk])
                        P.op('act', lambda e: e.activation(out=ssq, in_=ssq, func=AF.Exp, scale=-0.5), reads=[ssq_tk], writes=[ssq_tk], strict=True)
                        for g in range(SG):
                            gs_ = slice(g * 512, (g + 1) * 512)
                            P.op('dve', lambda e, g=g, gs_=gs_: e.scalar_tensor_tensor(out=yn[:, gs_], in0=yy[:, gs_], scalar=ssq[:, g:g + 1], in1=sng[:, gs_],
                                                                                       op0=ALU.mult, op1=ALU.mult),
                                 reads=[yy_tk, ssq_tk, sng_tk], writes=[yn_tk], strict=True)
                    st.append(s_norm)

                    def s_ynT():
                        b = next_rot()
                        bkb3 = banks[b][:, :].bitcast(BF16)
                        for q in range(SXC):
                            P.op('pe', lambda e, q=q: e.transpose(bkb3[:, q * 128:(q + 1) * 128], yn[:, q * 128:(q + 1) * 128], ident_b),
                                 reads=[yn_tk, k_tk], writes=[bank_tk[b]])
                        evac_copy(xbc[:, 0:SXC, bc], bkb3[:, 0:XW].rearrange("p (f t) -> p f t", f=SXC), [bank_tk[b]], xbc_tk[0:SXC])
                    st.append(s_ynT)
                    return st

                def interleave(lists):
                    n = max(len(l) for l in lists)
                    for k in range(n):
                        for l in lists:
                            if k < len(l):
                                l[k]()

                for pr in range(2):
                    jbs = [2 * pr, 2 * pr + 1]
                    interleave([prep_stages(jb) for jb in jbs])
                    if pr == 1 and c + 1 < NCH:
                        rmsnorm_chunk(1 - buf, 'mixg%d' % li, 1 - buf)
                    interleave([back_stages(jb) for jb in jbs])
                dense_out('ssd_w_out', jl, SXC, lambda fc: xbc[:, fc, :], xbc_tk[0:SXC], c, par)
            flush_finish()
            pend[0] = par
            AR.release(m)

        phase_load()
        for L in layers:
            kind, j = L[:3], int(L[3:])
            if kind == 'ffn':
                phase_ffn(j)
            elif kind == 'fox':
                phase_fox(j, 2 * j + 1)
            elif kind == 'ssd':
                phase_ssd(j, 2 * j)
        phase_final()
        P.check_deadlock()
        block = es.enter_context(nc.Block())
        P.materialize(block)
        build_program.stats = {e: len(v) for e, v in P.ops.items()}
        build_program.stats['arena_peak'] = AR.peak
        build_program.stats['nsem'] = P.nds
    return nc


FULL_LAYERS = ['ssd0', 'ffn0', 'fox0', 'ffn1', 'ssd1', 'ffn2', 'fox1', 'ffn3']
_cache = {}


def kernel(**inputs):
    inp = {k: np.asarray(v) for k, v in inputs.items()}
    x = inp['x']
    B, S, D = x.shape
    groups = [[2 * b, 2 * b + 1] for b in range(4)]
    if 'nc' not in _cache:
        _cache['nc'] = build_program(S, FULL_LAYERS, groups)
    nc = _cache['nc']
    per_r = []
    for r in range(TP):
        cl, consts = build_consts(inp, r)
        w = slice_weights(inp, r)
        w['consts'] = consts
        per_r.append(w)
    in_maps = []
    for core in range(8):
        b, r = core // 2, core % 2
        m = dict(per_r[r])
        m['x'] = np.ascontiguousarray(x[b])
        in_maps.append(m)
    res = run_bass_kernel_spmd(nc, in_maps, core_ids=list(range(8)))
    out = np.stack([np.asarray(res.results[2 * b]['out']) for b in range(B)], axis=0).astype(np.float32)
    return out
```

```python
import numpy as np
from contextlib import ExitStack
import concourse.bass as bass
import concourse.mybir as mybir
from concourse.bass_utils import run_bass_kernel_spmd

F32 = mybir.dt.float32
BF16 = mybir.dt.bfloat16
ALU = mybir.AluOpType
AF = mybir.ActivationFunctionType

D_MODEL = 1024
DEPTH = 4
EPS = 1e-6
SSD_D_INNER = 2048
SSD_HEADS = 32
SSD_CONV = 4
SSD_CONV_DIM = 3072
SSD_IN_DIM = 5152
FOX_HEADS = 16
FOX_D = 1024
FOX_IN_DIM = 4112
D_FF = 2816
FFN_CONV = 3
TT = 512
NEG = -30000.0
KC = 8
TP = 2
FH = FOX_HEADS // TP
FFC = (D_FF // 128) // TP
SH = SSD_HEADS // TP
SG = 4 // TP
SXC = SH * 64 // 128
SNC = SXC + 2 * SG

COMPUTE = ('pe', 'act', 'dve', 'pool')


class Tk:
    __slots__ = ('name', 'w', 'r')

    def __init__(self, name):
        self.name = name
        self.w = None
        self.r = {}


class DSem:
    def __init__(self, handle, name):
        self.h = handle
        self.name = name
        self.count = 0


class Prog:
    def __init__(self, nc, es):
        self.nc = nc
        self.es = es
        self.ops = {e: [] for e in ('pe', 'act', 'dve', 'pool', 'sp')}
        self.esem = {e: es.enter_context(nc.semaphore("s_" + e)) for e in COMPUTE}
        self.nds = 0
        self.dcache = {}
        self.out_deps = []

    def dsem(self, name):
        if name not in self.dcache:
            self.nds += 1
            self.dcache[name] = DSem(self.es.enter_context(self.nc.semaphore("d_%s_%d" % (name, self.nds))), name)
        return self.dcache[name]

    def _deps(self, eng, reads, writes, strict=False):
        deps = set()
        for t in reads:
            if t.w is not None:
                deps.add(t.w)
        for t in writes:
            if t.w is not None:
                deps.add(t.w)
            for v in t.r.values():
                deps.add(v)
        if strict:
            return deps
        return {d for d in deps if not (d[0] == 'E' and d[1] == eng)}

    def op(self, eng, fn, reads=(), writes=(), strict=False):
        deps = self._deps(eng, reads, writes, strict)
        idx = len(self.ops[eng])
        import sys as _s
        self.ops[eng].append({'fn': fn, 'deps': deps, 'inc': False, 'dma': None, 'note': _s._getframe(1).f_lineno})
        me = ('E', eng, idx)
        for t in reads:
            t.r[eng] = me
        for t in writes:
            t.w = me
            t.r = {}
        return me

    def dma(self, q, out, in_, sem, reads=(), writes=()):
        deps = self._deps(q, reads, writes, strict=True)
        sem.count += 16
        me = ('D', sem, sem.count)
        self.ops[q].append({'fn': (lambda e, o=out, i=in_: e.dma_start(out=o, in_=i)), 'deps': deps,
                            'inc': False, 'dma': sem, 'tok': me})
        for t in reads:
            t.r['D' + sem.name + str(id(sem))] = me
        for t in writes:
            t.w = me
            t.r = {}
        return me

    def cc(self, in_ap, out_ap, groups, slot, reads=(), writes=()):
        deps = self._deps('pool', reads, writes, strict=True)
        sem = self.dsem("cc%d" % slot)
        sem.count += 1
        me = ('D', sem, sem.count)

        def fn(e, i=in_ap, o=out_ap):
            return e.collective_compute("AllReduce", ALU.add, replica_groups=groups, ins=[i], outs=[o])
        self.ops['pool'].append({'fn': fn, 'deps': deps, 'inc': False, 'dma': None, 'cc': sem, 'tok': me})
        for t in reads:
            t.r['C' + str(id(sem))] = me
        for t in writes:
            t.w = me
            t.r = {}
        return me

    def dma_acc(self, out, in_, sem, reads=(), writes=()):
        deps = self._deps('pool', reads, writes, strict=True)
        sem.count += 16
        me = ('D', sem, sem.count)
        self.ops['pool'].append({'fn': (lambda e, o=out, i=in_: e.dma_start(out=o, in_=i, accum_op=ALU.add)), 'deps': deps,
                                 'inc': False, 'dma': sem, 'tok': me})
        for t in reads:
            t.r['D' + sem.name + str(id(sem))] = me
        for t in writes:
            t.w = me
            t.r = {}
        return me

    def barrier(self):
        toks = set()
        for e in COMPUTE:
            if self.ops[e]:
                for i in range(len(self.ops[e]) - 1, -1, -1):
                    if self.ops[e][i]['fn'] is not None and self.ops[e][i].get('tok') is None:
                        toks.add(('E', e, i))
                        break
        for sem in self.dcache.values():
            if sem.count > 0:
                toks.add(('D', sem, sem.count))
        for q in self.ops:
            deps = {d for d in toks if not (d[0] == 'E' and d[1] == q)}
            self.ops[q].append({'fn': None, 'deps': deps, 'inc': False, 'dma': None})

    def finalize_wait(self, q, toks):
        self.ops[q].append({'fn': None, 'deps': set(toks), 'inc': False, 'dma': None})

    def check_deadlock(self):
        pos = {e: 0 for e in self.ops}
        done = set()
        dtok = {}
        for e, lst in self.ops.items():
            cnts = {}
            for i, o in enumerate(lst):
                sem = o.get('cc') or o.get('dma')
                if sem is not None:
                    o.setdefault('_tok', None)
        progress = True
        while progress:
            progress = False
            for e, lst in self.ops.items():
                while pos[e] < len(lst):
                    o = lst[pos[e]]
                    ok = True
                    for d in o['deps']:
                        key = (d[0], d[1], d[2]) if d[0] == 'E' else ('D', id(d[1]), d[2])
                        if key not in done:
                            ok = False
                            break
                    if not ok:
                        break
                    done.add(('E', e, pos[e]))
                    if o.get('tok') is not None:
                        t = o['tok']
                        done.add(('D', id(t[1]), t[2]))
                    pos[e] += 1
                    progress = True
        stuck = {e: pos[e] for e in self.ops if pos[e] < len(self.ops[e])}
        if stuck:
            msg = []
            for e, p in stuck.items():
                o = self.ops[e][p]
                missing = []
                for d in o['deps']:
                    key = (d[0], d[1], d[2]) if d[0] == 'E' else ('D', id(d[1]), d[2])
                    if key not in done:
                        missing.append((d[0], d[1] if d[0] == 'E' else d[1].name, d[2]))
                msg.append("%s@%d/%d waits %s [%s]" % (e, p, len(self.ops[e]), missing, o.get('note')))
            raise RuntimeError("DEADLOCK: " + " | ".join(msg))

    def materialize(self, block):
        for e, lst in self.ops.items():
            for o in lst:
                for d in o['deps']:
                    if d[0] == 'E':
                        self.ops[d[1]][d[2]]['inc'] = True
        vals = {}
        for e in COMPUTE:
            c = 0
            for i, o in enumerate(self.ops[e]):
                if o['inc']:
                    c += 1
                    vals[(e, i)] = c
        self.vals = vals

        def run(eng_name, e):
            waited = {}
            for o in self.ops[eng_name]:
                for d in sorted(o['deps'], key=lambda d: (d[0], str(d[1]) if d[0] == 'E' else d[1].name, d[2])):
                    if d[0] == 'E':
                        key = ('E', d[1])
                        v = vals[(d[1], d[2])]
                        sh = self.esem[d[1]]
                    else:
                        key = ('D', id(d[1]))
                        v = d[2]
                        sh = d[1].h
                    if waited.get(key, 0) >= v:
                        continue
                    waited[key] = v
                    e.wait_ge(sh, v)
                if o['fn'] is None:
                    continue
                ins = o['fn'](e)
                if o.get('cc') is not None:
                    ins.then_inc(o['cc'].h)
                elif o['dma'] is not None:
                    ins.then_inc(o['dma'].h, 16)
                elif o['inc']:
                    ins.then_inc(self.esem[eng_name], 1)

        block.tensor(lambda e: run('pe', e))
        block.scalar(lambda e: run('act', e))
        block.vector(lambda e: run('dve', e))
        block.gpsimd(lambda e: run('pool', e))
        block.sync(lambda e: run('sp', e))


class Arena:
    def __init__(self, ap_f32, nwords):
        self.ap = ap_f32
        self.n = nwords
        self.top = 0
        self.peak = 0

    def mark(self):
        return self.top

    def release(self, m):
        self.top = m

    def alloc(self, shape, dtype, parts=128):
        n = int(np.prod(shape))
        words = n if dtype == F32 else (n + 1) // 2
        a = self.top
        self.top += words
        self.peak = max(self.peak, self.top)
        assert self.top <= self.n, "arena overflow %d > %d" % (self.top, self.n)
        v = self.ap[0:parts, a:a + words]
        if dtype != F32:
            v = v.bitcast(dtype)[:, 0:n]
        if len(shape) == 2:
            v = v.rearrange("p (a b) -> p a b", a=shape[0])
        elif len(shape) == 3:
            v = v.rearrange("p (a b c) -> p a b c", a=shape[0], b=shape[1])
        return v


class CL:
    def __init__(self):
        self.off = {}
        self.n = 0

    def add(self, name, width):
        self.off[name] = (self.n, width)
        self.n += width


def const_layout():
    c = CL()
    c.add('ident', 128)
    c.add('U', 128)
    c.add('maskT', 128)
    c.add('bd64', 128)
    c.add('Ubar', 128)
    for i in range(DEPTH):
        c.add('mixg%d' % i, KC)
        c.add('ffng%d' % i, KC)
        c.add('fcw%d' % i, FFC * 3)
        c.add('fcb%d' % i, FFC)
    c.add('fing', KC)
    for j in range(2):
        c.add('scw%d' % j, SNC * 4)
        c.add('scb%d' % j, SNC)
        c.add('dtb%d' % j, 1)
        c.add('alog%d' % j, SH)
        c.add('dsk%d' % j, SH)
        c.add('fbf%d' % j, 1)
        c.add('fqg%d' % j, 1)
        c.add('fkg%d' % j, 1)
    return c


def build_consts(inp, r):
    c = const_layout()
    A = np.zeros((128, c.n), np.float32)

    def put(name, arr):
        o, w = c.off[name]
        arr = np.asarray(arr, np.float32)
        A[:arr.shape[0], o:o + w] = arr.reshape(arr.shape[0], w)

    put('ident', np.eye(128))
    put('U', np.triu(np.ones((128, 128))))
    put('maskT', np.where(np.arange(128)[None, :] >= np.arange(128)[:, None], 0.0, NEG))
    put('bd64', np.kron(np.eye(2), np.ones((64, 64))))
    put('Ubar', np.tril(np.ones((128, 128)), -1))
    fsl = slice(r * FFC * 128, (r + 1) * FFC * 128)
    for i in range(DEPTH):
        put('mixg%d' % i, inp['mix_norm_g'][i].reshape(KC, 128).T)
        put('ffng%d' % i, inp['ffn_norm_g'][i].reshape(KC, 128).T)
        put('fcw%d' % i, inp['ffn_conv_w'][i][:, fsl].reshape(3, FFC, 128).transpose(2, 1, 0).reshape(128, FFC * 3))
        put('fcb%d' % i, inp['ffn_conv_b'][i][fsl].reshape(FFC, 128).T)
    put('fing', inp['final_norm_g'].reshape(KC, 128).T)
    xs_ = np.r_[r * SH * 64:(r + 1) * SH * 64, 2048 + r * SG * 128:2048 + (r + 1) * SG * 128, 2560 + r * SG * 128:2560 + (r + 1) * SG * 128]
    hsl = slice(r * SH, (r + 1) * SH)
    for j in range(2):
        put('scw%d' % j, inp['ssd_conv_w'][j][:, xs_].reshape(4, SNC, 128).transpose(2, 1, 0).reshape(128, SNC * 4))
        put('scb%d' % j, inp['ssd_conv_b'][j][xs_].reshape(SNC, 128).T)
        put('dtb%d' % j, inp['ssd_dt_bias'][j][hsl].reshape(SH, 1))
        put('alog%d' % j, np.broadcast_to(inp['ssd_a_log'][j][hsl][None, :], (128, SH)))
        put('dsk%d' % j, np.broadcast_to(inp['ssd_d'][j][hsl][None, :], (128, SH)))
        put('fbf%d' % j, inp['fox_b_f'][j][r * FH:(r + 1) * FH].reshape(FH, 1))
        put('fqg%d' % j, np.tile(inp['fox_q_norm_g'][j], 2).reshape(128, 1))
        put('fkg%d' % j, np.tile(inp['fox_k_norm_g'][j], 2).reshape(128, 1))
    return c, A


def slice_weights(inp, r):
    w = {}
    f0, f1 = r * FFC * 128, (r + 1) * FFC * 128
    w['ffn_w_up'] = np.ascontiguousarray(np.concatenate([inp['ffn_w_up'][:, :, f0:f1], inp['ffn_w_up'][:, :, D_FF + f0:D_FF + f1]], axis=2))
    w['ffn_w_down'] = np.ascontiguousarray(inp['ffn_w_down'][:, f0:f1, :])
    q0, q1 = r * FH * 64, (r + 1) * FH * 64
    fi = inp['fox_w_in']
    w['fox_w_in'] = np.ascontiguousarray(np.concatenate([fi[:, :, q0:q1], fi[:, :, 1024 + q0:1024 + q1], fi[:, :, 2048 + q0:2048 + q1],
                                                          fi[:, :, 3072 + q0:3072 + q1], fi[:, :, 4096 + r * FH:4096 + (r + 1) * FH]], axis=2))
    w['fox_w_out'] = np.ascontiguousarray(inp['fox_w_out'][:, q0:q1, :])
    x0, x1 = r * SH * 64, (r + 1) * SH * 64
    b0, b1 = r * SG * 128, (r + 1) * SG * 128
    si = inp['ssd_w_in']
    w['ssd_w_in'] = np.ascontiguousarray(np.concatenate([si[:, :, x0:x1], si[:, :, 2048 + x0:2048 + x1], si[:, :, 4096 + b0:4096 + b1],
                                                          si[:, :, 4608 + b0:4608 + b1], si[:, :, 5120 + r * SH:5120 + (r + 1) * SH]], axis=2))
    w['ssd_w_out'] = np.ascontiguousarray(inp['ssd_w_out'][:, x0:x1, :])
    w['sng'] = np.ascontiguousarray(np.broadcast_to(inp['ssd_norm_g'][:, None, x0:x1], (2, 128, SH * 64)), dtype=np.float32)
    return w


WNAMES = ['ssd_w_in', 'ssd_w_out', 'fox_w_in', 'fox_w_out', 'ffn_w_up', 'ffn_w_down']
SSD_LIN = 2 * SH * 64 + 2 * SG * 128 + SH
FOX_LIN = 4 * FH * 64 + FH
WSHAPES = {'ssd_w_in': (2, 1024, SSD_LIN), 'ssd_w_out': (2, SH * 64, 1024), 'fox_w_in': (2, 1024, FOX_LIN),
           'fox_w_out': (2, FH * 64, 1024), 'ffn_w_up': (4, 1024, 2 * FFC * 128), 'ffn_w_down': (4, FFC * 128, 1024)}


def build_program(seq, layers, groups, debug=None):
    NCH = seq // TT
    nc = bass.Bass("TRN2", target_bir_lowering=False)
    cl = const_layout()
    x_d = nc.dram_tensor("x", [D_MODEL, seq], F32, kind="ExternalInput").ap()
    c_d = nc.dram_tensor("consts", [128, cl.n], F32, kind="ExternalInput").ap()
    w_d = {n: nc.dram_tensor(n, list(WSHAPES[n]), F32, kind="ExternalInput").ap() for n in WNAMES}
    sng_d = nc.dram_tensor("sng", [2, 128, SH * 64], F32, kind="ExternalInput").ap()
    out_d = nc.dram_tensor("out", [D_MODEL, seq], F32, kind="ExternalOutput").ap()
    wb_d = {n: nc.dram_tensor(n + "_b", list(WSHAPES[n]), BF16).ap() for n in WNAMES}
    hT_d = nc.dram_tensor("hT_d", [D_MODEL, seq], F32).ap()
    part_t = [[nc.dram_tensor("part%d_%d" % (p, c), [D_MODEL, TT], F32) for c in range(NCH)] for p in range(2)]
    red_t = [[nc.dram_tensor("red%d_%d" % (p, c), [D_MODEL, TT], F32) for c in range(NCH)] for p in range(2)]
    qa_d = nc.dram_tensor("qa_d", [FH, 70, seq], BF16).ap()
    ka_d = nc.dram_tensor("ka_d", [FH, 70, seq], BF16).ap()
    v_d = nc.dram_tensor("v_d", [seq, FH * 65], BF16).ap()
    sg_d = nc.dram_tensor("sg_d", [FH * 64, seq], BF16).ap()
    ot_d = nc.dram_tensor("ot_d", [FH * 64, seq], BF16).ap()
    dbg_d = None
    if debug:
        dbg_d = nc.dram_tensor("dbg", list(debug), F32, kind="ExternalOutput").ap()

    es = ExitStack()
    with es:
        P = Prog(nc, es)
        NW = 48900
        arena_t = es.enter_context(nc.sbuf_tensor("arena", [128, NW], F32))
        AR = Arena(arena_t[:, :], NW)
        banks = [es.enter_context(nc.psum_tensor("bank%d" % i, [128, 512], F32)) for i in range(8)]
        bank_tk = [Tk("bank%d" % i) for i in range(8)]
        ACC = [0, 1, 2, 3]
        ROT = [4, 5, 6]
        MISC = 7
        rot_i = [0]

        def next_rot():
            b = ROT[rot_i[0] % 3]
            rot_i[0] += 1
            return b

        cst = AR.alloc([cl.n], F32)
        cst_tk = Tk("cst")
        s_c = P.dsem("cst")
        P.dma('sp', cst, c_d[:, :], s_c, writes=[cst_tk])

        def C(name, parts=128):
            o, w = cl.off[name]
            return cst[0:parts, o:o + w]

        ident_f = C('ident')
        ident_b = AR.alloc([128], BF16)
        ones_b = AR.alloc([128], BF16)
        nones_b = AR.alloc([128], BF16)
        ones_f = AR.alloc([128], F32)
        U_b = AR.alloc([128], BF16)
        mask_b = AR.alloc([4, 128], BF16)
        k_tk = Tk("konst")
        P.op('dve', lambda e: e.tensor_copy(out=ident_b, in_=ident_f), reads=[cst_tk], writes=[k_tk])
        P.op('dve', lambda e: e.memset(ones_b, 1.0), writes=[k_tk])
        P.op('dve', lambda e: e.memset(nones_b, -1.0), writes=[k_tk])
        P.op('dve', lambda e: e.memset(ones_f, 1.0), writes=[k_tk])
        P.op('dve', lambda e: e.tensor_copy(out=U_b, in_=C('U')), reads=[cst_tk], writes=[k_tk])
        for r in range(4):
            P.op('dve', lambda e, r=r: e.tensor_copy(out=mask_b[:, r, :], in_=C('maskT')), reads=[cst_tk], writes=[k_tk])

        wtk = {}
        used = set()
        for L in layers:
            kind, j = L[:3], int(L[3:])
            if kind == 'ssd':
                used |= {('ssd_w_in', j), ('ssd_w_out', j)}
            elif kind == 'fox':
                used |= {('fox_w_in', j), ('fox_w_out', j)}
            elif kind == 'ffn':
                used |= {('ffn_w_up', j), ('ffn_w_down', j)}
        order = []
        for L in layers:
            kind, j = L[:3], int(L[3:])
            names = {'ssd': ['ssd_w_in', 'ssd_w_out'], 'fox': ['fox_w_in', 'fox_w_out'], 'ffn': ['ffn_w_up', 'ffn_w_down']}[kind]
            for n in names:
                if (n, j) not in order:
                    order.append((n, j))
        for (n, j) in order:
            t = Tk("w_%s_%d" % (n, j))
            wtk[(n, j)] = t
            s = P.dsem("wc_%s_%d" % (n, j))
            K = WSHAPES[n][1]
            half = K // 2
            P.dma('pool', wb_d[n][j, 0:half, :], w_d[n][j, 0:half, :], s, writes=[t])
            P.dma('pool', wb_d[n][j, half:K, :], w_d[n][j, half:K, :], s, writes=[t])

        hc = [AR.alloc([KC, TT], F32) for _ in range(2)]
        hc_tk = [Tk("hc0"), Tk("hc1")]
        hc_sem = [P.dsem("hc0"), P.dsem("hc1")]
        hnb = [AR.alloc([KC, TT], BF16) for _ in range(2)]
        hnb_tk = [Tk("hn0"), Tk("hn1")]
        sq = AR.alloc([KC, TT], BF16)
        sq_tk = Tk("sq")
        rstd = AR.alloc([TT], F32)
        rstd_tk = Tk("rstd")
        pst = AR.alloc([KC, TT], F32)
        pst_tk = Tk("pst")
        pst_sem = P.dsem("pst")
        NWS = 3
        wslab = [AR.alloc([KC, 512], BF16) for _ in range(NWS)]
        ws_tk = [Tk("ws%d" % i) for i in range(NWS)]
        ws_sem = [P.dsem("ws%d" % i) for i in range(NWS)]
        ws_i = [0]
        NW2 = 2
        w2slab = [AR.alloc([4, 512], BF16) for _ in range(NW2)]
        w2_tk = [Tk("w2%d" % i) for i in range(NW2)]
        w2_sem = [P.dsem("w2%d" % i) for i in range(NW2)]
        w2_i = [0]
        hT_tk = [Tk("hTd%d" % c) for c in range(NCH)]
        part_tk = [[Tk("part%d_%d" % (p, c)) for c in range(NCH)] for p in range(2)]
        red_tk = [[Tk("red%d_%d" % (p, c)) for c in range(NCH)] for p in range(2)]
        st_sem = P.dsem("store")
        evac_i = [0]
        pend = [None]
        sub_i = [0]

        def evac_copy(out, in_, reads, writes):
            evac_i[0] += 1
            if evac_i[0] % 2:
                P.op('act', lambda e: e.copy(out=out, in_=in_), reads=reads, writes=writes)
            else:
                P.op('dve', lambda e: e.tensor_copy(out=out, in_=in_), reads=reads, writes=writes)

        def load_ws(wname, j, cols):
            i = ws_i[0] % NWS
            ws_i[0] += 1
            for (c0, ncol, dst) in cols:
                src = wb_d[wname][j, :, c0:c0 + ncol].rearrange("(k p) n -> p k n", p=128)
                P.dma('sp', wslab[i][:, :, dst:dst + ncol], src, ws_sem[i], reads=[wtk[(wname, j)]], writes=[ws_tk[i]])
            return wslab[i], ws_tk[i]

        def load_hc(c, buf, pp):
            src = hT_d[:, c * TT:(c + 1) * TT].rearrange("(k p) t -> p k t", p=128)
            P.dma('sp', hc[buf], src, hc_sem[buf], reads=[hT_tk[c]], writes=[hc_tk[buf]])
            if pp is not None:
                P.dma_acc(hc[buf], red_t[pp][c].ap().rearrange("(k p) t -> p k t", p=128), hc_sem[buf], reads=[red_tk[pp][c]], writes=[hc_tk[buf]])
                P.dma('pool', src, hc[buf], st_sem, reads=[hc_tk[buf]], writes=[hT_tk[c]])

        def store_hc(c, buf):
            dst = hT_d[:, c * TT:(c + 1) * TT].rearrange("(k p) t -> p k t", p=128)
            P.dma('pool', dst, hc[buf], st_sem, reads=[hc_tk[buf]], writes=[hT_tk[c]])

        def rmsnorm_chunk(buf, gname, hb):
            h = hc[buf]
            hn, hn_tk = hnb[hb], hnb_tk[hb]
            P.op('act', lambda e: e.activation(out=sq, in_=h, func=AF.Square), reads=[hc_tk[buf]], writes=[sq_tk])
            bk = banks[MISC]
            for kc in range(KC):
                P.op('pe', lambda e, kc=kc: e.matmul(bk[:, :], lhsT=ones_b, rhs=sq[:, kc, :], start=(kc == 0), stop=(kc == KC - 1)),
                     reads=[sq_tk, k_tk], writes=[bank_tk[MISC]])
            P.op('act', lambda e: e.activation(out=rstd, in_=bk[:, :], func=AF.Ln, scale=1.0 / D_MODEL, bias=epsc[:, 0:1]),
                 reads=[bank_tk[MISC], cst_tk, k_tk], writes=[rstd_tk])
            P.op('act', lambda e: e.activation(out=rstd, in_=rstd, func=AF.Exp, scale=-0.5), reads=[rstd_tk], writes=[rstd_tk], strict=True)
            g = C(gname)
            for kc in range(KC):
                P.op('dve', lambda e, kc=kc: e.scalar_tensor_tensor(out=hn[:, kc, :], in0=h[:, kc, :], scalar=g[:, kc:kc + 1], in1=rstd,
                                                                  op0=ALU.mult, op1=ALU.mult),
                     reads=[hc_tk[buf], rstd_tk, cst_tk], writes=[hn_tk])

        def proj_T(slab, slab_tk, col0, M, rhs_fn, bank, n=TT, extra_reads=()):
            for kc in range(KC):
                r_ = rhs_fn(kc)
                P.op('pe', lambda e, kc=kc, r_=r_: e.matmul(banks[bank][0:M, 0:n], lhsT=slab[:, kc, col0:col0 + M], rhs=r_,
                                                            start=(kc == 0), stop=(kc == KC - 1)),
                     reads=[slab_tk] + list(extra_reads), writes=[bank_tk[bank]])

        pending_finish = [None]

        def flush_finish():
            if pending_finish[0] is not None:
                f = pending_finish[0]
                pending_finish[0] = None
                f()

        def dense_out(wname, j, nfc, rhs_fn, rhs_tks, c, par, alt=False):
            flush_finish()
            for half in range(2):
                ngrp = (nfc + 3) // 4
                for gi in range(ngrp):
                    f0 = gi * 4
                    nf = min(4, nfc - f0)
                    i = w2_i[0] % NW2
                    w2_i[0] += 1
                    src = wb_d[wname][j, f0 * 128:(f0 + nf) * 128, half * 512:(half + 1) * 512].rearrange("(f p) n -> p f n", p=128)
                    P.dma('sp', w2slab[i][:, 0:nf, :], src, w2_sem[i], reads=[wtk[(wname, j)]], writes=[w2_tk[i]])
                    for f in range(nf):
                        fc = f0 + f
                        for dmi in range(4):
                            bk_ = (4 + dmi) if (alt and half == 1) else ACC[dmi]
                            r_ = rhs_fn(fc)
                            P.op('pe', lambda e, i=i, f=f, fc=fc, dmi=dmi, bk_=bk_, r_=r_: e.matmul(
                                banks[bk_][:, :], lhsT=w2slab[i][:, f, dmi * 128:(dmi + 1) * 128], rhs=r_,
                                start=(fc == 0), stop=(fc == nfc - 1)),
                                 reads=[w2_tk[i]] + list(rhs_tks), writes=[bank_tk[bk_]])
                for dmi in range(4):
                    kc = half * 4 + dmi
                    bk_ = (4 + dmi) if (alt and half == 1) else ACC[dmi]
                    evac_copy(pst[:, kc, :], banks[bk_][:, :], [bank_tk[bk_]], [pst_tk])
            def finish():
                P.dma('sp', part_t[par][c].ap().rearrange("(k p) t -> p k t", p=128), pst, pst_sem, reads=[pst_tk], writes=[part_tk[par][c]])
                P.cc(part_t[par][c].ap().opt(), red_t[par][c].ap().opt(), groups, c % 8, reads=[part_tk[par][c]], writes=[red_tk[par][c]])
            pending_finish[0] = finish

        def conv_silu(bank, M, K, wcol_fn, bcol, pc, pc_tk, acc, acc_tk, tails, tails_tk, out, out_reads, out_writes):
            H = K - 1
            P.op('act', lambda e: e.copy(out=pc[0:M, H:H + TT], in_=banks[bank][0:M, :]), reads=[bank_tk[bank]], writes=[pc_tk])
            P.op('pool', lambda e: e.tensor_copy(out=pc[0:M, 0:H], in_=tails[0:M, :]), reads=[tails_tk], writes=[pc_tk])
            P.op('dve', lambda e: e.tensor_scalar(out=acc[0:M, :], in0=pc[0:M, 0:TT], scalar1=wcol_fn(0), scalar2=bcol, op0=ALU.mult, op1=ALU.add),
                 reads=[pc_tk, cst_tk], writes=[acc_tk])
            for k in range(1, K):
                P.op('dve', lambda e, k=k: e.scalar_tensor_tensor(out=acc[0:M, :], in0=pc[0:M, k:k + TT], scalar=wcol_fn(k), in1=acc[0:M, :],
                                                                  op0=ALU.mult, op1=ALU.add),
                     reads=[pc_tk, cst_tk, acc_tk], writes=[acc_tk])
            P.op('pool', lambda e: e.tensor_copy(out=tails[0:M, :], in_=pc[0:M, TT:TT + H]), reads=[pc_tk], writes=[tails_tk])
            P.op('act', lambda e: e.activation(out=out, in_=acc[0:M, :], func=AF.Silu), reads=[acc_tk] + list(out_reads), writes=list(out_writes))

        epsc = AR.alloc([1], F32)
        P.op('dve', lambda e: e.memset(epsc, EPS), writes=[k_tk])
        eps64 = AR.alloc([1], F32)
        P.op('dve', lambda e: e.memset(eps64, 64 * EPS), writes=[k_tk])
        onec = AR.alloc([1], F32)
        P.op('dve', lambda e: e.memset(onec, 1.0), writes=[k_tk])
        bd64_b = AR.alloc([128], BF16)
        P.op('dve', lambda e: e.tensor_copy(out=bd64_b, in_=C('bd64')), reads=[cst_tk], writes=[k_tk])

        def phase_load():
            xl_sem = P.dsem("xload")
            for c in range(NCH):
                P.dma('sp', hT_d[:, c * TT:(c + 1) * TT], x_d[:, c * TT:(c + 1) * TT], xl_sem, writes=[hT_tk[c]])

        def phase_final():
            P.barrier()
            m = AR.mark()
            pp = pend[0]
            hob = [AR.alloc([KC, TT], F32) for _ in range(2)]
            hob_tk = [Tk("ho0"), Tk("ho1")]
            o_sem = P.dsem("out")
            g = C('fing')
            toks = []
            load_hc(0, 0, pp)
            for c in range(NCH):
                buf = c % 2
                if c + 1 < NCH:
                    load_hc(c + 1, 1 - buf, pp)
                h = hc[buf]
                ho, ho_tk = hob[buf], hob_tk[buf]
                P.op('act', lambda e, h=h: e.activation(out=sq, in_=h, func=AF.Square), reads=[hc_tk[buf]], writes=[sq_tk])
                bk = banks[MISC]
                for kc in range(KC):
                    P.op('pe', lambda e, kc=kc: e.matmul(bk[:, :], lhsT=ones_b, rhs=sq[:, kc, :], start=(kc == 0), stop=(kc == KC - 1)),
                         reads=[sq_tk, k_tk], writes=[bank_tk[MISC]])
                P.op('act', lambda e: e.activation(out=rstd, in_=bk[:, :], func=AF.Ln, scale=1.0 / D_MODEL, bias=epsc[:, 0:1]),
                     reads=[bank_tk[MISC], k_tk], writes=[rstd_tk])
                P.op('act', lambda e: e.activation(out=rstd, in_=rstd, func=AF.Exp, scale=-0.5), reads=[rstd_tk], writes=[rstd_tk], strict=True)
                for kc in range(KC):
                    P.op('dve', lambda e, kc=kc, h=h, ho=ho: e.scalar_tensor_tensor(out=ho[:, kc, :], in0=h[:, kc, :], scalar=g[:, kc:kc + 1], in1=rstd,
                                                                                    op0=ALU.mult, op1=ALU.mult),
                         reads=[hc_tk[buf], rstd_tk, cst_tk], writes=[ho_tk])
                t = P.dma('pool', out_d[:, c * TT:(c + 1) * TT].rearrange("(k p) t -> p k t", p=128), ho, o_sem, reads=[ho_tk])
                toks.append(t)
            P.finalize_wait('pool', [toks[-1]])
            AR.release(m)

        def phase_ffn(i):
            P.barrier()
            m = AR.mark()
            pp = pend[0]
            par = sub_i[0] % 2
            sub_i[0] += 1
            aT = AR.alloc([FFC, TT], BF16)
            aT_tk = [Tk("aT%d" % f) for f in range(FFC)]
            pcs = [AR.alloc([TT + 4], F32) for _ in range(2)]
            pc_tks = [Tk("pc0"), Tk("pc1")]
            accs = [AR.alloc([TT], F32) for _ in range(2)]
            acc_tks = [Tk("acc0"), Tk("acc1")]
            gs = [AR.alloc([TT], BF16) for _ in range(2)]
            gs_tks = [Tk("gs0"), Tk("gs1")]
            tails = AR.alloc([FFC, 2], F32)
            tails_tk = [Tk("tl%d" % f) for f in range(FFC)]
            P.op('pool', lambda e: e.memset(tails, 0.0), writes=tails_tk)
            fcw = C('fcw%d' % i)
            fcb = C('fcb%d' % i)
            GW = FFC * 128
            load_hc(0, 0, pp)
            rmsnorm_chunk(0, 'ffng%d' % i, 0)
            for c in range(NCH):
                buf = c % 2
                hn, hn_tk = hnb[buf], hnb_tk[buf]
                if c + 1 < NCH:
                    load_hc(c + 1, 1 - buf, pp)
                j0 = 0
                while j0 < FFC:
                    nj = min(2, FFC - j0)
                    slab, stk = load_ws('ffn_w_up', i, [(j0 * 128, nj * 128, 0), (GW + j0 * 128, nj * 128, 256)])
                    for u in range(nj):
                        j = j0 + u
                        pi = j % 2
                        b = next_rot()
                        proj_T(slab, stk, u * 128, 128, lambda kc: hn[:, kc, :], b, extra_reads=[hn_tk])
                        conv_silu(b, 128, 3, lambda k, j=j: fcw[:, j * 3 + k:j * 3 + k + 1], fcb[:, j:j + 1], pcs[pi], pc_tks[pi], accs[pi], acc_tks[pi],
                                  tails[:, j, :], tails_tk[j], gs[pi], [], [gs_tks[pi]])
                    for u in range(nj):
                        j = j0 + u
                        pi = j % 2
                        b = next_rot()
                        proj_T(slab, stk, 256 + u * 128, 128, lambda kc: hn[:, kc, :], b, extra_reads=[hn_tk])
                        P.op('dve', lambda e, j=j, pi=pi, b=b: e.tensor_tensor(out=aT[:, j, :], in0=gs[pi], in1=banks[b][:, :], op=ALU.mult),
                             reads=[gs_tks[pi], bank_tk[b]], writes=[aT_tk[j]])
                    j0 += nj
                    if j0 >= 4:
                        flush_finish()
                if c + 1 < NCH:
                    rmsnorm_chunk(1 - buf, 'ffng%d' % i, 1 - buf)
                dense_out('ffn_w_down', i, FFC, lambda fc: aT[:, fc, :], aT_tk, c, par)
            flush_finish()
            pend[0] = par
            AR.release(m)

        def phase_fox(jl, li):
            P.barrier()
            m = AR.mark()
            pp = pend[0]
            par = sub_i[0] % 2
            sub_i[0] += 1
            fw = 'fox_w_in'
            HW = FH * 64
            NPR = FH // 2
            qk_st = [AR.alloc([NPR, TT], BF16) for _ in range(2)]
            qk_tk = [Tk("qkst0"), Tk("qkst1")]
            qk_sem = [P.dsem("qkst0"), P.dsem("qkst1")]
            sqh = [AR.alloc([TT], BF16) for _ in range(2)]
            sqh_tk = [Tk("sqh0"), Tk("sqh1")]
            rh = [AR.alloc([TT], F32) for _ in range(2)]
            rh_tk = [Tk("rh0"), Tk("rh1")]
            vst = AR.alloc([4, FH, 65], BF16)
            vst_tk = Tk("vst")
            v_sem = P.dsem("vst")
            sgst = AR.alloc([HW // 128, TT], BF16)
            sgst_tk = Tk("sgst")
            sg_sem = P.dsem("sgst")
            wf = AR.alloc([KC, FH], BF16)
            wf_tk = Tk("wf")
            wf_sem = P.dsem("wf")
            ef = AR.alloc([TT], F32)
            sA = AR.alloc([TT], F32)
            sB = AR.alloc([TT], F32)
            f_tk = Tk("fchain")
            carry = AR.alloc([1], F32)
            r1 = AR.alloc([TT], F32)
            c3q = AR.alloc([3, TT], BF16)
            c3k = AR.alloc([3, TT], BF16)
            c3_tk = Tk("c3")
            c3_sem = P.dsem("c3")
            ones3 = AR.alloc([3, TT], BF16)
            o3_tk = Tk("ones3")
            o3_sem = P.dsem("o3")
            qa_tk = Tk("qa_d")
            ka_tk = Tk("ka_d")
            qa1_tk, qa2_tk, ka1_tk, ka2_tk = Tk("qa1"), Tk("qa2"), Tk("ka1"), Tk("ka2")
            vd_tk = Tk("v_d")
            sgd_tk = Tk("sg_d")
            otd_tk = Tk("ot_d")
            nones3 = AR.alloc([3, TT], BF16)
            onesF = AR.alloc([TT], F32)
            P.op('pool', lambda e: e.memset(ones3, 1.0), writes=[o3_tk])
            P.op('pool', lambda e: e.memset(nones3, -1.0), writes=[o3_tk])
            P.op('pool', lambda e: e.memset(onesF, 1.0), writes=[o3_tk])
            P.op('pool', lambda e: e.memset(carry, 0.0), writes=[f_tk])
            P.op('pool', lambda e: e.memset(vst, 1.0), writes=[vst_tk])
            for c in range(NCH):
                P.dma('sp', qa_d[:, 67:70, c * TT:(c + 1) * TT], ones3[0:FH], o3_sem, reads=[o3_tk], writes=[qa1_tk])
                P.dma('sp', ka_d[:, 64:67, c * TT:(c + 1) * TT], nones3[0:FH], o3_sem, reads=[o3_tk], writes=[ka1_tk])
            P.dma('sp', wf, wb_d[fw][jl, :, 4 * HW:4 * HW + FH].rearrange("(k p) n -> p k n", p=128), wf_sem, reads=[wtk[(fw, jl)]], writes=[wf_tk])
            qg = C('fqg%d' % jl)
            kg = C('fkg%d' % jl)
            nbf = C('fbf%d' % jl, FH)
            load_hc(0, 0, pp)
            rmsnorm_chunk(0, 'mixg%d' % li, 0)
            for c in range(NCH):
                buf = c % 2
                hn, hn_tk = hnb[buf], hnb_tk[buf]
                cols = slice(c * TT, (c + 1) * TT)
                if c + 1 < NCH:
                    load_hc(c + 1, 1 - buf, pp)
                bm = MISC
                proj_T(wf, wf_tk, 0, FH, lambda kc: hn[:, kc, :], bm, extra_reads=[hn_tk])
                P.op('dve', lambda e: e.tensor_scalar(out=ef[0:FH], in0=banks[bm][0:FH, :], scalar1=nbf[:, 0:1], scalar2=-1.0, op0=ALU.add, op1=ALU.mult),
                     reads=[bank_tk[bm], cst_tk, f_tk], writes=[f_tk])
                P.op('act', lambda e: e.activation(out=ef[0:FH], in_=ef[0:FH], func=AF.Exp), reads=[f_tk], writes=[f_tk])
                P.op('act', lambda e: e.activation(out=sA[0:FH], in_=ef[0:FH], func=AF.Ln, bias=onec[0:FH, 0:1]), reads=[f_tk, k_tk], writes=[f_tk], strict=True)
                P.op('dve', lambda e: e.tensor_tensor_scan(out=sB[0:FH], data0=onesF[0:FH], data1=sA[0:FH], initial=carry[0:FH, 0:1], op0=ALU.mult, op1=ALU.add),
                     reads=[f_tk, o3_tk], writes=[f_tk], strict=True)
                P.op('dve', lambda e: e.tensor_copy(out=carry[0:FH], in_=sB[0:FH, TT - 1:TT]), reads=[f_tk], writes=[f_tk], strict=True)
                P.op('act', lambda e: e.copy(out=c3k[0:FH, 0, :], in_=sB[0:FH]), reads=[f_tk, c3_tk], writes=[c3_tk])
                P.op('dve', lambda e: e.tensor_tensor(out=r1[0:FH], in0=sB[0:FH], in1=c3k[0:FH, 0, :], op=ALU.subtract), reads=[f_tk, c3_tk], writes=[f_tk])
                P.op('act', lambda e: e.copy(out=c3k[0:FH, 1, :], in_=r1[0:FH]), reads=[f_tk, c3_tk], writes=[c3_tk])
                P.op('dve', lambda e: e.tensor_tensor(out=sA[0:FH], in0=r1[0:FH], in1=c3k[0:FH, 1, :], op=ALU.subtract), reads=[f_tk, c3_tk], writes=[f_tk])
                P.op('act', lambda e: e.copy(out=c3k[0:FH, 2, :], in_=sA[0:FH]), reads=[f_tk, c3_tk], writes=[c3_tk])
                P.dma('pool', qa_d[:, 64:67, cols], c3k[0:FH], c3_sem, reads=[c3_tk], writes=[qa2_tk])
                P.dma('pool', ka_d[:, 67:70, cols], c3k[0:FH], c3_sem, reads=[c3_tk], writes=[ka2_tk])
                tasks = []
                for which in range(2):
                    for pr in range(NPR):
                        tasks.append((which, pr))
                slabs = {}
                tb = {}

                def t_front(ti):
                    which, pr = tasks[ti]
                    if pr == 0:
                        slabs[which] = load_ws(fw, jl, [(which * HW, HW, 0)])
                    slab, stk = slabs[which]
                    b = next_rot()
                    tb[ti] = b
                    proj_T(slab, stk, pr * 128, 128, lambda kc: hn[:, kc, :], b, extra_reads=[hn_tk])

                def t_back(ti):
                    which, pr = tasks[ti]
                    b = tb[ti]
                    x_ = ti % 2
                    P.op('act', lambda e: e.activation(out=sqh[x_], in_=banks[b][:, :], func=AF.Square), reads=[bank_tk[b]], writes=[sqh_tk[x_]])
                    P.op('pe', lambda e: e.matmul(banks[MISC][:, :], lhsT=bd64_b, rhs=sqh[x_], start=True, stop=True),
                         reads=[sqh_tk[x_], k_tk], writes=[bank_tk[MISC]])
                    if which == 0:
                        P.op('act', lambda e: e.activation(out=rh[x_], in_=banks[MISC][:, :], func=AF.Ln, scale=1.0, bias=eps64[:, 0:1]),
                             reads=[bank_tk[MISC], k_tk], writes=[rh_tk[x_]])
                    else:
                        P.op('act', lambda e: e.activation(out=rh[x_], in_=banks[MISC][:, :], func=AF.Ln, scale=1.0 / 64, bias=epsc[:, 0:1]),
                             reads=[bank_tk[MISC], k_tk], writes=[rh_tk[x_]])
                    P.op('act', lambda e: e.activation(out=rh[x_], in_=rh[x_], func=AF.Exp, scale=-0.5), reads=[rh_tk[x_]], writes=[rh_tk[x_]], strict=True)
                    gcol = qg if which == 0 else kg
                    P.op('dve', lambda e: e.scalar_tensor_tensor(out=qk_st[which][:, pr, :], in0=banks[b][:, :], scalar=gcol[:, 0:1], in1=rh[x_],
                                                                 op0=ALU.mult, op1=ALU.mult),
                         reads=[bank_tk[b], rh_tk[x_], cst_tk], writes=[qk_tk[which]])
                    if pr == NPR - 1:
                        dv = (qa_d if which == 0 else ka_d)[:, 0:64, cols].rearrange("(hp two) d t -> two d hp t", two=2)
                        for two in range(2):
                            P.dma('pool', dv[two], qk_st[which][two * 64:(two + 1) * 64], qk_sem[which], reads=[qk_tk[which]],
                                  writes=[qa_tk if which == 0 else ka_tk])

                for ti in range(len(tasks) + 1):
                    if ti < len(tasks):
                        t_front(ti)
                    if ti >= 1:
                        t_back(ti - 1)
                slab, stk = load_ws(fw, jl, [(2 * HW, HW, 0)])
                for jb in range(4):
                    b = next_rot()
                    for kc in range(KC):
                        P.op('pe', lambda e, kc=kc, jb=jb, b=b, slab=slab, hn=hn: e.matmul(banks[b][:, 0:HW], lhsT=hn[:, kc, jb * 128:(jb + 1) * 128], rhs=slab[:, kc, 0:HW],
                                                                                  start=(kc == 0), stop=(kc == KC - 1)),
                             reads=[stk, hn_tk], writes=[bank_tk[b]])
                    evac_copy(vst[:, jb, :, 0:64], banks[b][:, 0:HW].rearrange("p (h d) -> p h d", h=FH), [bank_tk[b]], [vst_tk])
                P.dma('pool', v_d[cols, :].rearrange("(j p) e -> p j e", p=128), vst.rearrange("p j h e -> p j (h e)"), v_sem, reads=[vst_tk], writes=[vd_tk])
                slab, stk = load_ws(fw, jl, [(3 * HW, HW, 0)])
                if c + 1 < NCH:
                    rmsnorm_chunk(1 - buf, 'mixg%d' % li, 1 - buf)
                for u in range(HW // 128):
                    b = next_rot()
                    proj_T(slab, stk, u * 128, 128, lambda kc: hn[:, kc, :], b, extra_reads=[hn_tk])
                    P.op('act', lambda e, b=b, u=u: e.activation(out=sgst[:, u, :], in_=banks[b][:, :], func=AF.Sigmoid),
                         reads=[bank_tk[b]], writes=[sgst_tk])
                P.dma('pool', sg_d[:, cols].rearrange("(k p) t -> p k t", p=128), sgst, sg_sem, reads=[sgst_tk], writes=[sgd_tk])
            AR.release(m)

            P.barrier()
            m = AR.mark()
            NB = seq // 128
            NQH = max(1, seq // 2048)
            QW = seq // NQH
            NBK = QW // 512
            Ka = [AR.alloc([seq], BF16) for _ in range(2)]
            Qa = [AR.alloc([seq], BF16) for _ in range(2)]
            Vh = [AR.alloc([NB, 65], BF16) for _ in range(2)]
            kqv_tk = [Tk("kqv0"), Tk("kqv1")]
            kqv_sem = [P.dsem("kqv0"), P.dsem("kqv1")]
            PT = [AR.alloc([512], BF16) for _ in range(3)]
            pt_tk = [Tk("pt%d" % i) for i in range(3)]
            pt_i = 0
            rrow = AR.alloc([512], F32)
            rrow_tk = Tk("rrow")
            bcs = AR.alloc([512], F32)
            bcs_tk = Tk("bcs")
            Ost = [AR.alloc([QW], BF16) for _ in range(2)]
            ost_tk = [Tk("ost0"), Tk("ost1")]
            ost_sem = [P.dsem("ost0"), P.dsem("ost1")]
            oi = 0
            NHL = FH

            def load_head(h, s):
                P.dma('sp', Ka[s][0:70], ka_d[h, :, :], kqv_sem[s], reads=[ka_tk, ka1_tk, ka2_tk], writes=[kqv_tk[s]])
                P.dma('sp', Qa[s][0:70], qa_d[h, :, :], kqv_sem[s], reads=[qa_tk, qa1_tk, qa2_tk], writes=[kqv_tk[s]])
                P.dma('sp', Vh[s], v_d[:, h * 65:(h + 1) * 65].rearrange("(kb p) e -> p kb e", p=128), kqv_sem[s], reads=[vd_tk], writes=[kqv_tk[s]])

            items = []
            for h in range(NHL):
                items.append(('load', h))
                for qh in range(NQH):
                    qb0 = qh * (QW // 128)
                    nqb = QW // 128
                    last_kb = qb0 + nqb - 1
                    for kb in range(last_kb + 1):
                        for a in range(NBK):
                            blk0 = qb0 + 4 * a
                            lo = max(kb, blk0)
                            hi = blk0 + 4
                            if lo >= hi:
                                continue
                            items.append(('step', h, kb, a, blk0, lo, (hi - lo) * 128))
                    items.append(('norm', h, qh))
            LOOK = 2
            st_bank = {}
            load_head(0, 0)

            def front(it):
                if it[0] != 'step':
                    return
                _, h, kb, a, blk0, lo, ncol = it
                s_ = h % 2
                diag = (lo == kb)
                b = next_rot()
                st_bank[id(it)] = b
                P.op('pe', lambda e: e.matmul(banks[b][:, 0:ncol], lhsT=Ka[s_][0:70, kb * 128:(kb + 1) * 128], rhs=Qa[s_][0:70, lo * 128:lo * 128 + ncol],
                                              start=True, stop=(not diag)), reads=[kqv_tk[s_]], writes=[bank_tk[b]])
                if diag:
                    P.op('pe', lambda e: e.matmul(banks[b][:, 0:128], lhsT=ident_b, rhs=mask_b[:, 0, :], start=False, stop=True),
                         reads=[k_tk], writes=[bank_tk[b]])

            def back(it):
                nonlocal pt_i, oi
                if it[0] == 'load':
                    if it[1] + 1 < NHL:
                        load_head(it[1] + 1, (it[1] + 1) % 2)
                    return
                if it[0] == 'step':
                    _, h, kb, a, blk0, lo, ncol = it
                    s_ = h % 2
                    b = st_bank.pop(id(it))
                    pi = pt_i % 3
                    pt_i += 1
                    P.op('act', lambda e: e.activation(out=PT[pi][:, 0:ncol], in_=banks[b][:, 0:ncol], func=AF.Exp),
                         reads=[bank_tk[b]], writes=[pt_tk[pi]])
                    c0 = (lo - blk0) * 128
                    lastk = (kb == blk0 + 3)
                    P.op('pe', lambda e: e.matmul(banks[ACC[a]][0:65, c0:c0 + ncol], lhsT=Vh[s_][:, kb, :], rhs=PT[pi][:, 0:ncol],
                                                  start=(kb == 0), stop=lastk, skip_group_check=True),
                         reads=[kqv_tk[s_], pt_tk[pi]], writes=[bank_tk[ACC[a]]])
                elif it[0] == 'norm':
                    _, h, qh = it
                    os_ = oi % 2
                    oi += 1
                    for a in range(NBK):
                        P.op('dve', lambda e, a=a: e.reciprocal(out=rrow[64:65, :], in_=banks[ACC[a]][64:65, :]), reads=[bank_tk[ACC[a]]], writes=[rrow_tk])
                        P.op('pe', lambda e: e.matmul(banks[MISC][0:64, :], lhsT=ones_f[64:65, 0:64], rhs=rrow[64:65, :], start=True, stop=True),
                             reads=[rrow_tk, k_tk], writes=[bank_tk[MISC]])
                        P.op('act', lambda e: e.copy(out=bcs[0:64], in_=banks[MISC][0:64, :]), reads=[bank_tk[MISC]], writes=[bcs_tk])
                        P.op('dve', lambda e, a=a, os_=os_: e.tensor_tensor(out=Ost[os_][0:64, a * 512:(a + 1) * 512], in0=banks[ACC[a]][0:64, :], in1=bcs[0:64], op=ALU.mult),
                             reads=[bank_tk[ACC[a]], bcs_tk], writes=[ost_tk[os_]])
                    P.dma('pool', ot_d[h * 64:(h + 1) * 64, qh * QW:(qh + 1) * QW], Ost[os_][0:64], ost_sem[os_], reads=[ost_tk[os_]], writes=[otd_tk])

            for i in range(len(items) + LOOK):
                if i < len(items):
                    front(items[i])
                if i - LOOK >= 0:
                    back(items[i - LOOK])
            AR.release(m)

            P.barrier()
            m = AR.mark()
            NFC = HW // 128
            oc = [AR.alloc([NFC, TT], BF16) for _ in range(2)]
            gc = [AR.alloc([NFC, TT], BF16) for _ in range(2)]
            og_tk = [Tk("og0"), Tk("og1")]
            og_sem = [P.dsem("og0"), P.dsem("og1")]
            yT = AR.alloc([NFC, TT], BF16)
            yT_tk = Tk("yT")

            def load_og(c, s):
                cols = slice(c * TT, (c + 1) * TT)
                P.dma('sp', oc[s], ot_d[:, cols].rearrange("(k p) t -> p k t", p=128), og_sem[s], reads=[otd_tk], writes=[og_tk[s]])
                P.dma('sp', gc[s], sg_d[:, cols].rearrange("(k p) t -> p k t", p=128), og_sem[s], reads=[sgd_tk], writes=[og_tk[s]])

            load_og(0, 0)
            for c in range(NCH):
                buf = c % 2
                if c + 1 < NCH:
                    load_og(c + 1, 1 - buf)
                for kc in range(NFC):
                    eng = 'pool' if kc % 2 else 'dve'
                    P.op(eng, lambda e, kc=kc, buf=buf: e.tensor_tensor(out=yT[:, kc, :], in0=oc[buf][:, kc, :], in1=gc[buf][:, kc, :], op=ALU.mult),
                         reads=[og_tk[buf]], writes=[yT_tk])
                dense_out('fox_w_out', jl, NFC, lambda fc: yT[:, fc, :], [yT_tk], c, par, alt=True)
            flush_finish()
            pend[0] = par
            AR.release(m)

        def phase_ssd(jl, li):
            P.barrier()
            m = AR.mark()
            pp = pend[0]
            par = sub_i[0] % 2
            sub_i[0] += 1
            sw = 'ssd_w_in'
            XW = SH * 64
            NHG = SH // 4
            zs = AR.alloc([4, XW], BF16)
            zs_tk = [Tk("zs%d" % j) for j in range(4)]
            xbc = AR.alloc([SNC, TT], BF16)
            xbc_tk = [Tk("xbc%d" % f) for f in range(SNC)]
            BOF, COF = SXC, SXC + SG
            pcs = [AR.alloc([TT + 4], F32) for _ in range(2)]
            pc_tks = [Tk("pc0"), Tk("pc1")]
            accs = [AR.alloc([TT], F32) for _ in range(2)]
            acc_tks = [Tk("acc0"), Tk("acc1")]
            tails = AR.alloc([SNC, 3], F32)
            tails_tk = [Tk("tl%d" % f) for f in range(SNC)]
            wdt = AR.alloc([KC, SH], BF16)
            wdt_tk = Tk("wdt")
            wdt_sem = P.dsem("wdt")
            dtT = AR.alloc([TT], F32)
            dtT_tk = Tk("dtT")
            arow = AR.alloc([SH], F32)
            arow_tk = Tk("arow")
            dtk = AR.alloc([4, SH], F32)
            dak = AR.alloc([4, SH], F32)
            dtk_tk = Tk("dtk")
            NSET = 2
            hst = AR.alloc([XW], F32)
            hst_tk = Tk("hst")
            hpbs = [AR.alloc([XW], BF16) for _ in range(3)]
            hpb_tks = [Tk("hpb0"), Tk("hpb1"), Tk("hpb2")]
            junk = AR.alloc([512], BF16)
            SETS = []
            for si in range(NSET):
                d = {}
                d['sm'] = AR.alloc([4, SH], F32); d['sm_tk'] = Tk("sm%d" % si)
                d['dcy'] = AR.alloc([SH, 128], BF16); d['dcy_tk'] = Tk("dcy%d" % si)
                d['cb'] = AR.alloc([SG, 128], BF16); d['cb_tk'] = Tk("cb%d" % si)
                d['xdt'] = AR.alloc([XW], BF16); d['xdt_tk'] = Tk("xdt%d" % si)
                d['xsk'] = AR.alloc([XW], BF16); d['xsk_tk'] = Tk("xsk%d" % si)
                d['Btk'] = AR.alloc([SG, 128], BF16); d['Btk_tk'] = Tk("Btk%d" % si)
                d['yy'] = AR.alloc([XW], F32); d['yy_tk'] = Tk("yy%d" % si)
                d['Dm'] = d['yy'].bitcast(BF16)[:, 0:2 * XW].rearrange("p (a b) -> p a b", a=SH); d['Dm_tk'] = d['yy_tk']
                d['ssq'] = AR.alloc([SG], F32); d['ssq_tk'] = Tk("ssq%d" % si)
                d['yn'] = AR.alloc([XW], BF16); d['yn_tk'] = Tk("yn%d" % si)
                d['xdw'] = d['yn']; d['xdw_tk'] = d['yn_tk']
                SETS.append(d)
            sng = AR.alloc([XW], F32)
            sng_tk = Tk("sng")
            sng_sem = P.dsem("sng")
            P.dma('sp', sng, sng_d[jl, :, :], sng_sem, writes=[sng_tk])
            dskr = C('dsk%d' % jl)
            scw = C('scw%d' % jl)
            scb = C('scb%d' % jl)
            dtb = C('dtb%d' % jl, SH)
            P.op('pool', lambda e: e.memset(tails, 0.0), writes=tails_tk)
            P.op('pool', lambda e: e.memset(hst, 0.0), writes=[hst_tk])
            P.op('pool', lambda e: e.memset(hpbs[0], 0.0), writes=[hpb_tks[0]])
            P.op('act', lambda e: e.activation(out=arow, in_=C('alog%d' % jl), func=AF.Exp), reads=[cst_tk], writes=[arow_tk])
            P.op('dve', lambda e: e.tensor_scalar(out=arow, in0=arow, scalar1=-1.0, scalar2=None, op0=ALU.mult), reads=[arow_tk], writes=[arow_tk])
            DTC = 2 * XW + 2 * SG * 128
            P.dma('sp', wdt, wb_d[sw][jl, :, DTC:DTC + SH].rearrange("(k p) n -> p k n", p=128), wdt_sem, reads=[wtk[(sw, jl)]], writes=[wdt_tk])
            load_hc(0, 0, pp)
            rmsnorm_chunk(0, 'mixg%d' % li, 0)
            for c in range(NCH):
                buf = c % 2
                hn, hn_tk = hnb[buf], hnb_tk[buf]
                if c + 1 < NCH:
                    load_hc(c + 1, 1 - buf, pp)
                proj_T(wdt, wdt_tk, 0, SH, lambda kc: hn[:, kc, :], MISC, extra_reads=[hn_tk])
                P.op('act', lambda e: e.activation(out=dtT[0:SH], in_=banks[MISC][0:SH, :], func=AF.Exp, bias=dtb[:, 0:1]),
                     reads=[bank_tk[MISC], cst_tk], writes=[dtT_tk])
                P.op('act', lambda e: e.activation(out=dtT[0:SH], in_=dtT[0:SH], func=AF.Ln, bias=onec[0:SH, 0:1]), reads=[dtT_tk, k_tk], writes=[dtT_tk], strict=True)
                for jb in range(4):
                    P.op('pe', lambda e, jb=jb: e.transpose(banks[MISC][:, jb * SH:(jb + 1) * SH], dtT[0:SH, jb * 128:(jb + 1) * 128], ident_f[0:SH, 0:SH]),
                         reads=[dtT_tk, cst_tk], writes=[bank_tk[MISC]])
                P.op('dve', lambda e: e.tensor_copy(out=dtk, in_=banks[MISC][:, 0:4 * SH].rearrange("p (j h) -> p j h", j=4)), reads=[bank_tk[MISC]], writes=[dtk_tk])
                for jb in range(4):
                    P.op('dve', lambda e, jb=jb: e.tensor_tensor(out=dak[:, jb, :], in0=dtk[:, jb, :], in1=arow, op=ALU.mult), reads=[dtk_tk, arow_tk], writes=[dtk_tk], strict=True)
                for sl in range(XW // 512):
                    slab, stk = load_ws(sw, jl, [(sl * 512, 512, 0)])
                    for jb in range(4):
                        b = next_rot()
                        for kc in range(KC):
                            P.op('pe', lambda e, kc=kc, jb=jb, b=b, slab=slab, hn=hn: e.matmul(banks[b][:, :], lhsT=hn[:, kc, jb * 128:(jb + 1) * 128], rhs=slab[:, kc, :],
                                                                                      start=(kc == 0), stop=(kc == KC - 1)),
                                 reads=[stk, hn_tk], writes=[bank_tk[b]])
                        P.op('act', lambda e, b=b, jb=jb, sl=sl: e.activation(out=zs[:, jb, sl * 512:(sl + 1) * 512], in_=banks[b][:, :], func=AF.Silu),
                             reads=[bank_tk[b]], writes=[zs_tk[jb]])
                flush_finish()
                for sl in range(SNC // 4):
                    slab, stk = load_ws(sw, jl, [(XW + sl * 512, 512, 0)])
                    for u in range(4):
                        f = sl * 4 + u
                        pi = f % 2
                        b = next_rot()
                        proj_T(slab, stk, u * 128, 128, lambda kc: hn[:, kc, :], b, extra_reads=[hn_tk])
                        conv_silu(b, 128, 4, lambda k, f=f: scw[:, f * 4 + k:f * 4 + k + 1], scb[:, f:f + 1], pcs[pi], pc_tks[pi], accs[pi], acc_tks[pi],
                                  tails[:, f, :], tails_tk[f], xbc[:, f, :], [], [xbc_tk[f]])
                def prep_stages(jb):
                    S = SETS[jb % NSET]
                    sm, sm_tk, Dm, Dm_tk, dcy, dcy_tk, cb, cb_tk = S['sm'], S['sm_tk'], S['Dm'], S['Dm_tk'], S['dcy'], S['dcy_tk'], S['cb'], S['cb_tk']
                    xdt, xdt_tk, xsk, xsk_tk, xdw, xdw_tk, Btk, Btk_tk = S['xdt'], S['xdt_tk'], S['xsk'], S['xsk_tk'], S['xdw'], S['xdw_tk'], S['Btk'], S['Btk_tk']
                    Mt, Mt_tk = dcy, dcy_tk
                    bc = slice(jb * 128, (jb + 1) * 128)
                    mo = (jb % NSET) * 4 * SH
                    st = []

                    def s_acs():
                        P.op('pe', lambda e: e.matmul(banks[MISC][:, mo:mo + SH], lhsT=C('U'), rhs=dak[:, jb, :], start=True, stop=True),
                             reads=[dtk_tk, cst_tk], writes=[bank_tk[MISC]])
                        P.op('pe', lambda e: e.matmul(banks[MISC][:, mo + SH:mo + 2 * SH], lhsT=C('Ubar'), rhs=dak[:, jb, :], start=True, stop=True),
                             reads=[dtk_tk, cst_tk], writes=[bank_tk[MISC]])
                        P.op('pe', lambda e: e.matmul(banks[MISC][:, mo + 2 * SH:mo + 3 * SH], lhsT=ones_f, rhs=dak[:, jb, :], start=True, stop=True),
                             reads=[dtk_tk, k_tk], writes=[bank_tk[MISC]])
                        P.op('act', lambda e: e.activation(out=sm[:, 0:3, :].rearrange("p a b -> p (a b)"), in_=banks[MISC][:, mo:mo + 3 * SH], func=AF.Exp),
                             reads=[bank_tk[MISC]], writes=[sm_tk])
                    st.append(s_acs)

                    def s_dm():
                        P.op('dve', lambda e: e.tensor_tensor(out=Dm, in0=U_b.unsqueeze(1).to_broadcast([128, SH, 128]),
                                                              in1=dak[:, jb, :].unsqueeze(2).to_broadcast([128, SH, 128]), op=ALU.mult),
                             reads=[dtk_tk, k_tk], writes=[Dm_tk])
                    st.append(s_dm)

                    def s_xT():
                        b = next_rot()
                        bkb = banks[b][:, :].bitcast(BF16)
                        for q in range(SXC):
                            P.op('pe', lambda e, q=q: e.transpose(bkb[:, q * 128:(q + 1) * 128], xbc[:, q, bc], ident_b),
                                 reads=[xbc_tk[q], k_tk], writes=[bank_tk[b]])
                        P.op('dve', lambda e: e.tensor_tensor(
                            out=xdt.rearrange("p (h d) -> p h d", h=SH), in0=bkb[:, 0:XW].rearrange("p (h d) -> p h d", h=SH),
                            in1=dtk[:, jb, :].unsqueeze(2).to_broadcast([128, SH, 64]), op=ALU.mult),
                             reads=[bank_tk[b], dtk_tk], writes=[xdt_tk])
                        P.op('dve', lambda e: e.tensor_tensor(
                            out=xsk.rearrange("p (h d) -> p h d", h=SH), in0=bkb[:, 0:XW].rearrange("p (h d) -> p h d", h=SH),
                            in1=dskr.unsqueeze(2).to_broadcast([128, SH, 64]), op=ALU.mult),
                             reads=[bank_tk[b], cst_tk], writes=[xsk_tk])
                    st.append(s_xT)

                    def mk_E(hg):
                        def s_E():
                            b = next_rot()
                            P.op('pe', lambda e: e.matmul(banks[b][:, :], lhsT=ones_b, rhs=Dm[:, hg * 4:(hg + 1) * 4, :].rearrange("p a b -> p (a b)"),
                                                          start=True, stop=False), reads=[Dm_tk, k_tk], writes=[bank_tk[b]])
                            for hh in range(4):
                                P.op('pe', lambda e, hh=hh: e.matmul(banks[b][:, hh * 128:(hh + 1) * 128], lhsT=Dm[:, hg * 4 + hh, :], rhs=nones_b,
                                                                     start=False, stop=False, skip_group_check=True), reads=[Dm_tk, k_tk], writes=[bank_tk[b]])
                            P.op('pe', lambda e: e.matmul(banks[b][:, :], lhsT=ident_b, rhs=mask_b.rearrange("p a b -> p (a b)"), start=False, stop=True, skip_group_check=True),
                                 reads=[k_tk], writes=[bank_tk[b]])
                            P.op('act', lambda e: e.activation(out=dcy[:, hg * 4:(hg + 1) * 4, :].rearrange("p a b -> p (a b)"), in_=banks[b][:, :], func=AF.Exp),
                                 reads=[bank_tk[b]], writes=[dcy_tk])
                        return s_E
                    for hg in range(NHG):
                        st.append(mk_E(hg))

                    def s_cb():
                        b = next_rot()
                        for g in range(SG):
                            P.op('pe', lambda e, g=g: e.matmul(banks[b][:, g * 128:(g + 1) * 128], lhsT=xbc[:, BOF + g, bc], rhs=xbc[:, COF + g, bc], start=True, stop=True),
                                 reads=[xbc_tk[BOF + g], xbc_tk[COF + g]], writes=[bank_tk[b]])
                        evac_copy(cb, banks[b][:, 0:SG * 128].rearrange("p (g l) -> p g l", g=SG), [bank_tk[b]], [cb_tk])
                        b2 = next_rot()
                        bkb2 = banks[b2][:, :].bitcast(BF16)
                        for g in range(SG):
                            P.op('pe', lambda e, g=g: e.transpose(bkb2[:, g * 128:(g + 1) * 128], xbc[:, BOF + g, bc], ident_b),
                                 reads=[xbc_tk[BOF + g], k_tk], writes=[bank_tk[b2]])
                        evac_copy(Btk, bkb2[:, 0:SG * 128].rearrange("p (g n) -> p g n", g=SG), [bank_tk[b2]], [Btk_tk])
                    st.append(s_cb)

                    def s_mt():
                        P.op('dve', lambda e: e.tensor_tensor(out=Mt.rearrange("p (g r) l -> p g r l", g=SG), in0=dcy.rearrange("p (g r) l -> p g r l", g=SG),
                                                              in1=cb.unsqueeze(2).to_broadcast([128, SG, 8, 128]), op=ALU.mult),
                             reads=[dcy_tk, cb_tk], writes=[Mt_tk])
                    st.append(s_mt)
                    return st

                def back_stages(jb):
                    S = SETS[jb % NSET]
                    sm, sm_tk, dcy, dcy_tk = S['sm'], S['sm_tk'], S['dcy'], S['dcy_tk']
                    xdt, xdt_tk, xsk, xsk_tk, xdw, xdw_tk, Btk, Btk_tk = S['xdt'], S['xdt_tk'], S['xsk'], S['xsk_tk'], S['xdw'], S['xdw_tk'], S['Btk'], S['Btk_tk']
                    yy, yy_tk, ssq, ssq_tk, yn, yn_tk = S['yy'], S['yy_tk'], S['ssq'], S['ssq_tk'], S['yn'], S['yn_tk']
                    Mt, Mt_tk = dcy, dcy_tk
                    gblk = c * 4 + jb
                    hin, hin_tk = hpbs[gblk % 3], hpb_tks[gblk % 3]
                    hout, hout_tk = hpbs[(gblk + 1) % 3], hpb_tks[(gblk + 1) % 3]
                    A0 = (jb % NSET) * SG
                    bc = slice(jb * 128, (jb + 1) * 128)
                    st = []

                    def s_yoff():
                        for g in range(SG):
                            P.op('pe', lambda e, g=g: e.matmul(banks[ACC[A0 + g]][:, :], lhsT=xbc[:, COF + g, bc], rhs=hin[:, g * 512:(g + 1) * 512], start=True, stop=True),
                                 reads=[xbc_tk[COF + g], hin_tk], writes=[bank_tk[ACC[A0 + g]]])
                            P.op('dve', lambda e, g=g: e.tensor_tensor(out=yy[:, g * 512:(g + 1) * 512].rearrange("p (h d) -> p h d", h=8),
                                                                       in0=banks[ACC[A0 + g]][:, :].rearrange("p (h d) -> p h d", h=8),
                                                                       in1=sm[:, 0, g * 8:(g + 1) * 8].unsqueeze(2).to_broadcast([128, 8, 64]), op=ALU.mult),
                                 reads=[bank_tk[ACC[A0 + g]], sm_tk], writes=[yy_tk])
                            P.op('pool', lambda e, g=g: e.tensor_tensor(out=yy[:, g * 512:(g + 1) * 512], in0=yy[:, g * 512:(g + 1) * 512], in1=xsk[:, g * 512:(g + 1) * 512], op=ALU.add),
                                 reads=[yy_tk, xsk_tk], writes=[yy_tk])
                        P.op('pool', lambda e: e.memset(ssq, 0.0), writes=[ssq_tk])
                    st.append(s_yoff)

                    def s_ydiag():
                        for h in range(SH):
                            g = h // 8
                            P.op('pe', lambda e, h=h, g=g: e.matmul(banks[ACC[A0 + g]][:, (h % 8) * 64:(h % 8 + 1) * 64], lhsT=Mt[:, h, :], rhs=xdt[:, h * 64:(h + 1) * 64],
                                                                    start=True, stop=True, skip_group_check=True),
                                 reads=[Mt_tk, xdt_tk], writes=[bank_tk[ACC[A0 + g]]])
                    st.append(s_ydiag)

                    def s_state():
                        P.op('pool', lambda e: e.tensor_tensor(out=xdw.rearrange("p (h d) -> p h d", h=SH), in0=xdt.rearrange("p (h d) -> p h d", h=SH),
                                                               in1=sm[:, 1, :].unsqueeze(2).to_broadcast([128, SH, 64]), op=ALU.mult),
                             reads=[xdt_tk, sm_tk], writes=[xdw_tk])
                        for g in range(SG):
                            b = next_rot()
                            gs_ = slice(g * 512, (g + 1) * 512)
                            P.op('pe', lambda e, b=b, g=g, gs_=gs_: e.matmul(banks[b][:, :], lhsT=Btk[:, g, :], rhs=xdw[:, gs_], start=True, stop=True),
                                 reads=[Btk_tk, xdw_tk], writes=[bank_tk[b]])
                            P.op('pool', lambda e, g=g, gs_=gs_: e.tensor_tensor(out=hst[:, gs_].rearrange("p (h d) -> p h d", h=8), in0=hst[:, gs_].rearrange("p (h d) -> p h d", h=8),
                                                                                in1=sm[:, 2, g * 8:(g + 1) * 8].unsqueeze(2).to_broadcast([128, 8, 64]), op=ALU.mult),
                                 reads=[hst_tk, sm_tk], writes=[hst_tk])
                            P.op('dve', lambda e, b=b, gs_=gs_: e.tensor_tensor(out=hst[:, gs_], in0=hst[:, gs_], in1=banks[b][:, :], op=ALU.add),
                                 reads=[bank_tk[b], hst_tk], writes=[hst_tk])
                            P.op('act', lambda e, gs_=gs_: e.copy(out=hout[:, gs_], in_=hst[:, gs_]), reads=[hst_tk], writes=[hout_tk])
                    st.insert(0, s_state)

                    def s_comb():
                        for g in range(SG):
                            gs_ = slice(g * 512, (g + 1) * 512)
                            P.op('dve', lambda e, g=g, gs_=gs_: e.tensor_tensor(out=yy[:, gs_], in0=yy[:, gs_], in1=banks[ACC[A0 + g]][:, :], op=ALU.add),
                                 reads=[bank_tk[ACC[A0 + g]], yy_tk], writes=[yy_tk])
                            P.op('dve', lambda e, gs_=gs_: e.tensor_tensor(out=yy[:, gs_], in0=yy[:, gs_], in1=zs[:, jb, gs_], op=ALU.mult),
                                 reads=[yy_tk, zs_tk[jb]], writes=[yy_tk])
                            P.op('act', lambda e, g=g, gs_=gs_: e.activation(out=junk, in_=yy[:, gs_], func=AF.Square, accum_out=ssq[:, g:g + 1]),
                                 reads=[yy_tk, ssq_tk], writes=[ssq_tk])
                    st.append(s_comb)

                    def s_norm():
                        P.op('act', lambda e: e.activation(out=ssq, in_=ssq, func=AF.Ln, scale=1.0 / 512, bias=epsc[:, 0:1]), reads=[ssq_tk, k_tk], writes=[ssq_tk])
                        P.op('act', lambda e: e.activation(out=ssq, in_=ssq, func=AF.Exp, scale=-0.5), reads=[ssq_tk], writes=[ssq_tk], strict=True)
                        for g in range(SG):
                            gs_ = slice(g * 512, (g + 1) * 512)
                            P.op('dve', lambda e, g=g, gs_=gs_: e.scalar_tensor_tensor(out=yn[:, gs_], in0=yy[:, gs_], scalar=ssq[:, g:g + 1], in1=sng[:, gs_],
                                                                                       op0=ALU.mult, op1=ALU.mult),
                                 reads=[yy_tk, ssq_tk, sng_tk], writes=[yn_tk], strict=True)
                    st.append(s_norm)

                    def s_ynT():
                        b = next_rot()
                        bkb3 = banks[b][:, :].bitcast(BF16)
                        for q in range(SXC):
                            P.op('pe', lambda e, q=q: e.transpose(bkb3[:, q * 128:(q + 1) * 128], yn[:, q * 128:(q + 1) * 128], ident_b),
                                 reads=[yn_tk, k_tk], writes=[bank_tk[b]])
                        evac_copy(xbc[:, 0:SXC, bc], bkb3[:, 0:XW].rearrange("p (f t) -> p f t", f=SXC), [bank_tk[b]], xbc_tk[0:SXC])
                    st.append(s_ynT)
                    return st

                def interleave(lists):
                    n = max(len(l) for l in lists)
                    for k in range(n):
                        for l in lists:
                            if k < len(l):
                                l[k]()

                for pr in range(2):
                    jbs = [2 * pr, 2 * pr + 1]
                    interleave([prep_stages(jb) for jb in jbs])
                    if pr == 1 and c + 1 < NCH:
                        rmsnorm_chunk(1 - buf, 'mixg%d' % li, 1 - buf)
                    interleave([back_stages(jb) for jb in jbs])
                dense_out('ssd_w_out', jl, SXC, lambda fc: xbc[:, fc, :], xbc_tk[0:SXC], c, par)
            flush_finish()
            pend[0] = par
            AR.release(m)

        phase_load()
        for L in layers:
            kind, j = L[:3], int(L[3:])
            if kind == 'ffn':
                phase_ffn(j)
            elif kind == 'fox':
                phase_fox(j, 2 * j + 1)
            elif kind == 'ssd':
                phase_ssd(j, 2 * j)
        phase_final()
        P.check_deadlock()
        block = es.enter_context(nc.Block())
        P.materialize(block)
        build_program.stats = {e: len(v) for e, v in P.ops.items()}
        build_program.stats['arena_peak'] = AR.peak
        build_program.stats['nsem'] = P.nds
    return nc


FULL_LAYERS = ['ssd0', 'ffn0', 'fox0', 'ffn1', 'ssd1', 'ffn2', 'fox1', 'ffn3']
_cache = {}


def kernel(**inputs):
    inp = {k: np.asarray(v) for k, v in inputs.items()}
    x = inp['x']
    B, S, D = x.shape
    groups = [[2 * b, 2 * b + 1] for b in range(4)]
    if 'nc' not in _cache:
        _cache['nc'] = build_program(S, FULL_LAYERS, groups)
    nc = _cache['nc']
    per_r = []
    for r in range(TP):
        cl, consts = build_consts(inp, r)
        w = slice_weights(inp, r)
        w['consts'] = consts
        per_r.append(w)
    in_maps = []
    for core in range(8):
        b, r = core // 2, core % 2
        m = dict(per_r[r])
        m['x'] = np.ascontiguousarray(x[b].T)
        in_maps.append(m)
    res = run_bass_kernel_spmd(nc, in_maps, core_ids=list(range(8)))
    out = np.stack([np.ascontiguousarray(np.asarray(res.results[2 * b]['out']).T) for b in range(B)], axis=0).astype(np.float32)
    return out
```

```python
import numpy as np
from contextlib import ExitStack
import concourse.bass as bass
import concourse.mybir as mybir
from concourse.bass_utils import run_bass_kernel_spmd

F32 = mybir.dt.float32
BF16 = mybir.dt.bfloat16
ALU = mybir.AluOpType
AF = mybir.ActivationFunctionType

D_MODEL = 1024
DEPTH = 4
EPS = 1e-6
SSD_D_INNER = 2048
SSD_HEADS = 32
SSD_CONV = 4
SSD_CONV_DIM = 3072
SSD_IN_DIM = 5152
FOX_HEADS = 16
FOX_D = 1024
FOX_IN_DIM = 4112
D_FF = 2816
FFN_CONV = 3
TT = 512
NEG = -30000.0
KC = 8
TP = 2
FH = FOX_HEADS // TP
FFC = (D_FF // 128) // TP
SH = SSD_HEADS // TP
SG = 4 // TP
SXC = SH * 64 // 128
SNC = SXC + 2 * SG

COMPUTE = ('pe', 'act', 'dve', 'pool')


class Tk:
    __slots__ = ('name', 'w', 'r')

    def __init__(self, name):
        self.name = name
        self.w = None
        self.r = {}


class DSem:
    def __init__(self, handle, name):
        self.h = handle
        self.name = name
        self.count = 0


class Prog:
    def __init__(self, nc, es):
        self.nc = nc
        self.es = es
        self.ops = {e: [] for e in ('pe', 'act', 'dve', 'pool', 'sp')}
        self.esem = {e: es.enter_context(nc.semaphore("s_" + e)) for e in COMPUTE}
        self.nds = 0
        self.dcache = {}
        self.out_deps = []

    def dsem(self, name):
        if name not in self.dcache:
            self.nds += 1
            self.dcache[name] = DSem(self.es.enter_context(self.nc.semaphore("d_%s_%d" % (name, self.nds))), name)
        return self.dcache[name]

    def _deps(self, eng, reads, writes, strict=False):
        deps = set()
        for t in reads:
            if t.w is not None:
                deps.add(t.w)
        for t in writes:
            if t.w is not None:
                deps.add(t.w)
            for v in t.r.values():
                deps.add(v)
        if strict:
            return deps
        return {d for d in deps if not (d[0] == 'E' and d[1] == eng)}

    def op(self, eng, fn, reads=(), writes=(), strict=False):
        deps = self._deps(eng, reads, writes, strict)
        idx = len(self.ops[eng])
        import sys as _s
        self.ops[eng].append({'fn': fn, 'deps': deps, 'inc': False, 'dma': None, 'note': _s._getframe(1).f_lineno})
        me = ('E', eng, idx)
        for t in reads:
            t.r[eng] = me
        for t in writes:
            t.w = me
            t.r = {}
        return me

    def dma(self, q, out, in_, sem, reads=(), writes=()):
        deps = self._deps(q, reads, writes, strict=True)
        sem.count += 16
        me = ('D', sem, sem.count)
        self.ops[q].append({'fn': (lambda e, o=out, i=in_: e.dma_start(out=o, in_=i)), 'deps': deps,
                            'inc': False, 'dma': sem, 'tok': me})
        for t in reads:
            t.r['D' + sem.name + str(id(sem))] = me
        for t in writes:
            t.w = me
            t.r = {}
        return me

    def cc(self, in_ap, out_ap, groups, slot, reads=(), writes=()):
        deps = self._deps('pool', reads, writes, strict=True)
        sem = self.dsem("cc%d" % slot)
        sem.count += 1
        me = ('D', sem, sem.count)

        def fn(e, i=in_ap, o=out_ap):
            return e.collective_compute("AllReduce", ALU.add, replica_groups=groups, ins=[i], outs=[o])
        self.ops['pool'].append({'fn': fn, 'deps': deps, 'inc': False, 'dma': None, 'cc': sem, 'tok': me})
        for t in reads:
            t.r['C' + str(id(sem))] = me
        for t in writes:
            t.w = me
            t.r = {}
        return me

    def dma_acc(self, out, in_, sem, reads=(), writes=()):
        deps = self._deps('pool', reads, writes, strict=True)
        sem.count += 16
        me = ('D', sem, sem.count)
        self.ops['pool'].append({'fn': (lambda e, o=out, i=in_: e.dma_start(out=o, in_=i, accum_op=ALU.add)), 'deps': deps,
                                 'inc': False, 'dma': sem, 'tok': me})
        for t in reads:
            t.r['D' + sem.name + str(id(sem))] = me
        for t in writes:
            t.w = me
            t.r = {}
        return me

    def barrier(self):
        toks = set()
        for e in COMPUTE:
            if self.ops[e]:
                for i in range(len(self.ops[e]) - 1, -1, -1):
                    if self.ops[e][i]['fn'] is not None and self.ops[e][i].get('tok') is None:
                        toks.add(('E', e, i))
                        break
        for sem in self.dcache.values():
            if sem.count > 0:
                toks.add(('D', sem, sem.count))
        for q in self.ops:
            deps = {d for d in toks if not (d[0] == 'E' and d[1] == q)}
            self.ops[q].append({'fn': None, 'deps': deps, 'inc': False, 'dma': None})

    def finalize_wait(self, q, toks):
        self.ops[q].append({'fn': None, 'deps': set(toks), 'inc': False, 'dma': None})

    def check_deadlock(self):
        pos = {e: 0 for e in self.ops}
        done = set()
        dtok = {}
        for e, lst in self.ops.items():
            cnts = {}
            for i, o in enumerate(lst):
                sem = o.get('cc') or o.get('dma')
                if sem is not None:
                    o.setdefault('_tok', None)
        progress = True
        while progress:
            progress = False
            for e, lst in self.ops.items():
                while pos[e] < len(lst):
                    o = lst[pos[e]]
                    ok = True
                    for d in o['deps']:
                        key = (d[0], d[1], d[2]) if d[0] == 'E' else ('D', id(d[1]), d[2])
                        if key not in done:
                            ok = False
                            break
                    if not ok:
                        break
                    done.add(('E', e, pos[e]))
                    if o.get('tok') is not None:
                        t = o['tok']
                        done.add(('D', id(t[1]), t[2]))
                    pos[e] += 1
                    progress = True
        stuck = {e: pos[e] for e in self.ops if pos[e] < len(self.ops[e])}
        if stuck:
            msg = []
            for e, p in stuck.items():
                o = self.ops[e][p]
                missing = []
                for d in o['deps']:
                    key = (d[0], d[1], d[2]) if d[0] == 'E' else ('D', id(d[1]), d[2])
                    if key not in done:
                        missing.append((d[0], d[1] if d[0] == 'E' else d[1].name, d[2]))
                msg.append("%s@%d/%d waits %s [%s]" % (e, p, len(self.ops[e]), missing, o.get('note')))
            raise RuntimeError("DEADLOCK: " + " | ".join(msg))

    def materialize(self, block):
        for e, lst in self.ops.items():
            for o in lst:
                for d in o['deps']:
                    if d[0] == 'E':
                        self.ops[d[1]][d[2]]['inc'] = True
        vals = {}
        for e in COMPUTE:
            c = 0
            for i, o in enumerate(self.ops[e]):
                if o['inc']:
                    c += 1
                    vals[(e, i)] = c
        self.vals = vals

        def run(eng_name, e):
            waited = {}
            for o in self.ops[eng_name]:
                for d in sorted(o['deps'], key=lambda d: (d[0], str(d[1]) if d[0] == 'E' else d[1].name, d[2])):
                    if d[0] == 'E':
                        key = ('E', d[1])
                        v = vals[(d[1], d[2])]
                        sh = self.esem[d[1]]
                    else:
                        key = ('D', id(d[1]))
                        v = d[2]
                        sh = d[1].h
                    if waited.get(key, 0) >= v:
                        continue
                    waited[key] = v
                    e.wait_ge(sh, v)
                if o['fn'] is None:
                    continue
                ins = o['fn'](e)
                if o.get('cc') is not None:
                    ins.then_inc(o['cc'].h)
                elif o['dma'] is not None:
                    ins.then_inc(o['dma'].h, 16)
                elif o['inc']:
                    ins.then_inc(self.esem[eng_name], 1)

        block.tensor(lambda e: run('pe', e))
        block.scalar(lambda e: run('act', e))
        block.vector(lambda e: run('dve', e))
        block.gpsimd(lambda e: run('pool', e))
        block.sync(lambda e: run('sp', e))


class Arena:
    def __init__(self, ap_f32, nwords):
        self.ap = ap_f32
        self.n = nwords
        self.top = 0
        self.peak = 0

    def mark(self):
        return self.top

    def release(self, m):
        self.top = m

    def alloc(self, shape, dtype, parts=128):
        n = int(np.prod(shape))
        words = n if dtype == F32 else (n + 1) // 2
        a = self.top
        self.top += words
        self.peak = max(self.peak, self.top)
        assert self.top <= self.n, "arena overflow %d > %d" % (self.top, self.n)
        v = self.ap[0:parts, a:a + words]
        if dtype != F32:
            v = v.bitcast(dtype)[:, 0:n]
        if len(shape) == 2:
            v = v.rearrange("p (a b) -> p a b", a=shape[0])
        elif len(shape) == 3:
            v = v.rearrange("p (a b c) -> p a b c", a=shape[0], b=shape[1])
        return v


class CL:
    def __init__(self):
        self.off = {}
        self.n = 0

    def add(self, name, width):
        self.off[name] = (self.n, width)
        self.n += width


def const_layout():
    c = CL()
    c.add('ident', 128)
    c.add('U', 128)
    c.add('maskT', 128)
    c.add('bd64', 128)
    c.add('Ubar', 128)
    for i in range(DEPTH):
        c.add('mixg%d' % i, KC)
        c.add('ffng%d' % i, KC)
        c.add('fcw%d' % i, FFC * 3)
        c.add('fcb%d' % i, FFC)
    c.add('fing', KC)
    for j in range(2):
        c.add('scw%d' % j, SNC * 4)
        c.add('scb%d' % j, SNC)
        c.add('dtb%d' % j, 1)
        c.add('alog%d' % j, SH)
        c.add('dsk%d' % j, SH)
        c.add('fbf%d' % j, 1)
        c.add('fqg%d' % j, 1)
        c.add('fkg%d' % j, 1)
    return c


def build_consts(inp, r):
    c = const_layout()
    A = np.zeros((128, c.n), np.float32)

    def put(name, arr):
        o, w = c.off[name]
        arr = np.asarray(arr, np.float32)
        A[:arr.shape[0], o:o + w] = arr.reshape(arr.shape[0], w)

    put('ident', np.eye(128))
    put('U', np.triu(np.ones((128, 128))))
    put('maskT', np.where(np.arange(128)[None, :] >= np.arange(128)[:, None], 0.0, NEG))
    put('bd64', np.kron(np.eye(2), np.ones((64, 64))))
    put('Ubar', np.tril(np.ones((128, 128)), -1))
    fsl = slice(r * FFC * 128, (r + 1) * FFC * 128)
    for i in range(DEPTH):
        put('mixg%d' % i, inp['mix_norm_g'][i].reshape(KC, 128).T)
        put('ffng%d' % i, inp['ffn_norm_g'][i].reshape(KC, 128).T)
        put('fcw%d' % i, inp['ffn_conv_w'][i][:, fsl].reshape(3, FFC, 128).transpose(2, 1, 0).reshape(128, FFC * 3))
        put('fcb%d' % i, inp['ffn_conv_b'][i][fsl].reshape(FFC, 128).T)
    put('fing', inp['final_norm_g'].reshape(KC, 128).T)
    xs_ = np.r_[r * SH * 64:(r + 1) * SH * 64, 2048 + r * SG * 128:2048 + (r + 1) * SG * 128, 2560 + r * SG * 128:2560 + (r + 1) * SG * 128]
    hsl = slice(r * SH, (r + 1) * SH)
    for j in range(2):
        put('scw%d' % j, inp['ssd_conv_w'][j][:, xs_].reshape(4, SNC, 128).transpose(2, 1, 0).reshape(128, SNC * 4))
        put('scb%d' % j, inp['ssd_conv_b'][j][xs_].reshape(SNC, 128).T)
        put('dtb%d' % j, inp['ssd_dt_bias'][j][hsl].reshape(SH, 1))
        put('alog%d' % j, np.broadcast_to(inp['ssd_a_log'][j][hsl][None, :], (128, SH)))
        put('dsk%d' % j, np.broadcast_to(inp['ssd_d'][j][hsl][None, :], (128, SH)))
        put('fbf%d' % j, inp['fox_b_f'][j][r * FH:(r + 1) * FH].reshape(FH, 1))
        put('fqg%d' % j, np.tile(inp['fox_q_norm_g'][j], 2).reshape(128, 1))
        put('fkg%d' % j, np.tile(inp['fox_k_norm_g'][j], 2).reshape(128, 1))
    return c, A


def slice_weights(inp, r):
    w = {}
    f0, f1 = r * FFC * 128, (r + 1) * FFC * 128
    w['ffn_w_up'] = np.ascontiguousarray(np.concatenate([inp['ffn_w_up'][:, :, f0:f1], inp['ffn_w_up'][:, :, D_FF + f0:D_FF + f1]], axis=2))
    w['ffn_w_down'] = np.ascontiguousarray(inp['ffn_w_down'][:, f0:f1, :])
    q0, q1 = r * FH * 64, (r + 1) * FH * 64
    fi = inp['fox_w_in']
    w['fox_w_in'] = np.ascontiguousarray(np.concatenate([fi[:, :, q0:q1], fi[:, :, 1024 + q0:1024 + q1], fi[:, :, 2048 + q0:2048 + q1],
                                                          fi[:, :, 3072 + q0:3072 + q1], fi[:, :, 4096 + r * FH:4096 + (r + 1) * FH]], axis=2))
    w['fox_w_out'] = np.ascontiguousarray(inp['fox_w_out'][:, q0:q1, :])
    x0, x1 = r * SH * 64, (r + 1) * SH * 64
    b0, b1 = r * SG * 128, (r + 1) * SG * 128
    si = inp['ssd_w_in']
    w['ssd_w_in'] = np.ascontiguousarray(np.concatenate([si[:, :, x0:x1], si[:, :, 2048 + x0:2048 + x1], si[:, :, 4096 + b0:4096 + b1],
                                                          si[:, :, 4608 + b0:4608 + b1], si[:, :, 5120 + r * SH:5120 + (r + 1) * SH]], axis=2))
    w['ssd_w_out'] = np.ascontiguousarray(inp['ssd_w_out'][:, x0:x1, :])
    w['sng'] = np.ascontiguousarray(np.broadcast_to(inp['ssd_norm_g'][:, None, x0:x1], (2, 128, SH * 64)), dtype=np.float32)
    return w


WNAMES = ['ssd_w_in', 'ssd_w_out', 'fox_w_in', 'fox_w_out', 'ffn_w_up', 'ffn_w_down']
SSD_LIN = 2 * SH * 64 + 2 * SG * 128 + SH
FOX_LIN = 4 * FH * 64 + FH
WSHAPES = {'ssd_w_in': (2, 1024, SSD_LIN), 'ssd_w_out': (2, SH * 64, 1024), 'fox_w_in': (2, 1024, FOX_LIN),
           'fox_w_out': (2, FH * 64, 1024), 'ffn_w_up': (4, 1024, 2 * FFC * 128), 'ffn_w_down': (4, FFC * 128, 1024)}


def build_program(seq, layers, groups, debug=None):
    NCH = seq // TT
    nc = bass.Bass("TRN2", target_bir_lowering=False)
    cl = const_layout()
    x_d = nc.dram_tensor("x", [D_MODEL, seq], F32, kind="ExternalInput").ap()
    c_d = nc.dram_tensor("consts", [128, cl.n], F32, kind="ExternalInput").ap()
    w_d = {n: nc.dram_tensor(n, list(WSHAPES[n]), F32, kind="ExternalInput").ap() for n in WNAMES}
    sng_d = nc.dram_tensor("sng", [2, 128, SH * 64], F32, kind="ExternalInput").ap()
    out_d = nc.dram_tensor("out", [D_MODEL, seq], F32, kind="ExternalOutput").ap()
    wb_d = {n: nc.dram_tensor(n + "_b", list(WSHAPES[n]), BF16).ap() for n in WNAMES}
    hT_d = nc.dram_tensor("hT_d", [D_MODEL, seq], F32).ap()
    part_t = [[nc.dram_tensor("part%d_%d" % (p, c), [D_MODEL, TT], F32) for c in range(NCH)] for p in range(2)]
    red_t = [[nc.dram_tensor("red%d_%d" % (p, c), [D_MODEL, TT], F32) for c in range(NCH)] for p in range(2)]
    qa_d = nc.dram_tensor("qa_d", [FH, 70, seq], BF16).ap()
    ka_d = nc.dram_tensor("ka_d", [FH, 70, seq], BF16).ap()
    v_d = nc.dram_tensor("v_d", [seq, FH * 65], BF16).ap()
    sg_d = nc.dram_tensor("sg_d", [FH * 64, seq], BF16).ap()
    ot_d = nc.dram_tensor("ot_d", [FH * 64, seq], BF16).ap()
    dbg_d = None
    if debug:
        dbg_d = nc.dram_tensor("dbg", list(debug), F32, kind="ExternalOutput").ap()

    es = ExitStack()
    with es:
        P = Prog(nc, es)
        NW = 48900
        arena_t = es.enter_context(nc.sbuf_tensor("arena", [128, NW], F32))
        AR = Arena(arena_t[:, :], NW)
        banks = [es.enter_context(nc.psum_tensor("bank%d" % i, [128, 512], F32)) for i in range(8)]
        bank_tk = [Tk("bank%d" % i) for i in range(8)]
        ACC = [0, 1, 2, 3]
        ROT = [4, 5, 6]
        MISC = 7
        rot_i = [0]

        def next_rot():
            b = ROT[rot_i[0] % 3]
            rot_i[0] += 1
            return b

        cst = AR.alloc([cl.n], F32)
        cst_tk = Tk("cst")
        s_c = P.dsem("cst")
        P.dma('sp', cst, c_d[:, :], s_c, writes=[cst_tk])

        def C(name, parts=128):
            o, w = cl.off[name]
            return cst[0:parts, o:o + w]

        ident_f = C('ident')
        ident_b = AR.alloc([128], BF16)
        ones_b = AR.alloc([128], BF16)
        nones_b = AR.alloc([128], BF16)
        ones_f = AR.alloc([128], F32)
        U_b = AR.alloc([128], BF16)
        mask_b = AR.alloc([4, 128], BF16)
        k_tk = Tk("konst")
        P.op('dve', lambda e: e.tensor_copy(out=ident_b, in_=ident_f), reads=[cst_tk], writes=[k_tk])
        P.op('dve', lambda e: e.memset(ones_b, 1.0), writes=[k_tk])
        P.op('dve', lambda e: e.memset(nones_b, -1.0), writes=[k_tk])
        P.op('dve', lambda e: e.memset(ones_f, 1.0), writes=[k_tk])
        P.op('dve', lambda e: e.tensor_copy(out=U_b, in_=C('U')), reads=[cst_tk], writes=[k_tk])
        for r in range(4):
            P.op('dve', lambda e, r=r: e.tensor_copy(out=mask_b[:, r, :], in_=C('maskT')), reads=[cst_tk], writes=[k_tk])

        wtk = {}
        used = set()
        for L in layers:
            kind, j = L[:3], int(L[3:])
            if kind == 'ssd':
                used |= {('ssd_w_in', j), ('ssd_w_out', j)}
            elif kind == 'fox':
                used |= {('fox_w_in', j), ('fox_w_out', j)}
            elif kind == 'ffn':
                used |= {('ffn_w_up', j), ('ffn_w_down', j)}
        order = []
        for L in layers:
            kind, j = L[:3], int(L[3:])
            names = {'ssd': ['ssd_w_in', 'ssd_w_out'], 'fox': ['fox_w_in', 'fox_w_out'], 'ffn': ['ffn_w_up', 'ffn_w_down']}[kind]
            for n in names:
                if (n, j) not in order:
                    order.append((n, j))
        for (n, j) in order:
            wtk[(n, j)] = Tk("w_%s_%d" % (n, j))
        conv_done = set()

        def issue_conv(li_):
            if li_ >= len(layers):
                return
            L_ = layers[li_]
            kind_, j_ = L_[:3], int(L_[3:])
            for n in {'ssd': ['ssd_w_in', 'ssd_w_out'], 'fox': ['fox_w_in', 'fox_w_out'], 'ffn': ['ffn_w_up', 'ffn_w_down']}[kind_]:
                if (n, j_) in conv_done:
                    continue
                conv_done.add((n, j_))
                t = wtk[(n, j_)]
                sm_ = P.dsem("wc_%s_%d" % (n, j_))
                K = WSHAPES[n][1]
                half = K // 2
                P.dma('pool', wb_d[n][j_, 0:half, :], w_d[n][j_, 0:half, :], sm_, writes=[t])
                P.dma('pool', wb_d[n][j_, half:K, :], w_d[n][j_, half:K, :], sm_, writes=[t])

        issue_conv(0)
        issue_conv(1)

        hc = [AR.alloc([KC, TT], F32) for _ in range(2)]
        hc_tk = [Tk("hc0"), Tk("hc1")]
        hc_sem = [P.dsem("hc0"), P.dsem("hc1")]
        hnb = [AR.alloc([KC, TT], BF16) for _ in range(2)]
        hnb_tk = [Tk("hn0"), Tk("hn1")]
        sq = AR.alloc([KC, TT], BF16)
        sq_tk = Tk("sq")
        rstd = AR.alloc([TT], F32)
        rstd_tk = Tk("rstd")
        pst = AR.alloc([KC, TT], F32)
        pst_tk = Tk("pst")
        pst_sem = P.dsem("pst")
        NWS = 3
        wslab = [AR.alloc([KC, 512], BF16) for _ in range(NWS)]
        ws_tk = [Tk("ws%d" % i) for i in range(NWS)]
        ws_sem = [P.dsem("ws%d" % i) for i in range(NWS)]
        ws_i = [0]
        NW2 = 2
        w2slab = [AR.alloc([4, 512], BF16) for _ in range(NW2)]
        w2_tk = [Tk("w2%d" % i) for i in range(NW2)]
        w2_sem = [P.dsem("w2%d" % i) for i in range(NW2)]
        w2_i = [0]
        hT_tk = [Tk("hTd%d" % c) for c in range(NCH)]
        part_tk = [[Tk("part%d_%d" % (p, c)) for c in range(NCH)] for p in range(2)]
        red_tk = [[Tk("red%d_%d" % (p, c)) for c in range(NCH)] for p in range(2)]
        st_sem = P.dsem("store")
        evac_i = [0]
        pend = [None]
        sub_i = [0]

        def evac_copy(out, in_, reads, writes):
            evac_i[0] += 1
            if evac_i[0] % 2:
                P.op('act', lambda e: e.copy(out=out, in_=in_), reads=reads, writes=writes)
            else:
                P.op('dve', lambda e: e.tensor_copy(out=out, in_=in_), reads=reads, writes=writes)

        def load_ws(wname, j, cols):
            i = ws_i[0] % NWS
            ws_i[0] += 1
            for (c0, ncol, dst) in cols:
                src = wb_d[wname][j, :, c0:c0 + ncol].rearrange("(k p) n -> p k n", p=128)
                P.dma('sp', wslab[i][:, :, dst:dst + ncol], src, ws_sem[i], reads=[wtk[(wname, j)]], writes=[ws_tk[i]])
            return wslab[i], ws_tk[i]

        def load_hc(c, buf, pp):
            src = hT_d[:, c * TT:(c + 1) * TT].rearrange("(k p) t -> p k t", p=128)
            P.dma('sp', hc[buf], src, hc_sem[buf], reads=[hT_tk[c]], writes=[hc_tk[buf]])
            if pp is not None:
                P.dma_acc(hc[buf], red_t[pp][c].ap().rearrange("(k p) t -> p k t", p=128), hc_sem[buf], reads=[red_tk[pp][c]], writes=[hc_tk[buf]])
                P.dma('pool', src, hc[buf], st_sem, reads=[hc_tk[buf]], writes=[hT_tk[c]])

        def store_hc(c, buf):
            dst = hT_d[:, c * TT:(c + 1) * TT].rearrange("(k p) t -> p k t", p=128)
            P.dma('pool', dst, hc[buf], st_sem, reads=[hc_tk[buf]], writes=[hT_tk[c]])

        def rmsnorm_chunk(buf, gname, hb):
            h = hc[buf]
            hn, hn_tk = hnb[hb], hnb_tk[hb]
            P.op('act', lambda e: e.activation(out=sq, in_=h, func=AF.Square), reads=[hc_tk[buf]], writes=[sq_tk])
            bk = banks[MISC]
            for kc in range(KC):
                P.op('pe', lambda e, kc=kc: e.matmul(bk[:, :], lhsT=ones_b, rhs=sq[:, kc, :], start=(kc == 0), stop=(kc == KC - 1)),
                     reads=[sq_tk, k_tk], writes=[bank_tk[MISC]])
            P.op('act', lambda e: e.activation(out=rstd, in_=bk[:, :], func=AF.Ln, scale=1.0 / D_MODEL, bias=epsc[:, 0:1]),
                 reads=[bank_tk[MISC], cst_tk, k_tk], writes=[rstd_tk])
            P.op('act', lambda e: e.activation(out=rstd, in_=rstd, func=AF.Exp, scale=-0.5), reads=[rstd_tk], writes=[rstd_tk], strict=True)
            g = C(gname)
            for kc in range(KC):
                P.op('dve', lambda e, kc=kc: e.scalar_tensor_tensor(out=hn[:, kc, :], in0=h[:, kc, :], scalar=g[:, kc:kc + 1], in1=rstd,
                                                                  op0=ALU.mult, op1=ALU.mult),
                     reads=[hc_tk[buf], rstd_tk, cst_tk], writes=[hn_tk])

        def proj_T(slab, slab_tk, col0, M, rhs_fn, bank, n=TT, extra_reads=()):
            for kc in range(KC):
                r_ = rhs_fn(kc)
                P.op('pe', lambda e, kc=kc, r_=r_: e.matmul(banks[bank][0:M, 0:n], lhsT=slab[:, kc, col0:col0 + M], rhs=r_,
                                                            start=(kc == 0), stop=(kc == KC - 1)),
                     reads=[slab_tk] + list(extra_reads), writes=[bank_tk[bank]])

        pending_finish = [None]

        def flush_finish():
            if pending_finish[0] is not None:
                f = pending_finish[0]
                pending_finish[0] = None
                f()

        def dense_out(wname, j, nfc, rhs_fn, rhs_tks, c, par, alt=False):
            flush_finish()
            for half in range(2):
                ngrp = (nfc + 3) // 4
                for gi in range(ngrp):
                    f0 = gi * 4
                    nf = min(4, nfc - f0)
                    i = w2_i[0] % NW2
                    w2_i[0] += 1
                    src = wb_d[wname][j, f0 * 128:(f0 + nf) * 128, half * 512:(half + 1) * 512].rearrange("(f p) n -> p f n", p=128)
                    P.dma('sp', w2slab[i][:, 0:nf, :], src, w2_sem[i], reads=[wtk[(wname, j)]], writes=[w2_tk[i]])
                    for f in range(nf):
                        fc = f0 + f
                        for dmi in range(4):
                            bk_ = (4 + dmi) if (alt and half == 1) else ACC[dmi]
                            r_ = rhs_fn(fc)
                            P.op('pe', lambda e, i=i, f=f, fc=fc, dmi=dmi, bk_=bk_, r_=r_: e.matmul(
                                banks[bk_][:, :], lhsT=w2slab[i][:, f, dmi * 128:(dmi + 1) * 128], rhs=r_,
                                start=(fc == 0), stop=(fc == nfc - 1)),
                                 reads=[w2_tk[i]] + list(rhs_tks), writes=[bank_tk[bk_]])
                for dmi in range(4):
                    kc = half * 4 + dmi
                    bk_ = (4 + dmi) if (alt and half == 1) else ACC[dmi]
                    evac_copy(pst[:, kc, :], banks[bk_][:, :], [bank_tk[bk_]], [pst_tk])
            def finish():
                P.dma('sp', part_t[par][c].ap().rearrange("(k p) t -> p k t", p=128), pst, pst_sem, reads=[pst_tk], writes=[part_tk[par][c]])
                P.cc(part_t[par][c].ap().opt(), red_t[par][c].ap().opt(), groups, c % 8, reads=[part_tk[par][c]], writes=[red_tk[par][c]])
            pending_finish[0] = finish

        def conv_silu(bank, M, K, wcol_fn, bcol, pc, pc_tk, acc, acc_tk, tails, tails_tk, out, out_reads, out_writes):
            H = K - 1
            P.op('act', lambda e: e.copy(out=pc[0:M, H:H + TT], in_=banks[bank][0:M, :]), reads=[bank_tk[bank]], writes=[pc_tk])
            P.op('pool', lambda e: e.tensor_copy(out=pc[0:M, 0:H], in_=tails[0:M, :]), reads=[tails_tk], writes=[pc_tk])
            P.op('dve', lambda e: e.tensor_scalar(out=acc[0:M, :], in0=pc[0:M, 0:TT], scalar1=wcol_fn(0), scalar2=bcol, op0=ALU.mult, op1=ALU.add),
                 reads=[pc_tk, cst_tk], writes=[acc_tk])
            for k in range(1, K):
                P.op('dve', lambda e, k=k: e.scalar_tensor_tensor(out=acc[0:M, :], in0=pc[0:M, k:k + TT], scalar=wcol_fn(k), in1=acc[0:M, :],
                                                                  op0=ALU.mult, op1=ALU.add),
                     reads=[pc_tk, cst_tk, acc_tk], writes=[acc_tk])
            P.op('pool', lambda e: e.tensor_copy(out=tails[0:M, :], in_=pc[0:M, TT:TT + H]), reads=[pc_tk], writes=[tails_tk])
            P.op('act', lambda e: e.activation(out=out, in_=acc[0:M, :], func=AF.Silu), reads=[acc_tk] + list(out_reads), writes=list(out_writes))

        epsc = AR.alloc([1], F32)
        P.op('dve', lambda e: e.memset(epsc, EPS), writes=[k_tk])
        eps64 = AR.alloc([1], F32)
        P.op('dve', lambda e: e.memset(eps64, 64 * EPS), writes=[k_tk])
        onec = AR.alloc([1], F32)
        P.op('dve', lambda e: e.memset(onec, 1.0), writes=[k_tk])
        bd64_b = AR.alloc([128], BF16)
        P.op('dve', lambda e: e.tensor_copy(out=bd64_b, in_=C('bd64')), reads=[cst_tk], writes=[k_tk])

        def phase_load():
            xl_sem = P.dsem("xload")
            for c in range(NCH):
                P.dma('sp', hT_d[:, c * TT:(c + 1) * TT], x_d[:, c * TT:(c + 1) * TT], xl_sem, writes=[hT_tk[c]])

        def phase_final():
            P.barrier()
            m = AR.mark()
            pp = pend[0]
            hob = [AR.alloc([KC, TT], F32) for _ in range(2)]
            hob_tk = [Tk("ho0"), Tk("ho1")]
            o_sem = P.dsem("out")
            g = C('fing')
            toks = []
            load_hc(0, 0, pp)
            for c in range(NCH):
                buf = c % 2
                if c + 1 < NCH:
                    load_hc(c + 1, 1 - buf, pp)
                h = hc[buf]
                ho, ho_tk = hob[buf], hob_tk[buf]
                P.op('act', lambda e, h=h: e.activation(out=sq, in_=h, func=AF.Square), reads=[hc_tk[buf]], writes=[sq_tk])
                bk = banks[MISC]
                for kc in range(KC):
                    P.op('pe', lambda e, kc=kc: e.matmul(bk[:, :], lhsT=ones_b, rhs=sq[:, kc, :], start=(kc == 0), stop=(kc == KC - 1)),
                         reads=[sq_tk, k_tk], writes=[bank_tk[MISC]])
                P.op('act', lambda e: e.activation(out=rstd, in_=bk[:, :], func=AF.Ln, scale=1.0 / D_MODEL, bias=epsc[:, 0:1]),
                     reads=[bank_tk[MISC], k_tk], writes=[rstd_tk])
                P.op('act', lambda e: e.activation(out=rstd, in_=rstd, func=AF.Exp, scale=-0.5), reads=[rstd_tk], writes=[rstd_tk], strict=True)
                for kc in range(KC):
                    P.op('dve', lambda e, kc=kc, h=h, ho=ho: e.scalar_tensor_tensor(out=ho[:, kc, :], in0=h[:, kc, :], scalar=g[:, kc:kc + 1], in1=rstd,
                                                                                    op0=ALU.mult, op1=ALU.mult),
                         reads=[hc_tk[buf], rstd_tk, cst_tk], writes=[ho_tk])
                t = P.dma('pool', out_d[:, c * TT:(c + 1) * TT].rearrange("(k p) t -> p k t", p=128), ho, o_sem, reads=[ho_tk])
                toks.append(t)
            P.finalize_wait('pool', [toks[-1]])
            AR.release(m)

        def phase_ffn(i):
            P.barrier()
            m = AR.mark()
            pp = pend[0]
            par = sub_i[0] % 2
            sub_i[0] += 1
            aT = AR.alloc([FFC, TT], BF16)
            aT_tk = [Tk("aT%d" % f) for f in range(FFC)]
            pcs = [AR.alloc([TT + 4], F32) for _ in range(2)]
            pc_tks = [Tk("pc0"), Tk("pc1")]
            accs = [AR.alloc([TT], F32) for _ in range(2)]
            acc_tks = [Tk("acc0"), Tk("acc1")]
            gs = [AR.alloc([TT], BF16) for _ in range(2)]
            gs_tks = [Tk("gs0"), Tk("gs1")]
            tails = AR.alloc([FFC, 2], F32)
            tails_tk = [Tk("tl%d" % f) for f in range(FFC)]
            P.op('pool', lambda e: e.memset(tails, 0.0), writes=tails_tk)
            fcw = C('fcw%d' % i)
            fcb = C('fcb%d' % i)
            GW = FFC * 128
            load_hc(0, 0, pp)
            rmsnorm_chunk(0, 'ffng%d' % i, 0)
            for c in range(NCH):
                buf = c % 2
                hn, hn_tk = hnb[buf], hnb_tk[buf]
                if c + 1 < NCH:
                    load_hc(c + 1, 1 - buf, pp)
                j0 = 0
                while j0 < FFC:
                    nj = min(2, FFC - j0)
                    slab, stk = load_ws('ffn_w_up', i, [(j0 * 128, nj * 128, 0), (GW + j0 * 128, nj * 128, 256)])
                    for u in range(nj):
                        j = j0 + u
                        pi = j % 2
                        b = next_rot()
                        proj_T(slab, stk, u * 128, 128, lambda kc: hn[:, kc, :], b, extra_reads=[hn_tk])
                        conv_silu(b, 128, 3, lambda k, j=j: fcw[:, j * 3 + k:j * 3 + k + 1], fcb[:, j:j + 1], pcs[pi], pc_tks[pi], accs[pi], acc_tks[pi],
                                  tails[:, j, :], tails_tk[j], gs[pi], [], [gs_tks[pi]])
                    for u in range(nj):
                        j = j0 + u
                        pi = j % 2
                        b = next_rot()
                        proj_T(slab, stk, 256 + u * 128, 128, lambda kc: hn[:, kc, :], b, extra_reads=[hn_tk])
                        P.op('dve', lambda e, j=j, pi=pi, b=b: e.tensor_tensor(out=aT[:, j, :], in0=gs[pi], in1=banks[b][:, :], op=ALU.mult),
                             reads=[gs_tks[pi], bank_tk[b]], writes=[aT_tk[j]])
                    j0 += nj
                    if j0 >= 4:
                        flush_finish()
                if c + 1 < NCH:
                    rmsnorm_chunk(1 - buf, 'ffng%d' % i, 1 - buf)
                dense_out('ffn_w_down', i, FFC, lambda fc: aT[:, fc, :], aT_tk, c, par)
            flush_finish()
            pend[0] = par
            AR.release(m)

        def phase_fox(jl, li):
            P.barrier()
            m = AR.mark()
            pp = pend[0]
            par = sub_i[0] % 2
            sub_i[0] += 1
            fw = 'fox_w_in'
            HW = FH * 64
            NPR = FH // 2
            qk_st = [AR.alloc([NPR, TT], BF16) for _ in range(2)]
            qk_tk = [Tk("qkst0"), Tk("qkst1")]
            qk_sem = [P.dsem("qkst0"), P.dsem("qkst1")]
            sqh = [AR.alloc([TT], BF16) for _ in range(2)]
            sqh_tk = [Tk("sqh0"), Tk("sqh1")]
            rh = [AR.alloc([TT], F32) for _ in range(2)]
            rh_tk = [Tk("rh0"), Tk("rh1")]
            vst = AR.alloc([4, FH, 65], BF16)
            vst_tk = Tk("vst")
            v_sem = P.dsem("vst")
            sgst = AR.alloc([HW // 128, TT], BF16)
            sgst_tk = Tk("sgst")
            sg_sem = P.dsem("sgst")
            wf = AR.alloc([KC, FH], BF16)
            wf_tk = Tk("wf")
            wf_sem = P.dsem("wf")
            ef = AR.alloc([TT], F32)
            sA = AR.alloc([TT], F32)
            sB = AR.alloc([TT], F32)
            f_tk = Tk("fchain")
            carry = AR.alloc([1], F32)
            r1 = AR.alloc([TT], F32)
            c3q = AR.alloc([3, TT], BF16)
            c3k = AR.alloc([3, TT], BF16)
            c3_tk = Tk("c3")
            c3_sem = P.dsem("c3")
            ones3 = AR.alloc([3, TT], BF16)
            o3_tk = Tk("ones3")
            o3_sem = P.dsem("o3")
            qa_tk = Tk("qa_d")
            ka_tk = Tk("ka_d")
            qa1_tk, qa2_tk, ka1_tk, ka2_tk = Tk("qa1"), Tk("qa2"), Tk("ka1"), Tk("ka2")
            vd_tk = Tk("v_d")
            sgd_tk = Tk("sg_d")
            otd_tk = Tk("ot_d")
            nones3 = AR.alloc([3, TT], BF16)
            onesF = AR.alloc([TT], F32)
            P.op('pool', lambda e: e.memset(ones3, 1.0), writes=[o3_tk])
            P.op('pool', lambda e: e.memset(nones3, -1.0), writes=[o3_tk])
            P.op('pool', lambda e: e.memset(onesF, 1.0), writes=[o3_tk])
            P.op('pool', lambda e: e.memset(carry, 0.0), writes=[f_tk])
            P.op('pool', lambda e: e.memset(vst, 1.0), writes=[vst_tk])
            for c in range(NCH):
                P.dma('sp', qa_d[:, 67:70, c * TT:(c + 1) * TT], ones3[0:FH], o3_sem, reads=[o3_tk], writes=[qa1_tk])
                P.dma('sp', ka_d[:, 64:67, c * TT:(c + 1) * TT], nones3[0:FH], o3_sem, reads=[o3_tk], writes=[ka1_tk])
            P.dma('sp', wf, wb_d[fw][jl, :, 4 * HW:4 * HW + FH].rearrange("(k p) n -> p k n", p=128), wf_sem, reads=[wtk[(fw, jl)]], writes=[wf_tk])
            qg = C('fqg%d' % jl)
            kg = C('fkg%d' % jl)
            nbf = C('fbf%d' % jl, FH)
            load_hc(0, 0, pp)
            rmsnorm_chunk(0, 'mixg%d' % li, 0)
            for c in range(NCH):
                buf = c % 2
                hn, hn_tk = hnb[buf], hnb_tk[buf]
                cols = slice(c * TT, (c + 1) * TT)
                if c + 1 < NCH:
                    load_hc(c + 1, 1 - buf, pp)
                bm = MISC
                proj_T(wf, wf_tk, 0, FH, lambda kc: hn[:, kc, :], bm, extra_reads=[hn_tk])
                P.op('dve', lambda e: e.tensor_scalar(out=ef[0:FH], in0=banks[bm][0:FH, :], scalar1=nbf[:, 0:1], scalar2=-1.0, op0=ALU.add, op1=ALU.mult),
                     reads=[bank_tk[bm], cst_tk, f_tk], writes=[f_tk])
                P.op('act', lambda e: e.activation(out=ef[0:FH], in_=ef[0:FH], func=AF.Exp), reads=[f_tk], writes=[f_tk])
                P.op('act', lambda e: e.activation(out=sA[0:FH], in_=ef[0:FH], func=AF.Ln, bias=onec[0:FH, 0:1]), reads=[f_tk, k_tk], writes=[f_tk], strict=True)
                P.op('dve', lambda e: e.tensor_tensor_scan(out=sB[0:FH], data0=onesF[0:FH], data1=sA[0:FH], initial=carry[0:FH, 0:1], op0=ALU.mult, op1=ALU.add),
                     reads=[f_tk, o3_tk], writes=[f_tk], strict=True)
                P.op('dve', lambda e: e.tensor_copy(out=carry[0:FH], in_=sB[0:FH, TT - 1:TT]), reads=[f_tk], writes=[f_tk], strict=True)
                P.op('act', lambda e: e.copy(out=c3k[0:FH, 0, :], in_=sB[0:FH]), reads=[f_tk, c3_tk], writes=[c3_tk])
                P.op('dve', lambda e: e.tensor_tensor(out=r1[0:FH], in0=sB[0:FH], in1=c3k[0:FH, 0, :], op=ALU.subtract), reads=[f_tk, c3_tk], writes=[f_tk])
                P.op('act', lambda e: e.copy(out=c3k[0:FH, 1, :], in_=r1[0:FH]), reads=[f_tk, c3_tk], writes=[c3_tk])
                P.op('dve', lambda e: e.tensor_tensor(out=sA[0:FH], in0=r1[0:FH], in1=c3k[0:FH, 1, :], op=ALU.subtract), reads=[f_tk, c3_tk], writes=[f_tk])
                P.op('act', lambda e: e.copy(out=c3k[0:FH, 2, :], in_=sA[0:FH]), reads=[f_tk, c3_tk], writes=[c3_tk])
                P.dma('pool', qa_d[:, 64:67, cols], c3k[0:FH], c3_sem, reads=[c3_tk], writes=[qa2_tk])
                P.dma('pool', ka_d[:, 67:70, cols], c3k[0:FH], c3_sem, reads=[c3_tk], writes=[ka2_tk])
                tasks = []
                for which in range(2):
                    for pr in range(NPR):
                        tasks.append((which, pr))
                slabs = {}
                tb = {}

                def t_front(ti):
                    which, pr = tasks[ti]
                    if pr == 0:
                        slabs[which] = load_ws(fw, jl, [(which * HW, HW, 0)])
                    slab, stk = slabs[which]
                    b = next_rot()
                    tb[ti] = b
                    proj_T(slab, stk, pr * 128, 128, lambda kc: hn[:, kc, :], b, extra_reads=[hn_tk])

                def t_back(ti):
                    which, pr = tasks[ti]
                    b = tb[ti]
                    x_ = ti % 2
                    P.op('act', lambda e: e.activation(out=sqh[x_], in_=banks[b][:, :], func=AF.Square), reads=[bank_tk[b]], writes=[sqh_tk[x_]])
                    P.op('pe', lambda e: e.matmul(banks[MISC][:, :], lhsT=bd64_b, rhs=sqh[x_], start=True, stop=True),
                         reads=[sqh_tk[x_], k_tk], writes=[bank_tk[MISC]])
                    if which == 0:
                        P.op('act', lambda e: e.activation(out=rh[x_], in_=banks[MISC][:, :], func=AF.Ln, scale=1.0, bias=eps64[:, 0:1]),
                             reads=[bank_tk[MISC], k_tk], writes=[rh_tk[x_]])
                    else:
                        P.op('act', lambda e: e.activation(out=rh[x_], in_=banks[MISC][:, :], func=AF.Ln, scale=1.0 / 64, bias=epsc[:, 0:1]),
                             reads=[bank_tk[MISC], k_tk], writes=[rh_tk[x_]])
                    P.op('act', lambda e: e.activation(out=rh[x_], in_=rh[x_], func=AF.Exp, scale=-0.5), reads=[rh_tk[x_]], writes=[rh_tk[x_]], strict=True)
                    gcol = qg if which == 0 else kg
                    P.op('dve', lambda e: e.scalar_tensor_tensor(out=qk_st[which][:, pr, :], in0=banks[b][:, :], scalar=gcol[:, 0:1], in1=rh[x_],
                                                                 op0=ALU.mult, op1=ALU.mult),
                         reads=[bank_tk[b], rh_tk[x_], cst_tk], writes=[qk_tk[which]])
                    if pr == NPR - 1:
                        dv = (qa_d if which == 0 else ka_d)[:, 0:64, cols].rearrange("(hp two) d t -> two d hp t", two=2)
                        for two in range(2):
                            P.dma('pool', dv[two], qk_st[which][two * 64:(two + 1) * 64], qk_sem[which], reads=[qk_tk[which]],
                                  writes=[qa_tk if which == 0 else ka_tk])

                for ti in range(len(tasks) + 1):
                    if ti < len(tasks):
                        t_front(ti)
                    if ti >= 1:
                        t_back(ti - 1)
                slab, stk = load_ws(fw, jl, [(2 * HW, HW, 0)])
                for jb in range(4):
                    b = next_rot()
                    for kc in range(KC):
                        P.op('pe', lambda e, kc=kc, jb=jb, b=b, slab=slab, hn=hn: e.matmul(banks[b][:, 0:HW], lhsT=hn[:, kc, jb * 128:(jb + 1) * 128], rhs=slab[:, kc, 0:HW],
                                                                                  start=(kc == 0), stop=(kc == KC - 1)),
                             reads=[stk, hn_tk], writes=[bank_tk[b]])
                    evac_copy(vst[:, jb, :, 0:64], banks[b][:, 0:HW].rearrange("p (h d) -> p h d", h=FH), [bank_tk[b]], [vst_tk])
                P.dma('pool', v_d[cols, :].rearrange("(j p) e -> p j e", p=128), vst.rearrange("p j h e -> p j (h e)"), v_sem, reads=[vst_tk], writes=[vd_tk])
                slab, stk = load_ws(fw, jl, [(3 * HW, HW, 0)])
                if c + 1 < NCH:
                    rmsnorm_chunk(1 - buf, 'mixg%d' % li, 1 - buf)
                for u in range(HW // 128):
                    b = next_rot()
                    proj_T(slab, stk, u * 128, 128, lambda kc: hn[:, kc, :], b, extra_reads=[hn_tk])
                    P.op('act', lambda e, b=b, u=u: e.activation(out=sgst[:, u, :], in_=banks[b][:, :], func=AF.Sigmoid),
                         reads=[bank_tk[b]], writes=[sgst_tk])
                P.dma('pool', sg_d[:, cols].rearrange("(k p) t -> p k t", p=128), sgst, sg_sem, reads=[sgst_tk], writes=[sgd_tk])
            AR.release(m)

            P.barrier()
            m = AR.mark()
            NB = seq // 128
            NQH = max(1, seq // 2048)
            QW = seq // NQH
            NBK = QW // 512
            Ka = [AR.alloc([seq], BF16) for _ in range(2)]
            Qa = [AR.alloc([seq], BF16) for _ in range(2)]
            Vh = [AR.alloc([NB, 65], BF16) for _ in range(2)]
            kqv_tk = [Tk("kqv0"), Tk("kqv1")]
            kqv_sem = [P.dsem("kqv0"), P.dsem("kqv1")]
            PT = [AR.alloc([512], BF16) for _ in range(3)]
            pt_tk = [Tk("pt%d" % i) for i in range(3)]
            pt_i = 0
            rrow = AR.alloc([512], F32)
            rrow_tk = Tk("rrow")
            bcs = AR.alloc([512], F32)
            bcs_tk = Tk("bcs")
            Ost = [AR.alloc([QW], BF16) for _ in range(2)]
            ost_tk = [Tk("ost0"), Tk("ost1")]
            ost_sem = [P.dsem("ost0"), P.dsem("ost1")]
            oi = 0
            NHL = FH

            def load_head(h, s):
                P.dma('sp', Ka[s][0:70], ka_d[h, :, :], kqv_sem[s], reads=[ka_tk, ka1_tk, ka2_tk], writes=[kqv_tk[s]])
                P.dma('sp', Qa[s][0:70], qa_d[h, :, :], kqv_sem[s], reads=[qa_tk, qa1_tk, qa2_tk], writes=[kqv_tk[s]])
                P.dma('sp', Vh[s], v_d[:, h * 65:(h + 1) * 65].rearrange("(kb p) e -> p kb e", p=128), kqv_sem[s], reads=[vd_tk], writes=[kqv_tk[s]])

            items = []
            for h in range(NHL):
                items.append(('load', h))
                for qh in range(NQH):
                    qb0 = qh * (QW // 128)
                    nqb = QW // 128
                    last_kb = qb0 + nqb - 1
                    for kb in range(last_kb + 1):
                        for a in range(NBK):
                            blk0 = qb0 + 4 * a
                            lo = max(kb, blk0)
                            hi = blk0 + 4
                            if lo >= hi:
                                continue
                            items.append(('step', h, kb, a, blk0, lo, (hi - lo) * 128))
                    items.append(('norm', h, qh))
            LOOK = 2
            st_bank = {}
            load_head(0, 0)

            def front(it):
                if it[0] != 'step':
                    return
                _, h, kb, a, blk0, lo, ncol = it
                s_ = h % 2
                diag = (lo == kb)
                b = next_rot()
                st_bank[id(it)] = b
                P.op('pe', lambda e: e.matmul(banks[b][:, 0:ncol], lhsT=Ka[s_][0:70, kb * 128:(kb + 1) * 128], rhs=Qa[s_][0:70, lo * 128:lo * 128 + ncol],
                                              start=True, stop=(not diag)), reads=[kqv_tk[s_]], writes=[bank_tk[b]])
                if diag:
                    P.op('pe', lambda e: e.matmul(banks[b][:, 0:128], lhsT=ident_b, rhs=mask_b[:, 0, :], start=False, stop=True),
                         reads=[k_tk], writes=[bank_tk[b]])

            def back(it):
                nonlocal pt_i, oi
                if it[0] == 'load':
                    if it[1] + 1 < NHL:
                        load_head(it[1] + 1, (it[1] + 1) % 2)
                    return
                if it[0] == 'step':
                    _, h, kb, a, blk0, lo, ncol = it
                    s_ = h % 2
                    b = st_bank.pop(id(it))
                    pi = pt_i % 3
                    pt_i += 1
                    P.op('act', lambda e: e.activation(out=PT[pi][:, 0:ncol], in_=banks[b][:, 0:ncol], func=AF.Exp),
                         reads=[bank_tk[b]], writes=[pt_tk[pi]])
                    c0 = (lo - blk0) * 128
                    lastk = (kb == blk0 + 3)
                    P.op('pe', lambda e: e.matmul(banks[ACC[a]][0:65, c0:c0 + ncol], lhsT=Vh[s_][:, kb, :], rhs=PT[pi][:, 0:ncol],
                                                  start=(kb == 0), stop=lastk, skip_group_check=True),
                         reads=[kqv_tk[s_], pt_tk[pi]], writes=[bank_tk[ACC[a]]])
                elif it[0] == 'norm':
                    _, h, qh = it
                    os_ = oi % 2
                    oi += 1
                    for a in range(NBK):
                        P.op('dve', lambda e, a=a: e.reciprocal(out=rrow[64:65, :], in_=banks[ACC[a]][64:65, :]), reads=[bank_tk[ACC[a]]], writes=[rrow_tk])
                        P.op('pe', lambda e: e.matmul(banks[MISC][0:64, :], lhsT=ones_f[64:65, 0:64], rhs=rrow[64:65, :], start=True, stop=True),
                             reads=[rrow_tk, k_tk], writes=[bank_tk[MISC]])
                        P.op('act', lambda e: e.copy(out=bcs[0:64], in_=banks[MISC][0:64, :]), reads=[bank_tk[MISC]], writes=[bcs_tk])
                        P.op('dve', lambda e, a=a, os_=os_: e.tensor_tensor(out=Ost[os_][0:64, a * 512:(a + 1) * 512], in0=banks[ACC[a]][0:64, :], in1=bcs[0:64], op=ALU.mult),
                             reads=[bank_tk[ACC[a]], bcs_tk], writes=[ost_tk[os_]])
                    P.dma('pool', ot_d[h * 64:(h + 1) * 64, qh * QW:(qh + 1) * QW], Ost[os_][0:64], ost_sem[os_], reads=[ost_tk[os_]], writes=[otd_tk])

            for i in range(len(items) + LOOK):
                if i < len(items):
                    front(items[i])
                if i - LOOK >= 0:
                    back(items[i - LOOK])
            AR.release(m)

            P.barrier()
            m = AR.mark()
            NFC = HW // 128
            oc = [AR.alloc([NFC, TT], BF16) for _ in range(2)]
            gc = [AR.alloc([NFC, TT], BF16) for _ in range(2)]
            og_tk = [Tk("og0"), Tk("og1")]
            og_sem = [P.dsem("og0"), P.dsem("og1")]
            yT = AR.alloc([NFC, TT], BF16)
            yT_tk = Tk("yT")

            def load_og(c, s):
                cols = slice(c * TT, (c + 1) * TT)
                P.dma('sp', oc[s], ot_d[:, cols].rearrange("(k p) t -> p k t", p=128), og_sem[s], reads=[otd_tk], writes=[og_tk[s]])
                P.dma('sp', gc[s], sg_d[:, cols].rearrange("(k p) t -> p k t", p=128), og_sem[s], reads=[sgd_tk], writes=[og_tk[s]])

            load_og(0, 0)
            for c in range(NCH):
                buf = c % 2
                if c + 1 < NCH:
                    load_og(c + 1, 1 - buf)
                for kc in range(NFC):
                    eng = 'pool' if kc % 2 else 'dve'
                    P.op(eng, lambda e, kc=kc, buf=buf: e.tensor_tensor(out=yT[:, kc, :], in0=oc[buf][:, kc, :], in1=gc[buf][:, kc, :], op=ALU.mult),
                         reads=[og_tk[buf]], writes=[yT_tk])
                dense_out('fox_w_out', jl, NFC, lambda fc: yT[:, fc, :], [yT_tk], c, par, alt=True)
            flush_finish()
            pend[0] = par
            AR.release(m)

        def phase_ssd(jl, li):
            P.barrier()
            m = AR.mark()
            pp = pend[0]
            par = sub_i[0] % 2
            sub_i[0] += 1
            sw = 'ssd_w_in'
            XW = SH * 64
            NHG = SH // 4
            zs = AR.alloc([4, XW], BF16)
            zs_tk = [Tk("zs%d" % j) for j in range(4)]
            xbc = AR.alloc([SNC, TT], BF16)
            xbc_tk = [Tk("xbc%d" % f) for f in range(SNC)]
            BOF, COF = SXC, SXC + SG
            pcs = [AR.alloc([TT + 4], F32) for _ in range(2)]
            pc_tks = [Tk("pc0"), Tk("pc1")]
            accs = [AR.alloc([TT], F32) for _ in range(2)]
            acc_tks = [Tk("acc0"), Tk("acc1")]
            tails = AR.alloc([SNC, 3], F32)
            tails_tk = [Tk("tl%d" % f) for f in range(SNC)]
            wdt = AR.alloc([KC, SH], BF16)
            wdt_tk = Tk("wdt")
            wdt_sem = P.dsem("wdt")
            dtT = AR.alloc([TT], F32)
            dtT_tk = Tk("dtT")
            arow = AR.alloc([SH], F32)
            arow_tk = Tk("arow")
            dtk = AR.alloc([4, SH], F32)
            dak = AR.alloc([4, SH], F32)
            dtk_tk = Tk("dtk")
            NSET = 2
            hst = AR.alloc([XW], F32)
            hst_tk = Tk("hst")
            hpbs = [AR.alloc([XW], BF16) for _ in range(3)]
            hpb_tks = [Tk("hpb0"), Tk("hpb1"), Tk("hpb2")]
            junk = AR.alloc([512], BF16)
            SETS = []
            for si in range(NSET):
                d = {}
                d['sm'] = AR.alloc([4, SH], F32); d['sm_tk'] = Tk("sm%d" % si)
                d['dcy'] = AR.alloc([SH, 128], BF16); d['dcy_tk'] = Tk("dcy%d" % si)
                d['cb'] = AR.alloc([SG, 128], BF16); d['cb_tk'] = Tk("cb%d" % si)
                d['xdt'] = AR.alloc([XW], BF16); d['xdt_tk'] = Tk("xdt%d" % si)
                d['xsk'] = AR.alloc([XW], BF16); d['xsk_tk'] = Tk("xsk%d" % si)
                d['Btk'] = AR.alloc([SG, 128], BF16); d['Btk_tk'] = Tk("Btk%d" % si)
                d['yy'] = AR.alloc([XW], F32); d['yy_tk'] = Tk("yy%d" % si)
                d['Dm'] = d['yy'].bitcast(BF16)[:, 0:2 * XW].rearrange("p (a b) -> p a b", a=SH); d['Dm_tk'] = d['yy_tk']
                d['ssq'] = AR.alloc([SG], F32); d['ssq_tk'] = Tk("ssq%d" % si)
                d['yn'] = AR.alloc([XW], BF16); d['yn_tk'] = Tk("yn%d" % si)
                d['xdw'] = d['yn']; d['xdw_tk'] = d['yn_tk']
                SETS.append(d)
            sng = AR.alloc([XW], F32)
            sng_tk = Tk("sng")
            sng_sem = P.dsem("sng")
            P.dma('sp', sng, sng_d[jl, :, :], sng_sem, writes=[sng_tk])
            dskr = C('dsk%d' % jl)
            scw = C('scw%d' % jl)
            scb = C('scb%d' % jl)
            dtb = C('dtb%d' % jl, SH)
            P.op('pool', lambda e: e.memset(tails, 0.0), writes=tails_tk)
            P.op('pool', lambda e: e.memset(hst, 0.0), writes=[hst_tk])
            P.op('pool', lambda e: e.memset(hpbs[0], 0.0), writes=[hpb_tks[0]])
            P.op('act', lambda e: e.activation(out=arow, in_=C('alog%d' % jl), func=AF.Exp), reads=[cst_tk], writes=[arow_tk])
            P.op('dve', lambda e: e.tensor_scalar(out=arow, in0=arow, scalar1=-1.0, scalar2=None, op0=ALU.mult), reads=[arow_tk], writes=[arow_tk])
            DTC = 2 * XW + 2 * SG * 128
            P.dma('sp', wdt, wb_d[sw][jl, :, DTC:DTC + SH].rearrange("(k p) n -> p k n", p=128), wdt_sem, reads=[wtk[(sw, jl)]], writes=[wdt_tk])
            load_hc(0, 0, pp)
            rmsnorm_chunk(0, 'mixg%d' % li, 0)
            for c in range(NCH):
                buf = c % 2
                hn, hn_tk = hnb[buf], hnb_tk[buf]
                if c + 1 < NCH:
                    load_hc(c + 1, 1 - buf, pp)
                proj_T(wdt, wdt_tk, 0, SH, lambda kc: hn[:, kc, :], MISC, extra_reads=[hn_tk])
                P.op('act', lambda e: e.activation(out=dtT[0:SH], in_=banks[MISC][0:SH, :], func=AF.Exp, bias=dtb[:, 0:1]),
                     reads=[bank_tk[MISC], cst_tk], writes=[dtT_tk])
                P.op('act', lambda e: e.activation(out=dtT[0:SH], in_=dtT[0:SH], func=AF.Ln, bias=onec[0:SH, 0:1]), reads=[dtT_tk, k_tk], writes=[dtT_tk], strict=True)
                for jb in range(4):
                    P.op('pe', lambda e, jb=jb: e.transpose(banks[MISC][:, jb * SH:(jb + 1) * SH], dtT[0:SH, jb * 128:(jb + 1) * 128], ident_f[0:SH, 0:SH]),
                         reads=[dtT_tk, cst_tk], writes=[bank_tk[MISC]])
                P.op('dve', lambda e: e.tensor_copy(out=dtk, in_=banks[MISC][:, 0:4 * SH].rearrange("p (j h) -> p j h", j=4)), reads=[bank_tk[MISC]], writes=[dtk_tk])
                for jb in range(4):
                    P.op('dve', lambda e, jb=jb: e.tensor_tensor(out=dak[:, jb, :], in0=dtk[:, jb, :], in1=arow, op=ALU.mult), reads=[dtk_tk, arow_tk], writes=[dtk_tk], strict=True)
                for sl in range(XW // 512):
                    slab, stk = load_ws(sw, jl, [(sl * 512, 512, 0)])
                    for jb in range(4):
                        b = next_rot()
                        for kc in range(KC):
                            P.op('pe', lambda e, kc=kc, jb=jb, b=b, slab=slab, hn=hn: e.matmul(banks[b][:, :], lhsT=hn[:, kc, jb * 128:(jb + 1) * 128], rhs=slab[:, kc, :],
                                                                                      start=(kc == 0), stop=(kc == KC - 1)),
                                 reads=[stk, hn_tk], writes=[bank_tk[b]])
                        P.op('act', lambda e, b=b, jb=jb, sl=sl: e.activation(out=zs[:, jb, sl * 512:(sl + 1) * 512], in_=banks[b][:, :], func=AF.Silu),
                             reads=[bank_tk[b]], writes=[zs_tk[jb]])
                flush_finish()
                for sl in range(SNC // 4):
                    slab, stk = load_ws(sw, jl, [(XW + sl * 512, 512, 0)])
                    for u in range(4):
                        f = sl * 4 + u
                        pi = f % 2
                        b = next_rot()
                        proj_T(slab, stk, u * 128, 128, lambda kc: hn[:, kc, :], b, extra_reads=[hn_tk])
                        conv_silu(b, 128, 4, lambda k, f=f: scw[:, f * 4 + k:f * 4 + k + 1], scb[:, f:f + 1], pcs[pi], pc_tks[pi], accs[pi], acc_tks[pi],
                                  tails[:, f, :], tails_tk[f], xbc[:, f, :], [], [xbc_tk[f]])
                def prep_stages(jb):
                    S = SETS[jb % NSET]
                    sm, sm_tk, Dm, Dm_tk, dcy, dcy_tk, cb, cb_tk = S['sm'], S['sm_tk'], S['Dm'], S['Dm_tk'], S['dcy'], S['dcy_tk'], S['cb'], S['cb_tk']
                    xdt, xdt_tk, xsk, xsk_tk, xdw, xdw_tk, Btk, Btk_tk = S['xdt'], S['xdt_tk'], S['xsk'], S['xsk_tk'], S['xdw'], S['xdw_tk'], S['Btk'], S['Btk_tk']
                    Mt, Mt_tk = dcy, dcy_tk
                    bc = slice(jb * 128, (jb + 1) * 128)
                    mo = (jb % NSET) * 4 * SH
                    st = []

                    def s_acs():
                        P.op('pe', lambda e: e.matmul(banks[MISC][:, mo:mo + SH], lhsT=C('U'), rhs=dak[:, jb, :], start=True, stop=True),
                             reads=[dtk_tk, cst_tk], writes=[bank_tk[MISC]])
                        P.op('pe', lambda e: e.matmul(banks[MISC][:, mo + SH:mo + 2 * SH], lhsT=C('Ubar'), rhs=dak[:, jb, :], start=True, stop=True),
                             reads=[dtk_tk, cst_tk], writes=[bank_tk[MISC]])
                        P.op('pe', lambda e: e.matmul(banks[MISC][:, mo + 2 * SH:mo + 3 * SH], lhsT=ones_f, rhs=dak[:, jb, :], start=True, stop=True),
                             reads=[dtk_tk, k_tk], writes=[bank_tk[MISC]])
                        P.op('act', lambda e: e.activation(out=sm[:, 0:3, :].rearrange("p a b -> p (a b)"), in_=banks[MISC][:, mo:mo + 3 * SH], func=AF.Exp),
                             reads=[bank_tk[MISC]], writes=[sm_tk])
                    st.append(s_acs)

                    def s_dm():
                        P.op('dve', lambda e: e.tensor_tensor(out=Dm, in0=U_b.unsqueeze(1).to_broadcast([128, SH, 128]),
                                                              in1=dak[:, jb, :].unsqueeze(2).to_broadcast([128, SH, 128]), op=ALU.mult),
                             reads=[dtk_tk, k_tk], writes=[Dm_tk])
                    st.append(s_dm)

                    def s_xT():
                        b = next_rot()
                        bkb = banks[b][:, :].bitcast(BF16)
                        for q in range(SXC):
                            P.op('pe', lambda e, q=q: e.transpose(bkb[:, q * 128:(q + 1) * 128], xbc[:, q, bc], ident_b),
                                 reads=[xbc_tk[q], k_tk], writes=[bank_tk[b]])
                        P.op('dve', lambda e: e.tensor_tensor(
                            out=xdt.rearrange("p (h d) -> p h d", h=SH), in0=bkb[:, 0:XW].rearrange("p (h d) -> p h d", h=SH),
                            in1=dtk[:, jb, :].unsqueeze(2).to_broadcast([128, SH, 64]), op=ALU.mult),
                             reads=[bank_tk[b], dtk_tk], writes=[xdt_tk])
                        P.op('dve', lambda e: e.tensor_tensor(
                            out=xsk.rearrange("p (h d) -> p h d", h=SH), in0=bkb[:, 0:XW].rearrange("p (h d) -> p h d", h=SH),
                            in1=dskr.unsqueeze(2).to_broadcast([128, SH, 64]), op=ALU.mult),
                             reads=[bank_tk[b], cst_tk], writes=[xsk_tk])
                    st.append(s_xT)

                    def mk_E(hg):
                        def s_E():
                            b = next_rot()
                            P.op('pe', lambda e: e.matmul(banks[b][:, :], lhsT=ones_b, rhs=Dm[:, hg * 4:(hg + 1) * 4, :].rearrange("p a b -> p (a b)"),
                                                          start=True, stop=False), reads=[Dm_tk, k_tk], writes=[bank_tk[b]])
                            for hh in range(4):
                                P.op('pe', lambda e, hh=hh: e.matmul(banks[b][:, hh * 128:(hh + 1) * 128], lhsT=Dm[:, hg * 4 + hh, :], rhs=nones_b,
                                                                     start=False, stop=False, skip_group_check=True), reads=[Dm_tk, k_tk], writes=[bank_tk[b]])
                            P.op('pe', lambda e: e.matmul(banks[b][:, :], lhsT=ident_b, rhs=mask_b.rearrange("p a b -> p (a b)"), start=False, stop=True, skip_group_check=True),
                                 reads=[k_tk], writes=[bank_tk[b]])
                            P.op('act', lambda e: e.activation(out=dcy[:, hg * 4:(hg + 1) * 4, :].rearrange("p a b -> p (a b)"), in_=banks[b][:, :], func=AF.Exp),
                                 reads=[bank_tk[b]], writes=[dcy_tk])
                        return s_E
                    for hg in range(NHG):
                        st.append(mk_E(hg))

                    def s_cb():
                        b = next_rot()
                        for g in range(SG):
                            P.op('pe', lambda e, g=g: e.matmul(banks[b][:, g * 128:(g + 1) * 128], lhsT=xbc[:, BOF + g, bc], rhs=xbc[:, COF + g, bc], start=True, stop=True),
                                 reads=[xbc_tk[BOF + g], xbc_tk[COF + g]], writes=[bank_tk[b]])
                        evac_copy(cb, banks[b][:, 0:SG * 128].rearrange("p (g l) -> p g l", g=SG), [bank_tk[b]], [cb_tk])
                        b2 = next_rot()
                        bkb2 = banks[b2][:, :].bitcast(BF16)
                        for g in range(SG):
                            P.op('pe', lambda e, g=g: e.transpose(bkb2[:, g * 128:(g + 1) * 128], xbc[:, BOF + g, bc], ident_b),
                                 reads=[xbc_tk[BOF + g], k_tk], writes=[bank_tk[b2]])
                        evac_copy(Btk, bkb2[:, 0:SG * 128].rearrange("p (g n) -> p g n", g=SG), [bank_tk[b2]], [Btk_tk])
                    st.append(s_cb)

                    def s_mt():
                        P.op('dve', lambda e: e.tensor_tensor(out=Mt.rearrange("p (g r) l -> p g r l", g=SG), in0=dcy.rearrange("p (g r) l -> p g r l", g=SG),
                                                              in1=cb.unsqueeze(2).to_broadcast([128, SG, 8, 128]), op=ALU.mult),
                             reads=[dcy_tk, cb_tk], writes=[Mt_tk])
                    st.append(s_mt)
                    return st

                def back_stages(jb):
                    S = SETS[jb % NSET]
                    sm, sm_tk, dcy, dcy_tk = S['sm'], S['sm_tk'], S['dcy'], S['dcy_tk']
                    xdt, xdt_tk, xsk, xsk_tk, xdw, xdw_tk, Btk, Btk_tk = S['xdt'], S['xdt_tk'], S['xsk'], S['xsk_tk'], S['xdw'], S['xdw_tk'], S['Btk'], S['Btk_tk']
                    yy, yy_tk, ssq, ssq_tk, yn, yn_tk = S['yy'], S['yy_tk'], S['ssq'], S['ssq_tk'], S['yn'], S['yn_tk']
                    Mt, Mt_tk = dcy, dcy_tk
                    gblk = c * 4 + jb
                    hin, hin_tk = hpbs[gblk % 3], hpb_tks[gblk % 3]
                    hout, hout_tk = hpbs[(gblk + 1) % 3], hpb_tks[(gblk + 1) % 3]
                    A0 = (jb % NSET) * SG
                    bc = slice(jb * 128, (jb + 1) * 128)
                    st = []

                    def s_yoff():
                        for g in range(SG):
                            P.op('pe', lambda e, g=g: e.matmul(banks[ACC[A0 + g]][:, :], lhsT=xbc[:, COF + g, bc], rhs=hin[:, g * 512:(g + 1) * 512], start=True, stop=True),
                                 reads=[xbc_tk[COF + g], hin_tk], writes=[bank_tk[ACC[A0 + g]]])
                            P.op('dve', lambda e, g=g: e.tensor_tensor(out=yy[:, g * 512:(g + 1) * 512].rearrange("p (h d) -> p h d", h=8),
                                                                       in0=banks[ACC[A0 + g]][:, :].rearrange("p (h d) -> p h d", h=8),
                                                                       in1=sm[:, 0, g * 8:(g + 1) * 8].unsqueeze(2).to_broadcast([128, 8, 64]), op=ALU.mult),
                                 reads=[bank_tk[ACC[A0 + g]], sm_tk], writes=[yy_tk])
                            P.op('pool', lambda e, g=g: e.tensor_tensor(out=yy[:, g * 512:(g + 1) * 512], in0=yy[:, g * 512:(g + 1) * 512], in1=xsk[:, g * 512:(g + 1) * 512], op=ALU.add),
                                 reads=[yy_tk, xsk_tk], writes=[yy_tk])
                        P.op('pool', lambda e: e.memset(ssq, 0.0), writes=[ssq_tk])
                    st.append(s_yoff)

                    def s_ydiag():
                        for h in range(SH):
                            g = h // 8
                            P.op('pe', lambda e, h=h, g=g: e.matmul(banks[ACC[A0 + g]][:, (h % 8) * 64:(h % 8 + 1) * 64], lhsT=Mt[:, h, :], rhs=xdt[:, h * 64:(h + 1) * 64],
                                                                    start=True, stop=True, skip_group_check=True),
                                 reads=[Mt_tk, xdt_tk], writes=[bank_tk[ACC[A0 + g]]])
                    st.append(s_ydiag)

                    def s_state():
                        P.op('pool', lambda e: e.tensor_tensor(out=xdw.rearrange("p (h d) -> p h d", h=SH), in0=xdt.rearrange("p (h d) -> p h d", h=SH),
                                                               in1=sm[:, 1, :].unsqueeze(2).to_broadcast([128, SH, 64]), op=ALU.mult),
                             reads=[xdt_tk, sm_tk], writes=[xdw_tk])
                        for g in range(SG):
                            b = next_rot()
                            gs_ = slice(g * 512, (g + 1) * 512)
                            P.op('pe', lambda e, b=b, g=g, gs_=gs_: e.matmul(banks[b][:, :], lhsT=Btk[:, g, :], rhs=xdw[:, gs_], start=True, stop=True),
                                 reads=[Btk_tk, xdw_tk], writes=[bank_tk[b]])
                            P.op('pool', lambda e, g=g, gs_=gs_: e.tensor_tensor(out=hst[:, gs_].rearrange("p (h d) -> p h d", h=8), in0=hst[:, gs_].rearrange("p (h d) -> p h d", h=8),
                                                                                in1=sm[:, 2, g * 8:(g + 1) * 8].unsqueeze(2).to_broadcast([128, 8, 64]), op=ALU.mult),
                                 reads=[hst_tk, sm_tk], writes=[hst_tk])
                            P.op('dve', lambda e, b=b, gs_=gs_: e.tensor_tensor(out=hst[:, gs_], in0=hst[:, gs_], in1=banks[b][:, :], op=ALU.add),
                                 reads=[bank_tk[b], hst_tk], writes=[hst_tk])
                            P.op('act', lambda e, gs_=gs_: e.copy(out=hout[:, gs_], in_=hst[:, gs_]), reads=[hst_tk], writes=[hout_tk])
                    st.insert(0, s_state)

                    def s_comb():
                        for g in range(SG):
                            gs_ = slice(g * 512, (g + 1) * 512)
                            P.op('dve', lambda e, g=g, gs_=gs_: e.tensor_tensor(out=yy[:, gs_], in0=yy[:, gs_], in1=banks[ACC[A0 + g]][:, :], op=ALU.add),
                                 reads=[bank_tk[ACC[A0 + g]], yy_tk], writes=[yy_tk])
                            P.op('dve', lambda e, gs_=gs_: e.tensor_tensor(out=yy[:, gs_], in0=yy[:, gs_], in1=zs[:, jb, gs_], op=ALU.mult),
                                 reads=[yy_tk, zs_tk[jb]], writes=[yy_tk])
                            P.op('act', lambda e, g=g, gs_=gs_: e.activation(out=junk, in_=yy[:, gs_], func=AF.Square, accum_out=ssq[:, g:g + 1]),
                                 reads=[yy_tk, ssq_tk], writes=[ssq_tk])
                    st.append(s_comb)

                    def s_norm():
                        P.op('act', lambda e: e.activation(out=ssq, in_=ssq, func=AF.Ln, scale=1.0 / 512, bias=epsc[:, 0:1]), reads=[ssq_tk, k_tk], writes=[ssq_tk])
                        P.op('act', lambda e: e.activation(out=ssq, in_=ssq, func=AF.Exp, scale=-0.5), reads=[ssq_tk], writes=[ssq_tk], strict=True)
                        for g in range(SG):
                            gs_ = slice(g * 512, (g + 1) * 512)
                            P.op('dve', lambda e, g=g, gs_=gs_: e.scalar_tensor_tensor(out=yn[:, gs_], in0=yy[:, gs_], scalar=ssq[:, g:g + 1], in1=sng[:, gs_],
                                                                                       op0=ALU.mult, op1=ALU.mult),
                                 reads=[yy_tk, ssq_tk, sng_tk], writes=[yn_tk], strict=True)
                    st.append(s_norm)

                    def s_ynT():
                        b = next_rot()
                        bkb3 = banks[b][:, :].bitcast(BF16)
                        for q in range(SXC):
                            P.op('pe', lambda e, q=q: e.transpose(bkb3[:, q * 128:(q + 1) * 128], yn[:, q * 128:(q + 1) * 128], ident_b),
                                 reads=[yn_tk, k_tk], writes=[bank_tk[b]])
                        evac_copy(xbc[:, 0:SXC, bc], bkb3[:, 0:XW].rearrange("p (f t) -> p f t", f=SXC), [bank_tk[b]], xbc_tk[0:SXC])
                    st.append(s_ynT)
                    return st

                def interleave(lists):
                    n = max(len(l) for l in lists)
                    for k in range(n):
                        for l in lists:
                            if k < len(l):
                                l[k]()

                for pr in range(2):
                    jbs = [2 * pr, 2 * pr + 1]
                    interleave([prep_stages(jb) for jb in jbs])
                    if pr == 1 and c + 1 < NCH:
                        rmsnorm_chunk(1 - buf, 'mixg%d' % li, 1 - buf)
                    interleave([back_stages(jb) for jb in jbs])
                dense_out('ssd_w_out', jl, SXC, lambda fc: xbc[:, fc, :], xbc_tk[0:SXC], c, par)
            flush_finish()
            pend[0] = par
            AR.release(m)

        phase_load()
        for li_, L in enumerate(layers):
            kind, j = L[:3], int(L[3:])
            if li_ >= 1:
                issue_conv(li_ + 1)
            if kind == 'ffn':
                phase_ffn(j)
            elif kind == 'fox':
                phase_fox(j, 2 * j + 1)
            elif kind == 'ssd':
                phase_ssd(j, 2 * j)
        phase_final()
        P.check_deadlock()
        block = es.enter_context(nc.Block())
        P.materialize(block)
        build_program.stats = {e: len(v) for e, v in P.ops.items()}
        build_program.stats['arena_peak'] = AR.peak
        build_program.stats['nsem'] = P.nds
    return nc


FULL_LAYERS = ['ssd0', 'ffn0', 'fox0', 'ffn1', 'ssd1', 'ffn2', 'fox1', 'ffn3']
_cache = {}


def kernel(**inputs):
    inp = {k: np.asarray(v) for k, v in inputs.items()}
    x = inp['x']
    B, S, D = x.shape
    groups = [[2 * b, 2 * b + 1] for b in range(4)]
    if 'nc' not in _cache:
        _cache['nc'] = build_program(S, FULL_LAYERS, groups)
    nc = _cache['nc']
    per_r = []
    for r in range(TP):
        cl, consts = build_consts(inp, r)
        w = slice_weights(inp, r)
        w['consts'] = consts
        per_r.append(w)
    in_maps = []
    for core in range(8):
        b, r = core // 2, core % 2
        m = dict(per_r[r])
        m['x'] = np.ascontiguousarray(x[b].T)
        in_maps.append(m)
    res = run_bass_kernel_spmd(nc, in_maps, core_ids=list(range(8)))
    out = np.stack([np.ascontiguousarray(np.asarray(res.results[2 * b]['out']).T) for b in range(B)], axis=0).astype(np.float32)
    return out
```

```python
import numpy as np
from contextlib import ExitStack
import concourse.bass as bass
import concourse.mybir as mybir
from concourse.bass_utils import run_bass_kernel_spmd

F32 = mybir.dt.float32
BF16 = mybir.dt.bfloat16
ALU = mybir.AluOpType
AF = mybir.ActivationFunctionType

D_MODEL = 1024
DEPTH = 4
EPS = 1e-6
SSD_D_INNER = 2048
SSD_HEADS = 32
SSD_CONV = 4
SSD_CONV_DIM = 3072
SSD_IN_DIM = 5152
FOX_HEADS = 16
FOX_D = 1024
FOX_IN_DIM = 4112
D_FF = 2816
FFN_CONV = 3
TT = 512
NEG = -30000.0
KC = 8
TP = 2
FH = FOX_HEADS // TP
FFC = (D_FF // 128) // TP
SH = SSD_HEADS // TP
SG = 4 // TP
SXC = SH * 64 // 128
SNC = SXC + 2 * SG

COMPUTE = ('pe', 'act', 'dve', 'pool')


class Tk:
    __slots__ = ('name', 'w', 'r')

    def __init__(self, name):
        self.name = name
        self.w = None
        self.r = {}


class DSem:
    def __init__(self, handle, name):
        self.h = handle
        self.name = name
        self.count = 0


class Prog:
    def __init__(self, nc, es):
        self.nc = nc
        self.es = es
        self.ops = {e: [] for e in ('pe', 'act', 'dve', 'pool', 'sp')}
        self.esem = {e: es.enter_context(nc.semaphore("s_" + e)) for e in COMPUTE}
        self.nds = 0
        self.dcache = {}
        self.out_deps = []

    def dsem(self, name):
        if name not in self.dcache:
            self.nds += 1
            self.dcache[name] = DSem(self.es.enter_context(self.nc.semaphore("d_%s_%d" % (name, self.nds))), name)
        return self.dcache[name]

    def _deps(self, eng, reads, writes, strict=False):
        deps = set()
        for t in reads:
            if t.w is not None:
                deps.add(t.w)
        for t in writes:
            if t.w is not None:
                deps.add(t.w)
            for v in t.r.values():
                deps.add(v)
        if strict:
            return deps
        return {d for d in deps if not (d[0] == 'E' and d[1] == eng)}

    def op(self, eng, fn, reads=(), writes=(), strict=False):
        deps = self._deps(eng, reads, writes, strict)
        idx = len(self.ops[eng])
        import sys as _s
        self.ops[eng].append({'fn': fn, 'deps': deps, 'inc': False, 'dma': None, 'note': _s._getframe(1).f_lineno})
        me = ('E', eng, idx)
        for t in reads:
            t.r[eng] = me
        for t in writes:
            t.w = me
            t.r = {}
        return me

    def dma(self, q, out, in_, sem, reads=(), writes=()):
        deps = self._deps(q, reads, writes, strict=True)
        sem.count += 16
        me = ('D', sem, sem.count)
        self.ops[q].append({'fn': (lambda e, o=out, i=in_: e.dma_start(out=o, in_=i)), 'deps': deps,
                            'inc': False, 'dma': sem, 'tok': me})
        for t in reads:
            t.r['D' + sem.name + str(id(sem))] = me
        for t in writes:
            t.w = me
            t.r = {}
        return me

    def cc(self, in_ap, out_ap, groups, slot, reads=(), writes=()):
        deps = self._deps('pool', reads, writes, strict=True)
        sem = self.dsem("cc%d" % slot)
        sem.count += 1
        me = ('D', sem, sem.count)

        def fn(e, i=in_ap, o=out_ap):
            return e.collective_compute("AllReduce", ALU.add, replica_groups=groups, ins=[i], outs=[o])
        self.ops['pool'].append({'fn': fn, 'deps': deps, 'inc': False, 'dma': None, 'cc': sem, 'tok': me})
        for t in reads:
            t.r['C' + str(id(sem))] = me
        for t in writes:
            t.w = me
            t.r = {}
        return me

    def dma_acc(self, out, in_, sem, reads=(), writes=()):
        deps = self._deps('pool', reads, writes, strict=True)
        sem.count += 16
        me = ('D', sem, sem.count)
        self.ops['pool'].append({'fn': (lambda e, o=out, i=in_: e.dma_start(out=o, in_=i, accum_op=ALU.add)), 'deps': deps,
                                 'inc': False, 'dma': sem, 'tok': me})
        for t in reads:
            t.r['D' + sem.name + str(id(sem))] = me
        for t in writes:
            t.w = me
            t.r = {}
        return me

    def barrier(self):
        toks = set()
        for e in COMPUTE:
            if self.ops[e]:
                for i in range(len(self.ops[e]) - 1, -1, -1):
                    if self.ops[e][i]['fn'] is not None and self.ops[e][i].get('tok') is None:
                        toks.add(('E', e, i))
                        break
        for sem in self.dcache.values():
            if sem.count > 0:
                toks.add(('D', sem, sem.count))
        for q in self.ops:
            deps = {d for d in toks if not (d[0] == 'E' and d[1] == q)}
            self.ops[q].append({'fn': None, 'deps': deps, 'inc': False, 'dma': None})

    def finalize_wait(self, q, toks):
        self.ops[q].append({'fn': None, 'deps': set(toks), 'inc': False, 'dma': None})

    def check_deadlock(self):
        pos = {e: 0 for e in self.ops}
        done = set()
        dtok = {}
        for e, lst in self.ops.items():
            cnts = {}
            for i, o in enumerate(lst):
                sem = o.get('cc') or o.get('dma')
                if sem is not None:
                    o.setdefault('_tok', None)
        progress = True
        while progress:
            progress = False
            for e, lst in self.ops.items():
                while pos[e] < len(lst):
                    o = lst[pos[e]]
                    ok = True
                    for d in o['deps']:
                        key = (d[0], d[1], d[2]) if d[0] == 'E' else ('D', id(d[1]), d[2])
                        if key not in done:
                            ok = False
                            break
                    if not ok:
                        break
                    done.add(('E', e, pos[e]))
                    if o.get('tok') is not None:
                        t = o['tok']
                        done.add(('D', id(t[1]), t[2]))
                    pos[e] += 1
                    progress = True
        stuck = {e: pos[e] for e in self.ops if pos[e] < len(self.ops[e])}
        if stuck:
            msg = []
            for e, p in stuck.items():
                o = self.ops[e][p]
                missing = []
                for d in o['deps']:
                    key = (d[0], d[1], d[2]) if d[0] == 'E' else ('D', id(d[1]), d[2])
                    if key not in done:
                        missing.append((d[0], d[1] if d[0] == 'E' else d[1].name, d[2]))
                msg.append("%s@%d/%d waits %s [%s]" % (e, p, len(self.ops[e]), missing, o.get('note')))
            raise RuntimeError("DEADLOCK: " + " | ".join(msg))

    def materialize(self, block):
        for e, lst in self.ops.items():
            for o in lst:
                for d in o['deps']:
                    if d[0] == 'E':
                        self.ops[d[1]][d[2]]['inc'] = True
        vals = {}
        for e in COMPUTE:
            c = 0
            for i, o in enumerate(self.ops[e]):
                if o['inc']:
                    c += 1
                    vals[(e, i)] = c
        self.vals = vals

        def run(eng_name, e):
            waited = {}
            for o in self.ops[eng_name]:
                for d in sorted(o['deps'], key=lambda d: (d[0], str(d[1]) if d[0] == 'E' else d[1].name, d[2])):
                    if d[0] == 'E':
                        key = ('E', d[1])
                        v = vals[(d[1], d[2])]
                        sh = self.esem[d[1]]
                    else:
                        key = ('D', id(d[1]))
                        v = d[2]
                        sh = d[1].h
                    if waited.get(key, 0) >= v:
                        continue
                    waited[key] = v
                    e.wait_ge(sh, v)
                if o['fn'] is None:
                    continue
                ins = o['fn'](e)
                if o.get('cc') is not None:
                    ins.then_inc(o['cc'].h)
                elif o['dma'] is not None:
                    ins.then_inc(o['dma'].h, 16)
                elif o['inc']:
                    ins.then_inc(self.esem[eng_name], 1)

        block.tensor(lambda e: run('pe', e))
        block.scalar(lambda e: run('act', e))
        block.vector(lambda e: run('dve', e))
        block.gpsimd(lambda e: run('pool', e))
        block.sync(lambda e: run('sp', e))


class Arena:
    def __init__(self, ap_f32, nwords):
        self.ap = ap_f32
        self.n = nwords
        self.top = 0
        self.peak = 0

    def mark(self):
        return self.top

    def release(self, m):
        self.top = m

    def alloc(self, shape, dtype, parts=128):
        n = int(np.prod(shape))
        words = n if dtype == F32 else (n + 1) // 2
        a = self.top
        self.top += words
        self.peak = max(self.peak, self.top)
        assert self.top <= self.n, "arena overflow %d > %d" % (self.top, self.n)
        v = self.ap[0:parts, a:a + words]
        if dtype != F32:
            v = v.bitcast(dtype)[:, 0:n]
        if len(shape) == 2:
            v = v.rearrange("p (a b) -> p a b", a=shape[0])
        elif len(shape) == 3:
            v = v.rearrange("p (a b c) -> p a b c", a=shape[0], b=shape[1])
        return v


class CL:
    def __init__(self):
        self.off = {}
        self.n = 0

    def add(self, name, width):
        self.off[name] = (self.n, width)
        self.n += width


def const_layout():
    c = CL()
    c.add('ident', 128)
    c.add('U', 128)
    c.add('maskT', 128)
    c.add('bd64', 128)
    c.add('Ubar', 128)
    for i in range(DEPTH):
        c.add('mixg%d' % i, KC)
        c.add('ffng%d' % i, KC)
        c.add('fcw%d' % i, FFC * 3)
        c.add('fcb%d' % i, FFC)
    c.add('fing', KC)
    for j in range(2):
        c.add('scw%d' % j, SNC * 4)
        c.add('scb%d' % j, SNC)
        c.add('dtb%d' % j, 1)
        c.add('alog%d' % j, SH)
        c.add('dsk%d' % j, SH)
        c.add('fbf%d' % j, 1)
        c.add('fqg%d' % j, 1)
        c.add('fkg%d' % j, 1)
    return c


def build_consts(inp, r):
    c = const_layout()
    A = np.zeros((128, c.n), np.float32)

    def put(name, arr):
        o, w = c.off[name]
        arr = np.asarray(arr, np.float32)
        A[:arr.shape[0], o:o + w] = arr.reshape(arr.shape[0], w)

    put('ident', np.eye(128))
    put('U', np.triu(np.ones((128, 128))))
    put('maskT', np.where(np.arange(128)[None, :] >= np.arange(128)[:, None], 0.0, NEG))
    put('bd64', np.kron(np.eye(2), np.ones((64, 64))))
    put('Ubar', np.tril(np.ones((128, 128)), -1))
    fsl = slice(r * FFC * 128, (r + 1) * FFC * 128)
    for i in range(DEPTH):
        put('mixg%d' % i, inp['mix_norm_g'][i].reshape(KC, 128).T)
        put('ffng%d' % i, inp['ffn_norm_g'][i].reshape(KC, 128).T)
        put('fcw%d' % i, inp['ffn_conv_w'][i][:, fsl].reshape(3, FFC, 128).transpose(2, 1, 0).reshape(128, FFC * 3))
        put('fcb%d' % i, inp['ffn_conv_b'][i][fsl].reshape(FFC, 128).T)
    put('fing', inp['final_norm_g'].reshape(KC, 128).T)
    xs_ = np.r_[r * SH * 64:(r + 1) * SH * 64, 2048 + r * SG * 128:2048 + (r + 1) * SG * 128, 2560 + r * SG * 128:2560 + (r + 1) * SG * 128]
    hsl = slice(r * SH, (r + 1) * SH)
    for j in range(2):
        put('scw%d' % j, inp['ssd_conv_w'][j][:, xs_].reshape(4, SNC, 128).transpose(2, 1, 0).reshape(128, SNC * 4))
        put('scb%d' % j, inp['ssd_conv_b'][j][xs_].reshape(SNC, 128).T)
        put('dtb%d' % j, inp['ssd_dt_bias'][j][hsl].reshape(SH, 1))
        put('alog%d' % j, np.broadcast_to(inp['ssd_a_log'][j][hsl][None, :], (128, SH)))
        put('dsk%d' % j, np.broadcast_to(inp['ssd_d'][j][hsl][None, :], (128, SH)))
        put('fbf%d' % j, inp['fox_b_f'][j][r * FH:(r + 1) * FH].reshape(FH, 1))
        put('fqg%d' % j, np.tile(inp['fox_q_norm_g'][j], 2).reshape(128, 1))
        put('fkg%d' % j, np.tile(inp['fox_k_norm_g'][j], 2).reshape(128, 1))
    return c, A


def slice_weights(inp, r):
    w = {}
    f0, f1 = r * FFC * 128, (r + 1) * FFC * 128
    w['ffn_w_up'] = np.ascontiguousarray(np.concatenate([inp['ffn_w_up'][:, :, f0:f1], inp['ffn_w_up'][:, :, D_FF + f0:D_FF + f1]], axis=2))
    w['ffn_w_down'] = np.ascontiguousarray(inp['ffn_w_down'][:, f0:f1, :])
    q0, q1 = r * FH * 64, (r + 1) * FH * 64
    fi = inp['fox_w_in']
    w['fox_w_in'] = np.ascontiguousarray(np.concatenate([fi[:, :, q0:q1], fi[:, :, 1024 + q0:1024 + q1], fi[:, :, 2048 + q0:2048 + q1],
                                                          fi[:, :, 3072 + q0:3072 + q1], fi[:, :, 4096 + r * FH:4096 + (r + 1) * FH]], axis=2))
    w['fox_w_out'] = np.ascontiguousarray(inp['fox_w_out'][:, q0:q1, :])
    x0, x1 = r * SH * 64, (r + 1) * SH * 64
    b0, b1 = r * SG * 128, (r + 1) * SG * 128
    si = inp['ssd_w_in']
    w['ssd_w_in'] = np.ascontiguousarray(np.concatenate([si[:, :, x0:x1], si[:, :, 2048 + x0:2048 + x1], si[:, :, 4096 + b0:4096 + b1],
                                                          si[:, :, 4608 + b0:4608 + b1], si[:, :, 5120 + r * SH:5120 + (r + 1) * SH]], axis=2))
    w['ssd_w_out'] = np.ascontiguousarray(inp['ssd_w_out'][:, x0:x1, :])
    w['sng'] = np.ascontiguousarray(np.broadcast_to(inp['ssd_norm_g'][:, None, x0:x1], (2, 128, SH * 64)), dtype=np.float32)
    return w


WNAMES = ['ssd_w_in', 'ssd_w_out', 'fox_w_in', 'fox_w_out', 'ffn_w_up', 'ffn_w_down']
SSD_LIN = 2 * SH * 64 + 2 * SG * 128 + SH
FOX_LIN = 4 * FH * 64 + FH
WSHAPES = {'ssd_w_in': (2, 1024, SSD_LIN), 'ssd_w_out': (2, SH * 64, 1024), 'fox_w_in': (2, 1024, FOX_LIN),
           'fox_w_out': (2, FH * 64, 1024), 'ffn_w_up': (4, 1024, 2 * FFC * 128), 'ffn_w_down': (4, FFC * 128, 1024)}


def build_program(seq, layers, groups, debug=None):
    NCH = seq // TT
    nc = bass.Bass("TRN2", target_bir_lowering=False)
    cl = const_layout()
    x_d = nc.dram_tensor("x", [D_MODEL, seq], F32, kind="ExternalInput").ap()
    c_d = nc.dram_tensor("consts", [128, cl.n], F32, kind="ExternalInput").ap()
    w_d = {n: nc.dram_tensor(n, list(WSHAPES[n]), F32, kind="ExternalInput").ap() for n in WNAMES}
    sng_d = nc.dram_tensor("sng", [2, 128, SH * 64], F32, kind="ExternalInput").ap()
    out_d = nc.dram_tensor("out", [D_MODEL, seq], F32, kind="ExternalOutput").ap()
    wb_d = {n: nc.dram_tensor(n + "_b", list(WSHAPES[n]), BF16).ap() for n in WNAMES}
    hT_d = nc.dram_tensor("hT_d", [D_MODEL, seq], F32).ap()
    part_t = [[nc.dram_tensor("part%d_%d" % (p, c), [D_MODEL, TT], F32) for c in range(NCH)] for p in range(2)]
    red_t = [[nc.dram_tensor("red%d_%d" % (p, c), [D_MODEL, TT], F32) for c in range(NCH)] for p in range(2)]
    qa_d = nc.dram_tensor("qa_d", [FH, 70, seq], BF16).ap()
    ka_d = nc.dram_tensor("ka_d", [FH, 70, seq], BF16).ap()
    v_d = nc.dram_tensor("v_d", [seq, FH * 65], BF16).ap()
    sg_d = nc.dram_tensor("sg_d", [FH * 64, seq], BF16).ap()
    ot_d = nc.dram_tensor("ot_d", [FH * 64, seq], BF16).ap()
    dbg_d = None
    if debug:
        dbg_d = nc.dram_tensor("dbg", list(debug), F32, kind="ExternalOutput").ap()

    es = ExitStack()
    with es:
        P = Prog(nc, es)
        NW = 48900
        arena_t = es.enter_context(nc.sbuf_tensor("arena", [128, NW], F32))
        AR = Arena(arena_t[:, :], NW)
        banks = [es.enter_context(nc.psum_tensor("bank%d" % i, [128, 512], F32)) for i in range(8)]
        bank_tk = [Tk("bank%d" % i) for i in range(8)]
        ACC = [0, 1, 2, 3]
        ROT = [4, 5, 6]
        MISC = 7
        rot_i = [0]

        def next_rot():
            b = ROT[rot_i[0] % 3]
            rot_i[0] += 1
            return b

        cst = AR.alloc([cl.n], F32)
        cst_tk = Tk("cst")
        s_c = P.dsem("cst")
        P.dma('sp', cst, c_d[:, :], s_c, writes=[cst_tk])

        def C(name, parts=128):
            o, w = cl.off[name]
            return cst[0:parts, o:o + w]

        ident_f = C('ident')
        ident_b = AR.alloc([128], BF16)
        ones_b = AR.alloc([128], BF16)
        nones_b = AR.alloc([128], BF16)
        ones_f = AR.alloc([128], F32)
        U_b = AR.alloc([128], BF16)
        mask_b = AR.alloc([4, 128], BF16)
        k_tk = Tk("konst")
        P.op('dve', lambda e: e.tensor_copy(out=ident_b, in_=ident_f), reads=[cst_tk], writes=[k_tk])
        P.op('dve', lambda e: e.memset(ones_b, 1.0), writes=[k_tk])
        P.op('dve', lambda e: e.memset(nones_b, -1.0), writes=[k_tk])
        P.op('dve', lambda e: e.memset(ones_f, 1.0), writes=[k_tk])
        P.op('dve', lambda e: e.tensor_copy(out=U_b, in_=C('U')), reads=[cst_tk], writes=[k_tk])
        for r in range(4):
            P.op('dve', lambda e, r=r: e.tensor_copy(out=mask_b[:, r, :], in_=C('maskT')), reads=[cst_tk], writes=[k_tk])

        wtk = {}
        used = set()
        for L in layers:
            kind, j = L[:3], int(L[3:])
            if kind == 'ssd':
                used |= {('ssd_w_in', j), ('ssd_w_out', j)}
            elif kind == 'fox':
                used |= {('fox_w_in', j), ('fox_w_out', j)}
            elif kind == 'ffn':
                used |= {('ffn_w_up', j), ('ffn_w_down', j)}
        order = []
        for L in layers:
            kind, j = L[:3], int(L[3:])
            names = {'ssd': ['ssd_w_in', 'ssd_w_out'], 'fox': ['fox_w_in', 'fox_w_out'], 'ffn': ['ffn_w_up', 'ffn_w_down']}[kind]
            for n in names:
                if (n, j) not in order:
                    order.append((n, j))
        for (n, j) in order:
            wtk[(n, j)] = Tk("w_%s_%d" % (n, j))
        conv_done = set()

        def issue_conv(li_, piece=None):
            if li_ >= len(layers):
                return
            L_ = layers[li_]
            kind_, j_ = L_[:3], int(L_[3:])
            for n in {'ssd': ['ssd_w_in', 'ssd_w_out'], 'fox': ['fox_w_in', 'fox_w_out'], 'ffn': ['ffn_w_up', 'ffn_w_down']}[kind_]:
                key_ = (n, j_, piece)
                if key_ in conv_done or (n, j_, None) in conv_done:
                    continue
                conv_done.add(key_)
                t = wtk[(n, j_)]
                sm_ = P.dsem("wc_%s_%d" % (n, j_))
                K = WSHAPES[n][1]
                if piece is None:
                    half = K // 2
                    P.dma('pool', wb_d[n][j_, 0:half, :], w_d[n][j_, 0:half, :], sm_, writes=[t])
                    P.dma('pool', wb_d[n][j_, half:K, :], w_d[n][j_, half:K, :], sm_, writes=[t])
                else:
                    r0 = (K * piece) // NCH
                    r1 = (K * (piece + 1)) // NCH
                    P.dma('pool', wb_d[n][j_, r0:r1, :], w_d[n][j_, r0:r1, :], sm_, writes=[t])

        cur_li = [0]

        def conv_piece(c):
            issue_conv(cur_li[0] + 1, piece=c)

        issue_conv(0)

        hc = [AR.alloc([KC, TT], F32) for _ in range(2)]
        hc_tk = [Tk("hc0"), Tk("hc1")]
        hc_sem = [P.dsem("hc0"), P.dsem("hc1")]
        hnb = [AR.alloc([KC, TT], BF16) for _ in range(2)]
        hnb_tk = [Tk("hn0"), Tk("hn1")]
        sq = AR.alloc([KC, TT], BF16)
        sq_tk = Tk("sq")
        rstd = AR.alloc([TT], F32)
        rstd_tk = Tk("rstd")
        pst = AR.alloc([KC, TT], F32)
        pst_tk = Tk("pst")
        pst_sem = P.dsem("pst")
        NWS = 3
        wslab = [AR.alloc([KC, 512], BF16) for _ in range(NWS)]
        ws_tk = [Tk("ws%d" % i) for i in range(NWS)]
        ws_sem = [P.dsem("ws%d" % i) for i in range(NWS)]
        ws_i = [0]
        NW2 = 2
        w2slab = [AR.alloc([4, 512], BF16) for _ in range(NW2)]
        w2_tk = [Tk("w2%d" % i) for i in range(NW2)]
        w2_sem = [P.dsem("w2%d" % i) for i in range(NW2)]
        w2_i = [0]
        hT_tk = [Tk("hTd%d" % c) for c in range(NCH)]
        part_tk = [[Tk("part%d_%d" % (p, c)) for c in range(NCH)] for p in range(2)]
        red_tk = [[Tk("red%d_%d" % (p, c)) for c in range(NCH)] for p in range(2)]
        st_sem = P.dsem("store")
        evac_i = [0]
        pend = [None]
        sub_i = [0]

        def evac_copy(out, in_, reads, writes):
            evac_i[0] += 1
            if evac_i[0] % 2:
                P.op('act', lambda e: e.copy(out=out, in_=in_), reads=reads, writes=writes)
            else:
                P.op('dve', lambda e: e.tensor_copy(out=out, in_=in_), reads=reads, writes=writes)

        def load_ws(wname, j, cols):
            i = ws_i[0] % NWS
            ws_i[0] += 1
            for (c0, ncol, dst) in cols:
                src = wb_d[wname][j, :, c0:c0 + ncol].rearrange("(k p) n -> p k n", p=128)
                P.dma('sp', wslab[i][:, :, dst:dst + ncol], src, ws_sem[i], reads=[wtk[(wname, j)]], writes=[ws_tk[i]])
            return wslab[i], ws_tk[i]

        def load_hc(c, buf, pp):
            src = hT_d[:, c * TT:(c + 1) * TT].rearrange("(k p) t -> p k t", p=128)
            P.dma('sp', hc[buf], src, hc_sem[buf], reads=[hT_tk[c]], writes=[hc_tk[buf]])
            if pp is not None:
                P.dma_acc(hc[buf], red_t[pp][c].ap().rearrange("(k p) t -> p k t", p=128), hc_sem[buf], reads=[red_tk[pp][c]], writes=[hc_tk[buf]])
                P.dma('pool', src, hc[buf], st_sem, reads=[hc_tk[buf]], writes=[hT_tk[c]])

        def store_hc(c, buf):
            dst = hT_d[:, c * TT:(c + 1) * TT].rearrange("(k p) t -> p k t", p=128)
            P.dma('pool', dst, hc[buf], st_sem, reads=[hc_tk[buf]], writes=[hT_tk[c]])

        def rmsnorm_chunk(buf, gname, hb):
            h = hc[buf]
            hn, hn_tk = hnb[hb], hnb_tk[hb]
            P.op('act', lambda e: e.activation(out=sq, in_=h, func=AF.Square), reads=[hc_tk[buf]], writes=[sq_tk])
            bk = banks[MISC]
            for kc in range(KC):
                P.op('pe', lambda e, kc=kc: e.matmul(bk[:, :], lhsT=ones_b, rhs=sq[:, kc, :], start=(kc == 0), stop=(kc == KC - 1)),
                     reads=[sq_tk, k_tk], writes=[bank_tk[MISC]])
            P.op('act', lambda e: e.activation(out=rstd, in_=bk[:, :], func=AF.Ln, scale=1.0 / D_MODEL, bias=epsc[:, 0:1]),
                 reads=[bank_tk[MISC], cst_tk, k_tk], writes=[rstd_tk])
            P.op('act', lambda e: e.activation(out=rstd, in_=rstd, func=AF.Exp, scale=-0.5), reads=[rstd_tk], writes=[rstd_tk], strict=True)
            g = C(gname)
            for kc in range(KC):
                P.op('dve', lambda e, kc=kc: e.scalar_tensor_tensor(out=hn[:, kc, :], in0=h[:, kc, :], scalar=g[:, kc:kc + 1], in1=rstd,
                                                                  op0=ALU.mult, op1=ALU.mult),
                     reads=[hc_tk[buf], rstd_tk, cst_tk], writes=[hn_tk])

        def proj_T(slab, slab_tk, col0, M, rhs_fn, bank, n=TT, extra_reads=()):
            for kc in range(KC):
                r_ = rhs_fn(kc)
                P.op('pe', lambda e, kc=kc, r_=r_: e.matmul(banks[bank][0:M, 0:n], lhsT=slab[:, kc, col0:col0 + M], rhs=r_,
                                                            start=(kc == 0), stop=(kc == KC - 1)),
                     reads=[slab_tk] + list(extra_reads), writes=[bank_tk[bank]])

        pending_finish = [None]

        def flush_finish():
            if pending_finish[0] is not None:
                f = pending_finish[0]
                pending_finish[0] = None
                f()

        def dense_out(wname, j, nfc, rhs_fn, rhs_tks, c, par, alt=False):
            flush_finish()
            for half in range(2):
                ngrp = (nfc + 3) // 4
                for gi in range(ngrp):
                    f0 = gi * 4
                    nf = min(4, nfc - f0)
                    i = w2_i[0] % NW2
                    w2_i[0] += 1
                    src = wb_d[wname][j, f0 * 128:(f0 + nf) * 128, half * 512:(half + 1) * 512].rearrange("(f p) n -> p f n", p=128)
                    P.dma('sp', w2slab[i][:, 0:nf, :], src, w2_sem[i], reads=[wtk[(wname, j)]], writes=[w2_tk[i]])
                    for f in range(nf):
                        fc = f0 + f
                        for dmi in range(4):
                            bk_ = (4 + dmi) if (alt and half == 1) else ACC[dmi]
                            r_ = rhs_fn(fc)
                            P.op('pe', lambda e, i=i, f=f, fc=fc, dmi=dmi, bk_=bk_, r_=r_: e.matmul(
                                banks[bk_][:, :], lhsT=w2slab[i][:, f, dmi * 128:(dmi + 1) * 128], rhs=r_,
                                start=(fc == 0), stop=(fc == nfc - 1)),
                                 reads=[w2_tk[i]] + list(rhs_tks), writes=[bank_tk[bk_]])
                for dmi in range(4):
                    kc = half * 4 + dmi
                    bk_ = (4 + dmi) if (alt and half == 1) else ACC[dmi]
                    evac_copy(pst[:, kc, :], banks[bk_][:, :], [bank_tk[bk_]], [pst_tk])
            def finish():
                P.dma('sp', part_t[par][c].ap().rearrange("(k p) t -> p k t", p=128), pst, pst_sem, reads=[pst_tk], writes=[part_tk[par][c]])
                P.cc(part_t[par][c].ap().opt(), red_t[par][c].ap().opt(), groups, c % 8, reads=[part_tk[par][c]], writes=[red_tk[par][c]])
            pending_finish[0] = finish

        def conv_silu(bank, M, K, wcol_fn, bcol, pc, pc_tk, acc, acc_tk, tails, tails_tk, out, out_reads, out_writes):
            H = K - 1
            P.op('act', lambda e: e.copy(out=pc[0:M, H:H + TT], in_=banks[bank][0:M, :]), reads=[bank_tk[bank]], writes=[pc_tk])
            P.op('pool', lambda e: e.tensor_copy(out=pc[0:M, 0:H], in_=tails[0:M, :]), reads=[tails_tk], writes=[pc_tk])
            P.op('dve', lambda e: e.tensor_scalar(out=acc[0:M, :], in0=pc[0:M, 0:TT], scalar1=wcol_fn(0), scalar2=bcol, op0=ALU.mult, op1=ALU.add),
                 reads=[pc_tk, cst_tk], writes=[acc_tk])
            for k in range(1, K):
                P.op('dve', lambda e, k=k: e.scalar_tensor_tensor(out=acc[0:M, :], in0=pc[0:M, k:k + TT], scalar=wcol_fn(k), in1=acc[0:M, :],
                                                                  op0=ALU.mult, op1=ALU.add),
                     reads=[pc_tk, cst_tk, acc_tk], writes=[acc_tk])
            P.op('pool', lambda e: e.tensor_copy(out=tails[0:M, :], in_=pc[0:M, TT:TT + H]), reads=[pc_tk], writes=[tails_tk])
            P.op('act', lambda e: e.activation(out=out, in_=acc[0:M, :], func=AF.Silu), reads=[acc_tk] + list(out_reads), writes=list(out_writes))

        epsc = AR.alloc([1], F32)
        P.op('dve', lambda e: e.memset(epsc, EPS), writes=[k_tk])
        eps64 = AR.alloc([1], F32)
        P.op('dve', lambda e: e.memset(eps64, 64 * EPS), writes=[k_tk])
        onec = AR.alloc([1], F32)
        P.op('dve', lambda e: e.memset(onec, 1.0), writes=[k_tk])
        bd64_b = AR.alloc([128], BF16)
        P.op('dve', lambda e: e.tensor_copy(out=bd64_b, in_=C('bd64')), reads=[cst_tk], writes=[k_tk])

        def phase_load():
            xl_sem = P.dsem("xload")
            for c in range(NCH):
                P.dma('sp', hT_d[:, c * TT:(c + 1) * TT], x_d[:, c * TT:(c + 1) * TT], xl_sem, writes=[hT_tk[c]])

        def phase_final():
            P.barrier()
            m = AR.mark()
            pp = pend[0]
            hob = [AR.alloc([KC, TT], F32) for _ in range(2)]
            hob_tk = [Tk("ho0"), Tk("ho1")]
            o_sem = P.dsem("out")
            g = C('fing')
            toks = []
            load_hc(0, 0, pp)
            for c in range(NCH):
                buf = c % 2
                if c + 1 < NCH:
                    load_hc(c + 1, 1 - buf, pp)
                h = hc[buf]
                ho, ho_tk = hob[buf], hob_tk[buf]
                P.op('act', lambda e, h=h: e.activation(out=sq, in_=h, func=AF.Square), reads=[hc_tk[buf]], writes=[sq_tk])
                bk = banks[MISC]
                for kc in range(KC):
                    P.op('pe', lambda e, kc=kc: e.matmul(bk[:, :], lhsT=ones_b, rhs=sq[:, kc, :], start=(kc == 0), stop=(kc == KC - 1)),
                         reads=[sq_tk, k_tk], writes=[bank_tk[MISC]])
                P.op('act', lambda e: e.activation(out=rstd, in_=bk[:, :], func=AF.Ln, scale=1.0 / D_MODEL, bias=epsc[:, 0:1]),
                     reads=[bank_tk[MISC], k_tk], writes=[rstd_tk])
                P.op('act', lambda e: e.activation(out=rstd, in_=rstd, func=AF.Exp, scale=-0.5), reads=[rstd_tk], writes=[rstd_tk], strict=True)
                for kc in range(KC):
                    P.op('dve', lambda e, kc=kc, h=h, ho=ho: e.scalar_tensor_tensor(out=ho[:, kc, :], in0=h[:, kc, :], scalar=g[:, kc:kc + 1], in1=rstd,
                                                                                    op0=ALU.mult, op1=ALU.mult),
                         reads=[hc_tk[buf], rstd_tk, cst_tk], writes=[ho_tk])
                t = P.dma('pool', out_d[:, c * TT:(c + 1) * TT].rearrange("(k p) t -> p k t", p=128), ho, o_sem, reads=[ho_tk])
                toks.append(t)
            P.finalize_wait('pool', [toks[-1]])
            AR.release(m)

        def phase_ffn(i):
            P.barrier()
            m = AR.mark()
            pp = pend[0]
            par = sub_i[0] % 2
            sub_i[0] += 1
            aT = AR.alloc([FFC, TT], BF16)
            aT_tk = [Tk("aT%d" % f) for f in range(FFC)]
            pcs = [AR.alloc([TT + 4], F32) for _ in range(2)]
            pc_tks = [Tk("pc0"), Tk("pc1")]
            accs = [AR.alloc([TT], F32) for _ in range(2)]
            acc_tks = [Tk("acc0"), Tk("acc1")]
            gs = [AR.alloc([TT], BF16) for _ in range(2)]
            gs_tks = [Tk("gs0"), Tk("gs1")]
            tails = AR.alloc([FFC, 2], F32)
            tails_tk = [Tk("tl%d" % f) for f in range(FFC)]
            P.op('pool', lambda e: e.memset(tails, 0.0), writes=tails_tk)
            fcw = C('fcw%d' % i)
            fcb = C('fcb%d' % i)
            GW = FFC * 128
            load_hc(0, 0, pp)
            rmsnorm_chunk(0, 'ffng%d' % i, 0)
            for c in range(NCH):
                buf = c % 2
                hn, hn_tk = hnb[buf], hnb_tk[buf]
                conv_piece(c)
                if c + 1 < NCH:
                    load_hc(c + 1, 1 - buf, pp)
                j0 = 0
                while j0 < FFC:
                    nj = min(2, FFC - j0)
                    slab, stk = load_ws('ffn_w_up', i, [(j0 * 128, nj * 128, 0), (GW + j0 * 128, nj * 128, 256)])
                    for u in range(nj):
                        j = j0 + u
                        pi = j % 2
                        b = next_rot()
                        proj_T(slab, stk, u * 128, 128, lambda kc: hn[:, kc, :], b, extra_reads=[hn_tk])
                        conv_silu(b, 128, 3, lambda k, j=j: fcw[:, j * 3 + k:j * 3 + k + 1], fcb[:, j:j + 1], pcs[pi], pc_tks[pi], accs[pi], acc_tks[pi],
                                  tails[:, j, :], tails_tk[j], gs[pi], [], [gs_tks[pi]])
                    for u in range(nj):
                        j = j0 + u
                        pi = j % 2
                        b = next_rot()
                        proj_T(slab, stk, 256 + u * 128, 128, lambda kc: hn[:, kc, :], b, extra_reads=[hn_tk])
                        P.op('dve', lambda e, j=j, pi=pi, b=b: e.tensor_tensor(out=aT[:, j, :], in0=gs[pi], in1=banks[b][:, :], op=ALU.mult),
                             reads=[gs_tks[pi], bank_tk[b]], writes=[aT_tk[j]])
                    j0 += nj
                    if j0 >= 4:
                        flush_finish()
                if c + 1 < NCH:
                    rmsnorm_chunk(1 - buf, 'ffng%d' % i, 1 - buf)
                dense_out('ffn_w_down', i, FFC, lambda fc: aT[:, fc, :], aT_tk, c, par)
            flush_finish()
            pend[0] = par
            AR.release(m)

        def phase_fox(jl, li):
            P.barrier()
            m = AR.mark()
            pp = pend[0]
            par = sub_i[0] % 2
            sub_i[0] += 1
            fw = 'fox_w_in'
            HW = FH * 64
            NPR = FH // 2
            qk_st = [AR.alloc([NPR, TT], BF16) for _ in range(2)]
            qk_tk = [Tk("qkst0"), Tk("qkst1")]
            qk_sem = [P.dsem("qkst0"), P.dsem("qkst1")]
            sqh = [AR.alloc([TT], BF16) for _ in range(2)]
            sqh_tk = [Tk("sqh0"), Tk("sqh1")]
            rh = [AR.alloc([TT], F32) for _ in range(2)]
            rh_tk = [Tk("rh0"), Tk("rh1")]
            vst = AR.alloc([4, FH, 65], BF16)
            vst_tk = Tk("vst")
            v_sem = P.dsem("vst")
            sgst = AR.alloc([HW // 128, TT], BF16)
            sgst_tk = Tk("sgst")
            sg_sem = P.dsem("sgst")
            wf = AR.alloc([KC, FH], BF16)
            wf_tk = Tk("wf")
            wf_sem = P.dsem("wf")
            ef = AR.alloc([TT], F32)
            sA = AR.alloc([TT], F32)
            sB = AR.alloc([TT], F32)
            f_tk = Tk("fchain")
            carry = AR.alloc([1], F32)
            r1 = AR.alloc([TT], F32)
            c3q = AR.alloc([3, TT], BF16)
            c3k = AR.alloc([3, TT], BF16)
            c3_tk = Tk("c3")
            c3_sem = P.dsem("c3")
            ones3 = AR.alloc([3, TT], BF16)
            o3_tk = Tk("ones3")
            o3_sem = P.dsem("o3")
            qa_tk = Tk("qa_d")
            ka_tk = Tk("ka_d")
            qa1_tk, qa2_tk, ka1_tk, ka2_tk = Tk("qa1"), Tk("qa2"), Tk("ka1"), Tk("ka2")
            vd_tk = Tk("v_d")
            sgd_tk = Tk("sg_d")
            otd_tk = Tk("ot_d")
            nones3 = AR.alloc([3, TT], BF16)
            onesF = AR.alloc([TT], F32)
            P.op('pool', lambda e: e.memset(ones3, 1.0), writes=[o3_tk])
            P.op('pool', lambda e: e.memset(nones3, -1.0), writes=[o3_tk])
            P.op('pool', lambda e: e.memset(onesF, 1.0), writes=[o3_tk])
            P.op('pool', lambda e: e.memset(carry, 0.0), writes=[f_tk])
            P.op('pool', lambda e: e.memset(vst, 1.0), writes=[vst_tk])
            for c in range(NCH):
                P.dma('sp', qa_d[:, 67:70, c * TT:(c + 1) * TT], ones3[0:FH], o3_sem, reads=[o3_tk], writes=[qa1_tk])
                P.dma('sp', ka_d[:, 64:67, c * TT:(c + 1) * TT], nones3[0:FH], o3_sem, reads=[o3_tk], writes=[ka1_tk])
            P.dma('sp', wf, wb_d[fw][jl, :, 4 * HW:4 * HW + FH].rearrange("(k p) n -> p k n", p=128), wf_sem, reads=[wtk[(fw, jl)]], writes=[wf_tk])
            qg = C('fqg%d' % jl)
            kg = C('fkg%d' % jl)
            nbf = C('fbf%d' % jl, FH)
            load_hc(0, 0, pp)
            rmsnorm_chunk(0, 'mixg%d' % li, 0)
            for c in range(NCH):
                buf = c % 2
                hn, hn_tk = hnb[buf], hnb_tk[buf]
                cols = slice(c * TT, (c + 1) * TT)
                conv_piece(c)
                if c + 1 < NCH:
                    load_hc(c + 1, 1 - buf, pp)
                bm = MISC
                proj_T(wf, wf_tk, 0, FH, lambda kc: hn[:, kc, :], bm, extra_reads=[hn_tk])
                P.op('dve', lambda e: e.tensor_scalar(out=ef[0:FH], in0=banks[bm][0:FH, :], scalar1=nbf[:, 0:1], scalar2=-1.0, op0=ALU.add, op1=ALU.mult),
                     reads=[bank_tk[bm], cst_tk, f_tk], writes=[f_tk])
                P.op('act', lambda e: e.activation(out=ef[0:FH], in_=ef[0:FH], func=AF.Exp), reads=[f_tk], writes=[f_tk])
                P.op('act', lambda e: e.activation(out=sA[0:FH], in_=ef[0:FH], func=AF.Ln, bias=onec[0:FH, 0:1]), reads=[f_tk, k_tk], writes=[f_tk], strict=True)
                P.op('dve', lambda e: e.tensor_tensor_scan(out=sB[0:FH], data0=onesF[0:FH], data1=sA[0:FH], initial=carry[0:FH, 0:1], op0=ALU.mult, op1=ALU.add),
                     reads=[f_tk, o3_tk], writes=[f_tk], strict=True)
                P.op('dve', lambda e: e.tensor_copy(out=carry[0:FH], in_=sB[0:FH, TT - 1:TT]), reads=[f_tk], writes=[f_tk], strict=True)
                P.op('act', lambda e: e.copy(out=c3k[0:FH, 0, :], in_=sB[0:FH]), reads=[f_tk, c3_tk], writes=[c3_tk])
                P.op('dve', lambda e: e.tensor_tensor(out=r1[0:FH], in0=sB[0:FH], in1=c3k[0:FH, 0, :], op=ALU.subtract), reads=[f_tk, c3_tk], writes=[f_tk])
                P.op('act', lambda e: e.copy(out=c3k[0:FH, 1, :], in_=r1[0:FH]), reads=[f_tk, c3_tk], writes=[c3_tk])
                P.op('dve', lambda e: e.tensor_tensor(out=sA[0:FH], in0=r1[0:FH], in1=c3k[0:FH, 1, :], op=ALU.subtract), reads=[f_tk, c3_tk], writes=[f_tk])
                P.op('act', lambda e: e.copy(out=c3k[0:FH, 2, :], in_=sA[0:FH]), reads=[f_tk, c3_tk], writes=[c3_tk])
                P.dma('pool', qa_d[:, 64:67, cols], c3k[0:FH], c3_sem, reads=[c3_tk], writes=[qa2_tk])
                P.dma('pool', ka_d[:, 67:70, cols], c3k[0:FH], c3_sem, reads=[c3_tk], writes=[ka2_tk])
                tasks = []
                for which in range(2):
                    for pr in range(NPR):
                        tasks.append((which, pr))
                slabs = {}
                tb = {}

                def t_front(ti):
                    which, pr = tasks[ti]
                    if pr == 0:
                        slabs[which] = load_ws(fw, jl, [(which * HW, HW, 0)])
                    slab, stk = slabs[which]
                    b = next_rot()
                    tb[ti] = b
                    proj_T(slab, stk, pr * 128, 128, lambda kc: hn[:, kc, :], b, extra_reads=[hn_tk])

                def t_back(ti):
                    which, pr = tasks[ti]
                    b = tb[ti]
                    x_ = ti % 2
                    P.op('act', lambda e: e.activation(out=sqh[x_], in_=banks[b][:, :], func=AF.Square), reads=[bank_tk[b]], writes=[sqh_tk[x_]])
                    P.op('pe', lambda e: e.matmul(banks[MISC][:, :], lhsT=bd64_b, rhs=sqh[x_], start=True, stop=True),
                         reads=[sqh_tk[x_], k_tk], writes=[bank_tk[MISC]])
                    if which == 0:
                        P.op('act', lambda e: e.activation(out=rh[x_], in_=banks[MISC][:, :], func=AF.Ln, scale=1.0, bias=eps64[:, 0:1]),
                             reads=[bank_tk[MISC], k_tk], writes=[rh_tk[x_]])
                    else:
                        P.op('act', lambda e: e.activation(out=rh[x_], in_=banks[MISC][:, :], func=AF.Ln, scale=1.0 / 64, bias=epsc[:, 0:1]),
                             reads=[bank_tk[MISC], k_tk], writes=[rh_tk[x_]])
                    P.op('act', lambda e: e.activation(out=rh[x_], in_=rh[x_], func=AF.Exp, scale=-0.5), reads=[rh_tk[x_]], writes=[rh_tk[x_]], strict=True)
                    gcol = qg if which == 0 else kg
                    P.op('dve', lambda e: e.scalar_tensor_tensor(out=qk_st[which][:, pr, :], in0=banks[b][:, :], scalar=gcol[:, 0:1], in1=rh[x_],
                                                                 op0=ALU.mult, op1=ALU.mult),
                         reads=[bank_tk[b], rh_tk[x_], cst_tk], writes=[qk_tk[which]])
                    if pr == NPR - 1:
                        dv = (qa_d if which == 0 else ka_d)[:, 0:64, cols].rearrange("(hp two) d t -> two d hp t", two=2)
                        for two in range(2):
                            P.dma('pool', dv[two], qk_st[which][two * 64:(two + 1) * 64], qk_sem[which], reads=[qk_tk[which]],
                                  writes=[qa_tk if which == 0 else ka_tk])

                for ti in range(len(tasks) + 1):
                    if ti < len(tasks):
                        t_front(ti)
                    if ti >= 1:
                        t_back(ti - 1)
                slab, stk = load_ws(fw, jl, [(2 * HW, HW, 0)])
                for jb in range(4):
                    b = next_rot()
                    for kc in range(KC):
                        P.op('pe', lambda e, kc=kc, jb=jb, b=b, slab=slab, hn=hn: e.matmul(banks[b][:, 0:HW], lhsT=hn[:, kc, jb * 128:(jb + 1) * 128], rhs=slab[:, kc, 0:HW],
                                                                                  start=(kc == 0), stop=(kc == KC - 1)),
                             reads=[stk, hn_tk], writes=[bank_tk[b]])
                    evac_copy(vst[:, jb, :, 0:64], banks[b][:, 0:HW].rearrange("p (h d) -> p h d", h=FH), [bank_tk[b]], [vst_tk])
                P.dma('pool', v_d[cols, :].rearrange("(j p) e -> p j e", p=128), vst.rearrange("p j h e -> p j (h e)"), v_sem, reads=[vst_tk], writes=[vd_tk])
                slab, stk = load_ws(fw, jl, [(3 * HW, HW, 0)])
                if c + 1 < NCH:
                    rmsnorm_chunk(1 - buf, 'mixg%d' % li, 1 - buf)
                for u in range(HW // 128):
                    b = next_rot()
                    proj_T(slab, stk, u * 128, 128, lambda kc: hn[:, kc, :], b, extra_reads=[hn_tk])
                    P.op('act', lambda e, b=b, u=u: e.activation(out=sgst[:, u, :], in_=banks[b][:, :], func=AF.Sigmoid),
                         reads=[bank_tk[b]], writes=[sgst_tk])
                P.dma('pool', sg_d[:, cols].rearrange("(k p) t -> p k t", p=128), sgst, sg_sem, reads=[sgst_tk], writes=[sgd_tk])
            AR.release(m)

            P.barrier()
            m = AR.mark()
            NB = seq // 128
            NQH = max(1, seq // 2048)
            QW = seq // NQH
            NBK = QW // 512
            Ka = [AR.alloc([seq], BF16) for _ in range(2)]
            Qa = [AR.alloc([seq], BF16) for _ in range(2)]
            Vh = [AR.alloc([NB, 65], BF16) for _ in range(2)]
            kqv_tk = [Tk("kqv0"), Tk("kqv1")]
            kqv_sem = [P.dsem("kqv0"), P.dsem("kqv1")]
            PT = [AR.alloc([512], BF16) for _ in range(3)]
            pt_tk = [Tk("pt%d" % i) for i in range(3)]
            pt_i = 0
            rrow = AR.alloc([512], F32)
            rrow_tk = Tk("rrow")
            bcs = AR.alloc([512], F32)
            bcs_tk = Tk("bcs")
            Ost = [AR.alloc([QW], BF16) for _ in range(2)]
            ost_tk = [Tk("ost0"), Tk("ost1")]
            ost_sem = [P.dsem("ost0"), P.dsem("ost1")]
            oi = 0
            NHL = FH

            def load_head(h, s):
                P.dma('sp', Ka[s][0:70], ka_d[h, :, :], kqv_sem[s], reads=[ka_tk, ka1_tk, ka2_tk], writes=[kqv_tk[s]])
                P.dma('sp', Qa[s][0:70], qa_d[h, :, :], kqv_sem[s], reads=[qa_tk, qa1_tk, qa2_tk], writes=[kqv_tk[s]])
                P.dma('sp', Vh[s], v_d[:, h * 65:(h + 1) * 65].rearrange("(kb p) e -> p kb e", p=128), kqv_sem[s], reads=[vd_tk], writes=[kqv_tk[s]])

            items = []
            for h in range(NHL):
                items.append(('load', h))
                for qh in range(NQH):
                    qb0 = qh * (QW // 128)
                    nqb = QW // 128
                    last_kb = qb0 + nqb - 1
                    for kb in range(last_kb + 1):
                        for a in range(NBK):
                            blk0 = qb0 + 4 * a
                            lo = max(kb, blk0)
                            hi = blk0 + 4
                            if lo >= hi:
                                continue
                            items.append(('step', h, kb, a, blk0, lo, (hi - lo) * 128))
                    items.append(('norm', h, qh))
            LOOK = 2
            st_bank = {}
            load_head(0, 0)

            def front(it):
                if it[0] != 'step':
                    return
                _, h, kb, a, blk0, lo, ncol = it
                s_ = h % 2
                diag = (lo == kb)
                b = next_rot()
                st_bank[id(it)] = b
                P.op('pe', lambda e: e.matmul(banks[b][:, 0:ncol], lhsT=Ka[s_][0:70, kb * 128:(kb + 1) * 128], rhs=Qa[s_][0:70, lo * 128:lo * 128 + ncol],
                                              start=True, stop=(not diag)), reads=[kqv_tk[s_]], writes=[bank_tk[b]])
                if diag:
                    P.op('pe', lambda e: e.matmul(banks[b][:, 0:128], lhsT=ident_b, rhs=mask_b[:, 0, :], start=False, stop=True),
                         reads=[k_tk], writes=[bank_tk[b]])

            def back(it):
                nonlocal pt_i, oi
                if it[0] == 'load':
                    if it[1] + 1 < NHL:
                        load_head(it[1] + 1, (it[1] + 1) % 2)
                    return
                if it[0] == 'step':
                    _, h, kb, a, blk0, lo, ncol = it
                    s_ = h % 2
                    b = st_bank.pop(id(it))
                    pi = pt_i % 3
                    pt_i += 1
                    P.op('act', lambda e: e.activation(out=PT[pi][:, 0:ncol], in_=banks[b][:, 0:ncol], func=AF.Exp),
                         reads=[bank_tk[b]], writes=[pt_tk[pi]])
                    c0 = (lo - blk0) * 128
                    lastk = (kb == blk0 + 3)
                    P.op('pe', lambda e: e.matmul(banks[ACC[a]][0:65, c0:c0 + ncol], lhsT=Vh[s_][:, kb, :], rhs=PT[pi][:, 0:ncol],
                                                  start=(kb == 0), stop=lastk, skip_group_check=True),
                         reads=[kqv_tk[s_], pt_tk[pi]], writes=[bank_tk[ACC[a]]])
                elif it[0] == 'norm':
                    _, h, qh = it
                    os_ = oi % 2
                    oi += 1
                    for a in range(NBK):
                        P.op('dve', lambda e, a=a: e.reciprocal(out=rrow[64:65, :], in_=banks[ACC[a]][64:65, :]), reads=[bank_tk[ACC[a]]], writes=[rrow_tk])
                        P.op('pe', lambda e: e.matmul(banks[MISC][0:64, :], lhsT=ones_f[64:65, 0:64], rhs=rrow[64:65, :], start=True, stop=True),
                             reads=[rrow_tk, k_tk], writes=[bank_tk[MISC]])
                        P.op('act', lambda e: e.copy(out=bcs[0:64], in_=banks[MISC][0:64, :]), reads=[bank_tk[MISC]], writes=[bcs_tk])
                        P.op('dve', lambda e, a=a, os_=os_: e.tensor_tensor(out=Ost[os_][0:64, a * 512:(a + 1) * 512], in0=banks[ACC[a]][0:64, :], in1=bcs[0:64], op=ALU.mult),
                             reads=[bank_tk[ACC[a]], bcs_tk], writes=[ost_tk[os_]])
                    P.dma('pool', ot_d[h * 64:(h + 1) * 64, qh * QW:(qh + 1) * QW], Ost[os_][0:64], ost_sem[os_], reads=[ost_tk[os_]], writes=[otd_tk])

            for i in range(len(items) + LOOK):
                if i < len(items):
                    front(items[i])
                if i - LOOK >= 0:
                    back(items[i - LOOK])
            AR.release(m)

            P.barrier()
            m = AR.mark()
            NFC = HW // 128
            oc = [AR.alloc([NFC, TT], BF16) for _ in range(2)]
            gc = [AR.alloc([NFC, TT], BF16) for _ in range(2)]
            og_tk = [Tk("og0"), Tk("og1")]
            og_sem = [P.dsem("og0"), P.dsem("og1")]
            yT = AR.alloc([NFC, TT], BF16)
            yT_tk = Tk("yT")

            def load_og(c, s):
                cols = slice(c * TT, (c + 1) * TT)
                P.dma('sp', oc[s], ot_d[:, cols].rearrange("(k p) t -> p k t", p=128), og_sem[s], reads=[otd_tk], writes=[og_tk[s]])
                P.dma('sp', gc[s], sg_d[:, cols].rearrange("(k p) t -> p k t", p=128), og_sem[s], reads=[sgd_tk], writes=[og_tk[s]])

            load_og(0, 0)
            for c in range(NCH):
                buf = c % 2
                if c + 1 < NCH:
                    load_og(c + 1, 1 - buf)
                for kc in range(NFC):
                    eng = 'pool' if kc % 2 else 'dve'
                    P.op(eng, lambda e, kc=kc, buf=buf: e.tensor_tensor(out=yT[:, kc, :], in0=oc[buf][:, kc, :], in1=gc[buf][:, kc, :], op=ALU.mult),
                         reads=[og_tk[buf]], writes=[yT_tk])
                dense_out('fox_w_out', jl, NFC, lambda fc: yT[:, fc, :], [yT_tk], c, par, alt=True)
            flush_finish()
            pend[0] = par
            AR.release(m)

        def phase_ssd(jl, li):
            P.barrier()
            m = AR.mark()
            pp = pend[0]
            par = sub_i[0] % 2
            sub_i[0] += 1
            sw = 'ssd_w_in'
            XW = SH * 64
            NHG = SH // 4
            zs = AR.alloc([4, XW], BF16)
            zs_tk = [Tk("zs%d" % j) for j in range(4)]
            xbc = AR.alloc([SNC, TT], BF16)
            xbc_tk = [Tk("xbc%d" % f) for f in range(SNC)]
            BOF, COF = SXC, SXC + SG
            pcs = [AR.alloc([TT + 4], F32) for _ in range(2)]
            pc_tks = [Tk("pc0"), Tk("pc1")]
            accs = [AR.alloc([TT], F32) for _ in range(2)]
            acc_tks = [Tk("acc0"), Tk("acc1")]
            tails = AR.alloc([SNC, 3], F32)
            tails_tk = [Tk("tl%d" % f) for f in range(SNC)]
            wdt = AR.alloc([KC, SH], BF16)
            wdt_tk = Tk("wdt")
            wdt_sem = P.dsem("wdt")
            dtT = AR.alloc([TT], F32)
            dtT_tk = Tk("dtT")
            arow = AR.alloc([SH], F32)
            arow_tk = Tk("arow")
            dtk = AR.alloc([4, SH], F32)
            dak = AR.alloc([4, SH], F32)
            dtk_tk = Tk("dtk")
            NSET = 2
            hst = AR.alloc([XW], F32)
            hst_tk = Tk("hst")
            hpbs = [AR.alloc([XW], BF16) for _ in range(3)]
            hpb_tks = [Tk("hpb0"), Tk("hpb1"), Tk("hpb2")]
            junk = AR.alloc([512], BF16)
            SETS = []
            for si in range(NSET):
                d = {}
                d['sm'] = AR.alloc([4, SH], F32); d['sm_tk'] = Tk("sm%d" % si)
                d['dcy'] = AR.alloc([SH, 128], BF16); d['dcy_tk'] = Tk("dcy%d" % si)
                d['cb'] = AR.alloc([SG, 128], BF16); d['cb_tk'] = Tk("cb%d" % si)
                d['xdt'] = AR.alloc([XW], BF16); d['xdt_tk'] = Tk("xdt%d" % si)
                d['xsk'] = AR.alloc([XW], BF16); d['xsk_tk'] = Tk("xsk%d" % si)
                d['Btk'] = AR.alloc([SG, 128], BF16); d['Btk_tk'] = Tk("Btk%d" % si)
                d['yy'] = AR.alloc([XW], F32); d['yy_tk'] = Tk("yy%d" % si)
                d['Dm'] = d['yy'].bitcast(BF16)[:, 0:2 * XW].rearrange("p (a b) -> p a b", a=SH); d['Dm_tk'] = d['yy_tk']
                d['ssq'] = AR.alloc([SG], F32); d['ssq_tk'] = Tk("ssq%d" % si)
                d['yn'] = AR.alloc([XW], BF16); d['yn_tk'] = Tk("yn%d" % si)
                d['xdw'] = d['yn']; d['xdw_tk'] = d['yn_tk']
                SETS.append(d)
            sng = AR.alloc([XW], F32)
            sng_tk = Tk("sng")
            sng_sem = P.dsem("sng")
            P.dma('sp', sng, sng_d[jl, :, :], sng_sem, writes=[sng_tk])
            dskr = C('dsk%d' % jl)
            scw = C('scw%d' % jl)
            scb = C('scb%d' % jl)
            dtb = C('dtb%d' % jl, SH)
            P.op('pool', lambda e: e.memset(tails, 0.0), writes=tails_tk)
            P.op('pool', lambda e: e.memset(hst, 0.0), writes=[hst_tk])
            P.op('pool', lambda e: e.memset(hpbs[0], 0.0), writes=[hpb_tks[0]])
            P.op('act', lambda e: e.activation(out=arow, in_=C('alog%d' % jl), func=AF.Exp), reads=[cst_tk], writes=[arow_tk])
            P.op('dve', lambda e: e.tensor_scalar(out=arow, in0=arow, scalar1=-1.0, scalar2=None, op0=ALU.mult), reads=[arow_tk], writes=[arow_tk])
            DTC = 2 * XW + 2 * SG * 128
            P.dma('sp', wdt, wb_d[sw][jl, :, DTC:DTC + SH].rearrange("(k p) n -> p k n", p=128), wdt_sem, reads=[wtk[(sw, jl)]], writes=[wdt_tk])
            load_hc(0, 0, pp)
            rmsnorm_chunk(0, 'mixg%d' % li, 0)
            for c in range(NCH):
                buf = c % 2
                hn, hn_tk = hnb[buf], hnb_tk[buf]
                conv_piece(c)
                if c + 1 < NCH:
                    load_hc(c + 1, 1 - buf, pp)
                proj_T(wdt, wdt_tk, 0, SH, lambda kc: hn[:, kc, :], MISC, extra_reads=[hn_tk])
                P.op('act', lambda e: e.activation(out=dtT[0:SH], in_=banks[MISC][0:SH, :], func=AF.Exp, bias=dtb[:, 0:1]),
                     reads=[bank_tk[MISC], cst_tk], writes=[dtT_tk])
                P.op('act', lambda e: e.activation(out=dtT[0:SH], in_=dtT[0:SH], func=AF.Ln, bias=onec[0:SH, 0:1]), reads=[dtT_tk, k_tk], writes=[dtT_tk], strict=True)
                for jb in range(4):
                    P.op('pe', lambda e, jb=jb: e.transpose(banks[MISC][:, jb * SH:(jb + 1) * SH], dtT[0:SH, jb * 128:(jb + 1) * 128], ident_f[0:SH, 0:SH]),
                         reads=[dtT_tk, cst_tk], writes=[bank_tk[MISC]])
                P.op('dve', lambda e: e.tensor_copy(out=dtk, in_=banks[MISC][:, 0:4 * SH].rearrange("p (j h) -> p j h", j=4)), reads=[bank_tk[MISC]], writes=[dtk_tk])
                for jb in range(4):
                    P.op('dve', lambda e, jb=jb: e.tensor_tensor(out=dak[:, jb, :], in0=dtk[:, jb, :], in1=arow, op=ALU.mult), reads=[dtk_tk, arow_tk], writes=[dtk_tk], strict=True)
                for sl in range(XW // 512):
                    slab, stk = load_ws(sw, jl, [(sl * 512, 512, 0)])
                    for jb in range(4):
                        b = next_rot()
                        for kc in range(KC):
                            P.op('pe', lambda e, kc=kc, jb=jb, b=b, slab=slab, hn=hn: e.matmul(banks[b][:, :], lhsT=hn[:, kc, jb * 128:(jb + 1) * 128], rhs=slab[:, kc, :],
                                                                                      start=(kc == 0), stop=(kc == KC - 1)),
                                 reads=[stk, hn_tk], writes=[bank_tk[b]])
                        P.op('act', lambda e, b=b, jb=jb, sl=sl: e.activation(out=zs[:, jb, sl * 512:(sl + 1) * 512], in_=banks[b][:, :], func=AF.Silu),
                             reads=[bank_tk[b]], writes=[zs_tk[jb]])
                flush_finish()
                for sl in range(SNC // 4):
                    slab, stk = load_ws(sw, jl, [(XW + sl * 512, 512, 0)])
                    for u in range(4):
                        f = sl * 4 + u
                        pi = f % 2
                        b = next_rot()
                        proj_T(slab, stk, u * 128, 128, lambda kc: hn[:, kc, :], b, extra_reads=[hn_tk])
                        conv_silu(b, 128, 4, lambda k, f=f: scw[:, f * 4 + k:f * 4 + k + 1], scb[:, f:f + 1], pcs[pi], pc_tks[pi], accs[pi], acc_tks[pi],
                                  tails[:, f, :], tails_tk[f], xbc[:, f, :], [], [xbc_tk[f]])
                def prep_stages(jb):
                    S = SETS[jb % NSET]
                    sm, sm_tk, Dm, Dm_tk, dcy, dcy_tk, cb, cb_tk = S['sm'], S['sm_tk'], S['Dm'], S['Dm_tk'], S['dcy'], S['dcy_tk'], S['cb'], S['cb_tk']
                    xdt, xdt_tk, xsk, xsk_tk, xdw, xdw_tk, Btk, Btk_tk = S['xdt'], S['xdt_tk'], S['xsk'], S['xsk_tk'], S['xdw'], S['xdw_tk'], S['Btk'], S['Btk_tk']
                    Mt, Mt_tk = dcy, dcy_tk
                    bc = slice(jb * 128, (jb + 1) * 128)
                    mo = (jb % NSET) * 4 * SH
                    st = []

                    def s_acs():
                        P.op('pe', lambda e: e.matmul(banks[MISC][:, mo:mo + SH], lhsT=C('U'), rhs=dak[:, jb, :], start=True, stop=True),
                             reads=[dtk_tk, cst_tk], writes=[bank_tk[MISC]])
                        P.op('pe', lambda e: e.matmul(banks[MISC][:, mo + SH:mo + 2 * SH], lhsT=C('Ubar'), rhs=dak[:, jb, :], start=True, stop=True),
                             reads=[dtk_tk, cst_tk], writes=[bank_tk[MISC]])
                        P.op('pe', lambda e: e.matmul(banks[MISC][:, mo + 2 * SH:mo + 3 * SH], lhsT=ones_f, rhs=dak[:, jb, :], start=True, stop=True),
                             reads=[dtk_tk, k_tk], writes=[bank_tk[MISC]])
                        P.op('act', lambda e: e.activation(out=sm[:, 0:3, :].rearrange("p a b -> p (a b)"), in_=banks[MISC][:, mo:mo + 3 * SH], func=AF.Exp),
                             reads=[bank_tk[MISC]], writes=[sm_tk])
                    st.append(s_acs)

                    def s_dm():
                        P.op('dve', lambda e: e.tensor_tensor(out=Dm, in0=U_b.unsqueeze(1).to_broadcast([128, SH, 128]),
                                                              in1=dak[:, jb, :].unsqueeze(2).to_broadcast([128, SH, 128]), op=ALU.mult),
                             reads=[dtk_tk, k_tk], writes=[Dm_tk])
                    st.append(s_dm)

                    def s_xT():
                        b = next_rot()
                        bkb = banks[b][:, :].bitcast(BF16)
                        for q in range(SXC):
                            P.op('pe', lambda e, q=q: e.transpose(bkb[:, q * 128:(q + 1) * 128], xbc[:, q, bc], ident_b),
                                 reads=[xbc_tk[q], k_tk], writes=[bank_tk[b]])
                        P.op('dve', lambda e: e.tensor_tensor(
                            out=xdt.rearrange("p (h d) -> p h d", h=SH), in0=bkb[:, 0:XW].rearrange("p (h d) -> p h d", h=SH),
                            in1=dtk[:, jb, :].unsqueeze(2).to_broadcast([128, SH, 64]), op=ALU.mult),
                             reads=[bank_tk[b], dtk_tk], writes=[xdt_tk])
                        P.op('dve', lambda e: e.tensor_tensor(
                            out=xsk.rearrange("p (h d) -> p h d", h=SH), in0=bkb[:, 0:XW].rearrange("p (h d) -> p h d", h=SH),
                            in1=dskr.unsqueeze(2).to_broadcast([128, SH, 64]), op=ALU.mult),
                             reads=[bank_tk[b], cst_tk], writes=[xsk_tk])
                    st.append(s_xT)

                    def mk_E(hg):
                        def s_E():
                            b = next_rot()
                            P.op('pe', lambda e: e.matmul(banks[b][:, :], lhsT=ones_b, rhs=Dm[:, hg * 4:(hg + 1) * 4, :].rearrange("p a b -> p (a b)"),
                                                          start=True, stop=False), reads=[Dm_tk, k_tk], writes=[bank_tk[b]])
                            for hh in range(4):
                                P.op('pe', lambda e, hh=hh: e.matmul(banks[b][:, hh * 128:(hh + 1) * 128], lhsT=Dm[:, hg * 4 + hh, :], rhs=nones_b,
                                                                     start=False, stop=False, skip_group_check=True), reads=[Dm_tk, k_tk], writes=[bank_tk[b]])
                            P.op('pe', lambda e: e.matmul(banks[b][:, :], lhsT=ident_b, rhs=mask_b.rearrange("p a b -> p (a b)"), start=False, stop=True, skip_group_check=True),
                                 reads=[k_tk], writes=[bank_tk[b]])
                            P.op('act', lambda e: e.activation(out=dcy[:, hg * 4:(hg + 1) * 4, :].rearrange("p a b -> p (a b)"), in_=banks[b][:, :], func=AF.Exp),
                                 reads=[bank_tk[b]], writes=[dcy_tk])
                        return s_E
                    for hg in range(NHG):
                        st.append(mk_E(hg))

                    def s_cb():
                        b = next_rot()
                        for g in range(SG):
                            P.op('pe', lambda e, g=g: e.matmul(banks[b][:, g * 128:(g + 1) * 128], lhsT=xbc[:, BOF + g, bc], rhs=xbc[:, COF + g, bc], start=True, stop=True),
                                 reads=[xbc_tk[BOF + g], xbc_tk[COF + g]], writes=[bank_tk[b]])
                        evac_copy(cb, banks[b][:, 0:SG * 128].rearrange("p (g l) -> p g l", g=SG), [bank_tk[b]], [cb_tk])
                        b2 = next_rot()
                        bkb2 = banks[b2][:, :].bitcast(BF16)
                        for g in range(SG):
                            P.op('pe', lambda e, g=g: e.transpose(bkb2[:, g * 128:(g + 1) * 128], xbc[:, BOF + g, bc], ident_b),
                                 reads=[xbc_tk[BOF + g], k_tk], writes=[bank_tk[b2]])
                        evac_copy(Btk, bkb2[:, 0:SG * 128].rearrange("p (g n) -> p g n", g=SG), [bank_tk[b2]], [Btk_tk])
                    st.append(s_cb)

                    def s_mt():
                        P.op('dve', lambda e: e.tensor_tensor(out=Mt.rearrange("p (g r) l -> p g r l", g=SG), in0=dcy.rearrange("p (g r) l -> p g r l", g=SG),
                                                              in1=cb.unsqueeze(2).to_broadcast([128, SG, 8, 128]), op=ALU.mult),
                             reads=[dcy_tk, cb_tk], writes=[Mt_tk])
                    st.append(s_mt)
                    return st

                def back_stages(jb):
                    S = SETS[jb % NSET]
                    sm, sm_tk, dcy, dcy_tk = S['sm'], S['sm_tk'], S['dcy'], S['dcy_tk']
                    xdt, xdt_tk, xsk, xsk_tk, xdw, xdw_tk, Btk, Btk_tk = S['xdt'], S['xdt_tk'], S['xsk'], S['xsk_tk'], S['xdw'], S['xdw_tk'], S['Btk'], S['Btk_tk']
                    yy, yy_tk, ssq, ssq_tk, yn, yn_tk = S['yy'], S['yy_tk'], S['ssq'], S['ssq_tk'], S['yn'], S['yn_tk']
                    Mt, Mt_tk = dcy, dcy_tk
                    gblk = c * 4 + jb
                    hin, hin_tk = hpbs[gblk % 3], hpb_tks[gblk % 3]
                    hout, hout_tk = hpbs[(gblk + 1) % 3], hpb_tks[(gblk + 1) % 3]
                    A0 = (jb % NSET) * SG
                    bc = slice(jb * 128, (jb + 1) * 128)
                    st = []

                    def s_yoff():
                        for g in range(SG):
                            P.op('pe', lambda e, g=g: e.matmul(banks[ACC[A0 + g]][:, :], lhsT=xbc[:, COF + g, bc], rhs=hin[:, g * 512:(g + 1) * 512], start=True, stop=True),
                                 reads=[xbc_tk[COF + g], hin_tk], writes=[bank_tk[ACC[A0 + g]]])
                            P.op('dve', lambda e, g=g: e.tensor_tensor(out=yy[:, g * 512:(g + 1) * 512].rearrange("p (h d) -> p h d", h=8),
                                                                       in0=banks[ACC[A0 + g]][:, :].rearrange("p (h d) -> p h d", h=8),
                                                                       in1=sm[:, 0, g * 8:(g + 1) * 8].unsqueeze(2).to_broadcast([128, 8, 64]), op=ALU.mult),
                                 reads=[bank_tk[ACC[A0 + g]], sm_tk], writes=[yy_tk])
                            P.op('pool', lambda e, g=g: e.tensor_tensor(out=yy[:, g * 512:(g + 1) * 512], in0=yy[:, g * 512:(g + 1) * 512], in1=xsk[:, g * 512:(g + 1) * 512], op=ALU.add),
                                 reads=[yy_tk, xsk_tk], writes=[yy_tk])
                        P.op('pool', lambda e: e.memset(ssq, 0.0), writes=[ssq_tk])
                    st.append(s_yoff)

                    def s_ydiag():
                        for h in range(SH):
                            g = h // 8
                            P.op('pe', lambda e, h=h, g=g: e.matmul(banks[ACC[A0 + g]][:, (h % 8) * 64:(h % 8 + 1) * 64], lhsT=Mt[:, h, :], rhs=xdt[:, h * 64:(h + 1) * 64],
                                                                    start=True, stop=True, skip_group_check=True),
                                 reads=[Mt_tk, xdt_tk], writes=[bank_tk[ACC[A0 + g]]])
                    st.append(s_ydiag)

                    def s_state():
                        P.op('pool', lambda e: e.tensor_tensor(out=xdw.rearrange("p (h d) -> p h d", h=SH), in0=xdt.rearrange("p (h d) -> p h d", h=SH),
                                                               in1=sm[:, 1, :].unsqueeze(2).to_broadcast([128, SH, 64]), op=ALU.mult),
                             reads=[xdt_tk, sm_tk], writes=[xdw_tk])
                        for g in range(SG):
                            b = next_rot()
                            gs_ = slice(g * 512, (g + 1) * 512)
                            P.op('pe', lambda e, b=b, g=g, gs_=gs_: e.matmul(banks[b][:, :], lhsT=Btk[:, g, :], rhs=xdw[:, gs_], start=True, stop=True),
                                 reads=[Btk_tk, xdw_tk], writes=[bank_tk[b]])
                            P.op('pool', lambda e, g=g, gs_=gs_: e.tensor_tensor(out=hst[:, gs_].rearrange("p (h d) -> p h d", h=8), in0=hst[:, gs_].rearrange("p (h d) -> p h d", h=8),
                                                                                in1=sm[:, 2, g * 8:(g + 1) * 8].unsqueeze(2).to_broadcast([128, 8, 64]), op=ALU.mult),
                                 reads=[hst_tk, sm_tk], writes=[hst_tk])
                            P.op('dve', lambda e, b=b, gs_=gs_: e.tensor_tensor(out=hst[:, gs_], in0=hst[:, gs_], in1=banks[b][:, :], op=ALU.add),
                                 reads=[bank_tk[b], hst_tk], writes=[hst_tk])
                            P.op('act', lambda e, gs_=gs_: e.copy(out=hout[:, gs_], in_=hst[:, gs_]), reads=[hst_tk], writes=[hout_tk])
                    st.insert(0, s_state)

                    def s_comb():
                        for g in range(SG):
                            gs_ = slice(g * 512, (g + 1) * 512)
                            P.op('dve', lambda e, g=g, gs_=gs_: e.tensor_tensor(out=yy[:, gs_], in0=yy[:, gs_], in1=banks[ACC[A0 + g]][:, :], op=ALU.add),
                                 reads=[bank_tk[ACC[A0 + g]], yy_tk], writes=[yy_tk])
                            P.op('dve', lambda e, gs_=gs_: e.tensor_tensor(out=yy[:, gs_], in0=yy[:, gs_], in1=zs[:, jb, gs_], op=ALU.mult),
                                 reads=[yy_tk, zs_tk[jb]], writes=[yy_tk])
                            P.op('act', lambda e, g=g, gs_=gs_: e.activation(out=junk, in_=yy[:, gs_], func=AF.Square, accum_out=ssq[:, g:g + 1]),
                                 reads=[yy_tk, ssq_tk], writes=[ssq_tk])
                    st.append(s_comb)

                    def s_norm():
                        P.op('act', lambda e: e.activation(out=ssq, in_=ssq, func=AF.Ln, scale=1.0 / 512, bias=epsc[:, 0:1]), reads=[ssq_tk, k_tk], writes=[ssq_tk])
                        P.op('act', lambda e: e.activation(out=ssq, in_=ssq, func=AF.Exp, scale=-0.5), reads=[ssq_tk], writes=[ssq_tk], strict=True)
                        for g in range(SG):
                            gs_ = slice(g * 512, (g + 1) * 512)
                            P.op('dve', lambda e, g=g, gs_=gs_: e.scalar_tensor_tensor(out=yn[:, gs_], in0=yy[:, gs_], scalar=ssq[:, g:g + 1], in1=sng[:, gs_],
                                                                                       op0=ALU.mult, op1=ALU.mult),
                                 reads=[yy_tk, ssq_tk, sng_tk], writes=[yn_tk], strict=True)
                    st.append(s_norm)

                    def s_ynT():
                        b = next_rot()
                        bkb3 = banks[b][:, :].bitcast(BF16)
                        for q in range(SXC):
                            P.op('pe', lambda e, q=q: e.transpose(bkb3[:, q * 128:(q + 1) * 128], yn[:, q * 128:(q + 1) * 128], ident_b),
                                 reads=[yn_tk, k_tk], writes=[bank_tk[b]])
                        evac_copy(xbc[:, 0:SXC, bc], bkb3[:, 0:XW].rearrange("p (f t) -> p f t", f=SXC), [bank_tk[b]], xbc_tk[0:SXC])
                    st.append(s_ynT)
                    return st

                def interleave(lists):
                    n = max(len(l) for l in lists)
                    for k in range(n):
                        for l in lists:
                            if k < len(l):
                                l[k]()

                for pr in range(2):
                    jbs = [2 * pr, 2 * pr + 1]
                    interleave([prep_stages(jb) for jb in jbs])
                    if pr == 1 and c + 1 < NCH:
                        rmsnorm_chunk(1 - buf, 'mixg%d' % li, 1 - buf)
                    interleave([back_stages(jb) for jb in jbs])
                dense_out('ssd_w_out', jl, SXC, lambda fc: xbc[:, fc, :], xbc_tk[0:SXC], c, par)
            flush_finish()
            pend[0] = par
            AR.release(m)

        phase_load()
        for li_, L in enumerate(layers):
            kind, j = L[:3], int(L[3:])
            cur_li[0] = li_
            if kind == 'ffn':
                phase_ffn(j)
            elif kind == 'fox':
                phase_fox(j, 2 * j + 1)
            elif kind == 'ssd':
                phase_ssd(j, 2 * j)
        phase_final()
        P.check_deadlock()
        block = es.enter_context(nc.Block())
        P.materialize(block)
        build_program.stats = {e: len(v) for e, v in P.ops.items()}
        build_program.stats['arena_peak'] = AR.peak
        build_program.stats['nsem'] = P.nds
    return nc


FULL_LAYERS = ['ssd0', 'ffn0', 'fox0', 'ffn1', 'ssd1', 'ffn2', 'fox1', 'ffn3']
_cache = {}


def kernel(**inputs):
    inp = {k: np.asarray(v) for k, v in inputs.items()}
    x = inp['x']
    B, S, D = x.shape
    groups = [[2 * b, 2 * b + 1] for b in range(4)]
    if 'nc' not in _cache:
        _cache['nc'] = build_program(S, FULL_LAYERS, groups)
    nc = _cache['nc']
    per_r = []
    for r in range(TP):
        cl, consts = build_consts(inp, r)
        w = slice_weights(inp, r)
        w['consts'] = consts
        per_r.append(w)
    in_maps = []
    for core in range(8):
        b, r = core // 2, core % 2
        m = dict(per_r[r])
        m['x'] = np.ascontiguousarray(x[b].T)
        in_maps.append(m)
    res = run_bass_kernel_spmd(nc, in_maps, core_ids=list(range(8)))
    out = np.stack([np.ascontiguousarray(np.asarray(res.results[2 * b]['out']).T) for b in range(B)], axis=0).astype(np.float32)
    return out
```

```python
import numpy as np
from contextlib import ExitStack
import concourse.bass as bass
import concourse.mybir as mybir
from concourse.bass_utils import run_bass_kernel_spmd

F32 = mybir.dt.float32
BF16 = mybir.dt.bfloat16
ALU = mybir.AluOpType
AF = mybir.ActivationFunctionType

D_MODEL = 1024
DEPTH = 4
EPS = 1e-6
SSD_D_INNER = 2048
SSD_HEADS = 32
SSD_CONV = 4
SSD_CONV_DIM = 3072
SSD_IN_DIM = 5152
FOX_HEADS = 16
FOX_D = 1024
FOX_IN_DIM = 4112
D_FF = 2816
FFN_CONV = 3
TT = 512
NEG = -30000.0
KC = 8
TP = 2
FH = FOX_HEADS // TP
FFC = (D_FF // 128) // TP
SH = SSD_HEADS // TP
SG = 4 // TP
SXC = SH * 64 // 128
SNC = SXC + 2 * SG

COMPUTE = ('pe', 'act', 'dve', 'pool')


class Tk:
    __slots__ = ('name', 'w', 'r')

    def __init__(self, name):
        self.name = name
        self.w = None
        self.r = {}


class DSem:
    def __init__(self, handle, name):
        self.h = handle
        self.name = name
        self.count = 0


class Prog:
    def __init__(self, nc, es):
        self.nc = nc
        self.es = es
        self.ops = {e: [] for e in ('pe', 'act', 'dve', 'pool', 'sp')}
        self.esem = {e: es.enter_context(nc.semaphore("s_" + e)) for e in COMPUTE}
        self.nds = 0
        self.dcache = {}
        self.out_deps = []

    def dsem(self, name):
        if name not in self.dcache:
            self.nds += 1
            self.dcache[name] = DSem(self.es.enter_context(self.nc.semaphore("d_%s_%d" % (name, self.nds))), name)
        return self.dcache[name]

    def _deps(self, eng, reads, writes, strict=False):
        deps = set()
        for t in reads:
            if t.w is not None:
                deps.add(t.w)
        for t in writes:
            if t.w is not None:
                deps.add(t.w)
            for v in t.r.values():
                deps.add(v)
        if strict:
            return deps
        return {d for d in deps if not (d[0] == 'E' and d[1] == eng)}

    def op(self, eng, fn, reads=(), writes=(), strict=False):
        deps = self._deps(eng, reads, writes, strict)
        idx = len(self.ops[eng])
        import sys as _s
        self.ops[eng].append({'fn': fn, 'deps': deps, 'inc': False, 'dma': None, 'note': _s._getframe(1).f_lineno})
        me = ('E', eng, idx)
        for t in reads:
            t.r[eng] = me
        for t in writes:
            t.w = me
            t.r = {}
        return me

    def dma(self, q, out, in_, sem, reads=(), writes=()):
        deps = self._deps(q, reads, writes, strict=True)
        sem.count += 16
        me = ('D', sem, sem.count)
        self.ops[q].append({'fn': (lambda e, o=out, i=in_: e.dma_start(out=o, in_=i)), 'deps': deps,
                            'inc': False, 'dma': sem, 'tok': me})
        for t in reads:
            t.r['D' + sem.name + str(id(sem))] = me
        for t in writes:
            t.w = me
            t.r = {}
        return me

    def cc(self, in_ap, out_ap, groups, slot, reads=(), writes=()):
        deps = self._deps('pool', reads, writes, strict=True)
        sem = self.dsem("cc%d" % slot)
        sem.count += 1
        me = ('D', sem, sem.count)

        def fn(e, i=in_ap, o=out_ap):
            return e.collective_compute("AllReduce", ALU.add, replica_groups=groups, ins=[i], outs=[o])
        self.ops['pool'].append({'fn': fn, 'deps': deps, 'inc': False, 'dma': None, 'cc': sem, 'tok': me})
        for t in reads:
            t.r['C' + str(id(sem))] = me
        for t in writes:
            t.w = me
            t.r = {}
        return me

    def dma_acc(self, out, in_, sem, reads=(), writes=()):
        deps = self._deps('pool', reads, writes, strict=True)
        sem.count += 16
        me = ('D', sem, sem.count)
        self.ops['pool'].append({'fn': (lambda e, o=out, i=in_: e.dma_start(out=o, in_=i, accum_op=ALU.add)), 'deps': deps,
                                 'inc': False, 'dma': sem, 'tok': me})
        for t in reads:
            t.r['D' + sem.name + str(id(sem))] = me
        for t in writes:
            t.w = me
            t.r = {}
        return me

    def barrier(self):
        toks = set()
        for e in COMPUTE:
            if self.ops[e]:
                for i in range(len(self.ops[e]) - 1, -1, -1):
                    if self.ops[e][i]['fn'] is not None and self.ops[e][i].get('tok') is None:
                        toks.add(('E', e, i))
                        break
        for sem in self.dcache.values():
            if sem.count > 0:
                toks.add(('D', sem, sem.count))
        for q in self.ops:
            deps = {d for d in toks if not (d[0] == 'E' and d[1] == q)}
            self.ops[q].append({'fn': None, 'deps': deps, 'inc': False, 'dma': None})

    def finalize_wait(self, q, toks):
        self.ops[q].append({'fn': None, 'deps': set(toks), 'inc': False, 'dma': None})

    def check_deadlock(self):
        pos = {e: 0 for e in self.ops}
        done = set()
        dtok = {}
        for e, lst in self.ops.items():
            cnts = {}
            for i, o in enumerate(lst):
                sem = o.get('cc') or o.get('dma')
                if sem is not None:
                    o.setdefault('_tok', None)
        progress = True
        while progress:
            progress = False
            for e, lst in self.ops.items():
                while pos[e] < len(lst):
                    o = lst[pos[e]]
                    ok = True
                    for d in o['deps']:
                        key = (d[0], d[1], d[2]) if d[0] == 'E' else ('D', id(d[1]), d[2])
                        if key not in done:
                            ok = False
                            break
                    if not ok:
                        break
                    done.add(('E', e, pos[e]))
                    if o.get('tok') is not None:
                        t = o['tok']
                        done.add(('D', id(t[1]), t[2]))
                    pos[e] += 1
                    progress = True
        stuck = {e: pos[e] for e in self.ops if pos[e] < len(self.ops[e])}
        if stuck:
            msg = []
            for e, p in stuck.items():
                o = self.ops[e][p]
                missing = []
                for d in o['deps']:
                    key = (d[0], d[1], d[2]) if d[0] == 'E' else ('D', id(d[1]), d[2])
                    if key not in done:
                        missing.append((d[0], d[1] if d[0] == 'E' else d[1].name, d[2]))
                msg.append("%s@%d/%d waits %s [%s]" % (e, p, len(self.ops[e]), missing, o.get('note')))
            raise RuntimeError("DEADLOCK: " + " | ".join(msg))

    def materialize(self, block):
        for e, lst in self.ops.items():
            for o in lst:
                for d in o['deps']:
                    if d[0] == 'E':
                        self.ops[d[1]][d[2]]['inc'] = True
        vals = {}
        for e in COMPUTE:
            c = 0
            for i, o in enumerate(self.ops[e]):
                if o['inc']:
                    c += 1
                    vals[(e, i)] = c
        self.vals = vals

        def run(eng_name, e):
            waited = {}
            for o in self.ops[eng_name]:
                for d in sorted(o['deps'], key=lambda d: (d[0], str(d[1]) if d[0] == 'E' else d[1].name, d[2])):
                    if d[0] == 'E':
                        key = ('E', d[1])
                        v = vals[(d[1], d[2])]
                        sh = self.esem[d[1]]
                    else:
                        key = ('D', id(d[1]))
                        v = d[2]
                        sh = d[1].h
                    if waited.get(key, 0) >= v:
                        continue
                    waited[key] = v
                    e.wait_ge(sh, v)
                if o['fn'] is None:
                    continue
                ins = o['fn'](e)
                if o.get('cc') is not None:
                    ins.then_inc(o['cc'].h)
                elif o['dma'] is not None:
                    ins.then_inc(o['dma'].h, 16)
                elif o['inc']:
                    ins.then_inc(self.esem[eng_name], 1)

        block.tensor(lambda e: run('pe', e))
        block.scalar(lambda e: run('act', e))
        block.vector(lambda e: run('dve', e))
        block.gpsimd(lambda e: run('pool', e))
        block.sync(lambda e: run('sp', e))


class Arena:
    def __init__(self, ap_f32, nwords):
        self.ap = ap_f32
        self.n = nwords
        self.top = 0
        self.peak = 0

    def mark(self):
        return self.top

    def release(self, m):
        self.top = m

    def alloc(self, shape, dtype, parts=128):
        n = int(np.prod(shape))
        words = n if dtype == F32 else (n + 1) // 2
        a = self.top
        self.top += words
        self.peak = max(self.peak, self.top)
        assert self.top <= self.n, "arena overflow %d > %d" % (self.top, self.n)
        v = self.ap[0:parts, a:a + words]
        if dtype != F32:
            v = v.bitcast(dtype)[:, 0:n]
        if len(shape) == 2:
            v = v.rearrange("p (a b) -> p a b", a=shape[0])
        elif len(shape) == 3:
            v = v.rearrange("p (a b c) -> p a b c", a=shape[0], b=shape[1])
        return v


class CL:
    def __init__(self):
        self.off = {}
        self.n = 0

    def add(self, name, width):
        self.off[name] = (self.n, width)
        self.n += width


def const_layout():
    c = CL()
    c.add('ident', 128)
    c.add('U', 128)
    c.add('maskT', 128)
    c.add('bd64', 128)
    c.add('Ubar', 128)
    for i in range(DEPTH):
        c.add('mixg%d' % i, KC)
        c.add('ffng%d' % i, KC)
        c.add('fcw%d' % i, FFC * 3)
        c.add('fcb%d' % i, FFC)
    c.add('fing', KC)
    for j in range(2):
        c.add('scw%d' % j, SNC * 4)
        c.add('scb%d' % j, SNC)
        c.add('dtb%d' % j, 1)
        c.add('alog%d' % j, SH)
        c.add('dsk%d' % j, SH)
        c.add('fbf%d' % j, 1)
        c.add('fqg%d' % j, 1)
        c.add('fkg%d' % j, 1)
    return c


def build_consts(inp, r):
    c = const_layout()
    A = np.zeros((128, c.n), np.float32)

    def put(name, arr):
        o, w = c.off[name]
        arr = np.asarray(arr, np.float32)
        A[:arr.shape[0], o:o + w] = arr.reshape(arr.shape[0], w)

    put('ident', np.eye(128))
    put('U', np.triu(np.ones((128, 128))))
    put('maskT', np.where(np.arange(128)[None, :] >= np.arange(128)[:, None], 0.0, NEG))
    put('bd64', np.kron(np.eye(2), np.ones((64, 64))))
    put('Ubar', np.tril(np.ones((128, 128)), -1))
    fsl = slice(r * FFC * 128, (r + 1) * FFC * 128)
    for i in range(DEPTH):
        put('mixg%d' % i, inp['mix_norm_g'][i].reshape(KC, 128).T)
        put('ffng%d' % i, inp['ffn_norm_g'][i].reshape(KC, 128).T)
        put('fcw%d' % i, inp['ffn_conv_w'][i][:, fsl].reshape(3, FFC, 128).transpose(2, 1, 0).reshape(128, FFC * 3))
        put('fcb%d' % i, inp['ffn_conv_b'][i][fsl].reshape(FFC, 128).T)
    put('fing', inp['final_norm_g'].reshape(KC, 128).T)
    xs_ = np.r_[r * SH * 64:(r + 1) * SH * 64, 2048 + r * SG * 128:2048 + (r + 1) * SG * 128, 2560 + r * SG * 128:2560 + (r + 1) * SG * 128]
    hsl = slice(r * SH, (r + 1) * SH)
    for j in range(2):
        put('scw%d' % j, inp['ssd_conv_w'][j][:, xs_].reshape(4, SNC, 128).transpose(2, 1, 0).reshape(128, SNC * 4))
        put('scb%d' % j, inp['ssd_conv_b'][j][xs_].reshape(SNC, 128).T)
        put('dtb%d' % j, inp['ssd_dt_bias'][j][hsl].reshape(SH, 1))
        put('alog%d' % j, np.broadcast_to(inp['ssd_a_log'][j][hsl][None, :], (128, SH)))
        put('dsk%d' % j, np.broadcast_to(inp['ssd_d'][j][hsl][None, :], (128, SH)))
        put('fbf%d' % j, inp['fox_b_f'][j][r * FH:(r + 1) * FH].reshape(FH, 1))
        put('fqg%d' % j, np.tile(inp['fox_q_norm_g'][j], 2).reshape(128, 1))
        put('fkg%d' % j, np.tile(inp['fox_k_norm_g'][j], 2).reshape(128, 1))
    return c, A


def slice_weights(inp, r):
    w = {}
    f0, f1 = r * FFC * 128, (r + 1) * FFC * 128
    w['ffn_w_up'] = np.ascontiguousarray(np.concatenate([inp['ffn_w_up'][:, :, f0:f1], inp['ffn_w_up'][:, :, D_FF + f0:D_FF + f1]], axis=2))
    w['ffn_w_down'] = np.ascontiguousarray(inp['ffn_w_down'][:, f0:f1, :])
    q0, q1 = r * FH * 64, (r + 1) * FH * 64
    fi = inp['fox_w_in']
    w['fox_w_in'] = np.ascontiguousarray(np.concatenate([fi[:, :, q0:q1], fi[:, :, 1024 + q0:1024 + q1], fi[:, :, 2048 + q0:2048 + q1],
                                                          fi[:, :, 3072 + q0:3072 + q1], fi[:, :, 4096 + r * FH:4096 + (r + 1) * FH]], axis=2))
    w['fox_w_out'] = np.ascontiguousarray(inp['fox_w_out'][:, q0:q1, :])
    x0, x1 = r * SH * 64, (r + 1) * SH * 64
    b0, b1 = r * SG * 128, (r + 1) * SG * 128
    si = inp['ssd_w_in']
    w['ssd_w_in'] = np.ascontiguousarray(np.concatenate([si[:, :, x0:x1], si[:, :, 2048 + x0:2048 + x1], si[:, :, 4096 + b0:4096 + b1],
                                                          si[:, :, 4608 + b0:4608 + b1], si[:, :, 5120 + r * SH:5120 + (r + 1) * SH]], axis=2))
    w['ssd_w_out'] = np.ascontiguousarray(inp['ssd_w_out'][:, x0:x1, :])
    w['sng'] = np.ascontiguousarray(np.broadcast_to(inp['ssd_norm_g'][:, None, x0:x1], (2, 128, SH * 64)), dtype=np.float32)
    return w


WNAMES = ['ssd_w_in', 'ssd_w_out', 'fox_w_in', 'fox_w_out', 'ffn_w_up', 'ffn_w_down']
SSD_LIN = 2 * SH * 64 + 2 * SG * 128 + SH
FOX_LIN = 4 * FH * 64 + FH
WSHAPES = {'ssd_w_in': (2, 1024, SSD_LIN), 'ssd_w_out': (2, SH * 64, 1024), 'fox_w_in': (2, 1024, FOX_LIN),
           'fox_w_out': (2, FH * 64, 1024), 'ffn_w_up': (4, 1024, 2 * FFC * 128), 'ffn_w_down': (4, FFC * 128, 1024)}


def build_program(seq, layers, groups, debug=None):
    NCH = seq // TT
    nc = bass.Bass("TRN2", target_bir_lowering=False)
    cl = const_layout()
    x_d = nc.dram_tensor("x", [D_MODEL, seq], F32, kind="ExternalInput").ap()
    c_d = nc.dram_tensor("consts", [128, cl.n], F32, kind="ExternalInput").ap()
    w_d = {n: nc.dram_tensor(n, list(WSHAPES[n]), F32, kind="ExternalInput").ap() for n in WNAMES}
    sng_d = nc.dram_tensor("sng", [2, 128, SH * 64], F32, kind="ExternalInput").ap()
    out_d = nc.dram_tensor("out", [D_MODEL, seq], F32, kind="ExternalOutput").ap()
    wb_d = {n: nc.dram_tensor(n + "_b", list(WSHAPES[n]), BF16).ap() for n in WNAMES}
    hT_d = nc.dram_tensor("hT_d", [D_MODEL, seq], F32).ap()
    part_t = [[nc.dram_tensor("part%d_%d" % (p, c), [D_MODEL, TT], F32) for c in range(NCH)] for p in range(2)]
    red_t = [[nc.dram_tensor("red%d_%d" % (p, c), [D_MODEL, TT], F32) for c in range(NCH)] for p in range(2)]
    qa_d = nc.dram_tensor("qa_d", [FH, 70, seq], BF16).ap()
    ka_d = nc.dram_tensor("ka_d", [FH, 70, seq], BF16).ap()
    v_d = nc.dram_tensor("v_d", [seq, FH * 65], BF16).ap()
    sg_d = nc.dram_tensor("sg_d", [FH * 64, seq], BF16).ap()
    ot_d = nc.dram_tensor("ot_d", [FH * 64, seq], BF16).ap()
    dbg_d = None
    if debug:
        dbg_d = nc.dram_tensor("dbg", list(debug), F32, kind="ExternalOutput").ap()

    es = ExitStack()
    with es:
        P = Prog(nc, es)
        NW = 48900
        arena_t = es.enter_context(nc.sbuf_tensor("arena", [128, NW], F32))
        AR = Arena(arena_t[:, :], NW)
        banks = [es.enter_context(nc.psum_tensor("bank%d" % i, [128, 512], F32)) for i in range(8)]
        bank_tk = [Tk("bank%d" % i) for i in range(8)]
        ACC = [0, 1, 2, 3]
        ROT = [4, 5, 6]
        MISC = 7
        rot_i = [0]

        def next_rot():
            b = ROT[rot_i[0] % 3]
            rot_i[0] += 1
            return b

        cst = AR.alloc([cl.n], F32)
        cst_tk = Tk("cst")
        s_c = P.dsem("cst")
        P.dma('sp', cst, c_d[:, :], s_c, writes=[cst_tk])

        def C(name, parts=128):
            o, w = cl.off[name]
            return cst[0:parts, o:o + w]

        ident_f = C('ident')
        ident_b = AR.alloc([128], BF16)
        ones_b = AR.alloc([128], BF16)
        nones_b = AR.alloc([128], BF16)
        ones_f = AR.alloc([128], F32)
        U_b = AR.alloc([128], BF16)
        mask_b = AR.alloc([4, 128], BF16)
        k_tk = Tk("konst")
        P.op('dve', lambda e: e.tensor_copy(out=ident_b, in_=ident_f), reads=[cst_tk], writes=[k_tk])
        P.op('dve', lambda e: e.memset(ones_b, 1.0), writes=[k_tk])
        P.op('dve', lambda e: e.memset(nones_b, -1.0), writes=[k_tk])
        P.op('dve', lambda e: e.memset(ones_f, 1.0), writes=[k_tk])
        P.op('dve', lambda e: e.tensor_copy(out=U_b, in_=C('U')), reads=[cst_tk], writes=[k_tk])
        for r in range(4):
            P.op('dve', lambda e, r=r: e.tensor_copy(out=mask_b[:, r, :], in_=C('maskT')), reads=[cst_tk], writes=[k_tk])

        wtk = {}
        used = set()
        for L in layers:
            kind, j = L[:3], int(L[3:])
            if kind == 'ssd':
                used |= {('ssd_w_in', j), ('ssd_w_out', j)}
            elif kind == 'fox':
                used |= {('fox_w_in', j), ('fox_w_out', j)}
            elif kind == 'ffn':
                used |= {('ffn_w_up', j), ('ffn_w_down', j)}
        order = []
        for L in layers:
            kind, j = L[:3], int(L[3:])
            names = {'ssd': ['ssd_w_in', 'ssd_w_out'], 'fox': ['fox_w_in', 'fox_w_out'], 'ffn': ['ffn_w_up', 'ffn_w_down']}[kind]
            for n in names:
                if (n, j) not in order:
                    order.append((n, j))
        for (n, j) in order:
            wtk[(n, j)] = Tk("w_%s_%d" % (n, j))
        conv_done = set()

        def issue_conv(li_, piece=None):
            if li_ >= len(layers):
                return
            L_ = layers[li_]
            kind_, j_ = L_[:3], int(L_[3:])
            for n in {'ssd': ['ssd_w_in', 'ssd_w_out'], 'fox': ['fox_w_in', 'fox_w_out'], 'ffn': ['ffn_w_up', 'ffn_w_down']}[kind_]:
                key_ = (n, j_, piece)
                if key_ in conv_done or (n, j_, None) in conv_done:
                    continue
                conv_done.add(key_)
                t = wtk[(n, j_)]
                sm_ = P.dsem("wc_%s_%d" % (n, j_))
                K = WSHAPES[n][1]
                if piece is None:
                    half = K // 2
                    P.dma('pool', wb_d[n][j_, 0:half, :], w_d[n][j_, 0:half, :], sm_, writes=[t])
                    P.dma('pool', wb_d[n][j_, half:K, :], w_d[n][j_, half:K, :], sm_, writes=[t])
                else:
                    r0 = (K * piece) // NCH
                    r1 = (K * (piece + 1)) // NCH
                    P.dma('pool', wb_d[n][j_, r0:r1, :], w_d[n][j_, r0:r1, :], sm_, writes=[t])

        cur_li = [0]

        def conv_piece(c):
            issue_conv(cur_li[0] + 1, piece=c)

        issue_conv(0)

        hc = [AR.alloc([KC, TT], F32) for _ in range(2)]
        hc_tk = [Tk("hc0"), Tk("hc1")]
        hc_sem = [P.dsem("hc0"), P.dsem("hc1")]
        hnb = [AR.alloc([KC, TT], BF16) for _ in range(2)]
        hnb_tk = [Tk("hn0"), Tk("hn1")]
        sq = AR.alloc([KC, TT], BF16)
        sq_tk = Tk("sq")
        rstd = AR.alloc([TT], F32)
        rstd_tk = Tk("rstd")
        pst = AR.alloc([KC, TT], F32)
        pst_tk = Tk("pst")
        pst_sem = P.dsem("pst")
        NWS = 3
        wslab = [AR.alloc([KC, 512], BF16) for _ in range(NWS)]
        ws_tk = [Tk("ws%d" % i) for i in range(NWS)]
        ws_sem = [P.dsem("ws%d" % i) for i in range(NWS)]
        ws_i = [0]
        NW2 = 2
        w2slab = [AR.alloc([4, 512], BF16) for _ in range(NW2)]
        w2_tk = [Tk("w2%d" % i) for i in range(NW2)]
        w2_sem = [P.dsem("w2%d" % i) for i in range(NW2)]
        w2_i = [0]
        hT_tk = [Tk("hTd%d" % c) for c in range(NCH)]
        part_tk = [[Tk("part%d_%d" % (p, c)) for c in range(NCH)] for p in range(2)]
        red_tk = [[Tk("red%d_%d" % (p, c)) for c in range(NCH)] for p in range(2)]
        st_sem = P.dsem("store")
        evac_i = [0]
        pend = [None]
        sub_i = [0]

        def evac_copy(out, in_, reads, writes):
            evac_i[0] += 1
            if evac_i[0] % 2:
                P.op('act', lambda e: e.copy(out=out, in_=in_), reads=reads, writes=writes)
            else:
                P.op('dve', lambda e: e.tensor_copy(out=out, in_=in_), reads=reads, writes=writes)

        def load_ws(wname, j, cols):
            i = ws_i[0] % NWS
            ws_i[0] += 1
            for (c0, ncol, dst) in cols:
                src = wb_d[wname][j, :, c0:c0 + ncol].rearrange("(k p) n -> p k n", p=128)
                P.dma('sp', wslab[i][:, :, dst:dst + ncol], src, ws_sem[i], reads=[wtk[(wname, j)]], writes=[ws_tk[i]])
            return wslab[i], ws_tk[i]

        def load_hc(c, buf, pp):
            src = hT_d[:, c * TT:(c + 1) * TT].rearrange("(k p) t -> p k t", p=128)
            if pp is None:
                xsrc = x_d[:, c * TT:(c + 1) * TT].rearrange("(k p) t -> p k t", p=128)
                P.dma('sp', hc[buf], xsrc, hc_sem[buf], writes=[hc_tk[buf]])
                P.dma('pool', src, hc[buf], st_sem, reads=[hc_tk[buf]], writes=[hT_tk[c]])
                return
            P.dma('sp', hc[buf], src, hc_sem[buf], reads=[hT_tk[c]], writes=[hc_tk[buf]])
            if pp is not None:
                P.dma_acc(hc[buf], red_t[pp][c].ap().rearrange("(k p) t -> p k t", p=128), hc_sem[buf], reads=[red_tk[pp][c]], writes=[hc_tk[buf]])
                P.dma('pool', src, hc[buf], st_sem, reads=[hc_tk[buf]], writes=[hT_tk[c]])

        def store_hc(c, buf):
            dst = hT_d[:, c * TT:(c + 1) * TT].rearrange("(k p) t -> p k t", p=128)
            P.dma('pool', dst, hc[buf], st_sem, reads=[hc_tk[buf]], writes=[hT_tk[c]])

        def rmsnorm_chunk(buf, gname, hb):
            h = hc[buf]
            hn, hn_tk = hnb[hb], hnb_tk[hb]
            P.op('act', lambda e: e.activation(out=sq, in_=h, func=AF.Square), reads=[hc_tk[buf]], writes=[sq_tk])
            bk = banks[MISC]
            for kc in range(KC):
                P.op('pe', lambda e, kc=kc: e.matmul(bk[:, :], lhsT=ones_b, rhs=sq[:, kc, :], start=(kc == 0), stop=(kc == KC - 1)),
                     reads=[sq_tk, k_tk], writes=[bank_tk[MISC]])
            P.op('act', lambda e: e.activation(out=rstd, in_=bk[:, :], func=AF.Ln, scale=1.0 / D_MODEL, bias=epsc[:, 0:1]),
                 reads=[bank_tk[MISC], cst_tk, k_tk], writes=[rstd_tk])
            P.op('act', lambda e: e.activation(out=rstd, in_=rstd, func=AF.Exp, scale=-0.5), reads=[rstd_tk], writes=[rstd_tk], strict=True)
            g = C(gname)
            for kc in range(KC):
                P.op('dve', lambda e, kc=kc: e.scalar_tensor_tensor(out=hn[:, kc, :], in0=h[:, kc, :], scalar=g[:, kc:kc + 1], in1=rstd,
                                                                  op0=ALU.mult, op1=ALU.mult),
                     reads=[hc_tk[buf], rstd_tk, cst_tk], writes=[hn_tk])

        def proj_T(slab, slab_tk, col0, M, rhs_fn, bank, n=TT, extra_reads=()):
            for kc in range(KC):
                r_ = rhs_fn(kc)
                P.op('pe', lambda e, kc=kc, r_=r_: e.matmul(banks[bank][0:M, 0:n], lhsT=slab[:, kc, col0:col0 + M], rhs=r_,
                                                            start=(kc == 0), stop=(kc == KC - 1)),
                     reads=[slab_tk] + list(extra_reads), writes=[bank_tk[bank]])

        pending_finish = [None]

        def flush_finish():
            if pending_finish[0] is not None:
                f = pending_finish[0]
                pending_finish[0] = None
                f()

        def dense_out(wname, j, nfc, rhs_fn, rhs_tks, c, par, alt=False):
            flush_finish()
            for half in range(2):
                ngrp = (nfc + 3) // 4
                for gi in range(ngrp):
                    f0 = gi * 4
                    nf = min(4, nfc - f0)
                    i = w2_i[0] % NW2
                    w2_i[0] += 1
                    src = wb_d[wname][j, f0 * 128:(f0 + nf) * 128, half * 512:(half + 1) * 512].rearrange("(f p) n -> p f n", p=128)
                    P.dma('sp', w2slab[i][:, 0:nf, :], src, w2_sem[i], reads=[wtk[(wname, j)]], writes=[w2_tk[i]])
                    for f in range(nf):
                        fc = f0 + f
                        for dmi in range(4):
                            bk_ = (4 + dmi) if (alt and half == 1) else ACC[dmi]
                            r_ = rhs_fn(fc)
                            P.op('pe', lambda e, i=i, f=f, fc=fc, dmi=dmi, bk_=bk_, r_=r_: e.matmul(
                                banks[bk_][:, :], lhsT=w2slab[i][:, f, dmi * 128:(dmi + 1) * 128], rhs=r_,
                                start=(fc == 0), stop=(fc == nfc - 1)),
                                 reads=[w2_tk[i]] + list(rhs_tks), writes=[bank_tk[bk_]])
                for dmi in range(4):
                    kc = half * 4 + dmi
                    bk_ = (4 + dmi) if (alt and half == 1) else ACC[dmi]
                    evac_copy(pst[:, kc, :], banks[bk_][:, :], [bank_tk[bk_]], [pst_tk])
            def finish():
                P.dma('sp', part_t[par][c].ap().rearrange("(k p) t -> p k t", p=128), pst, pst_sem, reads=[pst_tk], writes=[part_tk[par][c]])
                P.cc(part_t[par][c].ap().opt(), red_t[par][c].ap().opt(), groups, c % 8, reads=[part_tk[par][c]], writes=[red_tk[par][c]])
            pending_finish[0] = finish

        def conv_silu(bank, M, K, wcol_fn, bcol, pc, pc_tk, acc, acc_tk, tails, tails_tk, out, out_reads, out_writes):
            H = K - 1
            P.op('act', lambda e: e.copy(out=pc[0:M, H:H + TT], in_=banks[bank][0:M, :]), reads=[bank_tk[bank]], writes=[pc_tk])
            P.op('pool', lambda e: e.tensor_copy(out=pc[0:M, 0:H], in_=tails[0:M, :]), reads=[tails_tk], writes=[pc_tk])
            P.op('dve', lambda e: e.tensor_scalar(out=acc[0:M, :], in0=pc[0:M, 0:TT], scalar1=wcol_fn(0), scalar2=bcol, op0=ALU.mult, op1=ALU.add),
                 reads=[pc_tk, cst_tk], writes=[acc_tk])
            for k in range(1, K):
                P.op('dve', lambda e, k=k: e.scalar_tensor_tensor(out=acc[0:M, :], in0=pc[0:M, k:k + TT], scalar=wcol_fn(k), in1=acc[0:M, :],
                                                                  op0=ALU.mult, op1=ALU.add),
                     reads=[pc_tk, cst_tk, acc_tk], writes=[acc_tk])
            P.op('pool', lambda e: e.tensor_copy(out=tails[0:M, :], in_=pc[0:M, TT:TT + H]), reads=[pc_tk], writes=[tails_tk])
            P.op('act', lambda e: e.activation(out=out, in_=acc[0:M, :], func=AF.Silu), reads=[acc_tk] + list(out_reads), writes=list(out_writes))

        epsc = AR.alloc([1], F32)
        P.op('dve', lambda e: e.memset(epsc, EPS), writes=[k_tk])
        eps64 = AR.alloc([1], F32)
        P.op('dve', lambda e: e.memset(eps64, 64 * EPS), writes=[k_tk])
        onec = AR.alloc([1], F32)
        P.op('dve', lambda e: e.memset(onec, 1.0), writes=[k_tk])
        bd64_b = AR.alloc([128], BF16)
        P.op('dve', lambda e: e.tensor_copy(out=bd64_b, in_=C('bd64')), reads=[cst_tk], writes=[k_tk])

        def phase_load():
            pass

        def phase_final():
            P.barrier()
            m = AR.mark()
            pp = pend[0]
            hob = [AR.alloc([KC, TT], F32) for _ in range(2)]
            hob_tk = [Tk("ho0"), Tk("ho1")]
            o_sem = P.dsem("out")
            g = C('fing')
            toks = []
            load_hc(0, 0, pp)
            for c in range(NCH):
                buf = c % 2
                if c + 1 < NCH:
                    load_hc(c + 1, 1 - buf, pp)
                h = hc[buf]
                ho, ho_tk = hob[buf], hob_tk[buf]
                P.op('act', lambda e, h=h: e.activation(out=sq, in_=h, func=AF.Square), reads=[hc_tk[buf]], writes=[sq_tk])
                bk = banks[MISC]
                for kc in range(KC):
                    P.op('pe', lambda e, kc=kc: e.matmul(bk[:, :], lhsT=ones_b, rhs=sq[:, kc, :], start=(kc == 0), stop=(kc == KC - 1)),
                         reads=[sq_tk, k_tk], writes=[bank_tk[MISC]])
                P.op('act', lambda e: e.activation(out=rstd, in_=bk[:, :], func=AF.Ln, scale=1.0 / D_MODEL, bias=epsc[:, 0:1]),
                     reads=[bank_tk[MISC], k_tk], writes=[rstd_tk])
                P.op('act', lambda e: e.activation(out=rstd, in_=rstd, func=AF.Exp, scale=-0.5), reads=[rstd_tk], writes=[rstd_tk], strict=True)
                for kc in range(KC):
                    P.op('dve', lambda e, kc=kc, h=h, ho=ho: e.scalar_tensor_tensor(out=ho[:, kc, :], in0=h[:, kc, :], scalar=g[:, kc:kc + 1], in1=rstd,
                                                                                    op0=ALU.mult, op1=ALU.mult),
                         reads=[hc_tk[buf], rstd_tk, cst_tk], writes=[ho_tk])
                t = P.dma('pool', out_d[:, c * TT:(c + 1) * TT].rearrange("(k p) t -> p k t", p=128), ho, o_sem, reads=[ho_tk])
                toks.append(t)
            P.finalize_wait('pool', [toks[-1]])
            AR.release(m)

        def phase_ffn(i):
            P.barrier()
            m = AR.mark()
            pp = pend[0]
            par = sub_i[0] % 2
            sub_i[0] += 1
            aT = AR.alloc([FFC, TT], BF16)
            aT_tk = [Tk("aT%d" % f) for f in range(FFC)]
            pcs = [AR.alloc([TT + 4], F32) for _ in range(2)]
            pc_tks = [Tk("pc0"), Tk("pc1")]
            accs = [AR.alloc([TT], F32) for _ in range(2)]
            acc_tks = [Tk("acc0"), Tk("acc1")]
            gs = [AR.alloc([TT], BF16) for _ in range(2)]
            gs_tks = [Tk("gs0"), Tk("gs1")]
            tails = AR.alloc([FFC, 2], F32)
            tails_tk = [Tk("tl%d" % f) for f in range(FFC)]
            P.op('pool', lambda e: e.memset(tails, 0.0), writes=tails_tk)
            fcw = C('fcw%d' % i)
            fcb = C('fcb%d' % i)
            GW = FFC * 128
            load_hc(0, 0, pp)
            rmsnorm_chunk(0, 'ffng%d' % i, 0)
            for c in range(NCH):
                buf = c % 2
                hn, hn_tk = hnb[buf], hnb_tk[buf]
                conv_piece(c)
                if c + 1 < NCH:
                    load_hc(c + 1, 1 - buf, pp)
                j0 = 0
                while j0 < FFC:
                    nj = min(2, FFC - j0)
                    slab, stk = load_ws('ffn_w_up', i, [(j0 * 128, nj * 128, 0), (GW + j0 * 128, nj * 128, 256)])
                    for u in range(nj):
                        j = j0 + u
                        pi = j % 2
                        b = next_rot()
                        proj_T(slab, stk, u * 128, 128, lambda kc: hn[:, kc, :], b, extra_reads=[hn_tk])
                        conv_silu(b, 128, 3, lambda k, j=j: fcw[:, j * 3 + k:j * 3 + k + 1], fcb[:, j:j + 1], pcs[pi], pc_tks[pi], accs[pi], acc_tks[pi],
                                  tails[:, j, :], tails_tk[j], gs[pi], [], [gs_tks[pi]])
                    for u in range(nj):
                        j = j0 + u
                        pi = j % 2
                        b = next_rot()
                        proj_T(slab, stk, 256 + u * 128, 128, lambda kc: hn[:, kc, :], b, extra_reads=[hn_tk])
                        P.op('dve', lambda e, j=j, pi=pi, b=b: e.tensor_tensor(out=aT[:, j, :], in0=gs[pi], in1=banks[b][:, :], op=ALU.mult),
                             reads=[gs_tks[pi], bank_tk[b]], writes=[aT_tk[j]])
                    j0 += nj
                    if j0 >= 4:
                        flush_finish()
                if c + 1 < NCH:
                    rmsnorm_chunk(1 - buf, 'ffng%d' % i, 1 - buf)
                dense_out('ffn_w_down', i, FFC, lambda fc: aT[:, fc, :], aT_tk, c, par)
            flush_finish()
            pend[0] = par
            AR.release(m)

        def phase_fox(jl, li):
            P.barrier()
            m = AR.mark()
            pp = pend[0]
            par = sub_i[0] % 2
            sub_i[0] += 1
            fw = 'fox_w_in'
            HW = FH * 64
            NPR = FH // 2
            qk_st = [AR.alloc([NPR, TT], BF16) for _ in range(2)]
            qk_tk = [Tk("qkst0"), Tk("qkst1")]
            qk_sem = [P.dsem("qkst0"), P.dsem("qkst1")]
            sqh = [AR.alloc([TT], BF16) for _ in range(2)]
            sqh_tk = [Tk("sqh0"), Tk("sqh1")]
            rh = [AR.alloc([TT], F32) for _ in range(2)]
            rh_tk = [Tk("rh0"), Tk("rh1")]
            vst = AR.alloc([4, FH, 65], BF16)
            vst_tk = Tk("vst")
            v_sem = P.dsem("vst")
            sgst = AR.alloc([HW // 128, TT], BF16)
            sgst_tk = Tk("sgst")
            sg_sem = P.dsem("sgst")
            wf = AR.alloc([KC, FH], BF16)
            wf_tk = Tk("wf")
            wf_sem = P.dsem("wf")
            ef = AR.alloc([TT], F32)
            sA = AR.alloc([TT], F32)
            sB = AR.alloc([TT], F32)
            f_tk = Tk("fchain")
            carry = AR.alloc([1], F32)
            r1 = AR.alloc([TT], F32)
            c3q = AR.alloc([3, TT], BF16)
            c3k = AR.alloc([3, TT], BF16)
            c3_tk = Tk("c3")
            c3_sem = P.dsem("c3")
            ones3 = AR.alloc([3, TT], BF16)
            o3_tk = Tk("ones3")
            o3_sem = P.dsem("o3")
            qa_tk = Tk("qa_d")
            ka_tk = Tk("ka_d")
            qa1_tk, qa2_tk, ka1_tk, ka2_tk = Tk("qa1"), Tk("qa2"), Tk("ka1"), Tk("ka2")
            vd_tk = Tk("v_d")
            sgd_tk = Tk("sg_d")
            otd_tk = Tk("ot_d")
            nones3 = AR.alloc([3, TT], BF16)
            onesF = AR.alloc([TT], F32)
            P.op('pool', lambda e: e.memset(ones3, 1.0), writes=[o3_tk])
            P.op('pool', lambda e: e.memset(nones3, -1.0), writes=[o3_tk])
            P.op('pool', lambda e: e.memset(onesF, 1.0), writes=[o3_tk])
            P.op('pool', lambda e: e.memset(carry, 0.0), writes=[f_tk])
            P.op('pool', lambda e: e.memset(vst, 1.0), writes=[vst_tk])
            for c in range(NCH):
                P.dma('sp', qa_d[:, 67:70, c * TT:(c + 1) * TT], ones3[0:FH], o3_sem, reads=[o3_tk], writes=[qa1_tk])
                P.dma('sp', ka_d[:, 64:67, c * TT:(c + 1) * TT], nones3[0:FH], o3_sem, reads=[o3_tk], writes=[ka1_tk])
            P.dma('sp', wf, wb_d[fw][jl, :, 4 * HW:4 * HW + FH].rearrange("(k p) n -> p k n", p=128), wf_sem, reads=[wtk[(fw, jl)]], writes=[wf_tk])
            qg = C('fqg%d' % jl)
            kg = C('fkg%d' % jl)
            nbf = C('fbf%d' % jl, FH)
            load_hc(0, 0, pp)
            rmsnorm_chunk(0, 'mixg%d' % li, 0)
            for c in range(NCH):
                buf = c % 2
                hn, hn_tk = hnb[buf], hnb_tk[buf]
                cols = slice(c * TT, (c + 1) * TT)
                conv_piece(c)
                if c + 1 < NCH:
                    load_hc(c + 1, 1 - buf, pp)
                bm = MISC
                proj_T(wf, wf_tk, 0, FH, lambda kc: hn[:, kc, :], bm, extra_reads=[hn_tk])
                P.op('dve', lambda e: e.tensor_scalar(out=ef[0:FH], in0=banks[bm][0:FH, :], scalar1=nbf[:, 0:1], scalar2=-1.0, op0=ALU.add, op1=ALU.mult),
                     reads=[bank_tk[bm], cst_tk, f_tk], writes=[f_tk])
                P.op('act', lambda e: e.activation(out=ef[0:FH], in_=ef[0:FH], func=AF.Exp), reads=[f_tk], writes=[f_tk])
                P.op('act', lambda e: e.activation(out=sA[0:FH], in_=ef[0:FH], func=AF.Ln, bias=onec[0:FH, 0:1]), reads=[f_tk, k_tk], writes=[f_tk], strict=True)
                P.op('dve', lambda e: e.tensor_tensor_scan(out=sB[0:FH], data0=onesF[0:FH], data1=sA[0:FH], initial=carry[0:FH, 0:1], op0=ALU.mult, op1=ALU.add),
                     reads=[f_tk, o3_tk], writes=[f_tk], strict=True)
                P.op('dve', lambda e: e.tensor_copy(out=carry[0:FH], in_=sB[0:FH, TT - 1:TT]), reads=[f_tk], writes=[f_tk], strict=True)
                P.op('act', lambda e: e.copy(out=c3k[0:FH, 0, :], in_=sB[0:FH]), reads=[f_tk, c3_tk], writes=[c3_tk])
                P.op('dve', lambda e: e.tensor_tensor(out=r1[0:FH], in0=sB[0:FH], in1=c3k[0:FH, 0, :], op=ALU.subtract), reads=[f_tk, c3_tk], writes=[f_tk])
                P.op('act', lambda e: e.copy(out=c3k[0:FH, 1, :], in_=r1[0:FH]), reads=[f_tk, c3_tk], writes=[c3_tk])
                P.op('dve', lambda e: e.tensor_tensor(out=sA[0:FH], in0=r1[0:FH], in1=c3k[0:FH, 1, :], op=ALU.subtract), reads=[f_tk, c3_tk], writes=[f_tk])
                P.op('act', lambda e: e.copy(out=c3k[0:FH, 2, :], in_=sA[0:FH]), reads=[f_tk, c3_tk], writes=[c3_tk])
                P.dma('pool', qa_d[:, 64:67, cols], c3k[0:FH], c3_sem, reads=[c3_tk], writes=[qa2_tk])
                P.dma('pool', ka_d[:, 67:70, cols], c3k[0:FH], c3_sem, reads=[c3_tk], writes=[ka2_tk])
                tasks = []
                for which in range(2):
                    for pr in range(NPR):
                        tasks.append((which, pr))
                slabs = {}
                tb = {}

                def t_front(ti):
                    which, pr = tasks[ti]
                    if pr == 0:
                        slabs[which] = load_ws(fw, jl, [(which * HW, HW, 0)])
                    slab, stk = slabs[which]
                    b = next_rot()
                    tb[ti] = b
                    proj_T(slab, stk, pr * 128, 128, lambda kc: hn[:, kc, :], b, extra_reads=[hn_tk])

                def t_back(ti):
                    which, pr = tasks[ti]
                    b = tb[ti]
                    x_ = ti % 2
                    P.op('act', lambda e: e.activation(out=sqh[x_], in_=banks[b][:, :], func=AF.Square), reads=[bank_tk[b]], writes=[sqh_tk[x_]])
                    P.op('pe', lambda e: e.matmul(banks[MISC][:, :], lhsT=bd64_b, rhs=sqh[x_], start=True, stop=True),
                         reads=[sqh_tk[x_], k_tk], writes=[bank_tk[MISC]])
                    if which == 0:
                        P.op('act', lambda e: e.activation(out=rh[x_], in_=banks[MISC][:, :], func=AF.Ln, scale=1.0, bias=eps64[:, 0:1]),
                             reads=[bank_tk[MISC], k_tk], writes=[rh_tk[x_]])
                    else:
                        P.op('act', lambda e: e.activation(out=rh[x_], in_=banks[MISC][:, :], func=AF.Ln, scale=1.0 / 64, bias=epsc[:, 0:1]),
                             reads=[bank_tk[MISC], k_tk], writes=[rh_tk[x_]])
                    P.op('act', lambda e: e.activation(out=rh[x_], in_=rh[x_], func=AF.Exp, scale=-0.5), reads=[rh_tk[x_]], writes=[rh_tk[x_]], strict=True)
                    gcol = qg if which == 0 else kg
                    P.op('dve', lambda e: e.scalar_tensor_tensor(out=qk_st[which][:, pr, :], in0=banks[b][:, :], scalar=gcol[:, 0:1], in1=rh[x_],
                                                                 op0=ALU.mult, op1=ALU.mult),
                         reads=[bank_tk[b], rh_tk[x_], cst_tk], writes=[qk_tk[which]])
                    if pr == NPR - 1:
                        dv = (qa_d if which == 0 else ka_d)[:, 0:64, cols].rearrange("(hp two) d t -> two d hp t", two=2)
                        for two in range(2):
                            P.dma('pool', dv[two], qk_st[which][two * 64:(two + 1) * 64], qk_sem[which], reads=[qk_tk[which]],
                                  writes=[qa_tk if which == 0 else ka_tk])

                for ti in range(len(tasks) + 1):
                    if ti < len(tasks):
                        t_front(ti)
                    if ti >= 1:
                        t_back(ti - 1)
                slab, stk = load_ws(fw, jl, [(2 * HW, HW, 0)])
                for jb in range(4):
                    b = next_rot()
                    for kc in range(KC):
                        P.op('pe', lambda e, kc=kc, jb=jb, b=b, slab=slab, hn=hn: e.matmul(banks[b][:, 0:HW], lhsT=hn[:, kc, jb * 128:(jb + 1) * 128], rhs=slab[:, kc, 0:HW],
                                                                                  start=(kc == 0), stop=(kc == KC - 1)),
                             reads=[stk, hn_tk], writes=[bank_tk[b]])
                    evac_copy(vst[:, jb, :, 0:64], banks[b][:, 0:HW].rearrange("p (h d) -> p h d", h=FH), [bank_tk[b]], [vst_tk])
                P.dma('pool', v_d[cols, :].rearrange("(j p) e -> p j e", p=128), vst.rearrange("p j h e -> p j (h e)"), v_sem, reads=[vst_tk], writes=[vd_tk])
                slab, stk = load_ws(fw, jl, [(3 * HW, HW, 0)])
                if c + 1 < NCH:
                    rmsnorm_chunk(1 - buf, 'mixg%d' % li, 1 - buf)
                for u in range(HW // 128):
                    b = next_rot()
                    proj_T(slab, stk, u * 128, 128, lambda kc: hn[:, kc, :], b, extra_reads=[hn_tk])
                    P.op('act', lambda e, b=b, u=u: e.activation(out=sgst[:, u, :], in_=banks[b][:, :], func=AF.Sigmoid),
                         reads=[bank_tk[b]], writes=[sgst_tk])
                P.dma('pool', sg_d[:, cols].rearrange("(k p) t -> p k t", p=128), sgst, sg_sem, reads=[sgst_tk], writes=[sgd_tk])
            AR.release(m)

            P.barrier()
            m = AR.mark()
            NB = seq // 128
            NQH = max(1, seq // 2048)
            QW = seq // NQH
            NBK = QW // 512
            Ka = [AR.alloc([seq], BF16) for _ in range(2)]
            Qa = [AR.alloc([seq], BF16) for _ in range(2)]
            Vh = [AR.alloc([NB, 65], BF16) for _ in range(2)]
            kqv_tk = [Tk("kqv0"), Tk("kqv1")]
            kqv_sem = [P.dsem("kqv0"), P.dsem("kqv1")]
            PT = [AR.alloc([512], BF16) for _ in range(3)]
            pt_tk = [Tk("pt%d" % i) for i in range(3)]
            pt_i = 0
            rrow = AR.alloc([512], F32)
            rrow_tk = Tk("rrow")
            bcs = AR.alloc([512], F32)
            bcs_tk = Tk("bcs")
            Ost = [AR.alloc([QW], BF16) for _ in range(2)]
            ost_tk = [Tk("ost0"), Tk("ost1")]
            ost_sem = [P.dsem("ost0"), P.dsem("ost1")]
            oi = 0
            NHL = FH

            def load_head(h, s):
                P.dma('sp', Ka[s][0:70], ka_d[h, :, :], kqv_sem[s], reads=[ka_tk, ka1_tk, ka2_tk], writes=[kqv_tk[s]])
                P.dma('sp', Qa[s][0:70], qa_d[h, :, :], kqv_sem[s], reads=[qa_tk, qa1_tk, qa2_tk], writes=[kqv_tk[s]])
                P.dma('sp', Vh[s], v_d[:, h * 65:(h + 1) * 65].rearrange("(kb p) e -> p kb e", p=128), kqv_sem[s], reads=[vd_tk], writes=[kqv_tk[s]])

            items = []
            for h in range(NHL):
                items.append(('load', h))
                for qh in range(NQH):
                    qb0 = qh * (QW // 128)
                    nqb = QW // 128
                    last_kb = qb0 + nqb - 1
                    for kb in range(last_kb + 1):
                        for a in range(NBK):
                            blk0 = qb0 + 4 * a
                            lo = max(kb, blk0)
                            hi = blk0 + 4
                            if lo >= hi:
                                continue
                            items.append(('step', h, kb, a, blk0, lo, (hi - lo) * 128))
                    items.append(('norm', h, qh))
            LOOK = 2
            st_bank = {}
            load_head(0, 0)

            def front(it):
                if it[0] != 'step':
                    return
                _, h, kb, a, blk0, lo, ncol = it
                s_ = h % 2
                diag = (lo == kb)
                b = next_rot()
                st_bank[id(it)] = b
                P.op('pe', lambda e: e.matmul(banks[b][:, 0:ncol], lhsT=Ka[s_][0:70, kb * 128:(kb + 1) * 128], rhs=Qa[s_][0:70, lo * 128:lo * 128 + ncol],
                                              start=True, stop=(not diag)), reads=[kqv_tk[s_]], writes=[bank_tk[b]])
                if diag:
                    P.op('pe', lambda e: e.matmul(banks[b][:, 0:128], lhsT=ident_b, rhs=mask_b[:, 0, :], start=False, stop=True),
                         reads=[k_tk], writes=[bank_tk[b]])

            def back(it):
                nonlocal pt_i, oi
                if it[0] == 'load':
                    if it[1] + 1 < NHL:
                        load_head(it[1] + 1, (it[1] + 1) % 2)
                    return
                if it[0] == 'step':
                    _, h, kb, a, blk0, lo, ncol = it
                    s_ = h % 2
                    b = st_bank.pop(id(it))
                    pi = pt_i % 3
                    pt_i += 1
                    P.op('act', lambda e: e.activation(out=PT[pi][:, 0:ncol], in_=banks[b][:, 0:ncol], func=AF.Exp),
                         reads=[bank_tk[b]], writes=[pt_tk[pi]])
                    c0 = (lo - blk0) * 128
                    lastk = (kb == blk0 + 3)
                    P.op('pe', lambda e: e.matmul(banks[ACC[a]][0:65, c0:c0 + ncol], lhsT=Vh[s_][:, kb, :], rhs=PT[pi][:, 0:ncol],
                                                  start=(kb == 0), stop=lastk, skip_group_check=True),
                         reads=[kqv_tk[s_], pt_tk[pi]], writes=[bank_tk[ACC[a]]])
                elif it[0] == 'norm':
                    _, h, qh = it
                    os_ = oi % 2
                    oi += 1
                    for a in range(NBK):
                        P.op('dve', lambda e, a=a: e.reciprocal(out=rrow[64:65, :], in_=banks[ACC[a]][64:65, :]), reads=[bank_tk[ACC[a]]], writes=[rrow_tk])
                        P.op('pe', lambda e: e.matmul(banks[MISC][0:64, :], lhsT=ones_f[64:65, 0:64], rhs=rrow[64:65, :], start=True, stop=True),
                             reads=[rrow_tk, k_tk], writes=[bank_tk[MISC]])
                        P.op('act', lambda e: e.copy(out=bcs[0:64], in_=banks[MISC][0:64, :]), reads=[bank_tk[MISC]], writes=[bcs_tk])
                        P.op('dve', lambda e, a=a, os_=os_: e.tensor_tensor(out=Ost[os_][0:64, a * 512:(a + 1) * 512], in0=banks[ACC[a]][0:64, :], in1=bcs[0:64], op=ALU.mult),
                             reads=[bank_tk[ACC[a]], bcs_tk], writes=[ost_tk[os_]])
                    P.dma('pool', ot_d[h * 64:(h + 1) * 64, qh * QW:(qh + 1) * QW], Ost[os_][0:64], ost_sem[os_], reads=[ost_tk[os_]], writes=[otd_tk])

            for i in range(len(items) + LOOK):
                if i < len(items):
                    front(items[i])
                if i - LOOK >= 0:
                    back(items[i - LOOK])
            AR.release(m)

            P.barrier()
            m = AR.mark()
            NFC = HW // 128
            oc = [AR.alloc([NFC, TT], BF16) for _ in range(2)]
            gc = [AR.alloc([NFC, TT], BF16) for _ in range(2)]
            og_tk = [Tk("og0"), Tk("og1")]
            og_sem = [P.dsem("og0"), P.dsem("og1")]
            yT = AR.alloc([NFC, TT], BF16)
            yT_tk = Tk("yT")

            def load_og(c, s):
                cols = slice(c * TT, (c + 1) * TT)
                P.dma('sp', oc[s], ot_d[:, cols].rearrange("(k p) t -> p k t", p=128), og_sem[s], reads=[otd_tk], writes=[og_tk[s]])
                P.dma('sp', gc[s], sg_d[:, cols].rearrange("(k p) t -> p k t", p=128), og_sem[s], reads=[sgd_tk], writes=[og_tk[s]])

            load_og(0, 0)
            for c in range(NCH):
                buf = c % 2
                if c + 1 < NCH:
                    load_og(c + 1, 1 - buf)
                for kc in range(NFC):
                    eng = 'pool' if kc % 2 else 'dve'
                    P.op(eng, lambda e, kc=kc, buf=buf: e.tensor_tensor(out=yT[:, kc, :], in0=oc[buf][:, kc, :], in1=gc[buf][:, kc, :], op=ALU.mult),
                         reads=[og_tk[buf]], writes=[yT_tk])
                dense_out('fox_w_out', jl, NFC, lambda fc: yT[:, fc, :], [yT_tk], c, par, alt=True)
            flush_finish()
            pend[0] = par
            AR.release(m)

        def phase_ssd(jl, li):
            P.barrier()
            m = AR.mark()
            pp = pend[0]
            par = sub_i[0] % 2
            sub_i[0] += 1
            sw = 'ssd_w_in'
            XW = SH * 64
            NHG = SH // 4
            zs = AR.alloc([4, XW], BF16)
            zs_tk = [Tk("zs%d" % j) for j in range(4)]
            xbc = AR.alloc([SNC, TT], BF16)
            xbc_tk = [Tk("xbc%d" % f) for f in range(SNC)]
            BOF, COF = SXC, SXC + SG
            pcs = [AR.alloc([TT + 4], F32) for _ in range(2)]
            pc_tks = [Tk("pc0"), Tk("pc1")]
            accs = [AR.alloc([TT], F32) for _ in range(2)]
            acc_tks = [Tk("acc0"), Tk("acc1")]
            tails = AR.alloc([SNC, 3], F32)
            tails_tk = [Tk("tl%d" % f) for f in range(SNC)]
            wdt = AR.alloc([KC, SH], BF16)
            wdt_tk = Tk("wdt")
            wdt_sem = P.dsem("wdt")
            dtT = AR.alloc([TT], F32)
            dtT_tk = Tk("dtT")
            arow = AR.alloc([SH], F32)
            arow_tk = Tk("arow")
            dtk = AR.alloc([4, SH], F32)
            dak = AR.alloc([4, SH], F32)
            dtk_tk = Tk("dtk")
            NSET = 2
            hst = AR.alloc([XW], F32)
            hst_tk = Tk("hst")
            hpbs = [AR.alloc([XW], BF16) for _ in range(3)]
            hpb_tks = [Tk("hpb0"), Tk("hpb1"), Tk("hpb2")]
            junk = AR.alloc([512], BF16)
            SETS = []
            for si in range(NSET):
                d = {}
                d['sm'] = AR.alloc([4, SH], F32); d['sm_tk'] = Tk("sm%d" % si)
                d['dcy'] = AR.alloc([SH, 128], BF16); d['dcy_tk'] = Tk("dcy%d" % si)
                d['cb'] = AR.alloc([SG, 128], BF16); d['cb_tk'] = Tk("cb%d" % si)
                d['xdt'] = AR.alloc([XW], BF16); d['xdt_tk'] = Tk("xdt%d" % si)
                d['xsk'] = AR.alloc([XW], BF16); d['xsk_tk'] = Tk("xsk%d" % si)
                d['Btk'] = AR.alloc([SG, 128], BF16); d['Btk_tk'] = Tk("Btk%d" % si)
                d['yy'] = AR.alloc([XW], F32); d['yy_tk'] = Tk("yy%d" % si)
                d['Dm'] = d['yy'].bitcast(BF16)[:, 0:2 * XW].rearrange("p (a b) -> p a b", a=SH); d['Dm_tk'] = d['yy_tk']
                d['ssq'] = AR.alloc([SG], F32); d['ssq_tk'] = Tk("ssq%d" % si)
                d['yn'] = AR.alloc([XW], BF16); d['yn_tk'] = Tk("yn%d" % si)
                d['xdw'] = d['yn']; d['xdw_tk'] = d['yn_tk']
                SETS.append(d)
            sng = AR.alloc([XW], F32)
            sng_tk = Tk("sng")
            sng_sem = P.dsem("sng")
            P.dma('sp', sng, sng_d[jl, :, :], sng_sem, writes=[sng_tk])
            dskr = C('dsk%d' % jl)
            scw = C('scw%d' % jl)
            scb = C('scb%d' % jl)
            dtb = C('dtb%d' % jl, SH)
            P.op('pool', lambda e: e.memset(tails, 0.0), writes=tails_tk)
            P.op('pool', lambda e: e.memset(hst, 0.0), writes=[hst_tk])
            P.op('pool', lambda e: e.memset(hpbs[0], 0.0), writes=[hpb_tks[0]])
            P.op('act', lambda e: e.activation(out=arow, in_=C('alog%d' % jl), func=AF.Exp), reads=[cst_tk], writes=[arow_tk])
            P.op('dve', lambda e: e.tensor_scalar(out=arow, in0=arow, scalar1=-1.0, scalar2=None, op0=ALU.mult), reads=[arow_tk], writes=[arow_tk])
            DTC = 2 * XW + 2 * SG * 128
            P.dma('sp', wdt, wb_d[sw][jl, :, DTC:DTC + SH].rearrange("(k p) n -> p k n", p=128), wdt_sem, reads=[wtk[(sw, jl)]], writes=[wdt_tk])
            load_hc(0, 0, pp)
            rmsnorm_chunk(0, 'mixg%d' % li, 0)
            for c in range(NCH):
                buf = c % 2
                hn, hn_tk = hnb[buf], hnb_tk[buf]
                conv_piece(c)
                if c + 1 < NCH:
                    load_hc(c + 1, 1 - buf, pp)
                proj_T(wdt, wdt_tk, 0, SH, lambda kc: hn[:, kc, :], MISC, extra_reads=[hn_tk])
                P.op('act', lambda e: e.activation(out=dtT[0:SH], in_=banks[MISC][0:SH, :], func=AF.Exp, bias=dtb[:, 0:1]),
                     reads=[bank_tk[MISC], cst_tk], writes=[dtT_tk])
                P.op('act', lambda e: e.activation(out=dtT[0:SH], in_=dtT[0:SH], func=AF.Ln, bias=onec[0:SH, 0:1]), reads=[dtT_tk, k_tk], writes=[dtT_tk], strict=True)
                for jb in range(4):
                    P.op('pe', lambda e, jb=jb: e.transpose(banks[MISC][:, jb * SH:(jb + 1) * SH], dtT[0:SH, jb * 128:(jb + 1) * 128], ident_f[0:SH, 0:SH]),
                         reads=[dtT_tk, cst_tk], writes=[bank_tk[MISC]])
                P.op('dve', lambda e: e.tensor_copy(out=dtk, in_=banks[MISC][:, 0:4 * SH].rearrange("p (j h) -> p j h", j=4)), reads=[bank_tk[MISC]], writes=[dtk_tk])
                for jb in range(4):
                    P.op('dve', lambda e, jb=jb: e.tensor_tensor(out=dak[:, jb, :], in0=dtk[:, jb, :], in1=arow, op=ALU.mult), reads=[dtk_tk, arow_tk], writes=[dtk_tk], strict=True)
                for sl in range(XW // 512):
                    slab, stk = load_ws(sw, jl, [(sl * 512, 512, 0)])
                    for jb in range(4):
                        b = next_rot()
                        for kc in range(KC):
                            P.op('pe', lambda e, kc=kc, jb=jb, b=b, slab=slab, hn=hn: e.matmul(banks[b][:, :], lhsT=hn[:, kc, jb * 128:(jb + 1) * 128], rhs=slab[:, kc, :],
                                                                                      start=(kc == 0), stop=(kc == KC - 1)),
                                 reads=[stk, hn_tk], writes=[bank_tk[b]])
                        P.op('act', lambda e, b=b, jb=jb, sl=sl: e.activation(out=zs[:, jb, sl * 512:(sl + 1) * 512], in_=banks[b][:, :], func=AF.Silu),
                             reads=[bank_tk[b]], writes=[zs_tk[jb]])
                flush_finish()
                for sl in range(SNC // 4):
                    slab, stk = load_ws(sw, jl, [(XW + sl * 512, 512, 0)])
                    for u in range(4):
                        f = sl * 4 + u
                        pi = f % 2
                        b = next_rot()
                        proj_T(slab, stk, u * 128, 128, lambda kc: hn[:, kc, :], b, extra_reads=[hn_tk])
                        conv_silu(b, 128, 4, lambda k, f=f: scw[:, f * 4 + k:f * 4 + k + 1], scb[:, f:f + 1], pcs[pi], pc_tks[pi], accs[pi], acc_tks[pi],
                                  tails[:, f, :], tails_tk[f], xbc[:, f, :], [], [xbc_tk[f]])
                def prep_stages(jb):
                    S = SETS[jb % NSET]
                    sm, sm_tk, Dm, Dm_tk, dcy, dcy_tk, cb, cb_tk = S['sm'], S['sm_tk'], S['Dm'], S['Dm_tk'], S['dcy'], S['dcy_tk'], S['cb'], S['cb_tk']
                    xdt, xdt_tk, xsk, xsk_tk, xdw, xdw_tk, Btk, Btk_tk = S['xdt'], S['xdt_tk'], S['xsk'], S['xsk_tk'], S['xdw'], S['xdw_tk'], S['Btk'], S['Btk_tk']
                    Mt, Mt_tk = dcy, dcy_tk
                    bc = slice(jb * 128, (jb + 1) * 128)
                    mo = (jb % NSET) * 4 * SH
                    st = []

                    def s_acs():
                        P.op('pe', lambda e: e.matmul(banks[MISC][:, mo:mo + SH], lhsT=C('U'), rhs=dak[:, jb, :], start=True, stop=True),
                             reads=[dtk_tk, cst_tk], writes=[bank_tk[MISC]])
                        P.op('pe', lambda e: e.matmul(banks[MISC][:, mo + SH:mo + 2 * SH], lhsT=C('Ubar'), rhs=dak[:, jb, :], start=True, stop=True),
                             reads=[dtk_tk, cst_tk], writes=[bank_tk[MISC]])
                        P.op('pe', lambda e: e.matmul(banks[MISC][:, mo + 2 * SH:mo + 3 * SH], lhsT=ones_f, rhs=dak[:, jb, :], start=True, stop=True),
                             reads=[dtk_tk, k_tk], writes=[bank_tk[MISC]])
                        P.op('act', lambda e: e.activation(out=sm[:, 0:3, :].rearrange("p a b -> p (a b)"), in_=banks[MISC][:, mo:mo + 3 * SH], func=AF.Exp),
                             reads=[bank_tk[MISC]], writes=[sm_tk])
                    st.append(s_acs)

                    def s_dm():
                        P.op('dve', lambda e: e.tensor_tensor(out=Dm, in0=U_b.unsqueeze(1).to_broadcast([128, SH, 128]),
                                                              in1=dak[:, jb, :].unsqueeze(2).to_broadcast([128, SH, 128]), op=ALU.mult),
                             reads=[dtk_tk, k_tk], writes=[Dm_tk])
                    st.append(s_dm)

                    def s_xT():
                        b = next_rot()
                        bkb = banks[b][:, :].bitcast(BF16)
                        for q in range(SXC):
                            P.op('pe', lambda e, q=q: e.transpose(bkb[:, q * 128:(q + 1) * 128], xbc[:, q, bc], ident_b),
                                 reads=[xbc_tk[q], k_tk], writes=[bank_tk[b]])
                        P.op('dve', lambda e: e.tensor_tensor(
                            out=xdt.rearrange("p (h d) -> p h d", h=SH), in0=bkb[:, 0:XW].rearrange("p (h d) -> p h d", h=SH),
                            in1=dtk[:, jb, :].unsqueeze(2).to_broadcast([128, SH, 64]), op=ALU.mult),
                             reads=[bank_tk[b], dtk_tk], writes=[xdt_tk])
                        P.op('dve', lambda e: e.tensor_tensor(
                            out=xsk.rearrange("p (h d) -> p h d", h=SH), in0=bkb[:, 0:XW].rearrange("p (h d) -> p h d", h=SH),
                            in1=dskr.unsqueeze(2).to_broadcast([128, SH, 64]), op=ALU.mult),
                             reads=[bank_tk[b], cst_tk], writes=[xsk_tk])
                    st.append(s_xT)

                    def mk_E(hg):
                        def s_E():
                            b = next_rot()
                            P.op('pe', lambda e: e.matmul(banks[b][:, :], lhsT=ones_b, rhs=Dm[:, hg * 4:(hg + 1) * 4, :].rearrange("p a b -> p (a b)"),
                                                          start=True, stop=False), reads=[Dm_tk, k_tk], writes=[bank_tk[b]])
                            for hh in range(4):
                                P.op('pe', lambda e, hh=hh: e.matmul(banks[b][:, hh * 128:(hh + 1) * 128], lhsT=Dm[:, hg * 4 + hh, :], rhs=nones_b,
                                                                     start=False, stop=False, skip_group_check=True), reads=[Dm_tk, k_tk], writes=[bank_tk[b]])
                            P.op('pe', lambda e: e.matmul(banks[b][:, :], lhsT=ident_b, rhs=mask_b.rearrange("p a b -> p (a b)"), start=False, stop=True, skip_group_check=True),
                                 reads=[k_tk], writes=[bank_tk[b]])
                            P.op('act', lambda e: e.activation(out=dcy[:, hg * 4:(hg + 1) * 4, :].rearrange("p a b -> p (a b)"), in_=banks[b][:, :], func=AF.Exp),
                                 reads=[bank_tk[b]], writes=[dcy_tk])
                        return s_E
                    for hg in range(NHG):
                        st.append(mk_E(hg))

                    def s_cb():
                        b = next_rot()
                        for g in range(SG):
                            P.op('pe', lambda e, g=g: e.matmul(banks[b][:, g * 128:(g + 1) * 128], lhsT=xbc[:, BOF + g, bc], rhs=xbc[:, COF + g, bc], start=True, stop=True),
                                 reads=[xbc_tk[BOF + g], xbc_tk[COF + g]], writes=[bank_tk[b]])
                        evac_copy(cb, banks[b][:, 0:SG * 128].rearrange("p (g l) -> p g l", g=SG), [bank_tk[b]], [cb_tk])
                        b2 = next_rot()
                        bkb2 = banks[b2][:, :].bitcast(BF16)
                        for g in range(SG):
                            P.op('pe', lambda e, g=g: e.transpose(bkb2[:, g * 128:(g + 1) * 128], xbc[:, BOF + g, bc], ident_b),
                                 reads=[xbc_tk[BOF + g], k_tk], writes=[bank_tk[b2]])
                        evac_copy(Btk, bkb2[:, 0:SG * 128].rearrange("p (g n) -> p g n", g=SG), [bank_tk[b2]], [Btk_tk])
                    st.append(s_cb)

                    def s_mt():
                        P.op('dve', lambda e: e.tensor_tensor(out=Mt.rearrange("p (g r) l -> p g r l", g=SG), in0=dcy.rearrange("p (g r) l -> p g r l", g=SG),
                                                              in1=cb.unsqueeze(2).to_broadcast([128, SG, 8, 128]), op=ALU.mult),
                             reads=[dcy_tk, cb_tk], writes=[Mt_tk])
                    st.append(s_mt)
                    return st

                def back_stages(jb):
                    S = SETS[jb % NSET]
                    sm, sm_tk, dcy, dcy_tk = S['sm'], S['sm_tk'], S['dcy'], S['dcy_tk']
                    xdt, xdt_tk, xsk, xsk_tk, xdw, xdw_tk, Btk, Btk_tk = S['xdt'], S['xdt_tk'], S['xsk'], S['xsk_tk'], S['xdw'], S['xdw_tk'], S['Btk'], S['Btk_tk']
                    yy, yy_tk, ssq, ssq_tk, yn, yn_tk = S['yy'], S['yy_tk'], S['ssq'], S['ssq_tk'], S['yn'], S['yn_tk']
                    Mt, Mt_tk = dcy, dcy_tk
                    gblk = c * 4 + jb
                    hin, hin_tk = hpbs[gblk % 3], hpb_tks[gblk % 3]
                    hout, hout_tk = hpbs[(gblk + 1) % 3], hpb_tks[(gblk + 1) % 3]
                    A0 = (jb % NSET) * SG
                    bc = slice(jb * 128, (jb + 1) * 128)
                    st = []

                    def s_yoff():
                        for g in range(SG):
                            P.op('pe', lambda e, g=g: e.matmul(banks[ACC[A0 + g]][:, :], lhsT=xbc[:, COF + g, bc], rhs=hin[:, g * 512:(g + 1) * 512], start=True, stop=True),
                                 reads=[xbc_tk[COF + g], hin_tk], writes=[bank_tk[ACC[A0 + g]]])
                            P.op('dve', lambda e, g=g: e.tensor_tensor(out=yy[:, g * 512:(g + 1) * 512].rearrange("p (h d) -> p h d", h=8),
                                                                       in0=banks[ACC[A0 + g]][:, :].rearrange("p (h d) -> p h d", h=8),
                                                                       in1=sm[:, 0, g * 8:(g + 1) * 8].unsqueeze(2).to_broadcast([128, 8, 64]), op=ALU.mult),
                                 reads=[bank_tk[ACC[A0 + g]], sm_tk], writes=[yy_tk])
                            P.op('pool', lambda e, g=g: e.tensor_tensor(out=yy[:, g * 512:(g + 1) * 512], in0=yy[:, g * 512:(g + 1) * 512], in1=xsk[:, g * 512:(g + 1) * 512], op=ALU.add),
                                 reads=[yy_tk, xsk_tk], writes=[yy_tk])
                        P.op('pool', lambda e: e.memset(ssq, 0.0), writes=[ssq_tk])
                    st.append(s_yoff)

                    def s_ydiag():
                        for h in range(SH):
                            g = h // 8
                            P.op('pe', lambda e, h=h, g=g: e.matmul(banks[ACC[A0 + g]][:, (h % 8) * 64:(h % 8 + 1) * 64], lhsT=Mt[:, h, :], rhs=xdt[:, h * 64:(h + 1) * 64],
                                                                    start=True, stop=True, skip_group_check=True),
                                 reads=[Mt_tk, xdt_tk], writes=[bank_tk[ACC[A0 + g]]])
                    st.append(s_ydiag)

                    def s_state():
                        P.op('pool', lambda e: e.tensor_tensor(out=xdw.rearrange("p (h d) -> p h d", h=SH), in0=xdt.rearrange("p (h d) -> p h d", h=SH),
                                                               in1=sm[:, 1, :].unsqueeze(2).to_broadcast([128, SH, 64]), op=ALU.mult),
                             reads=[xdt_tk, sm_tk], writes=[xdw_tk])
                        for g in range(SG):
                            b = next_rot()
                            gs_ = slice(g * 512, (g + 1) * 512)
                            P.op('pe', lambda e, b=b, g=g, gs_=gs_: e.matmul(banks[b][:, :], lhsT=Btk[:, g, :], rhs=xdw[:, gs_], start=True, stop=True),
                                 reads=[Btk_tk, xdw_tk], writes=[bank_tk[b]])
                            P.op('pool', lambda e, g=g, gs_=gs_: e.tensor_tensor(out=hst[:, gs_].rearrange("p (h d) -> p h d", h=8), in0=hst[:, gs_].rearrange("p (h d) -> p h d", h=8),
                                                                                in1=sm[:, 2, g * 8:(g + 1) * 8].unsqueeze(2).to_broadcast([128, 8, 64]), op=ALU.mult),
                                 reads=[hst_tk, sm_tk], writes=[hst_tk])
                            P.op('dve', lambda e, b=b, gs_=gs_: e.tensor_tensor(out=hst[:, gs_], in0=hst[:, gs_], in1=banks[b][:, :], op=ALU.add),
                                 reads=[bank_tk[b], hst_tk], writes=[hst_tk])
                            P.op('act', lambda e, gs_=gs_: e.copy(out=hout[:, gs_], in_=hst[:, gs_]), reads=[hst_tk], writes=[hout_tk])
                    st.insert(0, s_state)

                    def s_comb():
                        for g in range(SG):
                            gs_ = slice(g * 512, (g + 1) * 512)
                            P.op('dve', lambda e, g=g, gs_=gs_: e.tensor_tensor(out=yy[:, gs_], in0=yy[:, gs_], in1=banks[ACC[A0 + g]][:, :], op=ALU.add),
                                 reads=[bank_tk[ACC[A0 + g]], yy_tk], writes=[yy_tk])
                            P.op('dve', lambda e, gs_=gs_: e.tensor_tensor(out=yy[:, gs_], in0=yy[:, gs_], in1=zs[:, jb, gs_], op=ALU.mult),
                                 reads=[yy_tk, zs_tk[jb]], writes=[yy_tk])
                            P.op('act', lambda e, g=g, gs_=gs_: e.activation(out=junk, in_=yy[:, gs_], func=AF.Square, accum_out=ssq[:, g:g + 1]),
                                 reads=[yy_tk, ssq_tk], writes=[ssq_tk])
                    st.append(s_comb)

                    def s_norm():
                        P.op('act', lambda e: e.activation(out=ssq, in_=ssq, func=AF.Ln, scale=1.0 / 512, bias=epsc[:, 0:1]), reads=[ssq_tk, k_tk], writes=[ssq_tk])
                        P.op('act', lambda e: e.activation(out=ssq, in_=ssq, func=AF.Exp, scale=-0.5), reads=[ssq_tk], writes=[ssq_tk], strict=True)
                        for g in range(SG):
                            gs_ = slice(g * 512, (g + 1) * 512)
                            P.op('dve', lambda e, g=g, gs_=gs_: e.scalar_tensor_tensor(out=yn[:, gs_], in0=yy[:, gs_], scalar=ssq[:, g:g + 1], in1=sng[:, gs_],
                                                                                       op0=ALU.mult, op1=ALU.mult),
                                 reads=[yy_tk, ssq_tk, sng_tk], writes=[yn_tk], strict=True)
                    st.append(s_norm)

                    def s_ynT():
                        b = next_rot()
                        bkb3 = banks[b][:, :].bitcast(BF16)
                        for q in range(SXC):
                            P.op('pe', lambda e, q=q: e.transpose(bkb3[:, q * 128:(q + 1) * 128], yn[:, q * 128:(q + 1) * 128], ident_b),
                                 reads=[yn_tk, k_tk], writes=[bank_tk[b]])
                        evac_copy(xbc[:, 0:SXC, bc], bkb3[:, 0:XW].rearrange("p (f t) -> p f t", f=SXC), [bank_tk[b]], xbc_tk[0:SXC])
                    st.append(s_ynT)
                    return st

                def interleave(lists):
                    n = max(len(l) for l in lists)
                    for k in range(n):
                        for l in lists:
                            if k < len(l):
                                l[k]()

                for pr in range(2):
                    jbs = [2 * pr, 2 * pr + 1]
                    interleave([prep_stages(jb) for jb in jbs])
                    if pr == 1 and c + 1 < NCH:
                        rmsnorm_chunk(1 - buf, 'mixg%d' % li, 1 - buf)
                    interleave([back_stages(jb) for jb in jbs])
                dense_out('ssd_w_out', jl, SXC, lambda fc: xbc[:, fc, :], xbc_tk[0:SXC], c, par)
            flush_finish()
            pend[0] = par
            AR.release(m)

        phase_load()
        for li_, L in enumerate(layers):
            kind, j = L[:3], int(L[3:])
            cur_li[0] = li_
            if kind == 'ffn':
                phase_ffn(j)
            elif kind == 'fox':
                phase_fox(j, 2 * j + 1)
            elif kind == 'ssd':
                phase_ssd(j, 2 * j)
        phase_final()
        P.check_deadlock()
        block = es.enter_context(nc.Block())
        P.materialize(block)
        build_program.stats = {e: len(v) for e, v in P.ops.items()}
        build_program.stats['arena_peak'] = AR.peak
        build_program.stats['nsem'] = P.nds
    return nc


FULL_LAYERS = ['ssd0', 'ffn0', 'fox0', 'ffn1', 'ssd1', 'ffn2', 'fox1', 'ffn3']
_cache = {}


def kernel(**inputs):
    inp = {k: np.asarray(v) for k, v in inputs.items()}
    x = inp['x']
    B, S, D = x.shape
    groups = [[2 * b, 2 * b + 1] for b in range(4)]
    if 'nc' not in _cache:
        _cache['nc'] = build_program(S, FULL_LAYERS, groups)
    nc = _cache['nc']
    per_r = []
    for r in range(TP):
        cl, consts = build_consts(inp, r)
        w = slice_weights(inp, r)
        w['consts'] = consts
        per_r.append(w)
    in_maps = []
    for core in range(8):
        b, r = core // 2, core % 2
        m = dict(per_r[r])
        m['x'] = np.ascontiguousarray(x[b].T)
        in_maps.append(m)
    res = run_bass_kernel_spmd(nc, in_maps, core_ids=list(range(8)))
    out = np.stack([np.ascontiguousarray(np.asarray(res.results[2 * b]['out']).T) for b in range(B)], axis=0).astype(np.float32)
    return out
```
